# Optimizing a Trainium2 kernel written in Bass

```python
import math
import jax, jax.numpy as jnp
from jax import lax
import numpy as np

D_MODEL = 4096
BATCH = 2
SEQ = 8192
DEPTH = 2

N_MIXERS = 2
NORM_EPS = 1e-6
MLA_HEADS = 32
Q_LORA = 1024
KV_LORA = 512
NOPE_DIM = 128
ROPE_DIM = 64
V_DIM = 128
QK_DIM = NOPE_DIM + ROPE_DIM
ROPE_THETA = 10000.0
Q_BLOCK = 128
MLA_WIDTH = MLA_HEADS * V_DIM
MLA_IN = Q_LORA + KV_LORA + ROPE_DIM + MLA_WIDTH
HG_EXPAND = 128
HG_HEADS = D_MODEL // HG_EXPAND
HG_K = HG_EXPAND
HG_V = HG_EXPAND
HG_WIDTH = HG_HEADS * HG_V
HG_IN = 2 * HG_HEADS * HG_K + 2 * HG_WIDTH
CHUNK = 64

kernel_name = "mla_hgrn2_interleaved_gated_trunk"


def rmsnorm(x, w):
    xf = x.astype(jnp.float32)
    y = xf * lax.rsqrt(jnp.mean(xf * xf, axis=-1, keepdims=True) + NORM_EPS)
    return (y * w.astype(jnp.float32)).astype(x.dtype)


def rope(t, positions):
    d = t.shape[-1]
    inv_freq = 1.0 / (ROPE_THETA ** (jnp.arange(0, d, 2, dtype=jnp.float32) / d))
    ang = positions.astype(jnp.float32)[..., None] * inv_freq
    cos = jnp.cos(ang)[:, :, None, :]
    sin = jnp.sin(ang)[:, :, None, :]
    tf = t.astype(jnp.float32)
    t1, t2 = tf[..., : d // 2], tf[..., d // 2:]
    out = jnp.concatenate([t1 * cos - t2 * sin, t2 * cos + t1 * sin], axis=-1)
    return out.astype(t.dtype)


def causal_block_attention(q, k, v, scale):
    B, S, H, D = q.shape
    nb = S // Q_BLOCK
    qb = q.reshape(B, nb, Q_BLOCK, H, D).transpose(1, 0, 2, 3, 4)
    kpos = jnp.arange(S)
    neg = jnp.finfo(jnp.float32).min

    def one_block(args):
        qi, blk = args
        s = jnp.einsum('bqhd,bkhd->bhqk', qi, k).astype(jnp.float32) * scale
        qpos = blk * Q_BLOCK + jnp.arange(Q_BLOCK)
        mask = kpos[None, :] <= qpos[:, None]
        s = jnp.where(mask[None, None], s, neg)
        p = jax.nn.softmax(s, axis=-1).astype(v.dtype)
        return jnp.einsum('bhqk,bkhd->bqhd', p, v)

    out = lax.map(one_block, (qb, jnp.arange(nb)))
    return out.transpose(1, 0, 2, 3, 4).reshape(B, S, H, v.shape[-1])


def mla_mixer(h, positions, w_in, q_norm, w_uq, kv_norm, w_ukv, w_o):
    B, S, _ = h.shape
    proj = h @ w_in
    c_q, c_kv, k_rot, z = jnp.split(
        proj, [Q_LORA, Q_LORA + KV_LORA, Q_LORA + KV_LORA + ROPE_DIM], axis=-1)
    q = (rmsnorm(c_q, q_norm) @ w_uq).reshape(B, S, MLA_HEADS, QK_DIM)
    q_nope, q_rot = q[..., :NOPE_DIM], rope(q[..., NOPE_DIM:], positions)
    k_rot = rope(k_rot[:, :, None, :], positions)
    kv = (rmsnorm(c_kv, kv_norm) @ w_ukv).reshape(B, S, MLA_HEADS, NOPE_DIM + V_DIM)
    k_nope, v = kv[..., :NOPE_DIM], kv[..., NOPE_DIM:]
    q_full = jnp.concatenate([q_nope, q_rot], axis=-1)
    k_full = jnp.concatenate(
        [k_nope, jnp.broadcast_to(k_rot, (B, S, MLA_HEADS, ROPE_DIM))], axis=-1)
    o = causal_block_attention(q_full, k_full, v, 1.0 / math.sqrt(QK_DIM))
    o = o.reshape(B, S, MLA_WIDTH) * jax.nn.silu(z)
    return o @ w_o


def hgrn2_chunk_recurrence(q, k, v, log_f):
    B, S, H, K = q.shape
    V = v.shape[-1]
    nc = S // CHUNK

    def to_chunks(t):
        return t.astype(jnp.float32).reshape(B, nc, CHUNK, H, t.shape[-1]).transpose(1, 0, 3, 2, 4)

    qc, kc, vc, gc = to_chunks(q), to_chunks(k), to_chunks(v), to_chunks(log_f)
    causal = jnp.tril(jnp.ones((CHUNK, CHUNK), dtype=bool))

    def step(state, xs):
        qi, ki, vi, gi = xs
        b = jnp.cumsum(gi, axis=-2)
        b_ref = b[..., CHUNK // 2:CHUNK // 2 + 1, :]
        b_end = b[..., -1:, :]
        att = jnp.einsum('bhck,bhsk->bhcs', qi * jnp.exp(b - b_ref), ki * jnp.exp(b_ref - b))
        att = jnp.where(causal, att, 0.0)
        o = (jnp.einsum('bhcs,bhsv->bhcv', att, vi)
             + jnp.einsum('bhck,bhkv->bhcv', qi * jnp.exp(b), state))
        new_state = (state * jnp.exp(b_end)[..., 0, :, None]
                     + jnp.einsum('bhsk,bhsv->bhkv', ki * jnp.exp(b_end - b), vi))
        return new_state, o

    init = jnp.zeros((B, H, K, V), dtype=jnp.float32)
    _, o = lax.scan(step, init, (qc, kc, vc, gc))
    return o.transpose(1, 0, 3, 2, 4).reshape(B, S, H, V).astype(v.dtype)


def hgrn2_mixer(h, lower_bound, w_in, g_norm, w_o):
    B, S, _ = h.shape
    fd = HG_HEADS * HG_K
    proj = h @ w_in
    q, f, i, z = jnp.split(proj, [fd, 2 * fd, 2 * fd + HG_WIDTH], axis=-1)
    q = jax.nn.silu(q)
    lb = lower_bound.astype(jnp.float32)
    forget = lb + (1.0 - lb) * jax.nn.sigmoid(f.astype(jnp.float32))
    k = (1.0 - forget).astype(h.dtype)
    log_f = jnp.log(forget)
    rs = lambda t: t.reshape(B, S, HG_HEADS, t.shape[-1] // HG_HEADS)
    o = hgrn2_chunk_recurrence(rs(q), rs(k), rs(i), rs(log_f))
    o = rmsnorm(o, g_norm).reshape(B, S, HG_WIDTH) * jax.nn.silu(z)
    return o @ w_o


def setup_inputs(seed: int = 0) -> dict:
    key = jax.random.key(seed)
    ks = jax.random.split(key, 20)
    nrm = lambda k, shape, fan_in: jax.random.normal(k, shape, jnp.float32) * (fan_in ** -0.5)
    gain = lambda k, n: 1.0 + 0.02 * jax.random.normal(k, (n,), jnp.float32)
    x = jax.random.normal(ks[0], (BATCH, SEQ, D_MODEL), jnp.float32)
    offsets = jax.random.randint(ks[1], (BATCH, 1), 0, 1024, dtype=jnp.int32)
    positions = (offsets + jnp.arange(SEQ, dtype=jnp.int32)[None, :]).astype(jnp.int32)
    return {
        "x": x,
        "positions": positions,
        "l0_norm": gain(ks[2], D_MODEL),
        "l0_w_in": nrm(ks[3], (D_MODEL, MLA_IN), D_MODEL),
        "l0_q_norm": gain(ks[4], Q_LORA),
        "l0_w_uq": nrm(ks[5], (Q_LORA, MLA_HEADS * QK_DIM), Q_LORA),
        "l0_kv_norm": gain(ks[6], KV_LORA),
        "l0_w_ukv": nrm(ks[7], (KV_LORA, MLA_HEADS * (NOPE_DIM + V_DIM)), KV_LORA),
        "l0_w_o": nrm(ks[8], (MLA_WIDTH, D_MODEL), MLA_WIDTH),
        "l1_norm": gain(ks[9], D_MODEL),
        "l1_w_in": nrm(ks[10], (D_MODEL, HG_IN), D_MODEL),
        "l1_g_norm": gain(ks[11], HG_V),
        "l1_w_o": nrm(ks[12], (HG_WIDTH, D_MODEL), HG_WIDTH),
        "lower_bounds": 0.1 * jax.random.normal(ks[13], (DEPTH, HG_HEADS * HG_K), jnp.float32),
        "final_norm": gain(ks[14], D_MODEL),
    }


def reference(x, positions, l0_norm, l0_w_in, l0_q_norm, l0_w_uq, l0_kv_norm, l0_w_ukv, l0_w_o,
              l1_norm, l1_w_in, l1_g_norm, l1_w_o, lower_bounds, final_norm):
    lb_soft = jax.nn.softmax(lower_bounds.astype(jnp.float32), axis=0)
    lb_all = jnp.cumsum(lb_soft, axis=0) - lb_soft[0]
    layers = [
        (l0_norm, (l0_w_in, l0_q_norm, l0_w_uq, l0_kv_norm, l0_w_ukv, l0_w_o)),
        (l1_norm, (l1_w_in, l1_g_norm, l1_w_o)),
    ]
    for i in range(DEPTH):
        norm_w, params = layers[i]
        h = rmsnorm(x, norm_w)
        if i % N_MIXERS == 0:
            delta = mla_mixer(h, positions, *params)
        else:
            delta = hgrn2_mixer(h, lb_all[i], *params)
        x = x + delta.astype(x.dtype)
    return rmsnorm(x, final_norm)
```

```python
import math
import numpy as np
from contextlib import ExitStack
import concourse.bass as bass
import concourse.mybir as mybir
from concourse.bass_utils import run_bass_kernel_spmd

F32 = mybir.dt.float32
BF16 = mybir.dt.bfloat16
I32 = mybir.dt.int32
AF = mybir.ActivationFunctionType
ALU = mybir.AluOpType

NCORES = 8
D = 4096
S = 8192
B = 2
TOK = 2048
NH = 32
EPS = 1e-6
EPOCH = 12000
TWO_PI = 2.0 * math.pi
CW1 = 6.28125
CW2 = TWO_PI - 6.28125
PI_LO = 3.1415925


def I(method, *args, **kw):
    return (method, args, kw)


class Lazy:
    def __init__(self, fn):
        self.fn = fn


class Buf:
    __slots__ = ("name", "w", "r", "sem", "cnt")

    def __init__(self, name=""):
        self.name = name
        self.w = None
        self.r = []
        self.sem = None
        self.cnt = 0


class Ring:
    def __init__(self, items):
        self.items = list(items)
        self.i = 0

    def next(self):
        it = self.items[self.i % len(self.items)]
        self.i += 1
        return it


class Slot:
    def __init__(self, t, name):
        self.t = t
        self.buf = Buf(name)


class Prog:
    ENGS = ("pe", "act", "dve", "pool", "sp")
    SEMUID = 0

    def __init__(self, nc, same_engine_sync=True):
        self.nc = nc
        self.es = ExitStack()
        self.sems = []
        self.dma_bufs = []
        self.lists = {e: [] for e in self.ENGS}
        self.cur_sem = {}
        self.cur_cnt = {}
        self.seen = {e: {} for e in self.ENGS}
        self.nsem = 0
        self.same_engine_sync = same_engine_sync
        self.ninst = 0
        for e in self.ENGS:
            self._new_epoch(e)

    def sem(self, name):
        self.nsem += 1
        Prog.SEMUID += 1
        h = self.nc.alloc_semaphore(name=f"{name}_{Prog.SEMUID}")
        self.sems.append(h)
        return h

    def _new_epoch(self, e):
        self.cur_sem[e] = self.sem(f"s_{e}")
        self.cur_cnt[e] = 0

    def sbuf(self, name, shape, dtype):
        Prog.SEMUID += 1
        return self.es.enter_context(self.nc.sbuf_tensor(f"{name}_u{Prog.SEMUID}", list(shape), dtype))

    def psum(self, name, shape, dtype=F32):
        Prog.SEMUID += 1
        return self.es.enter_context(self.nc.psum_tensor(f"{name}_u{Prog.SEMUID}", list(shape), dtype))

    def slot(self, name, shape, dtype):
        return Slot(self.sbuf(name, shape, dtype), name)

    def ring(self, name, n, shape, dtype):
        return Ring([self.slot(f"{name}{i}", shape, dtype) for i in range(n)])

    def pring(self, name, n):
        return Ring([Slot(self.psum(f"{name}{i}", [128, 512]), f"{name}{i}") for i in range(n)])

    def _deps(self, eng, reads, writes, own_sem=None):
        toks = []
        for b in reads:
            if b.w is not None:
                toks.append(b.w)
        for b in writes:
            if b.w is not None:
                toks.append(b.w)
            toks.extend(b.r)
        need = {}
        for (s, v, src) in toks:
            if src == eng and (eng in ("pe", "sp") or not self.same_engine_sync):
                continue
            if own_sem is not None and s is own_sem:
                continue
            k = id(s)
            if k not in need or need[k][1] < v:
                need[k] = (s, v)
        waits = []
        seen = self.seen[eng]
        for k, (s, v) in need.items():
            if seen.get(k, 0) >= v:
                continue
            seen[k] = v
            waits.append((s, v))
        return waits

    def _mark(self, tok, reads, writes):
        for b in reads:
            b.r.append(tok)
        for b in writes:
            b.w = tok
            b.r = []

    def op(self, eng, calls, reads=(), writes=()):
        if isinstance(calls, tuple):
            calls = [calls]
        waits = self._deps(eng, reads, writes)
        if self.cur_cnt[eng] >= EPOCH:
            self._new_epoch(eng)
        s = self.cur_sem[eng]
        self.cur_cnt[eng] += 1
        tok = (s, self.cur_cnt[eng], eng)
        self.lists[eng].append((waits, calls, s, 1))
        self._mark(tok, reads, writes)
        self.ninst += len(calls)
        return tok

    def dma(self, out, in_, reads=(), writes=(), queue="sp", owner=None):
        own = owner if owner is not None else (writes[0] if writes else reads[0])
        if own.sem is None:
            own.sem = self.sem("d_" + own.name)
            self.dma_bufs.append(own)
        waits = self._deps(queue, reads, writes, own_sem=own.sem)
        own.cnt += 16
        tok = (own.sem, own.cnt, "dma")
        self.lists[queue].append((waits, [I("dma_start", out=out, in_=in_)], own.sem, 16))
        self._mark(tok, reads, writes)
        self.ninst += 1
        return tok

    def finish(self, tokens):
        need = {}
        for (s, v, _) in tokens:
            k = id(s)
            if k not in need or need[k][1] < v:
                need[k] = (s, v)
        self.lists["sp"].append((list(need.values()), [], None, 0))

    def end_stage(self, last=False):
        need = []
        for e in self.ENGS:
            if e != "sp" and self.cur_cnt[e] > 0:
                need.append((self.cur_sem[e], self.cur_cnt[e]))
        for b in self.dma_bufs:
            need.append((b.sem, b.cnt))
        self.lists["sp"].append((need, [], None, 0))
        self.emit(close=False)
        nc = self.nc
        nc.all_engine_barrier()
        self.es.close()
        nc.clear_and_free_semaphores(self.sems)
        nc.all_engine_barrier()

    def emit(self, close=True):
        nc = self.nc
        lists = self.lists

        def replay(e, lst):
            for (waits, calls, s, inc) in lst:
                for (ws, wv) in waits:
                    e.wait_ge(ws, wv)
                inst = None
                for (m, a, k) in calls:
                    if any(isinstance(v, Lazy) for v in k.values()):
                        k = {kk: (v.fn(e) if isinstance(v, Lazy) else v) for kk, v in k.items()}
                    inst = getattr(e, m)(*a, **k)
                if inst is not None:
                    inst.then_inc(s, inc)

        with nc.Block() as block:
            @block.tensor
            def _(e):
                replay(e, lists["pe"])

            @block.scalar
            def _(e):
                replay(e, lists["act"])

            @block.vector
            def _(e):
                replay(e, lists["dve"])

            @block.gpsimd
            def _(e):
                replay(e, lists["pool"])

            @block.sync
            def _(e):
                replay(e, lists["sp"])
        if close:
            self.es.close()


class WStream:
    def __init__(self, P, nstage=2, nw=2, stage_elems=2048, w_elems=8192, cast_engs=("pool",)):
        self.P = P
        self.stage_elems = stage_elems
        self.stages = P.ring("wst", nstage, [128, stage_elems], F32)
        self.ws = P.ring("wbf", nw, [128, w_elems], BF16)
        self.cast = Ring(list(cast_engs))

    def load(self, W, k0, KC, segs, C):
        P = self.P
        slot = self.ws.next()
        wv = slot.t[:, 0:KC * C].rearrange("p (k c) -> p k c", c=C)
        kpp = max(1, self.stage_elems // C)
        for kc0 in range(0, KC, kpp):
            kn = min(kpp, KC - kc0)
            st = self.stages.next()
            stv = st.t[:, 0:kn * C].rearrange("p (k c) -> p k c", c=C)
            for (sc0, n, dc0) in segs:
                src = W[k0 + kc0 * 128:k0 + (kc0 + kn) * 128, sc0:sc0 + n].rearrange("(k p) c -> p k c", p=128)
                P.dma(stv[:, :, dc0:dc0 + n], src, writes=[st.buf])
            dst = wv[:, kc0:kc0 + kn, :]
            eng = self.cast.next()
            if eng == "act":
                P.op("act", I("activation", dst, stv, AF.Copy), reads=[st.buf], writes=[slot.buf])
            else:
                P.op(eng, I("tensor_copy", dst, stv), reads=[st.buf], writes=[slot.buf])
        return slot.buf, wv


def pipeline(jobs):
    if not jobs:
        return
    cur = jobs[0][0]()
    for i in range(len(jobs)):
        nxt = jobs[i + 1][0]() if i + 1 < len(jobs) else None
        jobs[i][1](cur)
        cur = nxt


def new_nc():
    return bass.Bass("TRN2", target_bir_lowering=False)


def build_l1():
    nc = new_nc()
    xT = nc.dram_tensor("xT", [D, TOK], F32, kind="ExternalInput").ap()
    pos = nc.dram_tensor("pos", [TOK], I32, kind="ExternalInput").ap()
    w_in = nc.dram_tensor("w_in", [D, 5696], F32, kind="ExternalInput").ap()
    w_uq = nc.dram_tensor("w_uq", [1024, 6144], F32, kind="ExternalInput").ap()
    w_ukv = nc.dram_tensor("w_ukv", [512, 8192], F32, kind="ExternalInput").ap()
    g0d = nc.dram_tensor("g0", [128, 32], F32, kind="ExternalInput").ap()
    qgd = nc.dram_tensor("qg", [128, 8], F32, kind="ExternalInput").ap()
    kvgd = nc.dram_tensor("kvg", [128, 4], F32, kind="ExternalInput").ap()
    rcd = nc.dram_tensor("ropec", [64, 2], F32, kind="ExternalInput").ap()
    QnT = nc.dram_tensor("QnT", [NH, 128, TOK], BF16, kind="ExternalOutput").ap()
    QrT = nc.dram_tensor("QrT", [NH, 64, TOK], BF16, kind="ExternalOutput").ap()
    KnT = nc.dram_tensor("KnT", [NH, 128, TOK], BF16, kind="ExternalOutput").ap()
    KrT = nc.dram_tensor("KrT", [64, TOK], BF16, kind="ExternalOutput").ap()
    Vo = nc.dram_tensor("V", [TOK, NH * 128], BF16, kind="ExternalOutput").ap()
    SZT = nc.dram_tensor("SZT", [D, TOK], BF16, kind="ExternalOutput").ap()

    P = Prog(nc)
    NT = 1024
    ws = WStream(P, nstage=2, nw=2)
    ones = P.sbuf("ones", [128, 128], BF16)
    g0 = P.sbuf("g0s", [128, 32], F32)
    qg = P.sbuf("qgs", [128, 8], F32)
    kvg = P.sbuf("kvgs", [128, 4], F32)
    rc = P.sbuf("rcs", [64, 2], F32)
    epsc = P.sbuf("epsc", [128, 1], F32)
    bconst = Buf("const")
    P.op("dve", I("memset", ones[:], 1.0), writes=[bconst])
    P.op("dve", I("memset", epsc[:], EPS), writes=[bconst])
    P.dma(g0[:], g0d, writes=[bconst])
    P.dma(qg[:], qgd, writes=[bconst])
    P.dma(kvg[:], kvgd, writes=[bconst])
    P.dma(rc[:], rcd, writes=[bconst])

    xg = P.sbuf("xg", [128, 32, NT], BF16)
    bxg = Buf("xg")
    sqb = P.ring("sqb", 2, [128, NT], BF16)
    t32 = P.ring("t32_", 4, [128, 512], F32)
    sq2 = P.ring("sq2_", 2, [128, 512], BF16)
    ost = P.ring("ost", 4, [128, 512], BF16)
    rstd_x = P.sbuf("rstd_x", [128, NT], F32)
    rstd_q = P.sbuf("rstd_q", [128, NT], F32)
    rstd_kv = P.sbuf("rstd_kv", [128, NT], F32)
    brx, brq, brkv = Buf("rstd_x"), Buf("rstd_q"), Buf("rstd_kv")
    cqg = P.sbuf("cqg", [128, 8, NT], BF16)
    bcqg = Buf("cqg")
    ckvn = P.sbuf("ckvn", [128, 4, NT], BF16)
    bckvn = Buf("ckvn")
    cosT = P.sbuf("cosT", [64, NT], F32)
    sinT = P.sbuf("sinT", [64, NT], F32)
    btab = Buf("ropetab")
    ra = P.sbuf("ra", [64, NT], F32)
    rb = P.sbuf("rb", [64, NT], F32)
    rcc = P.sbuf("rcc", [64, NT], F32)
    ri = P.sbuf("ri", [64, NT], I32)
    bra, brb, brcc, bri = Buf("ra"), Buf("rb"), Buf("rcc"), Buf("ri")

    acc = P.pring("acc", 4)
    ssq = [Slot(P.psum(f"ssq{i}", [128, 512]), f"ssq{i}") for i in range(2)]
    ssq2 = [Slot(P.psum(f"ssqb{i}", [128, 512]), f"ssqb{i}") for i in range(2)]
    out_toks = []

    def rstd_from(ssq_slot, dst, bdst, n, inv_n):
        t = t32.next()
        P.op("act", I("activation", t.t[:], ssq_slot.t[:], AF.Sqrt, bias=epsc[:, 0:1], scale=inv_n),
             reads=[ssq_slot.buf, bconst], writes=[t.buf])
        P.op("dve", I("reciprocal", dst[:, n * 512:(n + 1) * 512], t.t[:]), reads=[t.buf], writes=[bdst])

    def store(dst_ap, src_slot, src_ap):
        out_toks.append(P.dma(dst_ap, src_ap, reads=[src_slot.buf], owner=src_slot.buf))

    for p in range(TOK // NT):
        t0 = p * NT
        P.dma(ri[:], pos[t0:t0 + NT].partition_broadcast(64), writes=[bri])
        P.op("dve", I("tensor_copy", ra[:], ri[:]), reads=[bri], writes=[bra])
        P.op("dve", I("tensor_scalar", ra[:], ra[:], rc[:, 0:1], None, ALU.mult), reads=[bra, bconst], writes=[bra])
        P.op("dve", I("tensor_scalar", rb[:], ra[:], 1.0 / TWO_PI, None, ALU.mult), reads=[bra], writes=[brb])
        P.op("dve", I("tensor_copy", ri[:], rb[:]), reads=[brb], writes=[bri])
        P.op("dve", I("tensor_copy", rb[:], ri[:]), reads=[bri], writes=[brb])
        P.op("dve", I("scalar_tensor_tensor", ra[:], rb[:], -CW1, ra[:], ALU.mult, ALU.add), reads=[bra, brb], writes=[bra])
        P.op("dve", I("scalar_tensor_tensor", ra[:], rb[:], -CW2, ra[:], ALU.mult, ALU.add), reads=[bra, brb], writes=[bra])
        P.op("dve", I("tensor_single_scalar", rb[:], ra[:], 0.0, ALU.is_lt), reads=[bra], writes=[brb])
        P.op("dve", I("scalar_tensor_tensor", ra[:], rb[:], TWO_PI, ra[:], ALU.mult, ALU.add), reads=[bra, brb], writes=[bra])
        P.op("dve", I("tensor_scalar", rb[:], ra[:], -1.0, math.pi, ALU.mult, ALU.add), reads=[bra], writes=[brb])
        P.op("dve", I("tensor_scalar", rb[:], rb[:], PI_LO, -PI_LO, ALU.min, ALU.max), reads=[brb], writes=[brb])
        P.op("act", I("activation", sinT[:], rb[:], AF.Sin), reads=[brb], writes=[btab])
        P.op("dve", I("tensor_scalar", rcc[:], ra[:], math.pi / 2, None, ALU.add), reads=[bra], writes=[brcc])
        P.op("dve", I("tensor_single_scalar", rb[:], rcc[:], TWO_PI, ALU.is_ge), reads=[brcc, btab], writes=[brb])
        P.op("dve", I("scalar_tensor_tensor", rcc[:], rb[:], -TWO_PI, rcc[:], ALU.mult, ALU.add), reads=[brcc, brb], writes=[brcc])
        P.op("dve", I("tensor_scalar", rcc[:], rcc[:], -1.0, math.pi, ALU.mult, ALU.add), reads=[brcc], writes=[brcc])
        P.op("dve", I("tensor_scalar", rcc[:], rcc[:], PI_LO, -PI_LO, ALU.min, ALU.max), reads=[brcc], writes=[brcc])
        P.op("act", I("activation", cosT[:], rcc[:], AF.Sin), reads=[brcc], writes=[btab])
        P.op("dve", I("tensor_scalar", sinT[:], sinT[:], rc[:, 1:2], None, ALU.mult), reads=[btab, bconst], writes=[btab])

        for kc in range(32):
            st = ws.stages.next()
            stv = st.t[:, 0:NT]
            P.dma(stv, xT[kc * 128:(kc + 1) * 128, t0:t0 + NT], writes=[st.buf])
            sq = sqb.next()
            P.op("act", I("activation", sq.t[:], stv, AF.Square), reads=[st.buf], writes=[sq.buf])
            P.op("pe", [I("matmul", ssq[n].t[:], ones[:], sq.t[:, n * 512:(n + 1) * 512], start=(kc == 0), stop=(kc == 31))
                        for n in range(2)], reads=[sq.buf, bconst], writes=[ssq[0].buf, ssq[1].buf])
            P.op("dve", I("tensor_scalar", xg[:, kc, :], stv, g0[:, kc:kc + 1], None, ALU.mult),
                 reads=[st.buf, bconst], writes=[bxg])
        for n in range(2):
            rstd_from(ssq[n], rstd_x, brx, n, 1.0 / D)

        def gemm_x(wbuf, wv, mcols, epilogue):
            for (c0, M, tag) in mcols:
                for n in range(2):
                    a = acc.next()
                    P.op("pe", [I("matmul", a.t[0:M, :], wv[:, kc, c0:c0 + M], xg[:, kc, n * 512:(n + 1) * 512],
                                  start=(kc == 0), stop=(kc == 31)) for kc in range(32)],
                         reads=[wbuf, bxg], writes=[a.buf])
                    epilogue(a, M, tag, n)

        def ep_lat(kind):
            nm = 8 if kind == "q" else 4

            def ep(a, M, m, n):
                cs = slice(n * 512, (n + 1) * 512)
                t = t32.next()
                P.op("dve", I("tensor_tensor", t.t[:], a.t[:], rstd_x[:, cs], ALU.mult), reads=[a.buf, brx], writes=[t.buf])
                s2 = sq2.next()
                P.op("act", I("activation", s2.t[:], t.t[:], AF.Square), reads=[t.buf], writes=[s2.buf])
                P.op("pe", I("matmul", ssq2[n].t[:], ones[:], s2.t[:], start=(m == 0), stop=(m == nm - 1)),
                     reads=[s2.buf, bconst], writes=[ssq2[n].buf])
                if kind == "q":
                    P.op("pool", I("tensor_scalar", cqg[:, m, cs], t.t[:], qg[:, m:m + 1], None, ALU.mult),
                         reads=[t.buf, bconst], writes=[bcqg])
                else:
                    P.op("pool", I("tensor_scalar", ckvn[:, m, cs], t.t[:], kvg[:, m:m + 1], None, ALU.mult),
                         reads=[t.buf, bconst], writes=[bckvn])
            return ep

        def rope_store(aA, aB, rs, brs, n, dst_ap):
            tA = t32.next()
            tB = t32.next()
            cs = slice(n * 512, (n + 1) * 512)
            P.op("dve", I("tensor_tensor", tA.t[0:64, :], aA.t[0:64, :], rs[0:64, cs], ALU.mult), reads=[aA.buf, brs], writes=[tA.buf])
            P.op("dve", I("tensor_tensor", tB.t[0:64, :], aB.t[0:64, :], rs[0:64, cs], ALU.mult), reads=[aB.buf, brs], writes=[tB.buf])
            P.op("pool", I("tensor_tensor", tA.t[0:64, :], tA.t[0:64, :], cosT[:, cs], ALU.mult), reads=[tA.buf, btab], writes=[tA.buf])
            P.op("pool", I("tensor_tensor", tB.t[0:64, :], tB.t[0:64, :], sinT[:, cs], ALU.mult), reads=[tB.buf, btab], writes=[tB.buf])
            o = ost.next()
            P.op("dve", I("tensor_tensor", o.t[0:64, :], tA.t[0:64, :], tB.t[0:64, :], ALU.add), reads=[tA.buf, tB.buf], writes=[o.buf])
            store(dst_ap, o, o.t[0:64, :])

        def ep_z(a, M, m, n):
            cs = slice(n * 512, (n + 1) * 512)
            t = t32.next()
            P.op("dve", I("tensor_tensor", t.t[:], a.t[:], rstd_x[:, cs], ALU.mult), reads=[a.buf, brx], writes=[t.buf])
            o = ost.next()
            P.op("act", I("activation", o.t[:], t.t[:], AF.Silu), reads=[t.buf], writes=[o.buf])
            store(SZT[m * 128:(m + 1) * 128, t0 + n * 512:t0 + (n + 1) * 512], o, o.t[:])

        jobs = []
        for wt in range(4):
            jobs.append((lambda wt=wt: ws.load(w_in, 0, 32, [(wt * 256, 256, 0)], 256),
                         lambda L, wt=wt: gemm_x(L[0], L[1], [(0, 128, wt * 2), (128, 128, wt * 2 + 1)], ep_lat("q"))))

        def after_q(L):
            for n in range(2):
                rstd_from(ssq2[n], rstd_q, brq, n, 1.0 / 1024)
        for wt in range(2):
            def comp(L, wt=wt):
                if wt == 0:
                    after_q(L)
                gemm_x(L[0], L[1], [(0, 128, wt * 2), (128, 128, wt * 2 + 1)], ep_lat("kv"))
            jobs.append((lambda wt=wt: ws.load(w_in, 0, 32, [(1024 + wt * 256, 256, 0)], 256), comp))

        def comp_krot(L):
            wbuf, wv = L
            for n in range(2):
                rstd_from(ssq2[n], rstd_kv, brkv, n, 1.0 / 512)
            for m in range(4):
                for n in range(2):
                    cs = slice(n * 512, (n + 1) * 512)
                    P.op("dve", I("tensor_tensor", ckvn[:, m, cs], ckvn[:, m, cs], rstd_kv[:, cs], ALU.mult),
                         reads=[bckvn, brkv], writes=[bckvn])
            for n in range(2):
                aA, aB = acc.next(), acc.next()
                calls = []
                for (a, c0) in ((aA, 0), (aB, 64)):
                    for kc in range(32):
                        calls.append(I("matmul", a.t[0:64, :], wv[:, kc, c0:c0 + 64], xg[:, kc, n * 512:(n + 1) * 512],
                                       start=(kc == 0), stop=(kc == 31)))
                P.op("pe", calls, reads=[wbuf, bxg], writes=[aA.buf, aB.buf])
                rope_store(aA, aB, rstd_x, brx, n, KrT[:, t0 + n * 512:t0 + (n + 1) * 512])
        jobs.append((lambda: ws.load(w_in, 0, 32, [(1536, 64, 0), (1568, 32, 64), (1536, 32, 96)], 128), comp_krot))

        for wt in range(16):
            jobs.append((lambda wt=wt: ws.load(w_in, 0, 32, [(1600 + wt * 256, 256, 0)], 256),
                         lambda L, wt=wt: gemm_x(L[0], L[1], [(0, 128, wt * 2), (128, 128, wt * 2 + 1)], ep_z)))

        def comp_q(L, h):
            wbuf, wv = L
            for n in range(2):
                cs = slice(n * 512, (n + 1) * 512)
                a0, aA, aB = acc.next(), acc.next(), acc.next()
                calls = []
                for (a, c0, M) in ((a0, 0, 128), (aA, 128, 64), (aB, 192, 64)):
                    for kc in range(8):
                        calls.append(I("matmul", a.t[0:M, :], wv[:, kc, c0:c0 + M], cqg[:, kc, cs], start=(kc == 0), stop=(kc == 7)))
                P.op("pe", calls, reads=[wbuf, bcqg], writes=[a0.buf, aA.buf, aB.buf])
                o = ost.next()
                P.op("dve", I("tensor_tensor", o.t[:], a0.t[:], rstd_q[:, cs], ALU.mult), reads=[a0.buf, brq], writes=[o.buf])
                store(QnT[h, :, t0 + n * 512:t0 + (n + 1) * 512], o, o.t[:])
                rope_store(aA, aB, rstd_q, brq, n, QrT[h, :, t0 + n * 512:t0 + (n + 1) * 512])
        for h in range(NH):
            c = h * 192
            jobs.append((lambda c=c: ws.load(w_uq, 0, 8, [(c, 192, 0), (c + 160, 32, 192), (c + 128, 32, 224)], 256),
                         lambda L, h=h: comp_q(L, h)))

        def comp_kv(L, hg):
            wbuf, wv = L
            for hh in range(4):
                h = hg * 4 + hh
                for n in range(2):
                    cs = slice(n * 512, (n + 1) * 512)
                    a = acc.next()
                    P.op("pe", [I("matmul", a.t[:], wv[:, kc, hh * 256:hh * 256 + 128], ckvn[:, kc, cs], start=(kc == 0), stop=(kc == 3))
                                for kc in range(4)], reads=[wbuf, bckvn], writes=[a.buf])
                    o = ost.next()
                    P.op("act", I("activation", o.t[:], a.t[:], AF.Copy), reads=[a.buf], writes=[o.buf])
                    store(KnT[h, :, t0 + n * 512:t0 + (n + 1) * 512], o, o.t[:])
            wv4 = wv.rearrange("p k (h c) -> p k h c", c=256)
            for tb in range(NT // 128):
                a = acc.next()
                av = a.t[:].rearrange("p (h c) -> p h c", c=128)
                P.op("pe", [I("matmul", av, ckvn[:, kc, tb * 128:(tb + 1) * 128], wv4[:, kc, :, 128:256], start=(kc == 0), stop=(kc == 3))
                            for kc in range(4)], reads=[wbuf, bckvn], writes=[a.buf])
                o = ost.next()
                P.op("dve", I("tensor_copy", o.t[:], a.t[:]), reads=[a.buf], writes=[o.buf])
                store(Vo[t0 + tb * 128:t0 + (tb + 1) * 128, hg * 512:(hg + 1) * 512], o, o.t[:])
        for hg in range(NH // 4):
            jobs.append((lambda hg=hg: ws.load(w_ukv, 0, 4, [(hg * 1024, 1024, 0)], 1024),
                         lambda L, hg=hg: comp_kv(L, hg)))
        pipeline(jobs)

    P.finish(out_toks)
    P.emit()
    return nc


def rope_consts():
    inv = 1.0 / (10000.0 ** (np.arange(0, 64, 2, dtype=np.float32) / 64.0))
    inv = inv.astype(np.float32)
    rc = np.zeros((64, 2), np.float32)
    rc[:32, 0] = inv
    rc[32:, 0] = inv
    rc[:32, 1] = -1.0
    rc[32:, 1] = 1.0
    return rc


def chunked_gain(g):
    k = g.shape[0] // 128
    return np.ascontiguousarray(g.reshape(k, 128).T.astype(np.float32))


def l1_inputs(x, positions, l0_norm, l0_w_in, l0_q_norm, l0_w_uq, l0_kv_norm, l0_w_ukv):
    xf = x.reshape(B * S, D)
    pf = positions.reshape(B * S)
    maps = []
    for c in range(NCORES):
        maps.append({
            "xT": np.ascontiguousarray(xf[c * TOK:(c + 1) * TOK].T),
            "pos": np.ascontiguousarray(pf[c * TOK:(c + 1) * TOK]),
            "w_in": l0_w_in, "w_uq": l0_w_uq, "w_ukv": l0_w_ukv,
            "g0": chunked_gain(l0_norm), "qg": chunked_gain(l0_q_norm), "kvg": chunked_gain(l0_kv_norm),
            "ropec": rope_consts(),
        })
    return maps


HPC = 8


def build_l2():
    nc = new_nc()
    QnT = nc.dram_tensor("QnT", [HPC, 128, S], BF16, kind="ExternalInput").ap()
    QrT = nc.dram_tensor("QrT", [HPC, 64, S], BF16, kind="ExternalInput").ap()
    KnT = nc.dram_tensor("KnT", [HPC, 128, S], BF16, kind="ExternalInput").ap()
    KrT = nc.dram_tensor("KrT", [64, S], BF16, kind="ExternalInput").ap()
    Vh = nc.dram_tensor("Vh", [HPC, 128, S // 128, 128], BF16, kind="ExternalInput").ap()
    SZT = nc.dram_tensor("SZT", [HPC * 128, S], BF16, kind="ExternalInput").ap()
    trid = nc.dram_tensor("tri", [128, 128], BF16, kind="ExternalInput").ap()
    OGT = nc.dram_tensor("OGT", [HPC * 128, S], BF16, kind="ExternalOutput").ap()

    P = Prog(nc)
    scale = 1.0 / math.sqrt(192.0)
    ones = P.sbuf("ones", [128, 128], BF16)
    tri = P.sbuf("tri_sb", [128, 128], BF16)
    bconst = Buf("const")
    P.op("dve", I("memset", ones[:], 1.0), writes=[bconst])
    P.dma(tri[:], trid, writes=[bconst])
    kr = P.slot("kr", [64, S], BF16)
    P.dma(kr.t[:], KrT, writes=[kr.buf])
    kn = P.ring("kn", 2, [128, S], BF16)
    vv = P.ring("vv", 2, [128, S // 128, 128], BF16)
    qn = P.ring("qn", 3, [128, 512], BF16)
    qr = P.ring("qr", 3, [64, 512], BF16)
    sz = P.ring("sz", 3, [128, 512], BF16)
    pT = P.ring("pT", 3, [128, 512], BF16)
    rs = P.ring("rs", 2, [128, 512], F32)
    o32 = P.ring("o32", 2, [128, 512], F32)
    og = P.ring("og", 2, [128, 512], BF16)
    sT = P.pring("sT", 3)
    oT = P.pring("oT", 2)
    sm = P.pring("sm", 2)
    out_toks = []

    def load_head(h):
        k = kn.next()
        v = vv.next()
        P.dma(k.t[:], KnT[h], writes=[k.buf])
        P.dma(v.t[:], Vh[h], writes=[v.buf])
        return k, v

    def load_q(h, t):
        a, b_, c = qn.next(), qr.next(), sz.next()
        cs = slice(t * 512, (t + 1) * 512)
        P.dma(a.t[:], QnT[h, :, cs], writes=[a.buf])
        P.dma(b_.t[:], QrT[h, :, cs], writes=[b_.buf])
        P.dma(c.t[:], SZT[h * 128:(h + 1) * 128, cs], writes=[c.buf])
        return a, b_, c

    NQT = S // 512
    tiles = [(h, t) for h in range(HPC) for t in range(NQT)]
    blocks = []
    for ti, (h, t) in enumerate(tiles):
        nb = 4 * t + 4
        for kb in range(nb):
            blocks.append((ti, kb, nb))
    state = {}

    def tile_ctx(ti):
        if ti in state:
            return state[ti]
        h, t = tiles[ti]
        if t == 0:
            hk = load_head(h)
        else:
            hk = state[ti - 1]["hk"]
        q = load_q(h, t)
        ctx = {"hk": hk, "q": q, "oT": oT.next(), "sm": sm.next()}
        state[ti] = ctx
        return ctx

    def emit_S(bi):
        ti, kb, nb = blocks[bi]
        h, t = tiles[ti]
        ctx = tile_ctx(ti)
        k, v = ctx["hk"]
        a, b_, c = ctx["q"]
        j = kb - 4 * t
        c0 = 128 * j if j > 0 else 0
        s = sT.next()
        ks = slice(kb * 128, (kb + 1) * 128)
        P.op("pe", [I("matmul", s.t[:, c0:512], k.t[:, ks], a.t[:, c0:512], start=True, stop=False),
                    I("matmul", s.t[:, c0:512], kr.t[:, ks], b_.t[:, c0:512], start=False, stop=True)],
             reads=[k.buf, kr.buf, a.buf, b_.buf], writes=[s.buf])
        return s, c0, j

    pend = {}
    nB = len(blocks)
    AHEAD = 2
    for bi in range(min(AHEAD, nB)):
        pend[bi] = emit_S(bi)
    for bi in range(nB):
        ti, kb, nb = blocks[bi]
        h, t = tiles[ti]
        ctx = state[ti]
        k, v = ctx["hk"]
        s, c0, j = pend.pop(bi)
        p = pT.next()
        P.op("act", I("activation", p.t[:, c0:512], s.t[:, c0:512], AF.Exp, scale=scale), reads=[s.buf], writes=[p.buf])
        if j >= 0:
            P.op("pool", I("tensor_tensor", p.t[:, c0:c0 + 128], p.t[:, c0:c0 + 128], tri[:], ALU.mult),
                 reads=[p.buf, bconst], writes=[p.buf])
        o_, m_ = ctx["oT"], ctx["sm"]
        P.op("pe", [I("matmul", o_.t[:, c0:512], v.t[:, kb, :], p.t[:, c0:512], start=(kb == 0), stop=(kb == nb - 1)),
                    I("matmul", m_.t[:, c0:512], ones[:], p.t[:, c0:512], start=(kb == 0), stop=(kb == nb - 1))],
             reads=[v.buf, p.buf, bconst], writes=[o_.buf, m_.buf])
        if bi + AHEAD < nB:
            pend[bi + AHEAD] = emit_S(bi + AHEAD)
        if kb == nb - 1:
            a, b_, c = ctx["q"]
            r_ = rs.next()
            P.op("dve", I("reciprocal", r_.t[:], m_.t[:]), reads=[m_.buf], writes=[r_.buf])
            x_ = o32.next()
            P.op("dve", I("tensor_tensor", x_.t[:], o_.t[:], r_.t[:], ALU.mult), reads=[o_.buf, r_.buf], writes=[x_.buf])
            g_ = og.next()
            P.op("dve", I("tensor_tensor", g_.t[:], x_.t[:], c.t[:], ALU.mult), reads=[x_.buf, c.buf], writes=[g_.buf])
            out_toks.append(P.dma(OGT[h * 128:(h + 1) * 128, t * 512:(t + 1) * 512], g_.t[:], reads=[g_.buf],
                                  owner=g_.buf, queue="pool"))
            if ti + 1 < len(tiles):
                tile_ctx(ti + 1)

    P.finish(out_toks)
    P.emit()
    return nc


def tri_const():
    import ml_dtypes
    k = np.arange(128)[:, None]
    q = np.arange(128)[None, :]
    return (k <= q).astype(np.float32).astype(ml_dtypes.bfloat16)


def build_wo(final):
    nc = new_nc()
    ogT = nc.dram_tensor("ogT", [D, TOK], BF16, kind="ExternalInput").ap()
    xT = nc.dram_tensor("xT", [D, TOK], F32, kind="ExternalInput").ap()
    w_o = nc.dram_tensor("w_o", [D, D], F32, kind="ExternalInput").ap()
    if final:
        fgd = nc.dram_tensor("fg", [128, 32], F32, kind="ExternalInput").ap()
        x2T = nc.dram_tensor("x2T_scratch", [D, TOK], F32).ap()
    outT = nc.dram_tensor("outT", [D, TOK], F32, kind="ExternalOutput").ap()

    P = Prog(nc)
    NT = 1024
    ws = WStream(P, nstage=2, nw=2)
    bconst = Buf("const")
    ones = P.sbuf("ones", [128, 128], BF16)
    epsc = P.sbuf("epsc", [128, 1], F32)
    P.op("dve", I("memset", ones[:], 1.0), writes=[bconst])
    P.op("dve", I("memset", epsc[:], EPS), writes=[bconst])
    if final:
        fg = P.sbuf("fgs", [128, 32], F32)
        P.dma(fg[:], fgd, writes=[bconst])
    og = P.slot("og", [128, 32, NT], BF16)
    xs = P.ring("xs", 3, [128, 512], F32)
    t32 = P.ring("t32_", 3, [128, 512], F32)
    sq2 = P.ring("sq2_", 2, [128, 512], BF16)
    acc = P.pring("acc", 4)
    ssq = [Slot(P.psum(f"ssq{i}", [128, 512]), f"ssq{i}") for i in range(2)]
    rstd = P.sbuf("rstd", [128, NT], F32)
    brstd = Buf("rstd")
    bscr = Buf("x2scratch")
    out_toks = []

    for p in range(TOK // NT):
        t0 = p * NT
        for kc in range(32):
            P.dma(og.t[:, kc, :], ogT[kc * 128:(kc + 1) * 128, t0:t0 + NT], writes=[og.buf])

        def comp(L, wt):
            wbuf, wv = L
            for mi in range(2):
                m = wt * 2 + mi
                for n in range(2):
                    cs = slice(n * 512, (n + 1) * 512)
                    a = acc.next()
                    P.op("pe", [I("matmul", a.t[:], wv[:, kc, mi * 128:(mi + 1) * 128], og.t[:, kc, cs], start=(kc == 0), stop=(kc == 31))
                                for kc in range(32)], reads=[wbuf, og.buf], writes=[a.buf])
                    x_ = xs.next()
                    P.dma(x_.t[:], xT[m * 128:(m + 1) * 128, t0 + n * 512:t0 + (n + 1) * 512], writes=[x_.buf])
                    t = t32.next()
                    P.op("dve", I("tensor_tensor", t.t[:], a.t[:], x_.t[:], ALU.add), reads=[a.buf, x_.buf], writes=[t.buf])
                    if not final:
                        out_toks.append(P.dma(outT[m * 128:(m + 1) * 128, t0 + n * 512:t0 + (n + 1) * 512], t.t[:],
                                              reads=[t.buf], owner=t.buf, queue="pool"))
                    else:
                        P.dma(x2T[m * 128:(m + 1) * 128, t0 + n * 512:t0 + (n + 1) * 512], t.t[:],
                              reads=[t.buf], writes=[bscr], owner=bscr, queue="pool")
                        s2 = sq2.next()
                        P.op("act", I("activation", s2.t[:], t.t[:], AF.Square), reads=[t.buf], writes=[s2.buf])
                        P.op("pe", I("matmul", ssq[n].t[:], ones[:], s2.t[:], start=(m == 0), stop=(m == 31)),
                             reads=[s2.buf, bconst], writes=[ssq[n].buf])
        jobs = [(lambda wt=wt: ws.load(w_o, 0, 32, [(wt * 256, 256, 0)], 256), lambda L, wt=wt: comp(L, wt)) for wt in range(16)]
        pipeline(jobs)
        if final:
            for n in range(2):
                t = t32.next()
                P.op("act", I("activation", t.t[:], ssq[n].t[:], AF.Sqrt, bias=epsc[:, 0:1], scale=1.0 / D),
                     reads=[ssq[n].buf, bconst], writes=[t.buf])
                P.op("dve", I("reciprocal", rstd[:, n * 512:(n + 1) * 512], t.t[:]), reads=[t.buf], writes=[brstd])
            for m in range(32):
                for n in range(2):
                    cs = slice(n * 512, (n + 1) * 512)
                    x_ = xs.next()
                    P.dma(x_.t[:], x2T[m * 128:(m + 1) * 128, t0 + n * 512:t0 + (n + 1) * 512], reads=[bscr], writes=[x_.buf])
                    t = t32.next()
                    P.op("dve", I("scalar_tensor_tensor", t.t[:], x_.t[:], fg[:, m:m + 1], rstd[:, cs], ALU.mult, ALU.mult),
                         reads=[x_.buf, brstd, bconst], writes=[t.buf])
                    out_toks.append(P.dma(outT[m * 128:(m + 1) * 128, t0 + n * 512:t0 + (n + 1) * 512], t.t[:],
                                          reads=[t.buf], owner=t.buf, queue="pool"))
    P.finish(out_toks)
    P.emit()
    return nc


def build_l3b():
    nc = new_nc()
    xT = nc.dram_tensor("xT", [D, TOK], F32, kind="ExternalInput").ap()
    w_in = nc.dram_tensor("w_in", [D, 4 * D], F32, kind="ExternalInput").ap()
    g1d = nc.dram_tensor("g1", [128, 32], F32, kind="ExternalInput").ap()
    lbd = nc.dram_tensor("lbs", [128, 2, 32], F32, kind="ExternalInput").ap()
    q1T = nc.dram_tensor("q1T", [D, TOK], BF16, kind="ExternalOutput").ap()
    k1T = nc.dram_tensor("k1T", [D, TOK], BF16, kind="ExternalOutput").ap()
    g1T = nc.dram_tensor("g1T", [D, TOK], F32, kind="ExternalOutput").ap()
    v1T = nc.dram_tensor("v1T", [D, TOK], BF16, kind="ExternalOutput").ap()
    sz1T = nc.dram_tensor("sz1T", [D, TOK], BF16, kind="ExternalOutput").ap()

    P = Prog(nc)
    NT = 1024
    ws = WStream(P, nstage=2, nw=2)
    bconst = Buf("const")
    ones = P.sbuf("ones", [128, 128], BF16)
    epsc = P.sbuf("epsc", [128, 1], F32)
    g1 = P.sbuf("g1s", [128, 32], F32)
    lbs = P.sbuf("lbss", [128, 2, 32], F32)
    lb = P.sbuf("lb", [128, 32], F32)
    oml = P.sbuf("oml", [128, 32], F32)
    dlb = P.sbuf("dlb", [128, 32], F32)
    P.op("dve", I("memset", ones[:], 1.0), writes=[bconst])
    P.op("dve", I("memset", epsc[:], EPS), writes=[bconst])
    P.dma(g1[:], g1d, writes=[bconst])
    P.dma(lbs[:], lbd, writes=[bconst])
    P.op("dve", I("tensor_tensor", dlb[:], lbs[:, 1, :], lbs[:, 0, :], ALU.subtract), reads=[bconst], writes=[bconst])
    P.op("act", I("activation", dlb[:], dlb[:], AF.Exp, scale=-1.0), reads=[bconst], writes=[bconst])
    P.op("dve", I("tensor_scalar", lb[:], dlb[:], 1.0, None, ALU.add), reads=[bconst], writes=[bconst])
    P.op("dve", I("reciprocal", lb[:], lb[:]), reads=[bconst], writes=[bconst])
    P.op("dve", I("tensor_tensor", oml[:], dlb[:], lb[:], ALU.mult), reads=[bconst], writes=[bconst])

    xg = P.sbuf("xg", [128, 32, NT], BF16)
    bxg = Buf("xg")
    sqb = P.ring("sqb", 2, [128, NT], BF16)
    t32 = P.ring("t32_", 3, [128, 512], F32)
    e32 = P.ring("e32_", 2, [128, 512], F32)
    s32 = P.ring("s32_", 2, [128, 512], F32)
    f32 = P.ring("f32_", 2, [128, 512], F32)
    gout = P.ring("gout", 2, [128, 512], F32)
    ost = P.ring("ost", 4, [128, 512], BF16)
    rstd_x = P.sbuf("rstd_x", [128, NT], F32)
    brx = Buf("rstd_x")
    acc = P.pring("acc", 4)
    ssq = [Slot(P.psum(f"ssq{i}", [128, 512]), f"ssq{i}") for i in range(2)]
    out_toks = []

    def store(dst_ap, src_slot, src_ap):
        out_toks.append(P.dma(dst_ap, src_ap, reads=[src_slot.buf], owner=src_slot.buf, queue="pool"))

    for p in range(TOK // NT):
        t0 = p * NT
        for kc in range(32):
            st = ws.stages.next()
            stv = st.t[:, 0:NT]
            P.dma(stv, xT[kc * 128:(kc + 1) * 128, t0:t0 + NT], writes=[st.buf])
            sq = sqb.next()
            P.op("act", I("activation", sq.t[:], stv, AF.Square), reads=[st.buf], writes=[sq.buf])
            P.op("pe", [I("matmul", ssq[n].t[:], ones[:], sq.t[:, n * 512:(n + 1) * 512], start=(kc == 0), stop=(kc == 31))
                        for n in range(2)], reads=[sq.buf, bconst], writes=[ssq[0].buf, ssq[1].buf])
            P.op("dve", I("tensor_scalar", xg[:, kc, :], stv, g1[:, kc:kc + 1], None, ALU.mult),
                 reads=[st.buf, bconst], writes=[bxg])
        for n in range(2):
            t = t32.next()
            P.op("act", I("activation", t.t[:], ssq[n].t[:], AF.Sqrt, bias=epsc[:, 0:1], scale=1.0 / D),
                 reads=[ssq[n].buf, bconst], writes=[t.buf])
            P.op("dve", I("reciprocal", rstd_x[:, n * 512:(n + 1) * 512], t.t[:]), reads=[t.buf], writes=[brx])

        def comp(L, wt):
            wbuf, wv = L
            kind = wt // 16
            for mi in range(2):
                mm = (wt % 16) * 2 + mi
                for n in range(2):
                    cs = slice(n * 512, (n + 1) * 512)
                    dsl = (slice(mm * 128, (mm + 1) * 128), slice(t0 + n * 512, t0 + (n + 1) * 512))
                    a = acc.next()
                    P.op("pe", [I("matmul", a.t[:], wv[:, kc, mi * 128:(mi + 1) * 128], xg[:, kc, cs], start=(kc == 0), stop=(kc == 31))
                                for kc in range(32)], reads=[wbuf, bxg], writes=[a.buf])
                    t = t32.next()
                    P.op("dve", I("tensor_tensor", t.t[:], a.t[:], rstd_x[:, cs], ALU.mult), reads=[a.buf, brx], writes=[t.buf])
                    if kind == 0 or kind == 3:
                        o = ost.next()
                        P.op("act", I("activation", o.t[:], t.t[:], AF.Silu), reads=[t.buf], writes=[o.buf])
                        store((q1T if kind == 0 else sz1T)[dsl[0], dsl[1]], o, o.t[:])
                    elif kind == 2:
                        o = ost.next()
                        P.op("act", I("activation", o.t[:], t.t[:], AF.Copy), reads=[t.buf], writes=[o.buf])
                        store(v1T[dsl[0], dsl[1]], o, o.t[:])
                    else:
                        e = e32.next()
                        P.op("act", I("activation", e.t[:], t.t[:], AF.Exp, scale=-1.0), reads=[t.buf], writes=[e.buf])
                        s_ = s32.next()
                        P.op("pool", I("tensor_scalar", s_.t[:], e.t[:], 1.0, None, ALU.add), reads=[e.buf], writes=[s_.buf])
                        P.op("dve", I("reciprocal", s_.t[:], s_.t[:]), reads=[s_.buf], writes=[s_.buf])
                        o = ost.next()
                        P.op("dve", I("scalar_tensor_tensor", o.t[:], e.t[:], oml[:, mm:mm + 1], s_.t[:], ALU.mult, ALU.mult),
                             reads=[e.buf, s_.buf, bconst], writes=[o.buf])
                        store(k1T[dsl[0], dsl[1]], o, o.t[:])
                        f_ = f32.next()
                        P.op("pool", I("tensor_scalar", f_.t[:], s_.t[:], oml[:, mm:mm + 1], lb[:, mm:mm + 1], ALU.mult, ALU.add),
                             reads=[s_.buf, bconst], writes=[f_.buf])
                        g_ = gout.next()
                        P.op("act", I("activation", g_.t[:], f_.t[:], AF.Ln), reads=[f_.buf], writes=[g_.buf])
                        store(g1T[dsl[0], dsl[1]], g_, g_.t[:])
        jobs = [(lambda wt=wt: ws.load(w_in, 0, 32, [(wt * 256, 256, 0)], 256), lambda L, wt=wt: comp(L, wt)) for wt in range(64)]
        pipeline(jobs)
    P.finish(out_toks)
    P.emit()
    return nc


def build_l4():
    nc = new_nc()
    qT = nc.dram_tensor("qT", [HPC, 128, S], BF16, kind="ExternalInput").ap()
    kT = nc.dram_tensor("kT", [HPC, 128, S], BF16, kind="ExternalInput").ap()
    gT = nc.dram_tensor("gT", [HPC, 128, S], F32, kind="ExternalInput").ap()
    v2 = nc.dram_tensor("v2", [HPC, 128, S // 128, 128], BF16, kind="ExternalInput").ap()
    szT = nc.dram_tensor("szT", [HPC * 128, S], BF16, kind="ExternalInput").ap()
    gnd = nc.dram_tensor("gn", [128, 1], F32, kind="ExternalInput").ap()
    m2d = nc.dram_tensor("mask2", [128, 128], BF16, kind="ExternalInput").ap()
    idd = nc.dram_tensor("ident", [128, 128], BF16, kind="ExternalInput").ap()
    smd = nc.dram_tensor("scanmask", [128, 512], F32, kind="ExternalInput").ap()
    OGT = nc.dram_tensor("OG1T", [HPC * 128, S], BF16, kind="ExternalOutput").ap()

    P = Prog(nc)
    bconst = Buf("const")
    ones = P.sbuf("ones", [128, 128], BF16)
    epsc = P.sbuf("epsc", [128, 1], F32)
    gn = P.sbuf("gns", [128, 1], F32)
    mask2 = P.sbuf("mask2s", [128, 128], BF16)
    ident = P.sbuf("idents", [128, 128], BF16)
    scanm = P.sbuf("scanms", [128, 512], F32)
    P.op("dve", I("memset", ones[:], 1.0), writes=[bconst])
    P.op("dve", I("memset", epsc[:], EPS), writes=[bconst])
    P.dma(gn[:], gnd, writes=[bconst])
    P.dma(mask2[:], m2d, writes=[bconst])
    P.dma(ident[:], idd, writes=[bconst])
    P.dma(scanm[:], smd, writes=[bconst])

    H = HPC
    S32 = [P.slot(f"S32_{h}", [128, 128], F32) for h in range(H)]
    Sbf = [P.slot(f"Sbf_{h}", [128, 128], BF16) for h in range(H)]
    for h in range(H):
        P.op("dve", I("memset", S32[h].t[:], 0.0), writes=[S32[h].buf])
        P.op("pool", I("memset", Sbf[h].t[:], 0.0), writes=[Sbf[h].buf])
    Qh = [P.slot(f"Qh_{h}", [128, 512], BF16) for h in range(H)]
    Kht = [P.slot(f"Kht_{h}", [128, 512], BF16) for h in range(H)]
    vt = [P.slot(f"vt_{h}", [128, 4, 128], BF16) for h in range(H)]
    o32 = [P.slot(f"o32_{h}", [128, 512], F32) for h in range(H)]
    dec = [P.slot(f"dec_{h}", [128, 8], F32) for h in range(H)]
    szs = [P.slot(f"szs_{h}", [128, 512], BF16) for h in range(H)]
    qin = P.ring("qin", 2, [128, 512], BF16)
    kin = P.ring("kin", 2, [128, 512], BF16)
    gin = P.ring("gin", 2, [128, 512], F32)
    bt = P.ring("bt", 2, [128, 512], F32)
    d1 = P.ring("d1_", 2, [128, 512], F32)
    d4 = P.ring("d4_", 2, [128, 512], F32)
    E = P.ring("E_", 4, [128, 512], F32)
    Qt = P.ring("Qt", 2, [128, 512], BF16)
    Kt = P.ring("Kt", 2, [128, 512], BF16)
    KhT = P.ring("KhT", 2, [128, 512], BF16)
    attm = P.ring("attm", 2, [128, 512], BF16)
    sqb = P.ring("sqb", 2, [128, 512], BF16)
    sd = P.ring("sd", 2, [128, 512], F32)
    y32 = P.ring("y32", 2, [128, 512], F32)
    ogs = P.ring("ogs", 2, [128, 512], BF16)
    trp = Slot(P.psum("trp", [128, 1024], BF16), "trp")
    attp = Slot(P.psum("attp", [128, 512]), "attp")
    oIp = Slot(P.psum("oIp", [128, 512]), "oIp")
    oXp = P.pring("oXp", 2)
    dSp = [Slot(P.psum(f"dSp{i}", [128, 512]), f"dSp{i}") for i in range(2)]
    ssqp = Slot(P.psum("ssqp", [128, 512]), "ssqp")
    out_toks = []

    NSB = S // 512
    for sb in range(NSB):
        cs = slice(sb * 512, (sb + 1) * 512)
        for h in range(H):
            qi, ki, gi = qin.next(), kin.next(), gin.next()
            P.dma(qi.t[:], qT[h, :, cs], writes=[qi.buf])
            P.dma(ki.t[:], kT[h, :, cs], writes=[ki.buf])
            P.dma(gi.t[:], gT[h, :, cs], writes=[gi.buf])
            P.dma(vt[h].t[:], v2[h, :, sb * 4:(sb + 1) * 4, :], writes=[vt[h].buf])
            P.dma(szs[h].t[:], szT[h * 128:(h + 1) * 128, cs], writes=[szs[h].buf])
            b_ = bt.next()
            P.op("dve", I("tensor_tensor_scan", b_.t[:], scanm[:], gi.t[:], 0.0, ALU.mult, ALU.add),
                 reads=[gi.buf, bconst], writes=[b_.buf])
            b3 = b_.t[:].rearrange("p (n c) -> p n c", c=64)
            x1, x4 = d1.next(), d4.next()
            P.op("dve", I("tensor_tensor", x1.t[:].rearrange("p (n c) -> p n c", c=64), b3,
                          b3[:, :, 32:33].broadcast_to([128, 8, 64]), ALU.subtract), reads=[b_.buf], writes=[x1.buf])
            P.op("pool", I("tensor_tensor", x4.t[:].rearrange("p (n c) -> p n c", c=64), b3,
                           b3[:, :, 63:64].broadcast_to([128, 8, 64]), ALU.subtract), reads=[b_.buf], writes=[x4.buf])
            e1, e2, e3, e4 = E.next(), E.next(), E.next(), E.next()
            P.op("act", I("activation", e1.t[:], x1.t[:], AF.Exp), reads=[x1.buf], writes=[e1.buf])
            P.op("act", I("activation", e2.t[:], x1.t[:], AF.Exp, scale=-1.0), reads=[x1.buf], writes=[e2.buf])
            P.op("act", I("activation", e3.t[:], b_.t[:], AF.Exp), reads=[b_.buf], writes=[e3.buf])
            P.op("act", I("activation", e4.t[:], x4.t[:], AF.Exp, scale=-1.0), reads=[x4.buf], writes=[e4.buf])
            P.op("act", I("activation", dec[h].t[:], b3[:, :, 63], AF.Exp), reads=[b_.buf], writes=[dec[h].buf])
            qt_, kt_, kh_ = Qt.next(), Kt.next(), KhT.next()
            P.op("dve", I("tensor_tensor", qt_.t[:], qi.t[:], e1.t[:], ALU.mult), reads=[qi.buf, e1.buf], writes=[qt_.buf])
            P.op("pool", I("tensor_tensor", kt_.t[:], ki.t[:], e2.t[:], ALU.mult), reads=[ki.buf, e2.buf], writes=[kt_.buf])
            P.op("dve", I("tensor_tensor", Qh[h].t[:], qi.t[:], e3.t[:], ALU.mult), reads=[qi.buf, e3.buf], writes=[Qh[h].buf])
            P.op("pool", I("tensor_tensor", kh_.t[:], ki.t[:], e4.t[:], ALU.mult), reads=[ki.buf, e4.buf], writes=[kh_.buf])
            P.op("pe", [I("transpose", trp.t[:, pr * 128:(pr + 1) * 128], kh_.t[:, pr * 128:(pr + 1) * 128], ident[:]) for pr in range(4)],
                 reads=[kh_.buf, bconst], writes=[trp.buf])
            P.op("act", I("activation", Kht[h].t[:], trp.t[:, 0:512], AF.Copy), reads=[trp.buf], writes=[Kht[h].buf])
            P.op("pe", [I("matmul", attp.t[:, pr * 128:(pr + 1) * 128], kt_.t[:, pr * 128:(pr + 1) * 128], qt_.t[:, pr * 128:(pr + 1) * 128],
                          start=True, stop=True) for pr in range(4)], reads=[kt_.buf, qt_.buf], writes=[attp.buf])
            am = attm.next()
            P.op("dve", I("tensor_tensor", am.t[:].rearrange("p (n c) -> p n c", c=128), attp.t[:].rearrange("p (n c) -> p n c", c=128),
                          mask2[:].rearrange("p (o c) -> p o c", o=1).broadcast_to([128, 4, 128]), ALU.mult),
                 reads=[attp.buf, bconst], writes=[am.buf])
            P.op("pe", [I("matmul", oIp.t[:, pr * 128:(pr + 1) * 128], vt[h].t[:, pr, :], am.t[:, pr * 128:(pr + 1) * 128],
                          start=True, stop=True) for pr in range(4)], reads=[vt[h].buf, am.buf], writes=[oIp.buf])
            P.op("act", I("activation", o32[h].t[:], oIp.t[:], AF.Copy), reads=[oIp.buf], writes=[o32[h].buf])
        for pr in range(4):
            for ch in range(2):
                c0 = pr * 128 + ch * 64
                ps = slice(ch * 64, ch * 64 + 64)
                ox = oXp.next()
                for h in range(H):
                    P.op("pe", I("matmul", ox.t[:, h * 64:(h + 1) * 64], Sbf[h].t[:], Qh[h].t[:, c0:c0 + 64], start=True, stop=True),
                         reads=[Sbf[h].buf, Qh[h].buf], writes=[ox.buf])
                for h in range(H):
                    dsp = dSp[h // 4]
                    P.op("pe", I("matmul", dsp.t[:, (h % 4) * 128:(h % 4 + 1) * 128], Kht[h].t[ps, pr * 128:(pr + 1) * 128],
                                 vt[h].t[ps, pr, :], start=True, stop=True),
                         reads=[Kht[h].buf, vt[h].buf], writes=[dsp.buf])
                for h in range(H):
                    P.op("dve", I("tensor_tensor", o32[h].t[:, c0:c0 + 64], ox.t[:, h * 64:(h + 1) * 64], o32[h].t[:, c0:c0 + 64], ALU.add),
                         reads=[ox.buf, o32[h].buf], writes=[o32[h].buf])
                for h in range(H):
                    dsp = dSp[h // 4]
                    P.op("dve", I("scalar_tensor_tensor", S32[h].t[:], S32[h].t[:], dec[h].t[:, pr * 2 + ch:pr * 2 + ch + 1],
                                  dsp.t[:, (h % 4) * 128:(h % 4 + 1) * 128], ALU.mult, ALU.add),
                         reads=[S32[h].buf, dec[h].buf, dsp.buf], writes=[S32[h].buf])
                    P.op("pool", I("tensor_copy", Sbf[h].t[:], S32[h].t[:]), reads=[S32[h].buf], writes=[Sbf[h].buf])
        for h in range(H):
            sq = sqb.next()
            P.op("act", I("activation", sq.t[:], o32[h].t[:], AF.Square), reads=[o32[h].buf], writes=[sq.buf])
            P.op("pe", I("matmul", ssqp.t[:], ones[:], sq.t[:], start=True, stop=True), reads=[sq.buf, bconst], writes=[ssqp.buf])
            s_ = sd.next()
            P.op("act", I("activation", s_.t[:], ssqp.t[:], AF.Sqrt, bias=epsc[:, 0:1], scale=1.0 / 128), reads=[ssqp.buf, bconst], writes=[s_.buf])
            P.op("dve", I("reciprocal", s_.t[:], s_.t[:]), reads=[s_.buf], writes=[s_.buf])
            y = y32.next()
            P.op("dve", I("scalar_tensor_tensor", y.t[:], o32[h].t[:], gn[:, 0:1], s_.t[:], ALU.mult, ALU.mult),
                 reads=[o32[h].buf, s_.buf, bconst], writes=[y.buf])
            g_ = ogs.next()
            P.op("pool", I("tensor_tensor", g_.t[:], y.t[:], szs[h].t[:], ALU.mult), reads=[y.buf, szs[h].buf], writes=[g_.buf])
            out_toks.append(P.dma(OGT[h * 128:(h + 1) * 128, cs], g_.t[:], reads=[g_.buf], owner=g_.buf, queue="pool"))
    P.finish(out_toks)
    P.emit()
    return nc


def l4_consts():
    import ml_dtypes
    s = np.arange(128)[:, None]
    c = np.arange(128)[None, :]
    m2 = ((s // 64 == c // 64) & (s <= c)).astype(np.float32).astype(ml_dtypes.bfloat16)
    ident = np.eye(128, dtype=np.float32).astype(ml_dtypes.bfloat16)
    sm = np.ones((128, 512), np.float32)
    sm[:, ::64] = 0.0
    return m2, ident, sm


GROUPS = [[0, 1, 2, 3], [4, 5, 6, 7]]
NTP = 1024


def stage_a(nc, t):
    P = Prog(nc)
    NT = NTP
    xT, pos, w_lat, w_z, w_uq, w_ukv = t["xTall"], t["pos"], t["w_lat"], t["w_z"], t["w_uq"], t["w_ukv"]
    QnT, QrT, KnT, KrT, Vh, SZT = t["QnT"], t["QrT"], t["KnT"], t["KrT"], t["Vh"], t["SZT"]
    ws = WStream(P, nstage=2, nw=2)
    ones = P.sbuf("ones", [128, 128], BF16)
    g0 = P.sbuf("g0s", [128, 32], F32)
    qg = P.sbuf("qgs", [128, 8], F32)
    kvg = P.sbuf("kvgs", [128, 4], F32)
    rc = P.sbuf("rcs", [64, 2], F32)
    epsc = P.sbuf("epsc", [128, 1], F32)
    bconst = Buf("const")
    P.op("dve", I("memset", ones[:], 1.0), writes=[bconst])
    P.op("dve", I("memset", epsc[:], EPS), writes=[bconst])
    P.dma(g0[:], t["g0"], writes=[bconst])
    P.dma(qg[:], t["qg"], writes=[bconst])
    P.dma(kvg[:], t["kvg"], writes=[bconst])
    P.dma(rc[:], t["ropec"], writes=[bconst])

    xg = P.sbuf("xg", [128, 32, NT], BF16)
    bxg = Buf("xg")
    sqb = P.ring("sqb", 2, [128, NT], BF16)
    t32 = P.ring("t32_", 4, [128, 512], F32)
    sq2 = P.ring("sq2_", 2, [128, 512], BF16)
    ost = P.ring("ost", 4, [128, 512], BF16)
    rstd_x = P.sbuf("rstd_x", [128, NT], F32)
    rstd_q = P.sbuf("rstd_q", [128, NT], F32)
    rstd_kv = P.sbuf("rstd_kv", [128, NT], F32)
    brx, brq, brkv = Buf("rstd_x"), Buf("rstd_q"), Buf("rstd_kv")
    cqg = P.sbuf("cqg", [128, 8, NT], BF16)
    bcqg = Buf("cqg")
    ckvn = P.sbuf("ckvn", [128, 4, NT], BF16)
    bckvn = Buf("ckvn")
    cosT = P.sbuf("cosT", [64, NT], F32)
    sinT = P.sbuf("sinT", [64, NT], F32)
    btab = Buf("ropetab")
    ra = P.sbuf("ra", [64, NT], F32)
    rb = P.sbuf("rb", [64, NT], F32)
    rcc = P.sbuf("rcc", [64, NT], F32)
    ri = P.sbuf("ri", [64, NT], I32)
    bra, brb, brcc, bri = Buf("ra"), Buf("rb"), Buf("rcc"), Buf("ri")
    acc = P.pring("acc", 4)
    ssq = [Slot(P.psum(f"ssq{i}", [128, 512]), f"ssq{i}") for i in range(2)]
    ssq2 = [Slot(P.psum(f"ssqb{i}", [128, 512]), f"ssqb{i}") for i in range(2)]

    def rstd_from(ssq_slot, dst, bdst, n, inv_n):
        tt = t32.next()
        P.op("act", I("activation", tt.t[:], ssq_slot.t[:], AF.Sqrt, bias=epsc[:, 0:1], scale=inv_n),
             reads=[ssq_slot.buf, bconst], writes=[tt.buf])
        P.op("dve", I("reciprocal", dst[:, n * 512:(n + 1) * 512], tt.t[:]), reads=[tt.buf], writes=[bdst])

    def store(dst_ap, src_slot, src_ap):
        P.dma(dst_ap, src_ap, reads=[src_slot.buf], owner=src_slot.buf, queue="pool")

    for p in range(S // NT):
        t0 = p * NT
        P.dma(ri[:], pos[t0:t0 + NT].partition_broadcast(64), writes=[bri])
        P.op("dve", I("tensor_copy", ra[:], ri[:]), reads=[bri], writes=[bra])
        P.op("dve", I("tensor_scalar", ra[:], ra[:], rc[:, 0:1], None, ALU.mult), reads=[bra, bconst], writes=[bra])
        P.op("dve", I("tensor_scalar", rb[:], ra[:], 1.0 / TWO_PI, None, ALU.mult), reads=[bra], writes=[brb])
        P.op("dve", I("tensor_copy", ri[:], rb[:]), reads=[brb], writes=[bri])
        P.op("dve", I("tensor_copy", rb[:], ri[:]), reads=[bri], writes=[brb])
        P.op("dve", I("scalar_tensor_tensor", ra[:], rb[:], -CW1, ra[:], ALU.mult, ALU.add), reads=[bra, brb], writes=[bra])
        P.op("dve", I("scalar_tensor_tensor", ra[:], rb[:], -CW2, ra[:], ALU.mult, ALU.add), reads=[bra, brb], writes=[bra])
        P.op("dve", I("tensor_single_scalar", rb[:], ra[:], 0.0, ALU.is_lt), reads=[bra], writes=[brb])
        P.op("dve", I("scalar_tensor_tensor", ra[:], rb[:], TWO_PI, ra[:], ALU.mult, ALU.add), reads=[bra, brb], writes=[bra])
        P.op("dve", I("tensor_scalar", rb[:], ra[:], -1.0, math.pi, ALU.mult, ALU.add), reads=[bra], writes=[brb])
        P.op("dve", I("tensor_scalar", rb[:], rb[:], PI_LO, -PI_LO, ALU.min, ALU.max), reads=[brb], writes=[brb])
        P.op("act", I("activation", sinT[:], rb[:], AF.Sin), reads=[brb], writes=[btab])
        P.op("dve", I("tensor_scalar", rcc[:], ra[:], math.pi / 2, None, ALU.add), reads=[bra], writes=[brcc])
        P.op("dve", I("tensor_single_scalar", rb[:], rcc[:], TWO_PI, ALU.is_ge), reads=[brcc, btab], writes=[brb])
        P.op("dve", I("scalar_tensor_tensor", rcc[:], rb[:], -TWO_PI, rcc[:], ALU.mult, ALU.add), reads=[brcc, brb], writes=[brcc])
        P.op("dve", I("tensor_scalar", rcc[:], rcc[:], -1.0, math.pi, ALU.mult, ALU.add), reads=[brcc], writes=[brcc])
        P.op("dve", I("tensor_scalar", rcc[:], rcc[:], PI_LO, -PI_LO, ALU.min, ALU.max), reads=[brcc], writes=[brcc])
        P.op("act", I("activation", cosT[:], rcc[:], AF.Sin), reads=[brcc], writes=[btab])
        P.op("dve", I("tensor_scalar", sinT[:], sinT[:], rc[:, 1:2], None, ALU.mult), reads=[btab, bconst], writes=[btab])

        for kc in range(32):
            st = ws.stages.next()
            stv = st.t[:, 0:NT]
            P.dma(stv, xT[kc * 128:(kc + 1) * 128, t0:t0 + NT], writes=[st.buf])
            sq = sqb.next()
            P.op("act", I("activation", sq.t[:], stv, AF.Square), reads=[st.buf], writes=[sq.buf])
            P.op("pe", [I("matmul", ssq[n].t[:], ones[:], sq.t[:, n * 512:(n + 1) * 512], start=(kc == 0), stop=(kc == 31))
                        for n in range(2)], reads=[sq.buf, bconst], writes=[ssq[0].buf, ssq[1].buf])
            P.op("dve", I("tensor_scalar", xg[:, kc, :], stv, g0[:, kc:kc + 1], None, ALU.mult),
                 reads=[st.buf, bconst], writes=[bxg])
        for n in range(2):
            rstd_from(ssq[n], rstd_x, brx, n, 1.0 / D)

        def gemm_x(wbuf, wv, mcols, epilogue):
            for (c0, M, tag) in mcols:
                for n in range(2):
                    a = acc.next()
                    P.op("pe", [I("matmul", a.t[0:M, :], wv[:, kc, c0:c0 + M], xg[:, kc, n * 512:(n + 1) * 512],
                                  start=(kc == 0), stop=(kc == 31)) for kc in range(32)],
                         reads=[wbuf, bxg], writes=[a.buf])
                    epilogue(a, M, tag, n)

        def ep_lat(kind):
            nm = 8 if kind == "q" else 4

            def ep(a, M, m, n):
                cs = slice(n * 512, (n + 1) * 512)
                tt = t32.next()
                P.op("dve", I("tensor_tensor", tt.t[:], a.t[:], rstd_x[:, cs], ALU.mult), reads=[a.buf, brx], writes=[tt.buf])
                s2 = sq2.next()
                P.op("act", I("activation", s2.t[:], tt.t[:], AF.Square), reads=[tt.buf], writes=[s2.buf])
                P.op("pe", I("matmul", ssq2[n].t[:], ones[:], s2.t[:], start=(m == 0), stop=(m == nm - 1)),
                     reads=[s2.buf, bconst], writes=[ssq2[n].buf])
                if kind == "q":
                    P.op("pool", I("tensor_scalar", cqg[:, m, cs], tt.t[:], qg[:, m:m + 1], None, ALU.mult),
                         reads=[tt.buf, bconst], writes=[bcqg])
                else:
                    P.op("pool", I("tensor_scalar", ckvn[:, m, cs], tt.t[:], kvg[:, m:m + 1], None, ALU.mult),
                         reads=[tt.buf, bconst], writes=[bckvn])
            return ep

        def rope_store(aA, aB, rs, brs, n, dst_ap):
            tA = t32.next()
            tB = t32.next()
            cs = slice(n * 512, (n + 1) * 512)
            P.op("dve", I("tensor_tensor", tA.t[0:64, :], aA.t[0:64, :], rs[0:64, cs], ALU.mult), reads=[aA.buf, brs], writes=[tA.buf])
            P.op("dve", I("tensor_tensor", tB.t[0:64, :], aB.t[0:64, :], rs[0:64, cs], ALU.mult), reads=[aB.buf, brs], writes=[tB.buf])
            P.op("pool", I("tensor_tensor", tA.t[0:64, :], tA.t[0:64, :], cosT[:, cs], ALU.mult), reads=[tA.buf, btab], writes=[tA.buf])
            P.op("pool", I("tensor_tensor", tB.t[0:64, :], tB.t[0:64, :], sinT[:, cs], ALU.mult), reads=[tB.buf, btab], writes=[tB.buf])
            o = ost.next()
            P.op("dve", I("tensor_tensor", o.t[0:64, :], tA.t[0:64, :], tB.t[0:64, :], ALU.add), reads=[tA.buf, tB.buf], writes=[o.buf])
            store(dst_ap, o, o.t[0:64, :])

        def ep_z(a, M, m, n):
            cs = slice(n * 512, (n + 1) * 512)
            tt = t32.next()
            P.op("dve", I("tensor_tensor", tt.t[:], a.t[:], rstd_x[:, cs], ALU.mult), reads=[a.buf, brx], writes=[tt.buf])
            o = ost.next()
            P.op("act", I("activation", o.t[:], tt.t[:], AF.Silu), reads=[tt.buf], writes=[o.buf])
            store(SZT[m * 128:(m + 1) * 128, t0 + n * 512:t0 + (n + 1) * 512], o, o.t[:])

        jobs = []
        for wt in range(4):
            jobs.append((lambda wt=wt: ws.load(w_lat, 0, 32, [(wt * 256, 256, 0)], 256),
                         lambda L, wt=wt: gemm_x(L[0], L[1], [(0, 128, wt * 2), (128, 128, wt * 2 + 1)], ep_lat("q"))))

        def after_q():
            for n in range(2):
                rstd_from(ssq2[n], rstd_q, brq, n, 1.0 / 1024)
        for wt in range(2):
            def comp(L, wt=wt):
                if wt == 0:
                    after_q()
                gemm_x(L[0], L[1], [(0, 128, wt * 2), (128, 128, wt * 2 + 1)], ep_lat("kv"))
            jobs.append((lambda wt=wt: ws.load(w_lat, 0, 32, [(1024 + wt * 256, 256, 0)], 256), comp))

        def comp_krot(L):
            wbuf, wv = L
            for n in range(2):
                rstd_from(ssq2[n], rstd_kv, brkv, n, 1.0 / 512)
            for m in range(4):
                for n in range(2):
                    cs = slice(n * 512, (n + 1) * 512)
                    P.op("dve", I("tensor_tensor", ckvn[:, m, cs], ckvn[:, m, cs], rstd_kv[:, cs], ALU.mult),
                         reads=[bckvn, brkv], writes=[bckvn])
            for n in range(2):
                aA, aB = acc.next(), acc.next()
                calls = []
                for (a, c0) in ((aA, 0), (aB, 64)):
                    for kc in range(32):
                        calls.append(I("matmul", a.t[0:64, :], wv[:, kc, c0:c0 + 64], xg[:, kc, n * 512:(n + 1) * 512],
                                       start=(kc == 0), stop=(kc == 31)))
                P.op("pe", calls, reads=[wbuf, bxg], writes=[aA.buf, aB.buf])
                rope_store(aA, aB, rstd_x, brx, n, KrT[:, t0 + n * 512:t0 + (n + 1) * 512])
        jobs.append((lambda: ws.load(w_lat, 0, 32, [(1536, 128, 0)], 128), comp_krot))

        for wt in range(4):
            jobs.append((lambda wt=wt: ws.load(w_z, 0, 32, [(wt * 256, 256, 0)], 256),
                         lambda L, wt=wt: gemm_x(L[0], L[1], [(0, 128, wt * 2), (128, 128, wt * 2 + 1)], ep_z)))

        def comp_q(L, h):
            wbuf, wv = L
            for n in range(2):
                cs = slice(n * 512, (n + 1) * 512)
                a0, aA, aB = acc.next(), acc.next(), acc.next()
                calls = []
                for (a, c0, M) in ((a0, 0, 128), (aA, 128, 64), (aB, 192, 64)):
                    for kc in range(8):
                        calls.append(I("matmul", a.t[0:M, :], wv[:, kc, c0:c0 + M], cqg[:, kc, cs], start=(kc == 0), stop=(kc == 7)))
                P.op("pe", calls, reads=[wbuf, bcqg], writes=[a0.buf, aA.buf, aB.buf])
                o = ost.next()
                P.op("dve", I("tensor_tensor", o.t[:], a0.t[:], rstd_q[:, cs], ALU.mult), reads=[a0.buf, brq], writes=[o.buf])
                store(QnT[h, :, t0 + n * 512:t0 + (n + 1) * 512], o, o.t[:])
                rope_store(aA, aB, rstd_q, brq, n, QrT[h, :, t0 + n * 512:t0 + (n + 1) * 512])
        for h in range(HPC):
            jobs.append((lambda h=h: ws.load(w_uq, 0, 8, [(h * 256, 256, 0)], 256), lambda L, h=h: comp_q(L, h)))

        def comp_kv(L, hg):
            wbuf, wv = L
            for hh in range(4):
                h = hg * 4 + hh
                for n in range(2):
                    cs = slice(n * 512, (n + 1) * 512)
                    a = acc.next()
                    P.op("pe", [I("matmul", a.t[:], wv[:, kc, hh * 256:hh * 256 + 128], ckvn[:, kc, cs], start=(kc == 0), stop=(kc == 3))
                                for kc in range(4)], reads=[wbuf, bckvn], writes=[a.buf])
                    o = ost.next()
                    P.op("act", I("activation", o.t[:], a.t[:], AF.Copy), reads=[a.buf], writes=[o.buf])
                    store(KnT[h, :, t0 + n * 512:t0 + (n + 1) * 512], o, o.t[:])
            wv4 = wv.rearrange("p k (h c) -> p k h c", c=256)
            for tb in range(NT // 128):
                a = acc.next()
                av = a.t[:].rearrange("p (h c) -> p h c", c=128)
                P.op("pe", [I("matmul", av, ckvn[:, kc, tb * 128:(tb + 1) * 128], wv4[:, kc, :, 128:256], start=(kc == 0), stop=(kc == 3))
                            for kc in range(4)], reads=[wbuf, bckvn], writes=[a.buf])
                o = ost.next()
                P.op("dve", I("tensor_copy", o.t[:], a.t[:]), reads=[a.buf], writes=[o.buf])
                blk = (t0 + tb * 128) // 128
                store(Vh[hg * 4:(hg + 1) * 4, :, blk, :].rearrange("h p d -> p h d"), o, o.t[:].rearrange("p (h d) -> p h d", d=128))
        for hg in range(HPC // 4):
            jobs.append((lambda hg=hg: ws.load(w_ukv, 0, 4, [(hg * 1024, 1024, 0)], 1024), lambda L, hg=hg: comp_kv(L, hg)))
        pipeline(jobs)
    P.end_stage()


def stage_b(nc, t):
    QnT, QrT, KnT, KrT, Vh, SZT, trid, OGT = t["QnT"], t["QrT"], t["KnT"], t["KrT"], t["Vh"], t["SZT"], t["tri"], t["OGT"]
    P = Prog(nc)
    scale = 1.0 / math.sqrt(192.0)
    ones = P.sbuf("ones", [128, 128], BF16)
    tri = P.sbuf("tri_sb", [128, 128], BF16)
    bconst = Buf("const")
    P.op("dve", I("memset", ones[:], 1.0), writes=[bconst])
    P.dma(tri[:], trid, writes=[bconst])
    kr = P.slot("kr", [64, S], BF16)
    P.dma(kr.t[:], KrT, writes=[kr.buf])
    kn = P.ring("kn", 2, [128, S], BF16)
    vv = P.ring("vv", 2, [128, S // 128, 128], BF16)
    qn = P.ring("qn", 3, [128, 512], BF16)
    qr = P.ring("qr", 3, [64, 512], BF16)
    sz = P.ring("sz", 3, [128, 512], BF16)
    pT = P.ring("pT", 3, [128, 512], BF16)
    rs = P.ring("rs", 2, [128, 512], F32)
    o32 = P.ring("o32", 2, [128, 512], F32)
    og = P.ring("og", 2, [128, 512], BF16)
    sT = P.pring("sT", 3)
    oT = P.pring("oT", 2)
    sm = P.pring("sm", 2)
    out_toks = []

    def load_head(h):
        k = kn.next()
        v = vv.next()
        P.dma(k.t[:], KnT[h], writes=[k.buf])
        P.dma(v.t[:], Vh[h], writes=[v.buf])
        return k, v

    def load_q(h, t):
        a, b_, c = qn.next(), qr.next(), sz.next()
        cs = slice(t * 512, (t + 1) * 512)
        P.dma(a.t[:], QnT[h, :, cs], writes=[a.buf])
        P.dma(b_.t[:], QrT[h, :, cs], writes=[b_.buf])
        P.dma(c.t[:], SZT[h * 128:(h + 1) * 128, cs], writes=[c.buf])
        return a, b_, c

    NQT = S // 512
    tiles = [(h, t) for h in range(HPC) for t in range(NQT)]
    blocks = []
    for ti, (h, t) in enumerate(tiles):
        nb = 4 * t + 4
        for kb in range(nb):
            blocks.append((ti, kb, nb))
    state = {}

    def tile_ctx(ti):
        if ti in state:
            return state[ti]
        h, t = tiles[ti]
        if t == 0:
            hk = load_head(h)
        else:
            hk = state[ti - 1]["hk"]
        q = load_q(h, t)
        ctx = {"hk": hk, "q": q, "oT": oT.next(), "sm": sm.next()}
        state[ti] = ctx
        return ctx

    def emit_S(bi):
        ti, kb, nb = blocks[bi]
        h, t = tiles[ti]
        ctx = tile_ctx(ti)
        k, v = ctx["hk"]
        a, b_, c = ctx["q"]
        j = kb - 4 * t
        c0 = 128 * j if j > 0 else 0
        s = sT.next()
        ks = slice(kb * 128, (kb + 1) * 128)
        P.op("pe", [I("matmul", s.t[:, c0:512], k.t[:, ks], a.t[:, c0:512], start=True, stop=False),
                    I("matmul", s.t[:, c0:512], kr.t[:, ks], b_.t[:, c0:512], start=False, stop=True)],
             reads=[k.buf, kr.buf, a.buf, b_.buf], writes=[s.buf])
        return s, c0, j

    pend = {}
    nB = len(blocks)
    AHEAD = 2
    for bi in range(min(AHEAD, nB)):
        pend[bi] = emit_S(bi)
    for bi in range(nB):
        ti, kb, nb = blocks[bi]
        h, t = tiles[ti]
        ctx = state[ti]
        k, v = ctx["hk"]
        s, c0, j = pend.pop(bi)
        p = pT.next()
        P.op("act", I("activation", p.t[:, c0:512], s.t[:, c0:512], AF.Exp, scale=scale), reads=[s.buf], writes=[p.buf])
        if j >= 0:
            P.op("pool", I("tensor_tensor", p.t[:, c0:c0 + 128], p.t[:, c0:c0 + 128], tri[:], ALU.mult),
                 reads=[p.buf, bconst], writes=[p.buf])
        o_, m_ = ctx["oT"], ctx["sm"]
        P.op("pe", [I("matmul", o_.t[:, c0:512], v.t[:, kb, :], p.t[:, c0:512], start=(kb == 0), stop=(kb == nb - 1)),
                    I("matmul", m_.t[:, c0:512], ones[:], p.t[:, c0:512], start=(kb == 0), stop=(kb == nb - 1))],
             reads=[v.buf, p.buf, bconst], writes=[o_.buf, m_.buf])
        if bi + AHEAD < nB:
            pend[bi + AHEAD] = emit_S(bi + AHEAD)
        if kb == nb - 1:
            a, b_, c = ctx["q"]
            r_ = rs.next()
            P.op("dve", I("reciprocal", r_.t[:], m_.t[:]), reads=[m_.buf], writes=[r_.buf])
            x_ = o32.next()
            P.op("dve", I("tensor_tensor", x_.t[:], o_.t[:], r_.t[:], ALU.mult), reads=[o_.buf, r_.buf], writes=[x_.buf])
            g_ = og.next()
            P.op("dve", I("tensor_tensor", g_.t[:], x_.t[:], c.t[:], ALU.mult), reads=[x_.buf, c.buf], writes=[g_.buf])
            out_toks.append(P.dma(OGT[h * 128:(h + 1) * 128, t * 512:(t + 1) * 512], g_.t[:], reads=[g_.buf],
                                  owner=g_.buf, queue="pool"))
            if ti + 1 < len(tiles):
                tile_ctx(ti + 1)

    P.end_stage()


def stage_e(nc, t):
    qT, kT, gT, v2, szT, gnd, m2d, idd, smd, OGT = (t["q1T"], t["k1T"], t["g1T"], t["v2"], t["sz1T"], t["gn"], t["mask2"],
                                                    t["ident"], t["scanmask"], t["OG1T"])
    P = Prog(nc)
    bconst = Buf("const")
    ones = P.sbuf("ones", [128, 128], BF16)
    epsc = P.sbuf("epsc", [128, 1], F32)
    gn = P.sbuf("gns", [128, 1], F32)
    mask2 = P.sbuf("mask2s", [128, 128], BF16)
    ident = P.sbuf("idents", [128, 128], BF16)
    scanm = P.sbuf("scanms", [128, 512], F32)
    P.op("dve", I("memset", ones[:], 1.0), writes=[bconst])
    P.op("dve", I("memset", epsc[:], EPS), writes=[bconst])
    P.dma(gn[:], gnd, writes=[bconst])
    P.dma(mask2[:], m2d, writes=[bconst])
    P.dma(ident[:], idd, writes=[bconst])
    P.dma(scanm[:], smd, writes=[bconst])

    H = HPC
    S32 = [P.slot(f"S32_{h}", [128, 128], F32) for h in range(H)]
    Sbf = [P.slot(f"Sbf_{h}", [128, 128], BF16) for h in range(H)]
    for h in range(H):
        P.op("dve", I("memset", S32[h].t[:], 0.0), writes=[S32[h].buf])
        P.op("pool", I("memset", Sbf[h].t[:], 0.0), writes=[Sbf[h].buf])
    Qh = [P.slot(f"Qh_{h}", [128, 512], BF16) for h in range(H)]
    Kht = [P.slot(f"Kht_{h}", [128, 512], BF16) for h in range(H)]
    vt = [P.slot(f"vt_{h}", [128, 4, 128], BF16) for h in range(H)]
    o32 = [P.slot(f"o32_{h}", [128, 512], F32) for h in range(H)]
    dec = [P.slot(f"dec_{h}", [128, 8], F32) for h in range(H)]
    szs = [P.slot(f"szs_{h}", [128, 512], BF16) for h in range(H)]
    qin = P.ring("qin", 2, [128, 512], BF16)
    kin = P.ring("kin", 2, [128, 512], BF16)
    gin = P.ring("gin", 2, [128, 512], F32)
    bt = P.ring("bt", 2, [128, 512], F32)
    d1 = P.ring("d1_", 2, [128, 512], F32)
    d4 = P.ring("d4_", 2, [128, 512], F32)
    E = P.ring("E_", 4, [128, 512], F32)
    Qt = P.ring("Qt", 2, [128, 512], BF16)
    Kt = P.ring("Kt", 2, [128, 512], BF16)
    KhT = P.ring("KhT", 2, [128, 512], BF16)
    attm = P.ring("attm", 2, [128, 512], BF16)
    sqb = P.ring("sqb", 2, [128, 512], BF16)
    sd = P.ring("sd", 2, [128, 512], F32)
    y32 = P.ring("y32", 2, [128, 512], F32)
    ogs = P.ring("ogs", 2, [128, 512], BF16)
    trp = Slot(P.psum("trp", [128, 1024], BF16), "trp")
    attp = Slot(P.psum("attp", [128, 512]), "attp")
    oIp = Slot(P.psum("oIp", [128, 512]), "oIp")
    oXp = P.pring("oXp", 2)
    dSp = [Slot(P.psum(f"dSp{i}", [128, 512]), f"dSp{i}") for i in range(2)]
    ssqp = Slot(P.psum("ssqp", [128, 512]), "ssqp")
    out_toks = []

    NSB = S // 512
    for sb in range(NSB):
        cs = slice(sb * 512, (sb + 1) * 512)
        for h in range(H):
            qi, ki, gi = qin.next(), kin.next(), gin.next()
            P.dma(qi.t[:], qT[h, :, cs], writes=[qi.buf])
            P.dma(ki.t[:], kT[h, :, cs], writes=[ki.buf])
            P.dma(gi.t[:], gT[h, :, cs], writes=[gi.buf])
            P.dma(vt[h].t[:], v2[h, :, sb * 4:(sb + 1) * 4, :], writes=[vt[h].buf])
            P.dma(szs[h].t[:], szT[h * 128:(h + 1) * 128, cs], writes=[szs[h].buf])
            b_ = bt.next()
            P.op("dve", I("tensor_tensor_scan", b_.t[:], scanm[:], gi.t[:], 0.0, ALU.mult, ALU.add),
                 reads=[gi.buf, bconst], writes=[b_.buf])
            b3 = b_.t[:].rearrange("p (n c) -> p n c", c=64)
            x1, x4 = d1.next(), d4.next()
            P.op("dve", I("tensor_tensor", x1.t[:].rearrange("p (n c) -> p n c", c=64), b3,
                          b3[:, :, 32:33].broadcast_to([128, 8, 64]), ALU.subtract), reads=[b_.buf], writes=[x1.buf])
            P.op("pool", I("tensor_tensor", x4.t[:].rearrange("p (n c) -> p n c", c=64), b3,
                           b3[:, :, 63:64].broadcast_to([128, 8, 64]), ALU.subtract), reads=[b_.buf], writes=[x4.buf])
            e1, e2, e3, e4 = E.next(), E.next(), E.next(), E.next()
            P.op("act", I("activation", e1.t[:], x1.t[:], AF.Exp), reads=[x1.buf], writes=[e1.buf])
            P.op("act", I("activation", e2.t[:], x1.t[:], AF.Exp, scale=-1.0), reads=[x1.buf], writes=[e2.buf])
            P.op("act", I("activation", e3.t[:], b_.t[:], AF.Exp), reads=[b_.buf], writes=[e3.buf])
            P.op("act", I("activation", e4.t[:], x4.t[:], AF.Exp, scale=-1.0), reads=[x4.buf], writes=[e4.buf])
            P.op("act", I("activation", dec[h].t[:], b3[:, :, 63], AF.Exp), reads=[b_.buf], writes=[dec[h].buf])
            qt_, kt_, kh_ = Qt.next(), Kt.next(), KhT.next()
            P.op("dve", I("tensor_tensor", qt_.t[:], qi.t[:], e1.t[:], ALU.mult), reads=[qi.buf, e1.buf], writes=[qt_.buf])
            P.op("pool", I("tensor_tensor", kt_.t[:], ki.t[:], e2.t[:], ALU.mult), reads=[ki.buf, e2.buf], writes=[kt_.buf])
            P.op("dve", I("tensor_tensor", Qh[h].t[:], qi.t[:], e3.t[:], ALU.mult), reads=[qi.buf, e3.buf], writes=[Qh[h].buf])
            P.op("pool", I("tensor_tensor", kh_.t[:], ki.t[:], e4.t[:], ALU.mult), reads=[ki.buf, e4.buf], writes=[kh_.buf])
            P.op("pe", [I("transpose", trp.t[:, pr * 128:(pr + 1) * 128], kh_.t[:, pr * 128:(pr + 1) * 128], ident[:]) for pr in range(4)],
                 reads=[kh_.buf, bconst], writes=[trp.buf])
            P.op("act", I("activation", Kht[h].t[:], trp.t[:, 0:512], AF.Copy), reads=[trp.buf], writes=[Kht[h].buf])
            P.op("pe", [I("matmul", attp.t[:, pr * 128:(pr + 1) * 128], kt_.t[:, pr * 128:(pr + 1) * 128], qt_.t[:, pr * 128:(pr + 1) * 128],
                          start=True, stop=True) for pr in range(4)], reads=[kt_.buf, qt_.buf], writes=[attp.buf])
            am = attm.next()
            P.op("dve", I("tensor_tensor", am.t[:].rearrange("p (n c) -> p n c", c=128), attp.t[:].rearrange("p (n c) -> p n c", c=128),
                          mask2[:].rearrange("p (o c) -> p o c", o=1).broadcast_to([128, 4, 128]), ALU.mult),
                 reads=[attp.buf, bconst], writes=[am.buf])
            P.op("pe", [I("matmul", oIp.t[:, pr * 128:(pr + 1) * 128], vt[h].t[:, pr, :], am.t[:, pr * 128:(pr + 1) * 128],
                          start=True, stop=True) for pr in range(4)], reads=[vt[h].buf, am.buf], writes=[oIp.buf])
            P.op("act", I("activation", o32[h].t[:], oIp.t[:], AF.Copy), reads=[oIp.buf], writes=[o32[h].buf])
        for pr in range(4):
            for ch in range(2):
                c0 = pr * 128 + ch * 64
                ps = slice(ch * 64, ch * 64 + 64)
                ox = oXp.next()
                for h in range(H):
                    P.op("pe", I("matmul", ox.t[:, h * 64:(h + 1) * 64], Sbf[h].t[:], Qh[h].t[:, c0:c0 + 64], start=True, stop=True),
                         reads=[Sbf[h].buf, Qh[h].buf], writes=[ox.buf])
                for h in range(H):
                    dsp = dSp[h // 4]
                    P.op("pe", I("matmul", dsp.t[:, (h % 4) * 128:(h % 4 + 1) * 128], Kht[h].t[ps, pr * 128:(pr + 1) * 128],
                                 vt[h].t[ps, pr, :], start=True, stop=True),
                         reads=[Kht[h].buf, vt[h].buf], writes=[dsp.buf])
                for h in range(H):
                    P.op("dve", I("tensor_tensor", o32[h].t[:, c0:c0 + 64], ox.t[:, h * 64:(h + 1) * 64], o32[h].t[:, c0:c0 + 64], ALU.add),
                         reads=[ox.buf, o32[h].buf], writes=[o32[h].buf])
                for h in range(H):
                    dsp = dSp[h // 4]
                    P.op("dve", I("scalar_tensor_tensor", S32[h].t[:], S32[h].t[:], dec[h].t[:, pr * 2 + ch:pr * 2 + ch + 1],
                                  dsp.t[:, (h % 4) * 128:(h % 4 + 1) * 128], ALU.mult, ALU.add),
                         reads=[S32[h].buf, dec[h].buf, dsp.buf], writes=[S32[h].buf])
                    P.op("pool", I("tensor_copy", Sbf[h].t[:], S32[h].t[:]), reads=[S32[h].buf], writes=[Sbf[h].buf])
        for h in range(H):
            sq = sqb.next()
            P.op("act", I("activation", sq.t[:], o32[h].t[:], AF.Square), reads=[o32[h].buf], writes=[sq.buf])
            P.op("pe", I("matmul", ssqp.t[:], ones[:], sq.t[:], start=True, stop=True), reads=[sq.buf, bconst], writes=[ssqp.buf])
            s_ = sd.next()
            P.op("act", I("activation", s_.t[:], ssqp.t[:], AF.Sqrt, bias=epsc[:, 0:1], scale=1.0 / 128), reads=[ssqp.buf, bconst], writes=[s_.buf])
            P.op("dve", I("reciprocal", s_.t[:], s_.t[:]), reads=[s_.buf], writes=[s_.buf])
            y = y32.next()
            P.op("dve", I("scalar_tensor_tensor", y.t[:], o32[h].t[:], gn[:, 0:1], s_.t[:], ALU.mult, ALU.mult),
                 reads=[o32[h].buf, s_.buf, bconst], writes=[y.buf])
            g_ = ogs.next()
            P.op("pool", I("tensor_tensor", g_.t[:], y.t[:], szs[h].t[:], ALU.mult), reads=[y.buf, szs[h].buf], writes=[g_.buf])
            out_toks.append(P.dma(OGT[h * 128:(h + 1) * 128, cs], g_.t[:], reads=[g_.buf], owner=g_.buf, queue="pool"))
    P.end_stage()


def stage_gather(nc, src_ap, dst_ap, rows):
    P = Prog(nc)
    R = src_ap.shape[0]
    for j in range(R // rows):
        P.op("pool", I("collective_compute", "AllGather", ALU.bypass, replica_groups=GROUPS,
                       ins=[src_ap[j * rows:(j + 1) * rows, :]], outs=[dst_ap[j]]))
    P.end_stage()


def stage_wo(nc, t, final):
    P = Prog(nc)
    NT = NTP
    gOG, xT, w_o, outT = t["gOG"], t["xT"], t["w_o"], t["outT"]
    rcache = {}

    def rank_of(e):
        if "r" not in rcache:
            rcache["r"] = e.partition_id() % 4
        return rcache["r"]
    ws = WStream(P, nstage=2, nw=2)
    bconst = Buf("const")
    ones = P.sbuf("ones", [128, 128], BF16)
    epsc = P.sbuf("epsc", [128, 1], F32)
    P.op("dve", I("memset", ones[:], 1.0), writes=[bconst])
    P.op("dve", I("memset", epsc[:], EPS), writes=[bconst])
    if final:
        fg = P.sbuf("fgs", [128, 32], F32)
        P.dma(fg[:], t["fg"], writes=[bconst])
        x2T = t["x2T"]
    og = P.slot("og", [128, 32, NT], BF16)
    xs = P.ring("xs", 3, [128, 512], F32)
    t32 = P.ring("t32_", 3, [128, 512], F32)
    sq2 = P.ring("sq2_", 2, [128, 512], BF16)
    acc = P.pring("acc", 4)
    ssq = [Slot(P.psum(f"ssq{i}", [128, 512]), f"ssq{i}") for i in range(2)]
    rstd = P.sbuf("rstd", [128, NT], F32)
    brstd = Buf("rstd")
    bscr = Buf("x2scratch")

    ogown = t["ogown"]
    bown = Buf("ogown")
    g4 = gOG.rearrange("j (g i) s -> j g i s", i=64)
    for q in range(4):
        P.dma(ogown[q * 1024:(q + 1) * 1024, :].rearrange("(j i) s -> j i s", i=64),
              Lazy(lambda e, q=q: g4[:, q, :, bass.ds(rank_of(e) * TOK, TOK)]), writes=[bown])
    for p in range(TOK // NT):
        t0 = p * NT
        for kc in range(32):
            P.dma(og.t[:, kc, :], ogown[kc * 128:(kc + 1) * 128, t0:t0 + NT], reads=[bown], writes=[og.buf])

        def comp(L, wt):
            wbuf, wv = L
            for mi in range(2):
                m = wt * 2 + mi
                for n in range(2):
                    a = acc.next()
                    P.op("pe", [I("matmul", a.t[:], wv[:, kc, mi * 128:(mi + 1) * 128], og.t[:, kc, n * 512:(n + 1) * 512],
                                  start=(kc == 0), stop=(kc == 31)) for kc in range(32)], reads=[wbuf, og.buf], writes=[a.buf])
                    x_ = xs.next()
                    dsl = (slice(m * 128, (m + 1) * 128), slice(t0 + n * 512, t0 + (n + 1) * 512))
                    P.dma(x_.t[:], xT[dsl[0], dsl[1]], writes=[x_.buf])
                    tt = t32.next()
                    P.op("dve", I("tensor_tensor", tt.t[:], a.t[:], x_.t[:], ALU.add), reads=[a.buf, x_.buf], writes=[tt.buf])
                    if not final:
                        P.dma(outT[dsl[0], dsl[1]], tt.t[:], reads=[tt.buf], owner=tt.buf, queue="pool")
                    else:
                        P.dma(x2T[dsl[0], dsl[1]], tt.t[:], reads=[tt.buf], writes=[bscr], owner=bscr, queue="pool")
                        s2 = sq2.next()
                        P.op("act", I("activation", s2.t[:], tt.t[:], AF.Square), reads=[tt.buf], writes=[s2.buf])
                        P.op("pe", I("matmul", ssq[n].t[:], ones[:], s2.t[:], start=(m == 0), stop=(m == 31)),
                             reads=[s2.buf, bconst], writes=[ssq[n].buf])
        jobs = [(lambda wt=wt: ws.load(w_o, 0, 32, [(wt * 256, 256, 0)], 256), lambda L, wt=wt: comp(L, wt)) for wt in range(16)]
        pipeline(jobs)
        if final:
            for n in range(2):
                tt = t32.next()
                P.op("act", I("activation", tt.t[:], ssq[n].t[:], AF.Sqrt, bias=epsc[:, 0:1], scale=1.0 / D),
                     reads=[ssq[n].buf, bconst], writes=[tt.buf])
                P.op("dve", I("reciprocal", rstd[:, n * 512:(n + 1) * 512], tt.t[:]), reads=[tt.buf], writes=[brstd])
            for m in range(32):
                for n in range(2):
                    cs = slice(n * 512, (n + 1) * 512)
                    dsl = (slice(m * 128, (m + 1) * 128), slice(t0 + n * 512, t0 + (n + 1) * 512))
                    x_ = xs.next()
                    P.dma(x_.t[:], x2T[dsl[0], dsl[1]], reads=[bscr], writes=[x_.buf])
                    tt = t32.next()
                    P.op("dve", I("scalar_tensor_tensor", tt.t[:], x_.t[:], fg[:, m:m + 1], rstd[:, cs], ALU.mult, ALU.mult),
                         reads=[x_.buf, brstd, bconst], writes=[tt.buf])
                    P.dma(outT[dsl[0], dsl[1]], tt.t[:], reads=[tt.buf], owner=tt.buf, queue="pool")
    P.end_stage()


def stage_d(nc, t):
    P = Prog(nc)
    NT = NTP
    gX1, w1 = t["gX1"], t["w1"]
    q1T, k1T, g1T, v2, sz1T = t["q1T"], t["k1T"], t["g1T"], t["v2"], t["sz1T"]
    ws = WStream(P, nstage=2, nw=2)
    bconst = Buf("const")
    ones = P.sbuf("ones", [128, 128], BF16)
    epsc = P.sbuf("epsc", [128, 1], F32)
    g1 = P.sbuf("g1s", [128, 32], F32)
    lbs = P.sbuf("lbss", [128, 2, HPC], F32)
    lb = P.sbuf("lb", [128, HPC], F32)
    oml = P.sbuf("oml", [128, HPC], F32)
    dlb = P.sbuf("dlb", [128, HPC], F32)
    ident = P.sbuf("identd", [128, 128], BF16)
    P.op("dve", I("memset", ones[:], 1.0), writes=[bconst])
    P.op("dve", I("memset", epsc[:], EPS), writes=[bconst])
    P.dma(g1[:], t["g1"], writes=[bconst])
    P.dma(lbs[:], t["lbs"], writes=[bconst])
    P.dma(ident[:], t["ident"], writes=[bconst])
    P.op("dve", I("tensor_tensor", dlb[:], lbs[:, 1, :], lbs[:, 0, :], ALU.subtract), reads=[bconst], writes=[bconst])
    P.op("act", I("activation", dlb[:], dlb[:], AF.Exp, scale=-1.0), reads=[bconst], writes=[bconst])
    P.op("dve", I("tensor_scalar", lb[:], dlb[:], 1.0, None, ALU.add), reads=[bconst], writes=[bconst])
    P.op("dve", I("reciprocal", lb[:], lb[:]), reads=[bconst], writes=[bconst])
    P.op("dve", I("tensor_tensor", oml[:], dlb[:], lb[:], ALU.mult), reads=[bconst], writes=[bconst])

    xg = P.sbuf("xg", [128, 32, NT], BF16)
    bxg = Buf("xg")
    sqb = P.ring("sqb", 2, [128, NT], BF16)
    t32 = P.ring("t32_", 3, [128, 512], F32)
    e32 = P.ring("e32_", 2, [128, 512], F32)
    s32 = P.ring("s32_", 2, [128, 512], F32)
    f32r = P.ring("f32_", 2, [128, 512], F32)
    gout = P.ring("gout", 2, [128, 512], F32)
    ost = P.ring("ost", 4, [128, 512], BF16)
    vtok = P.ring("vtok", 2, [128, 512], BF16)
    rstd_x = P.sbuf("rstd_x", [128, NT], F32)
    brx = Buf("rstd_x")
    acc = P.pring("acc", 4)
    ssq = [Slot(P.psum(f"ssq{i}", [128, 512]), f"ssq{i}") for i in range(2)]
    trp = Slot(P.psum("trpd", [128, 1024], BF16), "trpd")

    def store(dst_ap, src_slot, src_ap):
        P.dma(dst_ap, src_ap, reads=[src_slot.buf], owner=src_slot.buf, queue="pool")

    for p in range(S // NT):
        t0 = p * NT
        rk, c0 = p // 2, (p % 2) * NT
        for kc in range(32):
            st = ws.stages.next()
            stv = st.t[:, 0:NT]
            P.dma(stv, gX1[kc, rk * 128:(rk + 1) * 128, c0:c0 + NT], writes=[st.buf])
            sq = sqb.next()
            P.op("act", I("activation", sq.t[:], stv, AF.Square), reads=[st.buf], writes=[sq.buf])
            P.op("pe", [I("matmul", ssq[n].t[:], ones[:], sq.t[:, n * 512:(n + 1) * 512], start=(kc == 0), stop=(kc == 31))
                        for n in range(2)], reads=[sq.buf, bconst], writes=[ssq[0].buf, ssq[1].buf])
            P.op("dve", I("tensor_scalar", xg[:, kc, :], stv, g1[:, kc:kc + 1], None, ALU.mult),
                 reads=[st.buf, bconst], writes=[bxg])
        for n in range(2):
            tt = t32.next()
            P.op("act", I("activation", tt.t[:], ssq[n].t[:], AF.Sqrt, bias=epsc[:, 0:1], scale=1.0 / D),
                 reads=[ssq[n].buf, bconst], writes=[tt.buf])
            P.op("dve", I("reciprocal", rstd_x[:, n * 512:(n + 1) * 512], tt.t[:]), reads=[tt.buf], writes=[brx])

        def comp(L, wt):
            wbuf, wv = L
            kind = wt // 4
            for mi in range(2):
                mm = (wt % 4) * 2 + mi
                for n in range(2):
                    cs = slice(n * 512, (n + 1) * 512)
                    dsl = (slice(mm * 128, (mm + 1) * 128), slice(t0 + n * 512, t0 + (n + 1) * 512))
                    a = acc.next()
                    P.op("pe", [I("matmul", a.t[:], wv[:, kc, mi * 128:(mi + 1) * 128], xg[:, kc, cs], start=(kc == 0), stop=(kc == 31))
                                for kc in range(32)], reads=[wbuf, bxg], writes=[a.buf])
                    tt = t32.next()
                    P.op("dve", I("tensor_tensor", tt.t[:], a.t[:], rstd_x[:, cs], ALU.mult), reads=[a.buf, brx], writes=[tt.buf])
                    if kind == 0 or kind == 3:
                        o = ost.next()
                        P.op("act", I("activation", o.t[:], tt.t[:], AF.Silu), reads=[tt.buf], writes=[o.buf])
                        store((q1T if kind == 0 else sz1T)[dsl[0], dsl[1]], o, o.t[:])
                    elif kind == 2:
                        o = ost.next()
                        P.op("act", I("activation", o.t[:], tt.t[:], AF.Copy), reads=[tt.buf], writes=[o.buf])
                        P.op("pe", [I("transpose", trp.t[:, j * 128:(j + 1) * 128], o.t[:, j * 128:(j + 1) * 128], ident[:]) for j in range(4)],
                             reads=[o.buf, bconst], writes=[trp.buf])
                        vt_ = vtok.next()
                        P.op("dve", I("tensor_copy", vt_.t[:], trp.t[:, 0:512]), reads=[trp.buf], writes=[vt_.buf])
                        blk0 = (t0 + n * 512) // 128
                        store(v2[mm, :, blk0:blk0 + 4, :], vt_, vt_.t[:].rearrange("p (b d) -> p b d", d=128))
                    else:
                        e = e32.next()
                        P.op("act", I("activation", e.t[:], tt.t[:], AF.Exp, scale=-1.0), reads=[tt.buf], writes=[e.buf])
                        s_ = s32.next()
                        P.op("pool", I("tensor_scalar", s_.t[:], e.t[:], 1.0, None, ALU.add), reads=[e.buf], writes=[s_.buf])
                        P.op("dve", I("reciprocal", s_.t[:], s_.t[:]), reads=[s_.buf], writes=[s_.buf])
                        o = ost.next()
                        P.op("dve", I("scalar_tensor_tensor", o.t[:], e.t[:], oml[:, mm:mm + 1], s_.t[:], ALU.mult, ALU.mult),
                             reads=[e.buf, s_.buf, bconst], writes=[o.buf])
                        store(k1T[dsl[0], dsl[1]], o, o.t[:])
                        f_ = f32r.next()
                        P.op("pool", I("tensor_scalar", f_.t[:], s_.t[:], oml[:, mm:mm + 1], lb[:, mm:mm + 1], ALU.mult, ALU.add),
                             reads=[s_.buf, bconst], writes=[f_.buf])
                        g_ = gout.next()
                        P.op("act", I("activation", g_.t[:], f_.t[:], AF.Ln), reads=[f_.buf], writes=[g_.buf])
                        store(g1T[dsl[0], dsl[1]], g_, g_.t[:])
        jobs = [(lambda wt=wt: ws.load(w1, 0, 32, [(wt * 256, 256, 0)], 256), lambda L, wt=wt: comp(L, wt)) for wt in range(16)]
        pipeline(jobs)
    P.end_stage()


def build_fused():
    nc = new_nc()

    def ext(name, shape, dt):
        return nc.dram_tensor(name, list(shape), dt, kind="ExternalInput").ap()

    def internal(name, shape, dt):
        return nc.dram_tensor(name, list(shape), dt).ap()

    t = {}
    t["xTall"] = ext("xTall", [D, S], F32)
    t["pos"] = ext("pos", [S], I32)
    t["w_lat"] = ext("w_lat", [D, 1664], F32)
    t["w_z"] = ext("w_z", [D, 1024], F32)
    t["w_uq"] = ext("w_uq", [1024, HPC * 256], F32)
    t["w_ukv"] = ext("w_ukv", [512, HPC * 256], F32)
    t["g0"] = ext("g0", [128, 32], F32)
    t["qg"] = ext("qg", [128, 8], F32)
    t["kvg"] = ext("kvg", [128, 4], F32)
    t["ropec"] = ext("ropec", [64, 2], F32)
    t["tri"] = ext("tri", [128, 128], BF16)
    t["xTown"] = ext("xTown", [D, TOK], F32)
    t["w_o0"] = ext("w_o0", [D, D], F32)
    t["w1"] = ext("w1", [D, 4 * HPC * 128], F32)
    t["g1"] = ext("g1", [128, 32], F32)
    t["lbs"] = ext("lbs", [128, 2, HPC], F32)
    t["gn"] = ext("gn", [128, 1], F32)
    t["mask2"] = ext("mask2", [128, 128], BF16)
    t["ident"] = ext("ident", [128, 128], BF16)
    t["scanmask"] = ext("scanmask", [128, 512], F32)
    t["w_o1"] = ext("w_o1", [D, D], F32)
    t["fg"] = ext("fg", [128, 32], F32)
    outT = nc.dram_tensor("outT", [D, TOK], F32, kind="ExternalOutput").ap()

    t["QnT"] = internal("QnT_i", [HPC, 128, S], BF16)
    t["QrT"] = internal("QrT_i", [HPC, 64, S], BF16)
    t["KnT"] = internal("KnT_i", [HPC, 128, S], BF16)
    t["KrT"] = internal("KrT_i", [64, S], BF16)
    t["Vh"] = internal("Vh_i", [HPC, 128, S // 128, 128], BF16)
    t["SZT"] = internal("SZT_i", [HPC * 128, S], BF16)
    t["OGT"] = internal("OGT_i", [HPC * 128, S], BF16)
    gOG = internal("gOG_i", [16, 256, S], BF16)
    x1T = internal("x1T_i", [D, TOK], F32)
    gX1 = internal("gX1_i", [32, 512, TOK], F32)
    q1T = internal("q1T_i", [HPC * 128, S], BF16)
    k1T = internal("k1T_i", [HPC * 128, S], BF16)
    g1T = internal("g1T_i", [HPC * 128, S], F32)
    v2 = internal("v2_i", [HPC, 128, S // 128, 128], BF16)
    sz1T = internal("sz1T_i", [HPC * 128, S], BF16)
    OG1T = internal("OG1T_i", [HPC * 128, S], BF16)
    gOG1 = internal("gOG1_i", [16, 256, S], BF16)
    x2T = internal("x2T_i", [D, TOK], F32)

    stage_a(nc, t)
    stage_b(nc, t)
    stage_gather(nc, t["OGT"], gOG, 64)
    ogown0 = internal("ogown0_i", [D, TOK], BF16)
    ogown1 = internal("ogown1_i", [D, TOK], BF16)
    stage_wo(nc, {"gOG": gOG, "xT": t["xTown"], "w_o": t["w_o0"], "outT": x1T, "ogown": ogown0}, final=False)
    stage_gather(nc, x1T, gX1, 128)
    stage_d(nc, {"gX1": gX1, "w1": t["w1"], "g1": t["g1"], "lbs": t["lbs"], "ident": t["ident"],
                 "q1T": q1T, "k1T": k1T, "g1T": g1T, "v2": v2, "sz1T": sz1T})
    h3 = lambda ap: ap.rearrange("(h k) s -> h k s", k=128)
    stage_e(nc, {"q1T": h3(q1T), "k1T": h3(k1T), "g1T": h3(g1T), "v2": v2, "sz1T": sz1T, "gn": t["gn"], "mask2": t["mask2"],
                 "ident": t["ident"], "scanmask": t["scanmask"], "OG1T": OG1T})
    stage_gather(nc, OG1T, gOG1, 64)
    stage_wo(nc, {"gOG": gOG1, "xT": x1T, "w_o": t["w_o1"], "outT": outT, "fg": t["fg"], "x2T": x2T, "ogown": ogown1}, final=True)
    return nc


def kernel(x, positions, l0_norm, l0_w_in, l0_q_norm, l0_w_uq, l0_kv_norm, l0_w_ukv, l0_w_o,
           l1_norm, l1_w_in, l1_g_norm, l1_w_o, lower_bounds, final_norm):
    f32 = lambda a: np.ascontiguousarray(np.asarray(a, dtype=np.float32))
    x = np.asarray(x, dtype=np.float32)
    positions = np.asarray(positions, dtype=np.int32)
    l0_w_in, l0_w_uq, l0_w_ukv, l1_w_in = (np.asarray(a, dtype=np.float32) for a in (l0_w_in, l0_w_uq, l0_w_ukv, l1_w_in))
    lower_bounds = np.asarray(lower_bounds, dtype=np.float32)
    w_lat = np.ascontiguousarray(np.concatenate(
        [l0_w_in[:, :1600], l0_w_in[:, 1568:1600], l0_w_in[:, 1536:1568]], axis=1))
    m2, ident, sm = l4_consts()
    shared = {
        "w_lat": w_lat, "g0": chunked_gain(f32(l0_norm)), "qg": chunked_gain(f32(l0_q_norm)), "kvg": chunked_gain(f32(l0_kv_norm)),
        "ropec": rope_consts(), "tri": tri_const(), "w_o0": f32(l0_w_o), "g1": chunked_gain(f32(l1_norm)),
        "gn": f32(l1_g_norm).reshape(128, 1), "mask2": m2, "ident": ident, "scanmask": sm, "w_o1": f32(l1_w_o),
        "fg": chunked_gain(f32(final_norm)),
    }
    per_g = []
    for g in range(4):
        hs = range(g * HPC, (g + 1) * HPC)
        uq = np.concatenate([np.concatenate([l0_w_uq[:, h * 192:h * 192 + 192], l0_w_uq[:, h * 192 + 160:h * 192 + 192],
                                             l0_w_uq[:, h * 192 + 128:h * 192 + 160]], axis=1) for h in hs], axis=1)
        c0, c1 = g * HPC * 128, (g + 1) * HPC * 128
        per_g.append({
            "w_z": np.ascontiguousarray(l0_w_in[:, 1600 + c0:1600 + c1]),
            "w_uq": np.ascontiguousarray(uq),
            "w_ukv": np.ascontiguousarray(l0_w_ukv[:, g * HPC * 256:(g + 1) * HPC * 256]),
            "w1": np.ascontiguousarray(np.concatenate([l1_w_in[:, s_ * D + c0:s_ * D + c1] for s_ in range(4)], axis=1)),
            "lbs": np.ascontiguousarray(lower_bounds[:, c0:c1].reshape(2, HPC, 128).transpose(2, 0, 1)),
        })
    maps = []
    for b in range(B):
        xTall = np.ascontiguousarray(x[b].T)
        posb = np.ascontiguousarray(positions[b])
        for r in range(4):
            m = dict(shared)
            m.update(per_g[r])
            m["xTall"] = xTall
            m["pos"] = posb
            m["xTown"] = np.ascontiguousarray(xTall[:, r * TOK:(r + 1) * TOK])
            maps.append(m)
    nc = build_fused()
    res = run_bass_kernel_spmd(nc, maps, core_ids=list(range(NCORES)))
    out = np.empty((B, S, D), np.float32)
    for c in range(NCORES):
        b, r = divmod(c, 4)
        out[b, r * TOK:(r + 1) * TOK] = np.asarray(res.results[c]["outT"]).T
    return out
```

```python
import math
import numpy as np
from contextlib import ExitStack
import concourse.bass as bass
import concourse.mybir as mybir
from concourse.bass_utils import run_bass_kernel_spmd

F32 = mybir.dt.float32
BF16 = mybir.dt.bfloat16
I32 = mybir.dt.int32
AF = mybir.ActivationFunctionType
ALU = mybir.AluOpType

NCORES = 8
D = 4096
S = 8192
B = 2
TOK = 2048
NH = 32
EPS = 1e-6
EPOCH = 12000
TWO_PI = 2.0 * math.pi
CW1 = 6.28125
CW2 = TWO_PI - 6.28125
PI_LO = 3.1415925


def I(method, *args, **kw):
    return (method, args, kw)


class Lazy:
    def __init__(self, fn):
        self.fn = fn


class Buf:
    __slots__ = ("name", "w", "r", "sem", "cnt")

    def __init__(self, name=""):
        self.name = name
        self.w = None
        self.r = []
        self.sem = None
        self.cnt = 0


class Ring:
    def __init__(self, items):
        self.items = list(items)
        self.i = 0

    def next(self):
        it = self.items[self.i % len(self.items)]
        self.i += 1
        return it


class Slot:
    def __init__(self, t, name):
        self.t = t
        self.buf = Buf(name)


class Prog:
    ENGS = ("pe", "act", "dve", "pool", "sp")
    SEMUID = 0

    def __init__(self, nc, same_engine_sync=True):
        self.nc = nc
        self.es = ExitStack()
        self.sems = []
        self.dma_bufs = []
        self.lists = {e: [] for e in self.ENGS}
        self.cur_sem = {}
        self.cur_cnt = {}
        self.seen = {e: {} for e in self.ENGS}
        self.nsem = 0
        self.same_engine_sync = same_engine_sync
        self.ninst = 0
        for e in self.ENGS:
            self._new_epoch(e)

    def sem(self, name):
        self.nsem += 1
        Prog.SEMUID += 1
        h = self.nc.alloc_semaphore(name=f"{name}_{Prog.SEMUID}")
        self.sems.append(h)
        return h

    def _new_epoch(self, e):
        self.cur_sem[e] = self.sem(f"s_{e}")
        self.cur_cnt[e] = 0

    def sbuf(self, name, shape, dtype):
        Prog.SEMUID += 1
        return self.es.enter_context(self.nc.sbuf_tensor(f"{name}_u{Prog.SEMUID}", list(shape), dtype))

    def psum(self, name, shape, dtype=F32):
        Prog.SEMUID += 1
        return self.es.enter_context(self.nc.psum_tensor(f"{name}_u{Prog.SEMUID}", list(shape), dtype))

    def slot(self, name, shape, dtype):
        return Slot(self.sbuf(name, shape, dtype), name)

    def ring(self, name, n, shape, dtype):
        return Ring([self.slot(f"{name}{i}", shape, dtype) for i in range(n)])

    def pring(self, name, n):
        return Ring([Slot(self.psum(f"{name}{i}", [128, 512]), f"{name}{i}") for i in range(n)])

    def _deps(self, eng, reads, writes, own_sem=None):
        toks = []
        for b in reads:
            if b.w is not None:
                toks.append(b.w)
        for b in writes:
            if b.w is not None:
                toks.append(b.w)
            toks.extend(b.r)
        need = {}
        for (s, v, src) in toks:
            if src == eng and (eng in ("pe", "sp") or not self.same_engine_sync):
                continue
            if own_sem is not None and s is own_sem:
                continue
            k = id(s)
            if k not in need or need[k][1] < v:
                need[k] = (s, v)
        waits = []
        seen = self.seen[eng]
        for k, (s, v) in need.items():
            if seen.get(k, 0) >= v:
                continue
            seen[k] = v
            waits.append((s, v))
        return waits

    def _mark(self, tok, reads, writes):
        for b in reads:
            b.r.append(tok)
        for b in writes:
            b.w = tok
            b.r = []

    def op(self, eng, calls, reads=(), writes=()):
        if isinstance(calls, tuple):
            calls = [calls]
        waits = self._deps(eng, reads, writes)
        if self.cur_cnt[eng] >= EPOCH:
            self._new_epoch(eng)
        s = self.cur_sem[eng]
        self.cur_cnt[eng] += 1
        tok = (s, self.cur_cnt[eng], eng)
        self.lists[eng].append((waits, calls, s, 1))
        self._mark(tok, reads, writes)
        self.ninst += len(calls)
        return tok

    def dma(self, out, in_, reads=(), writes=(), queue="sp", owner=None):
        own = owner if owner is not None else (writes[0] if writes else reads[0])
        if own.sem is None:
            own.sem = self.sem("d_" + own.name)
            self.dma_bufs.append(own)
        waits = self._deps(queue, reads, writes, own_sem=own.sem)
        own.cnt += 16
        tok = (own.sem, own.cnt, "dma")
        self.lists[queue].append((waits, [I("dma_start", out=out, in_=in_)], own.sem, 16))
        self._mark(tok, reads, writes)
        self.ninst += 1
        return tok

    def finish(self, tokens):
        need = {}
        for (s, v, _) in tokens:
            k = id(s)
            if k not in need or need[k][1] < v:
                need[k] = (s, v)
        self.lists["sp"].append((list(need.values()), [], None, 0))

    def end_stage(self, last=False):
        need = []
        for e in self.ENGS:
            if e != "sp" and self.cur_cnt[e] > 0:
                need.append((self.cur_sem[e], self.cur_cnt[e]))
        for b in self.dma_bufs:
            need.append((b.sem, b.cnt))
        self.lists["sp"].append((need, [], None, 0))
        self.emit(close=False)
        nc = self.nc
        nc.all_engine_barrier()
        self.es.close()
        nc.clear_and_free_semaphores(self.sems)
        nc.all_engine_barrier()

    def emit(self, close=True):
        nc = self.nc
        lists = self.lists

        def replay(e, lst):
            for (waits, calls, s, inc) in lst:
                for (ws, wv) in waits:
                    e.wait_ge(ws, wv)
                inst = None
                for (m, a, k) in calls:
                    if any(isinstance(v, Lazy) for v in k.values()):
                        k = {kk: (v.fn(e) if isinstance(v, Lazy) else v) for kk, v in k.items()}
                    inst = getattr(e, m)(*a, **k)
                if inst is not None:
                    inst.then_inc(s, inc)

        with nc.Block() as block:
            @block.tensor
            def _(e):
                replay(e, lists["pe"])

            @block.scalar
            def _(e):
                replay(e, lists["act"])

            @block.vector
            def _(e):
                replay(e, lists["dve"])

            @block.gpsimd
            def _(e):
                replay(e, lists["pool"])

            @block.sync
            def _(e):
                replay(e, lists["sp"])
        if close:
            self.es.close()


class WStream:
    def __init__(self, P, nstage=2, nw=2, stage_elems=2048, w_elems=8192, cast_engs=("pool",)):
        self.P = P
        self.stage_elems = stage_elems
        self.stages = P.ring("wst", nstage, [128, stage_elems], F32)
        self.ws = P.ring("wbf", nw, [128, w_elems], BF16)
        self.cast = Ring(list(cast_engs))
        self.cache = {}

    def load(self, W, k0, KC, segs, C, key=None, first=True):
        P = self.P
        slot = self.ws.next()
        wv = slot.t[:, 0:KC * C].rearrange("p (k c) -> p k c", c=C)
        if key is not None and not first:
            cap, cbuf = self.cache[key]
            P.dma(slot.t[:, 0:KC * C], cap, reads=[cbuf], writes=[slot.buf])
            return slot.buf, wv
        kpp = max(1, self.stage_elems // C)
        for kc0 in range(0, KC, kpp):
            kn = min(kpp, KC - kc0)
            st = self.stages.next()
            stv = st.t[:, 0:kn * C].rearrange("p (k c) -> p k c", c=C)
            for (sc0, n, dc0) in segs:
                src = W[k0 + kc0 * 128:k0 + (kc0 + kn) * 128, sc0:sc0 + n].rearrange("(k p) c -> p k c", p=128)
                P.dma(stv[:, :, dc0:dc0 + n], src, writes=[st.buf])
            dst = wv[:, kc0:kc0 + kn, :]
            eng = self.cast.next()
            if eng == "act":
                P.op("act", I("activation", dst, stv, AF.Copy), reads=[st.buf], writes=[slot.buf])
            else:
                P.op(eng, I("tensor_copy", dst, stv), reads=[st.buf], writes=[slot.buf])
        if key is not None:
            Prog.SEMUID += 1
            cap = P.nc.dram_tensor(f"wcache_{Prog.SEMUID}", [128, KC * C], BF16).ap()
            cbuf = Buf(f"wc{Prog.SEMUID}")
            self.cache[key] = (cap, cbuf)
            P.dma(cap, slot.t[:, 0:KC * C], reads=[slot.buf], writes=[cbuf], owner=cbuf, queue="pool")
        return slot.buf, wv


def pipeline(jobs):
    if not jobs:
        return
    cur = jobs[0][0]()
    for i in range(len(jobs)):
        nxt = jobs[i + 1][0]() if i + 1 < len(jobs) else None
        jobs[i][1](cur)
        cur = nxt


def new_nc():
    return bass.Bass("TRN2", target_bir_lowering=False)


def build_l1():
    nc = new_nc()
    xT = nc.dram_tensor("xT", [D, TOK], F32, kind="ExternalInput").ap()
    pos = nc.dram_tensor("pos", [TOK], I32, kind="ExternalInput").ap()
    w_in = nc.dram_tensor("w_in", [D, 5696], F32, kind="ExternalInput").ap()
    w_uq = nc.dram_tensor("w_uq", [1024, 6144], F32, kind="ExternalInput").ap()
    w_ukv = nc.dram_tensor("w_ukv", [512, 8192], F32, kind="ExternalInput").ap()
    g0d = nc.dram_tensor("g0", [128, 32], F32, kind="ExternalInput").ap()
    qgd = nc.dram_tensor("qg", [128, 8], F32, kind="ExternalInput").ap()
    kvgd = nc.dram_tensor("kvg", [128, 4], F32, kind="ExternalInput").ap()
    rcd = nc.dram_tensor("ropec", [64, 2], F32, kind="ExternalInput").ap()
    QnT = nc.dram_tensor("QnT", [NH, 128, TOK], BF16, kind="ExternalOutput").ap()
    QrT = nc.dram_tensor("QrT", [NH, 64, TOK], BF16, kind="ExternalOutput").ap()
    KnT = nc.dram_tensor("KnT", [NH, 128, TOK], BF16, kind="ExternalOutput").ap()
    KrT = nc.dram_tensor("KrT", [64, TOK], BF16, kind="ExternalOutput").ap()
    Vo = nc.dram_tensor("V", [TOK, NH * 128], BF16, kind="ExternalOutput").ap()
    SZT = nc.dram_tensor("SZT", [D, TOK], BF16, kind="ExternalOutput").ap()

    P = Prog(nc)
    NT = 1024
    ws = WStream(P, nstage=2, nw=2)
    ones = P.sbuf("ones", [128, 128], BF16)
    g0 = P.sbuf("g0s", [128, 32], F32)
    qg = P.sbuf("qgs", [128, 8], F32)
    kvg = P.sbuf("kvgs", [128, 4], F32)
    rc = P.sbuf("rcs", [64, 2], F32)
    epsc = P.sbuf("epsc", [128, 1], F32)
    bconst = Buf("const")
    P.op("dve", I("memset", ones[:], 1.0), writes=[bconst])
    P.op("dve", I("memset", epsc[:], EPS), writes=[bconst])
    P.dma(g0[:], g0d, writes=[bconst])
    P.dma(qg[:], qgd, writes=[bconst])
    P.dma(kvg[:], kvgd, writes=[bconst])
    P.dma(rc[:], rcd, writes=[bconst])

    xg = P.sbuf("xg", [128, 32, NT], BF16)
    bxg = Buf("xg")
    sqb = P.ring("sqb", 2, [128, NT], BF16)
    t32 = P.ring("t32_", 4, [128, 512], F32)
    sq2 = P.ring("sq2_", 2, [128, 512], BF16)
    ost = P.ring("ost", 4, [128, 512], BF16)
    rstd_x = P.sbuf("rstd_x", [128, NT], F32)
    rstd_q = P.sbuf("rstd_q", [128, NT], F32)
    rstd_kv = P.sbuf("rstd_kv", [128, NT], F32)
    brx, brq, brkv = Buf("rstd_x"), Buf("rstd_q"), Buf("rstd_kv")
    cqg = P.sbuf("cqg", [128, 8, NT], BF16)
    bcqg = Buf("cqg")
    ckvn = P.sbuf("ckvn", [128, 4, NT], BF16)
    bckvn = Buf("ckvn")
    cosT = P.sbuf("cosT", [64, NT], F32)
    sinT = P.sbuf("sinT", [64, NT], F32)
    btab = Buf("ropetab")
    ra = P.sbuf("ra", [64, NT], F32)
    rb = P.sbuf("rb", [64, NT], F32)
    rcc = P.sbuf("rcc", [64, NT], F32)
    ri = P.sbuf("ri", [64, NT], I32)
    bra, brb, brcc, bri = Buf("ra"), Buf("rb"), Buf("rcc"), Buf("ri")

    acc = P.pring("acc", 4)
    ssq = [Slot(P.psum(f"ssq{i}", [128, 512]), f"ssq{i}") for i in range(2)]
    ssq2 = [Slot(P.psum(f"ssqb{i}", [128, 512]), f"ssqb{i}") for i in range(2)]
    out_toks = []

    def rstd_from(ssq_slot, dst, bdst, n, inv_n):
        t = t32.next()
        P.op("act", I("activation", t.t[:], ssq_slot.t[:], AF.Sqrt, bias=epsc[:, 0:1], scale=inv_n),
             reads=[ssq_slot.buf, bconst], writes=[t.buf])
        P.op("dve", I("reciprocal", dst[:, n * 512:(n + 1) * 512], t.t[:]), reads=[t.buf], writes=[bdst])

    def store(dst_ap, src_slot, src_ap):
        out_toks.append(P.dma(dst_ap, src_ap, reads=[src_slot.buf], owner=src_slot.buf))

    for p in range(TOK // NT):
        t0 = p * NT
        P.dma(ri[:], pos[t0:t0 + NT].partition_broadcast(64), writes=[bri])
        P.op("dve", I("tensor_copy", ra[:], ri[:]), reads=[bri], writes=[bra])
        P.op("dve", I("tensor_scalar", ra[:], ra[:], rc[:, 0:1], None, ALU.mult), reads=[bra, bconst], writes=[bra])
        P.op("dve", I("tensor_scalar", rb[:], ra[:], 1.0 / TWO_PI, None, ALU.mult), reads=[bra], writes=[brb])
        P.op("dve", I("tensor_copy", ri[:], rb[:]), reads=[brb], writes=[bri])
        P.op("dve", I("tensor_copy", rb[:], ri[:]), reads=[bri], writes=[brb])
        P.op("dve", I("scalar_tensor_tensor", ra[:], rb[:], -CW1, ra[:], ALU.mult, ALU.add), reads=[bra, brb], writes=[bra])
        P.op("dve", I("scalar_tensor_tensor", ra[:], rb[:], -CW2, ra[:], ALU.mult, ALU.add), reads=[bra, brb], writes=[bra])
        P.op("dve", I("tensor_single_scalar", rb[:], ra[:], 0.0, ALU.is_lt), reads=[bra], writes=[brb])
        P.op("dve", I("scalar_tensor_tensor", ra[:], rb[:], TWO_PI, ra[:], ALU.mult, ALU.add), reads=[bra, brb], writes=[bra])
        P.op("dve", I("tensor_scalar", rb[:], ra[:], -1.0, math.pi, ALU.mult, ALU.add), reads=[bra], writes=[brb])
        P.op("dve", I("tensor_scalar", rb[:], rb[:], PI_LO, -PI_LO, ALU.min, ALU.max), reads=[brb], writes=[brb])
        P.op("act", I("activation", sinT[:], rb[:], AF.Sin), reads=[brb], writes=[btab])
        P.op("dve", I("tensor_scalar", rcc[:], ra[:], math.pi / 2, None, ALU.add), reads=[bra], writes=[brcc])
        P.op("dve", I("tensor_single_scalar", rb[:], rcc[:], TWO_PI, ALU.is_ge), reads=[brcc, btab], writes=[brb])
        P.op("dve", I("scalar_tensor_tensor", rcc[:], rb[:], -TWO_PI, rcc[:], ALU.mult, ALU.add), reads=[brcc, brb], writes=[brcc])
        P.op("dve", I("tensor_scalar", rcc[:], rcc[:], -1.0, math.pi, ALU.mult, ALU.add), reads=[brcc], writes=[brcc])
        P.op("dve", I("tensor_scalar", rcc[:], rcc[:], PI_LO, -PI_LO, ALU.min, ALU.max), reads=[brcc], writes=[brcc])
        P.op("act", I("activation", cosT[:], rcc[:], AF.Sin), reads=[brcc], writes=[btab])
        P.op("dve", I("tensor_scalar", sinT[:], sinT[:], rc[:, 1:2], None, ALU.mult), reads=[btab, bconst], writes=[btab])

        for kc in range(32):
            st = ws.stages.next()
            stv = st.t[:, 0:NT]
            P.dma(stv, xT[kc * 128:(kc + 1) * 128, t0:t0 + NT], writes=[st.buf])
            sq = sqb.next()
            P.op("act", I("activation", sq.t[:], stv, AF.Square), reads=[st.buf], writes=[sq.buf])
            P.op("pe", [I("matmul", ssq[n].t[:], ones[:], sq.t[:, n * 512:(n + 1) * 512], start=(kc == 0), stop=(kc == 31))
                        for n in range(2)], reads=[sq.buf, bconst], writes=[ssq[0].buf, ssq[1].buf])
            P.op("dve", I("tensor_scalar", xg[:, kc, :], stv, g0[:, kc:kc + 1], None, ALU.mult),
                 reads=[st.buf, bconst], writes=[bxg])
        for n in range(2):
            rstd_from(ssq[n], rstd_x, brx, n, 1.0 / D)

        def gemm_x(wbuf, wv, mcols, epilogue):
            for (c0, M, tag) in mcols:
                for n in range(2):
                    a = acc.next()
                    P.op("pe", [I("matmul", a.t[0:M, :], wv[:, kc, c0:c0 + M], xg[:, kc, n * 512:(n + 1) * 512],
                                  start=(kc == 0), stop=(kc == 31)) for kc in range(32)],
                         reads=[wbuf, bxg], writes=[a.buf])
                    epilogue(a, M, tag, n)

        def ep_lat(kind):
            nm = 8 if kind == "q" else 4

            def ep(a, M, m, n):
                cs = slice(n * 512, (n + 1) * 512)
                t = t32.next()
                P.op("dve", I("tensor_tensor", t.t[:], a.t[:], rstd_x[:, cs], ALU.mult), reads=[a.buf, brx], writes=[t.buf])
                s2 = sq2.next()
                P.op("act", I("activation", s2.t[:], t.t[:], AF.Square), reads=[t.buf], writes=[s2.buf])
                P.op("pe", I("matmul", ssq2[n].t[:], ones[:], s2.t[:], start=(m == 0), stop=(m == nm - 1)),
                     reads=[s2.buf, bconst], writes=[ssq2[n].buf])
                if kind == "q":
                    P.op("pool", I("tensor_scalar", cqg[:, m, cs], t.t[:], qg[:, m:m + 1], None, ALU.mult),
                         reads=[t.buf, bconst], writes=[bcqg])
                else:
                    P.op("pool", I("tensor_scalar", ckvn[:, m, cs], t.t[:], kvg[:, m:m + 1], None, ALU.mult),
                         reads=[t.buf, bconst], writes=[bckvn])
            return ep

        def rope_store(aA, aB, rs, brs, n, dst_ap):
            tA = t32.next()
            tB = t32.next()
            cs = slice(n * 512, (n + 1) * 512)
            P.op("dve", I("tensor_tensor", tA.t[0:64, :], aA.t[0:64, :], rs[0:64, cs], ALU.mult), reads=[aA.buf, brs], writes=[tA.buf])
            P.op("dve", I("tensor_tensor", tB.t[0:64, :], aB.t[0:64, :], rs[0:64, cs], ALU.mult), reads=[aB.buf, brs], writes=[tB.buf])
            P.op("pool", I("tensor_tensor", tA.t[0:64, :], tA.t[0:64, :], cosT[:, cs], ALU.mult), reads=[tA.buf, btab], writes=[tA.buf])
            P.op("pool", I("tensor_tensor", tB.t[0:64, :], tB.t[0:64, :], sinT[:, cs], ALU.mult), reads=[tB.buf, btab], writes=[tB.buf])
            o = ost.next()
            P.op("dve", I("tensor_tensor", o.t[0:64, :], tA.t[0:64, :], tB.t[0:64, :], ALU.add), reads=[tA.buf, tB.buf], writes=[o.buf])
            store(dst_ap, o, o.t[0:64, :])

        def ep_z(a, M, m, n):
            cs = slice(n * 512, (n + 1) * 512)
            t = t32.next()
            P.op("dve", I("tensor_tensor", t.t[:], a.t[:], rstd_x[:, cs], ALU.mult), reads=[a.buf, brx], writes=[t.buf])
            o = ost.next()
            P.op("act", I("activation", o.t[:], t.t[:], AF.Silu), reads=[t.buf], writes=[o.buf])
            store(SZT[m * 128:(m + 1) * 128, t0 + n * 512:t0 + (n + 1) * 512], o, o.t[:])

        jobs = []
        for wt in range(4):
            jobs.append((lambda wt=wt: ws.load(w_in, 0, 32, [(wt * 256, 256, 0)], 256),
                         lambda L, wt=wt: gemm_x(L[0], L[1], [(0, 128, wt * 2), (128, 128, wt * 2 + 1)], ep_lat("q"))))

        def after_q(L):
            for n in range(2):
                rstd_from(ssq2[n], rstd_q, brq, n, 1.0 / 1024)
        for wt in range(2):
            def comp(L, wt=wt):
                if wt == 0:
                    after_q(L)
                gemm_x(L[0], L[1], [(0, 128, wt * 2), (128, 128, wt * 2 + 1)], ep_lat("kv"))
            jobs.append((lambda wt=wt: ws.load(w_in, 0, 32, [(1024 + wt * 256, 256, 0)], 256), comp))

        def comp_krot(L):
            wbuf, wv = L
            for n in range(2):
                rstd_from(ssq2[n], rstd_kv, brkv, n, 1.0 / 512)
            for m in range(4):
                for n in range(2):
                    cs = slice(n * 512, (n + 1) * 512)
                    P.op("dve", I("tensor_tensor", ckvn[:, m, cs], ckvn[:, m, cs], rstd_kv[:, cs], ALU.mult),
                         reads=[bckvn, brkv], writes=[bckvn])
            for n in range(2):
                aA, aB = acc.next(), acc.next()
                calls = []
                for (a, c0) in ((aA, 0), (aB, 64)):
                    for kc in range(32):
                        calls.append(I("matmul", a.t[0:64, :], wv[:, kc, c0:c0 + 64], xg[:, kc, n * 512:(n + 1) * 512],
                                       start=(kc == 0), stop=(kc == 31)))
                P.op("pe", calls, reads=[wbuf, bxg], writes=[aA.buf, aB.buf])
                rope_store(aA, aB, rstd_x, brx, n, KrT[:, t0 + n * 512:t0 + (n + 1) * 512])
        jobs.append((lambda: ws.load(w_in, 0, 32, [(1536, 64, 0), (1568, 32, 64), (1536, 32, 96)], 128), comp_krot))

        for wt in range(16):
            jobs.append((lambda wt=wt: ws.load(w_in, 0, 32, [(1600 + wt * 256, 256, 0)], 256),
                         lambda L, wt=wt: gemm_x(L[0], L[1], [(0, 128, wt * 2), (128, 128, wt * 2 + 1)], ep_z)))

        def comp_q(L, h):
            wbuf, wv = L
            for n in range(2):
                cs = slice(n * 512, (n + 1) * 512)
                a0, aA, aB = acc.next(), acc.next(), acc.next()
                calls = []
                for (a, c0, M) in ((a0, 0, 128), (aA, 128, 64), (aB, 192, 64)):
                    for kc in range(8):
                        calls.append(I("matmul", a.t[0:M, :], wv[:, kc, c0:c0 + M], cqg[:, kc, cs], start=(kc == 0), stop=(kc == 7)))
                P.op("pe", calls, reads=[wbuf, bcqg], writes=[a0.buf, aA.buf, aB.buf])
                o = ost.next()
                P.op("dve", I("tensor_tensor", o.t[:], a0.t[:], rstd_q[:, cs], ALU.mult), reads=[a0.buf, brq], writes=[o.buf])
                store(QnT[h, :, t0 + n * 512:t0 + (n + 1) * 512], o, o.t[:])
                rope_store(aA, aB, rstd_q, brq, n, QrT[h, :, t0 + n * 512:t0 + (n + 1) * 512])
        for h in range(NH):
            c = h * 192
            jobs.append((lambda c=c: ws.load(w_uq, 0, 8, [(c, 192, 0), (c + 160, 32, 192), (c + 128, 32, 224)], 256),
                         lambda L, h=h: comp_q(L, h)))

        def comp_kv(L, hg):
            wbuf, wv = L
            for hh in range(4):
                h = hg * 4 + hh
                for n in range(2):
                    cs = slice(n * 512, (n + 1) * 512)
                    a = acc.next()
                    P.op("pe", [I("matmul", a.t[:], wv[:, kc, hh * 256:hh * 256 + 128], ckvn[:, kc, cs], start=(kc == 0), stop=(kc == 3))
                                for kc in range(4)], reads=[wbuf, bckvn], writes=[a.buf])
                    o = ost.next()
                    P.op("act", I("activation", o.t[:], a.t[:], AF.Copy), reads=[a.buf], writes=[o.buf])
                    store(KnT[h, :, t0 + n * 512:t0 + (n + 1) * 512], o, o.t[:])
            wv4 = wv.rearrange("p k (h c) -> p k h c", c=256)
            for tb in range(NT // 128):
                a = acc.next()
                av = a.t[:].rearrange("p (h c) -> p h c", c=128)
                P.op("pe", [I("matmul", av, ckvn[:, kc, tb * 128:(tb + 1) * 128], wv4[:, kc, :, 128:256], start=(kc == 0), stop=(kc == 3))
                            for kc in range(4)], reads=[wbuf, bckvn], writes=[a.buf])
                o = ost.next()
                P.op("dve", I("tensor_copy", o.t[:], a.t[:]), reads=[a.buf], writes=[o.buf])
                store(Vo[t0 + tb * 128:t0 + (tb + 1) * 128, hg * 512:(hg + 1) * 512], o, o.t[:])
        for hg in range(NH // 4):
            jobs.append((lambda hg=hg: ws.load(w_ukv, 0, 4, [(hg * 1024, 1024, 0)], 1024),
                         lambda L, hg=hg: comp_kv(L, hg)))
        pipeline(jobs)

    P.finish(out_toks)
    P.emit()
    return nc


def rope_consts():
    inv = 1.0 / (10000.0 ** (np.arange(0, 64, 2, dtype=np.float32) / 64.0))
    inv = inv.astype(np.float32)
    rc = np.zeros((64, 2), np.float32)
    rc[:32, 0] = inv
    rc[32:, 0] = inv
    rc[:32, 1] = -1.0
    rc[32:, 1] = 1.0
    return rc


def chunked_gain(g):
    k = g.shape[0] // 128
    return np.ascontiguousarray(g.reshape(k, 128).T.astype(np.float32))


def l1_inputs(x, positions, l0_norm, l0_w_in, l0_q_norm, l0_w_uq, l0_kv_norm, l0_w_ukv):
    xf = x.reshape(B * S, D)
    pf = positions.reshape(B * S)
    maps = []
    for c in range(NCORES):
        maps.append({
            "xT": np.ascontiguousarray(xf[c * TOK:(c + 1) * TOK].T),
            "pos": np.ascontiguousarray(pf[c * TOK:(c + 1) * TOK]),
            "w_in": l0_w_in, "w_uq": l0_w_uq, "w_ukv": l0_w_ukv,
            "g0": chunked_gain(l0_norm), "qg": chunked_gain(l0_q_norm), "kvg": chunked_gain(l0_kv_norm),
            "ropec": rope_consts(),
        })
    return maps


HPC = 8


def build_l2():
    nc = new_nc()
    QnT = nc.dram_tensor("QnT", [HPC, 128, S], BF16, kind="ExternalInput").ap()
    QrT = nc.dram_tensor("QrT", [HPC, 64, S], BF16, kind="ExternalInput").ap()
    KnT = nc.dram_tensor("KnT", [HPC, 128, S], BF16, kind="ExternalInput").ap()
    KrT = nc.dram_tensor("KrT", [64, S], BF16, kind="ExternalInput").ap()
    Vh = nc.dram_tensor("Vh", [HPC, 128, S // 128, 128], BF16, kind="ExternalInput").ap()
    SZT = nc.dram_tensor("SZT", [HPC * 128, S], BF16, kind="ExternalInput").ap()
    trid = nc.dram_tensor("tri", [128, 128], BF16, kind="ExternalInput").ap()
    OGT = nc.dram_tensor("OGT", [HPC * 128, S], BF16, kind="ExternalOutput").ap()

    P = Prog(nc)
    scale = 1.0 / math.sqrt(192.0)
    ones = P.sbuf("ones", [128, 128], BF16)
    tri = P.sbuf("tri_sb", [128, 128], BF16)
    bconst = Buf("const")
    P.op("dve", I("memset", ones[:], 1.0), writes=[bconst])
    P.dma(tri[:], trid, writes=[bconst])
    kr = P.slot("kr", [64, S], BF16)
    P.dma(kr.t[:], KrT, writes=[kr.buf])
    kn = P.ring("kn", 2, [128, S], BF16)
    vv = P.ring("vv", 2, [128, S // 128, 128], BF16)
    qn = P.ring("qn", 3, [128, 512], BF16)
    qr = P.ring("qr", 3, [64, 512], BF16)
    sz = P.ring("sz", 3, [128, 512], BF16)
    pT = P.ring("pT", 3, [128, 512], BF16)
    rs = P.ring("rs", 2, [128, 512], F32)
    o32 = P.ring("o32", 2, [128, 512], F32)
    og = P.ring("og", 2, [128, 512], BF16)
    sT = P.pring("sT", 3)
    oT = P.pring("oT", 2)
    sm = P.pring("sm", 2)
    out_toks = []

    def load_head(h):
        k = kn.next()
        v = vv.next()
        P.dma(k.t[:], KnT[h], writes=[k.buf])
        P.dma(v.t[:], Vh[h], writes=[v.buf])
        return k, v

    def load_q(h, t):
        a, b_, c = qn.next(), qr.next(), sz.next()
        cs = slice(t * 512, (t + 1) * 512)
        P.dma(a.t[:], QnT[h, :, cs], writes=[a.buf])
        P.dma(b_.t[:], QrT[h, :, cs], writes=[b_.buf])
        P.dma(c.t[:], SZT[h * 128:(h + 1) * 128, cs], writes=[c.buf])
        return a, b_, c

    NQT = S // 512
    tiles = [(h, t) for h in range(HPC) for t in range(NQT)]
    blocks = []
    for ti, (h, t) in enumerate(tiles):
        nb = 4 * t + 4
        for kb in range(nb):
            blocks.append((ti, kb, nb))
    state = {}

    def tile_ctx(ti):
        if ti in state:
            return state[ti]
        h, t = tiles[ti]
        if t == 0:
            hk = load_head(h)
        else:
            hk = state[ti - 1]["hk"]
        q = load_q(h, t)
        ctx = {"hk": hk, "q": q, "oT": oT.next(), "sm": sm.next()}
        state[ti] = ctx
        return ctx

    def emit_S(bi):
        ti, kb, nb = blocks[bi]
        h, t = tiles[ti]
        ctx = tile_ctx(ti)
        k, v = ctx["hk"]
        a, b_, c = ctx["q"]
        j = kb - 4 * t
        c0 = 128 * j if j > 0 else 0
        s = sT.next()
        ks = slice(kb * 128, (kb + 1) * 128)
        P.op("pe", [I("matmul", s.t[:, c0:512], k.t[:, ks], a.t[:, c0:512], start=True, stop=False),
                    I("matmul", s.t[:, c0:512], kr.t[:, ks], b_.t[:, c0:512], start=False, stop=True)],
             reads=[k.buf, kr.buf, a.buf, b_.buf], writes=[s.buf])
        return s, c0, j

    pend = {}
    nB = len(blocks)
    AHEAD = 2
    for bi in range(min(AHEAD, nB)):
        pend[bi] = emit_S(bi)
    for bi in range(nB):
        ti, kb, nb = blocks[bi]
        h, t = tiles[ti]
        ctx = state[ti]
        k, v = ctx["hk"]
        s, c0, j = pend.pop(bi)
        p = pT.next()
        P.op("act", I("activation", p.t[:, c0:512], s.t[:, c0:512], AF.Exp, scale=scale), reads=[s.buf], writes=[p.buf])
        if j >= 0:
            P.op("pool", I("tensor_tensor", p.t[:, c0:c0 + 128], p.t[:, c0:c0 + 128], tri[:], ALU.mult),
                 reads=[p.buf, bconst], writes=[p.buf])
        o_, m_ = ctx["oT"], ctx["sm"]
        P.op("pe", [I("matmul", o_.t[:, c0:512], v.t[:, kb, :], p.t[:, c0:512], start=(kb == 0), stop=(kb == nb - 1)),
                    I("matmul", m_.t[:, c0:512], ones[:], p.t[:, c0:512], start=(kb == 0), stop=(kb == nb - 1))],
             reads=[v.buf, p.buf, bconst], writes=[o_.buf, m_.buf])
        if bi + AHEAD < nB:
            pend[bi + AHEAD] = emit_S(bi + AHEAD)
        if kb == nb - 1:
            a, b_, c = ctx["q"]
            r_ = rs.next()
            P.op("dve", I("reciprocal", r_.t[:], m_.t[:]), reads=[m_.buf], writes=[r_.buf])
            x_ = o32.next()
            P.op("dve", I("tensor_tensor", x_.t[:], o_.t[:], r_.t[:], ALU.mult), reads=[o_.buf, r_.buf], writes=[x_.buf])
            g_ = og.next()
            P.op("dve", I("tensor_tensor", g_.t[:], x_.t[:], c.t[:], ALU.mult), reads=[x_.buf, c.buf], writes=[g_.buf])
            out_toks.append(P.dma(OGT[h * 128:(h + 1) * 128, t * 512:(t + 1) * 512], g_.t[:], reads=[g_.buf],
                                  owner=g_.buf, queue="pool"))
            if ti + 1 < len(tiles):
                tile_ctx(ti + 1)

    P.finish(out_toks)
    P.emit()
    return nc


def tri_const():
    import ml_dtypes
    k = np.arange(128)[:, None]
    q = np.arange(128)[None, :]
    return (k <= q).astype(np.float32).astype(ml_dtypes.bfloat16)


def build_wo(final):
    nc = new_nc()
    ogT = nc.dram_tensor("ogT", [D, TOK], BF16, kind="ExternalInput").ap()
    xT = nc.dram_tensor("xT", [D, TOK], F32, kind="ExternalInput").ap()
    w_o = nc.dram_tensor("w_o", [D, D], F32, kind="ExternalInput").ap()
    if final:
        fgd = nc.dram_tensor("fg", [128, 32], F32, kind="ExternalInput").ap()
        x2T = nc.dram_tensor("x2T_scratch", [D, TOK], F32).ap()
    outT = nc.dram_tensor("outT", [D, TOK], F32, kind="ExternalOutput").ap()

    P = Prog(nc)
    NT = 1024
    ws = WStream(P, nstage=2, nw=2)
    bconst = Buf("const")
    ones = P.sbuf("ones", [128, 128], BF16)
    epsc = P.sbuf("epsc", [128, 1], F32)
    P.op("dve", I("memset", ones[:], 1.0), writes=[bconst])
    P.op("dve", I("memset", epsc[:], EPS), writes=[bconst])
    if final:
        fg = P.sbuf("fgs", [128, 32], F32)
        P.dma(fg[:], fgd, writes=[bconst])
    og = P.slot("og", [128, 32, NT], BF16)
    xs = P.ring("xs", 3, [128, 512], F32)
    t32 = P.ring("t32_", 3, [128, 512], F32)
    sq2 = P.ring("sq2_", 2, [128, 512], BF16)
    acc = P.pring("acc", 4)
    ssq = [Slot(P.psum(f"ssq{i}", [128, 512]), f"ssq{i}") for i in range(2)]
    rstd = P.sbuf("rstd", [128, NT], F32)
    brstd = Buf("rstd")
    bscr = Buf("x2scratch")
    out_toks = []

    for p in range(TOK // NT):
        t0 = p * NT
        for kc in range(32):
            P.dma(og.t[:, kc, :], ogT[kc * 128:(kc + 1) * 128, t0:t0 + NT], writes=[og.buf])

        def comp(L, wt):
            wbuf, wv = L
            for mi in range(2):
                m = wt * 2 + mi
                for n in range(2):
                    cs = slice(n * 512, (n + 1) * 512)
                    a = acc.next()
                    P.op("pe", [I("matmul", a.t[:], wv[:, kc, mi * 128:(mi + 1) * 128], og.t[:, kc, cs], start=(kc == 0), stop=(kc == 31))
                                for kc in range(32)], reads=[wbuf, og.buf], writes=[a.buf])
                    x_ = xs.next()
                    P.dma(x_.t[:], xT[m * 128:(m + 1) * 128, t0 + n * 512:t0 + (n + 1) * 512], writes=[x_.buf])
                    t = t32.next()
                    P.op("dve", I("tensor_tensor", t.t[:], a.t[:], x_.t[:], ALU.add), reads=[a.buf, x_.buf], writes=[t.buf])
                    if not final:
                        out_toks.append(P.dma(outT[m * 128:(m + 1) * 128, t0 + n * 512:t0 + (n + 1) * 512], t.t[:],
                                              reads=[t.buf], owner=t.buf, queue="pool"))
                    else:
                        P.dma(x2T[m * 128:(m + 1) * 128, t0 + n * 512:t0 + (n + 1) * 512], t.t[:],
                              reads=[t.buf], writes=[bscr], owner=bscr, queue="pool")
                        s2 = sq2.next()
                        P.op("act", I("activation", s2.t[:], t.t[:], AF.Square), reads=[t.buf], writes=[s2.buf])
                        P.op("pe", I("matmul", ssq[n].t[:], ones[:], s2.t[:], start=(m == 0), stop=(m == 31)),
                             reads=[s2.buf, bconst], writes=[ssq[n].buf])
        jobs = [(lambda wt=wt: ws.load(w_o, 0, 32, [(wt * 256, 256, 0)], 256), lambda L, wt=wt: comp(L, wt)) for wt in range(16)]
        pipeline(jobs)
        if final:
            for n in range(2):
                t = t32.next()
                P.op("act", I("activation", t.t[:], ssq[n].t[:], AF.Sqrt, bias=epsc[:, 0:1], scale=1.0 / D),
                     reads=[ssq[n].buf, bconst], writes=[t.buf])
                P.op("dve", I("reciprocal", rstd[:, n * 512:(n + 1) * 512], t.t[:]), reads=[t.buf], writes=[brstd])
            for m in range(32):
                for n in range(2):
                    cs = slice(n * 512, (n + 1) * 512)
                    x_ = xs.next()
                    P.dma(x_.t[:], x2T[m * 128:(m + 1) * 128, t0 + n * 512:t0 + (n + 1) * 512], reads=[bscr], writes=[x_.buf])
                    t = t32.next()
                    P.op("dve", I("scalar_tensor_tensor", t.t[:], x_.t[:], fg[:, m:m + 1], rstd[:, cs], ALU.mult, ALU.mult),
                         reads=[x_.buf, brstd, bconst], writes=[t.buf])
                    out_toks.append(P.dma(outT[m * 128:(m + 1) * 128, t0 + n * 512:t0 + (n + 1) * 512], t.t[:],
                                          reads=[t.buf], owner=t.buf, queue="pool"))
    P.finish(out_toks)
    P.emit()
    return nc


def build_l3b():
    nc = new_nc()
    xT = nc.dram_tensor("xT", [D, TOK], F32, kind="ExternalInput").ap()
    w_in = nc.dram_tensor("w_in", [D, 4 * D], F32, kind="ExternalInput").ap()
    g1d = nc.dram_tensor("g1", [128, 32], F32, kind="ExternalInput").ap()
    lbd = nc.dram_tensor("lbs", [128, 2, 32], F32, kind="ExternalInput").ap()
    q1T = nc.dram_tensor("q1T", [D, TOK], BF16, kind="ExternalOutput").ap()
    k1T = nc.dram_tensor("k1T", [D, TOK], BF16, kind="ExternalOutput").ap()
    g1T = nc.dram_tensor("g1T", [D, TOK], F32, kind="ExternalOutput").ap()
    v1T = nc.dram_tensor("v1T", [D, TOK], BF16, kind="ExternalOutput").ap()
    sz1T = nc.dram_tensor("sz1T", [D, TOK], BF16, kind="ExternalOutput").ap()

    P = Prog(nc)
    NT = 1024
    ws = WStream(P, nstage=2, nw=2)
    bconst = Buf("const")
    ones = P.sbuf("ones", [128, 128], BF16)
    epsc = P.sbuf("epsc", [128, 1], F32)
    g1 = P.sbuf("g1s", [128, 32], F32)
    lbs = P.sbuf("lbss", [128, 2, 32], F32)
    lb = P.sbuf("lb", [128, 32], F32)
    oml = P.sbuf("oml", [128, 32], F32)
    dlb = P.sbuf("dlb", [128, 32], F32)
    P.op("dve", I("memset", ones[:], 1.0), writes=[bconst])
    P.op("dve", I("memset", epsc[:], EPS), writes=[bconst])
    P.dma(g1[:], g1d, writes=[bconst])
    P.dma(lbs[:], lbd, writes=[bconst])
    P.op("dve", I("tensor_tensor", dlb[:], lbs[:, 1, :], lbs[:, 0, :], ALU.subtract), reads=[bconst], writes=[bconst])
    P.op("act", I("activation", dlb[:], dlb[:], AF.Exp, scale=-1.0), reads=[bconst], writes=[bconst])
    P.op("dve", I("tensor_scalar", lb[:], dlb[:], 1.0, None, ALU.add), reads=[bconst], writes=[bconst])
    P.op("dve", I("reciprocal", lb[:], lb[:]), reads=[bconst], writes=[bconst])
    P.op("dve", I("tensor_tensor", oml[:], dlb[:], lb[:], ALU.mult), reads=[bconst], writes=[bconst])

    xg = P.sbuf("xg", [128, 32, NT], BF16)
    bxg = Buf("xg")
    sqb = P.ring("sqb", 2, [128, NT], BF16)
    t32 = P.ring("t32_", 3, [128, 512], F32)
    e32 = P.ring("e32_", 2, [128, 512], F32)
    s32 = P.ring("s32_", 2, [128, 512], F32)
    f32 = P.ring("f32_", 2, [128, 512], F32)
    gout = P.ring("gout", 2, [128, 512], F32)
    ost = P.ring("ost", 4, [128, 512], BF16)
    rstd_x = P.sbuf("rstd_x", [128, NT], F32)
    brx = Buf("rstd_x")
    acc = P.pring("acc", 4)
    ssq = [Slot(P.psum(f"ssq{i}", [128, 512]), f"ssq{i}") for i in range(2)]
    out_toks = []

    def store(dst_ap, src_slot, src_ap):
        out_toks.append(P.dma(dst_ap, src_ap, reads=[src_slot.buf], owner=src_slot.buf, queue="pool"))

    for p in range(TOK // NT):
        t0 = p * NT
        for kc in range(32):
            st = ws.stages.next()
            stv = st.t[:, 0:NT]
            P.dma(stv, xT[kc * 128:(kc + 1) * 128, t0:t0 + NT], writes=[st.buf])
            sq = sqb.next()
            P.op("act", I("activation", sq.t[:], stv, AF.Square), reads=[st.buf], writes=[sq.buf])
            P.op("pe", [I("matmul", ssq[n].t[:], ones[:], sq.t[:, n * 512:(n + 1) * 512], start=(kc == 0), stop=(kc == 31))
                        for n in range(2)], reads=[sq.buf, bconst], writes=[ssq[0].buf, ssq[1].buf])
            P.op("dve", I("tensor_scalar", xg[:, kc, :], stv, g1[:, kc:kc + 1], None, ALU.mult),
                 reads=[st.buf, bconst], writes=[bxg])
        for n in range(2):
            t = t32.next()
            P.op("act", I("activation", t.t[:], ssq[n].t[:], AF.Sqrt, bias=epsc[:, 0:1], scale=1.0 / D),
                 reads=[ssq[n].buf, bconst], writes=[t.buf])
            P.op("dve", I("reciprocal", rstd_x[:, n * 512:(n + 1) * 512], t.t[:]), reads=[t.buf], writes=[brx])

        def comp(L, wt):
            wbuf, wv = L
            kind = wt // 16
            for mi in range(2):
                mm = (wt % 16) * 2 + mi
                for n in range(2):
                    cs = slice(n * 512, (n + 1) * 512)
                    dsl = (slice(mm * 128, (mm + 1) * 128), slice(t0 + n * 512, t0 + (n + 1) * 512))
                    a = acc.next()
                    P.op("pe", [I("matmul", a.t[:], wv[:, kc, mi * 128:(mi + 1) * 128], xg[:, kc, cs], start=(kc == 0), stop=(kc == 31))
                                for kc in range(32)], reads=[wbuf, bxg], writes=[a.buf])
                    t = t32.next()
                    P.op("dve", I("tensor_tensor", t.t[:], a.t[:], rstd_x[:, cs], ALU.mult), reads=[a.buf, brx], writes=[t.buf])
                    if kind == 0 or kind == 3:
                        o = ost.next()
                        P.op("act", I("activation", o.t[:], t.t[:], AF.Silu), reads=[t.buf], writes=[o.buf])
                        store((q1T if kind == 0 else sz1T)[dsl[0], dsl[1]], o, o.t[:])
                    elif kind == 2:
                        o = ost.next()
                        P.op("act", I("activation", o.t[:], t.t[:], AF.Copy), reads=[t.buf], writes=[o.buf])
                        store(v1T[dsl[0], dsl[1]], o, o.t[:])
                    else:
                        e = e32.next()
                        P.op("act", I("activation", e.t[:], t.t[:], AF.Exp, scale=-1.0), reads=[t.buf], writes=[e.buf])
                        s_ = s32.next()
                        P.op("pool", I("tensor_scalar", s_.t[:], e.t[:], 1.0, None, ALU.add), reads=[e.buf], writes=[s_.buf])
                        P.op("dve", I("reciprocal", s_.t[:], s_.t[:]), reads=[s_.buf], writes=[s_.buf])
                        o = ost.next()
                        P.op("dve", I("scalar_tensor_tensor", o.t[:], e.t[:], oml[:, mm:mm + 1], s_.t[:], ALU.mult, ALU.mult),
                             reads=[e.buf, s_.buf, bconst], writes=[o.buf])
                        store(k1T[dsl[0], dsl[1]], o, o.t[:])
                        f_ = f32.next()
                        P.op("pool", I("tensor_scalar", f_.t[:], s_.t[:], oml[:, mm:mm + 1], lb[:, mm:mm + 1], ALU.mult, ALU.add),
                             reads=[s_.buf, bconst], writes=[f_.buf])
                        g_ = gout.next()
                        P.op("act", I("activation", g_.t[:], f_.t[:], AF.Ln), reads=[f_.buf], writes=[g_.buf])
                        store(g1T[dsl[0], dsl[1]], g_, g_.t[:])
        jobs = [(lambda wt=wt: ws.load(w_in, 0, 32, [(wt * 256, 256, 0)], 256), lambda L, wt=wt: comp(L, wt)) for wt in range(64)]
        pipeline(jobs)
    P.finish(out_toks)
    P.emit()
    return nc


def build_l4():
    nc = new_nc()
    qT = nc.dram_tensor("qT", [HPC, 128, S], BF16, kind="ExternalInput").ap()
    kT = nc.dram_tensor("kT", [HPC, 128, S], BF16, kind="ExternalInput").ap()
    gT = nc.dram_tensor("gT", [HPC, 128, S], F32, kind="ExternalInput").ap()
    v2 = nc.dram_tensor("v2", [HPC, 128, S // 128, 128], BF16, kind="ExternalInput").ap()
    szT = nc.dram_tensor("szT", [HPC * 128, S], BF16, kind="ExternalInput").ap()
    gnd = nc.dram_tensor("gn", [128, 1], F32, kind="ExternalInput").ap()
    m2d = nc.dram_tensor("mask2", [128, 128], BF16, kind="ExternalInput").ap()
    idd = nc.dram_tensor("ident", [128, 128], BF16, kind="ExternalInput").ap()
    smd = nc.dram_tensor("scanmask", [128, 512], F32, kind="ExternalInput").ap()
    OGT = nc.dram_tensor("OG1T", [HPC * 128, S], BF16, kind="ExternalOutput").ap()

    P = Prog(nc)
    bconst = Buf("const")
    ones = P.sbuf("ones", [128, 128], BF16)
    epsc = P.sbuf("epsc", [128, 1], F32)
    gn = P.sbuf("gns", [128, 1], F32)
    mask2 = P.sbuf("mask2s", [128, 128], BF16)
    ident = P.sbuf("idents", [128, 128], BF16)
    scanm = P.sbuf("scanms", [128, 512], F32)
    P.op("dve", I("memset", ones[:], 1.0), writes=[bconst])
    P.op("dve", I("memset", epsc[:], EPS), writes=[bconst])
    P.dma(gn[:], gnd, writes=[bconst])
    P.dma(mask2[:], m2d, writes=[bconst])
    P.dma(ident[:], idd, writes=[bconst])
    P.dma(scanm[:], smd, writes=[bconst])

    H = HPC
    S32 = [P.slot(f"S32_{h}", [128, 128], F32) for h in range(H)]
    Sbf = [P.slot(f"Sbf_{h}", [128, 128], BF16) for h in range(H)]
    for h in range(H):
        P.op("dve", I("memset", S32[h].t[:], 0.0), writes=[S32[h].buf])
        P.op("pool", I("memset", Sbf[h].t[:], 0.0), writes=[Sbf[h].buf])
    Qh = [P.slot(f"Qh_{h}", [128, 512], BF16) for h in range(H)]
    Kht = [P.slot(f"Kht_{h}", [128, 512], BF16) for h in range(H)]
    vt = [P.slot(f"vt_{h}", [128, 4, 128], BF16) for h in range(H)]
    o32 = [P.slot(f"o32_{h}", [128, 512], F32) for h in range(H)]
    dec = [P.slot(f"dec_{h}", [128, 8], F32) for h in range(H)]
    szs = [P.slot(f"szs_{h}", [128, 512], BF16) for h in range(H)]
    qin = P.ring("qin", 2, [128, 512], BF16)
    kin = P.ring("kin", 2, [128, 512], BF16)
    gin = P.ring("gin", 2, [128, 512], F32)
    bt = P.ring("bt", 2, [128, 512], F32)
    d1 = P.ring("d1_", 2, [128, 512], F32)
    d4 = P.ring("d4_", 2, [128, 512], F32)
    E = P.ring("E_", 4, [128, 512], F32)
    Qt = P.ring("Qt", 2, [128, 512], BF16)
    Kt = P.ring("Kt", 2, [128, 512], BF16)
    KhT = P.ring("KhT", 2, [128, 512], BF16)
    attm = P.ring("attm", 2, [128, 512], BF16)
    sqb = P.ring("sqb", 2, [128, 512], BF16)
    sd = P.ring("sd", 2, [128, 512], F32)
    y32 = P.ring("y32", 2, [128, 512], F32)
    ogs = P.ring("ogs", 2, [128, 512], BF16)
    trp = Slot(P.psum("trp", [128, 1024], BF16), "trp")
    attp = Slot(P.psum("attp", [128, 512]), "attp")
    oIp = Slot(P.psum("oIp", [128, 512]), "oIp")
    oXp = P.pring("oXp", 2)
    dSp = [Slot(P.psum(f"dSp{i}", [128, 512]), f"dSp{i}") for i in range(2)]
    ssqp = Slot(P.psum("ssqp", [128, 512]), "ssqp")
    out_toks = []

    NSB = S // 512
    for sb in range(NSB):
        cs = slice(sb * 512, (sb + 1) * 512)
        for h in range(H):
            qi, ki, gi = qin.next(), kin.next(), gin.next()
            P.dma(qi.t[:], qT[h, :, cs], writes=[qi.buf])
            P.dma(ki.t[:], kT[h, :, cs], writes=[ki.buf])
            P.dma(gi.t[:], gT[h, :, cs], writes=[gi.buf])
            P.dma(vt[h].t[:], v2[h, :, sb * 4:(sb + 1) * 4, :], writes=[vt[h].buf])
            P.dma(szs[h].t[:], szT[h * 128:(h + 1) * 128, cs], writes=[szs[h].buf])
            b_ = bt.next()
            P.op("dve", I("tensor_tensor_scan", b_.t[:], scanm[:], gi.t[:], 0.0, ALU.mult, ALU.add),
                 reads=[gi.buf, bconst], writes=[b_.buf])
            b3 = b_.t[:].rearrange("p (n c) -> p n c", c=64)
            x1, x4 = d1.next(), d4.next()
            P.op("dve", I("tensor_tensor", x1.t[:].rearrange("p (n c) -> p n c", c=64), b3,
                          b3[:, :, 32:33].broadcast_to([128, 8, 64]), ALU.subtract), reads=[b_.buf], writes=[x1.buf])
            P.op("pool", I("tensor_tensor", x4.t[:].rearrange("p (n c) -> p n c", c=64), b3,
                           b3[:, :, 63:64].broadcast_to([128, 8, 64]), ALU.subtract), reads=[b_.buf], writes=[x4.buf])
            e1, e2, e3, e4 = E.next(), E.next(), E.next(), E.next()
            P.op("act", I("activation", e1.t[:], x1.t[:], AF.Exp), reads=[x1.buf], writes=[e1.buf])
            P.op("act", I("activation", e2.t[:], x1.t[:], AF.Exp, scale=-1.0), reads=[x1.buf], writes=[e2.buf])
            P.op("act", I("activation", e3.t[:], b_.t[:], AF.Exp), reads=[b_.buf], writes=[e3.buf])
            P.op("act", I("activation", e4.t[:], x4.t[:], AF.Exp, scale=-1.0), reads=[x4.buf], writes=[e4.buf])
            P.op("act", I("activation", dec[h].t[:], b3[:, :, 63], AF.Exp), reads=[b_.buf], writes=[dec[h].buf])
            qt_, kt_, kh_ = Qt.next(), Kt.next(), KhT.next()
            P.op("dve", I("tensor_tensor", qt_.t[:], qi.t[:], e1.t[:], ALU.mult), reads=[qi.buf, e1.buf], writes=[qt_.buf])
            P.op("pool", I("tensor_tensor", kt_.t[:], ki.t[:], e2.t[:], ALU.mult), reads=[ki.buf, e2.buf], writes=[kt_.buf])
            P.op("dve", I("tensor_tensor", Qh[h].t[:], qi.t[:], e3.t[:], ALU.mult), reads=[qi.buf, e3.buf], writes=[Qh[h].buf])
            P.op("pool", I("tensor_tensor", kh_.t[:], ki.t[:], e4.t[:], ALU.mult), reads=[ki.buf, e4.buf], writes=[kh_.buf])
            P.op("pe", [I("transpose", trp.t[:, pr * 128:(pr + 1) * 128], kh_.t[:, pr * 128:(pr + 1) * 128], ident[:]) for pr in range(4)],
                 reads=[kh_.buf, bconst], writes=[trp.buf])
            P.op("act", I("activation", Kht[h].t[:], trp.t[:, 0:512], AF.Copy), reads=[trp.buf], writes=[Kht[h].buf])
            P.op("pe", [I("matmul", attp.t[:, pr * 128:(pr + 1) * 128], kt_.t[:, pr * 128:(pr + 1) * 128], qt_.t[:, pr * 128:(pr + 1) * 128],
                          start=True, stop=True) for pr in range(4)], reads=[kt_.buf, qt_.buf], writes=[attp.buf])
            am = attm.next()
            P.op("dve", I("tensor_tensor", am.t[:].rearrange("p (n c) -> p n c", c=128), attp.t[:].rearrange("p (n c) -> p n c", c=128),
                          mask2[:].rearrange("p (o c) -> p o c", o=1).broadcast_to([128, 4, 128]), ALU.mult),
                 reads=[attp.buf, bconst], writes=[am.buf])
            P.op("pe", [I("matmul", oIp.t[:, pr * 128:(pr + 1) * 128], vt[h].t[:, pr, :], am.t[:, pr * 128:(pr + 1) * 128],
                          start=True, stop=True) for pr in range(4)], reads=[vt[h].buf, am.buf], writes=[oIp.buf])
            P.op("act", I("activation", o32[h].t[:], oIp.t[:], AF.Copy), reads=[oIp.buf], writes=[o32[h].buf])
        for pr in range(4):
            for ch in range(2):
                c0 = pr * 128 + ch * 64
                ps = slice(ch * 64, ch * 64 + 64)
                ox = oXp.next()
                for h in range(H):
                    P.op("pe", I("matmul", ox.t[:, h * 64:(h + 1) * 64], Sbf[h].t[:], Qh[h].t[:, c0:c0 + 64], start=True, stop=True),
                         reads=[Sbf[h].buf, Qh[h].buf], writes=[ox.buf])
                for h in range(H):
                    dsp = dSp[h // 4]
                    P.op("pe", I("matmul", dsp.t[:, (h % 4) * 128:(h % 4 + 1) * 128], Kht[h].t[ps, pr * 128:(pr + 1) * 128],
                                 vt[h].t[ps, pr, :], start=True, stop=True),
                         reads=[Kht[h].buf, vt[h].buf], writes=[dsp.buf])
                for h in range(H):
                    P.op("dve", I("tensor_tensor", o32[h].t[:, c0:c0 + 64], ox.t[:, h * 64:(h + 1) * 64], o32[h].t[:, c0:c0 + 64], ALU.add),
                         reads=[ox.buf, o32[h].buf], writes=[o32[h].buf])
                for h in range(H):
                    dsp = dSp[h // 4]
                    P.op("dve", I("scalar_tensor_tensor", S32[h].t[:], S32[h].t[:], dec[h].t[:, pr * 2 + ch:pr * 2 + ch + 1],
                                  dsp.t[:, (h % 4) * 128:(h % 4 + 1) * 128], ALU.mult, ALU.add),
                         reads=[S32[h].buf, dec[h].buf, dsp.buf], writes=[S32[h].buf])
                    P.op("pool", I("tensor_copy", Sbf[h].t[:], S32[h].t[:]), reads=[S32[h].buf], writes=[Sbf[h].buf])
        for h in range(H):
            sq = sqb.next()
            P.op("act", I("activation", sq.t[:], o32[h].t[:], AF.Square), reads=[o32[h].buf], writes=[sq.buf])
            P.op("pe", I("matmul", ssqp.t[:], ones[:], sq.t[:], start=True, stop=True), reads=[sq.buf, bconst], writes=[ssqp.buf])
            s_ = sd.next()
            P.op("act", I("activation", s_.t[:], ssqp.t[:], AF.Sqrt, bias=epsc[:, 0:1], scale=1.0 / 128), reads=[ssqp.buf, bconst], writes=[s_.buf])
            P.op("dve", I("reciprocal", s_.t[:], s_.t[:]), reads=[s_.buf], writes=[s_.buf])
            y = y32.next()
            P.op("dve", I("scalar_tensor_tensor", y.t[:], o32[h].t[:], gn[:, 0:1], s_.t[:], ALU.mult, ALU.mult),
                 reads=[o32[h].buf, s_.buf, bconst], writes=[y.buf])
            g_ = ogs.next()
            P.op("pool", I("tensor_tensor", g_.t[:], y.t[:], szs[h].t[:], ALU.mult), reads=[y.buf, szs[h].buf], writes=[g_.buf])
            out_toks.append(P.dma(OGT[h * 128:(h + 1) * 128, cs], g_.t[:], reads=[g_.buf], owner=g_.buf, queue="pool"))
    P.finish(out_toks)
    P.emit()
    return nc


def l4_consts():
    import ml_dtypes
    s = np.arange(128)[:, None]
    c = np.arange(128)[None, :]
    m2 = ((s // 64 == c // 64) & (s <= c)).astype(np.float32).astype(ml_dtypes.bfloat16)
    ident = np.eye(128, dtype=np.float32).astype(ml_dtypes.bfloat16)
    sm = np.ones((128, 512), np.float32)
    sm[:, ::64] = 0.0
    return m2, ident, sm


GROUPS = [[0, 1, 2, 3], [4, 5, 6, 7]]
NTP = 1024


def stage_a(nc, t):
    P = Prog(nc)
    NT = NTP
    xT, pos, w_lat, w_z, w_uq, w_ukv = t["xTall"], t["pos"], t["w_lat"], t["w_z"], t["w_uq"], t["w_ukv"]
    QnT, QrT, KnT, KrT, Vh, SZT = t["QnT"], t["QrT"], t["KnT"], t["KrT"], t["Vh"], t["SZT"]
    ws = WStream(P, nstage=2, nw=2)
    ones = P.sbuf("ones", [128, 128], BF16)
    g0 = P.sbuf("g0s", [128, 32], F32)
    qg = P.sbuf("qgs", [128, 8], F32)
    kvg = P.sbuf("kvgs", [128, 4], F32)
    rc = P.sbuf("rcs", [64, 2], F32)
    epsc = P.sbuf("epsc", [128, 1], F32)
    bconst = Buf("const")
    P.op("dve", I("memset", ones[:], 1.0), writes=[bconst])
    P.op("dve", I("memset", epsc[:], EPS), writes=[bconst])
    P.dma(g0[:], t["g0"], writes=[bconst])
    P.dma(qg[:], t["qg"], writes=[bconst])
    P.dma(kvg[:], t["kvg"], writes=[bconst])
    P.dma(rc[:], t["ropec"], writes=[bconst])

    xg = P.sbuf("xg", [128, 32, NT], BF16)
    bxg = Buf("xg")
    sqb = P.ring("sqb", 2, [128, NT], BF16)
    t32 = P.ring("t32_", 4, [128, 512], F32)
    sq2 = P.ring("sq2_", 2, [128, 512], BF16)
    ost = P.ring("ost", 4, [128, 512], BF16)
    rstd_x = P.sbuf("rstd_x", [128, NT], F32)
    rstd_q = P.sbuf("rstd_q", [128, NT], F32)
    rstd_kv = P.sbuf("rstd_kv", [128, NT], F32)
    brx, brq, brkv = Buf("rstd_x"), Buf("rstd_q"), Buf("rstd_kv")
    cqg = P.sbuf("cqg", [128, 8, NT], BF16)
    bcqg = Buf("cqg")
    ckvn = P.sbuf("ckvn", [128, 4, NT], BF16)
    bckvn = Buf("ckvn")
    cosT = P.sbuf("cosT", [64, NT], F32)
    sinT = P.sbuf("sinT", [64, NT], F32)
    btab = Buf("ropetab")
    ra = P.sbuf("ra", [64, NT], F32)
    rb = P.sbuf("rb", [64, NT], F32)
    rcc = P.sbuf("rcc", [64, NT], F32)
    ri = P.sbuf("ri", [64, NT], I32)
    bra, brb, brcc, bri = Buf("ra"), Buf("rb"), Buf("rcc"), Buf("ri")
    acc = P.pring("acc", 4)
    ssq = [Slot(P.psum(f"ssq{i}", [128, 512]), f"ssq{i}") for i in range(2)]
    ssq2 = [Slot(P.psum(f"ssqb{i}", [128, 512]), f"ssqb{i}") for i in range(2)]

    def rstd_from(ssq_slot, dst, bdst, n, inv_n):
        tt = t32.next()
        P.op("act", I("activation", tt.t[:], ssq_slot.t[:], AF.Sqrt, bias=epsc[:, 0:1], scale=inv_n),
             reads=[ssq_slot.buf, bconst], writes=[tt.buf])
        P.op("dve", I("reciprocal", dst[:, n * 512:(n + 1) * 512], tt.t[:]), reads=[tt.buf], writes=[bdst])

    def store(dst_ap, src_slot, src_ap):
        P.dma(dst_ap, src_ap, reads=[src_slot.buf], owner=src_slot.buf, queue="pool")

    for p in range(S // NT):
        t0 = p * NT
        P.dma(ri[:], pos[t0:t0 + NT].partition_broadcast(64), writes=[bri])
        P.op("dve", I("tensor_copy", ra[:], ri[:]), reads=[bri], writes=[bra])
        P.op("dve", I("tensor_scalar", ra[:], ra[:], rc[:, 0:1], None, ALU.mult), reads=[bra, bconst], writes=[bra])
        P.op("dve", I("tensor_scalar", rb[:], ra[:], 1.0 / TWO_PI, None, ALU.mult), reads=[bra], writes=[brb])
        P.op("dve", I("tensor_copy", ri[:], rb[:]), reads=[brb], writes=[bri])
        P.op("dve", I("tensor_copy", rb[:], ri[:]), reads=[bri], writes=[brb])
        P.op("dve", I("scalar_tensor_tensor", ra[:], rb[:], -CW1, ra[:], ALU.mult, ALU.add), reads=[bra, brb], writes=[bra])
        P.op("dve", I("scalar_tensor_tensor", ra[:], rb[:], -CW2, ra[:], ALU.mult, ALU.add), reads=[bra, brb], writes=[bra])
        P.op("dve", I("tensor_single_scalar", rb[:], ra[:], 0.0, ALU.is_lt), reads=[bra], writes=[brb])
        P.op("dve", I("scalar_tensor_tensor", ra[:], rb[:], TWO_PI, ra[:], ALU.mult, ALU.add), reads=[bra, brb], writes=[bra])
        P.op("dve", I("tensor_scalar", rb[:], ra[:], -1.0, math.pi, ALU.mult, ALU.add), reads=[bra], writes=[brb])
        P.op("dve", I("tensor_scalar", rb[:], rb[:], PI_LO, -PI_LO, ALU.min, ALU.max), reads=[brb], writes=[brb])
        P.op("act", I("activation", sinT[:], rb[:], AF.Sin), reads=[brb], writes=[btab])
        P.op("dve", I("tensor_scalar", rcc[:], ra[:], math.pi / 2, None, ALU.add), reads=[bra], writes=[brcc])
        P.op("dve", I("tensor_single_scalar", rb[:], rcc[:], TWO_PI, ALU.is_ge), reads=[brcc, btab], writes=[brb])
        P.op("dve", I("scalar_tensor_tensor", rcc[:], rb[:], -TWO_PI, rcc[:], ALU.mult, ALU.add), reads=[brcc, brb], writes=[brcc])
        P.op("dve", I("tensor_scalar", rcc[:], rcc[:], -1.0, math.pi, ALU.mult, ALU.add), reads=[brcc], writes=[brcc])
        P.op("dve", I("tensor_scalar", rcc[:], rcc[:], PI_LO, -PI_LO, ALU.min, ALU.max), reads=[brcc], writes=[brcc])
        P.op("act", I("activation", cosT[:], rcc[:], AF.Sin), reads=[brcc], writes=[btab])
        P.op("dve", I("tensor_scalar", sinT[:], sinT[:], rc[:, 1:2], None, ALU.mult), reads=[btab, bconst], writes=[btab])

        for kc in range(32):
            st = ws.stages.next()
            stv = st.t[:, 0:NT]
            P.dma(stv, xT[kc * 128:(kc + 1) * 128, t0:t0 + NT], writes=[st.buf])
            sq = sqb.next()
            P.op("act", I("activation", sq.t[:], stv, AF.Square), reads=[st.buf], writes=[sq.buf])
            P.op("pe", [I("matmul", ssq[n].t[:], ones[:], sq.t[:, n * 512:(n + 1) * 512], start=(kc == 0), stop=(kc == 31))
                        for n in range(2)], reads=[sq.buf, bconst], writes=[ssq[0].buf, ssq[1].buf])
            P.op("dve", I("tensor_scalar", xg[:, kc, :], stv, g0[:, kc:kc + 1], None, ALU.mult),
                 reads=[st.buf, bconst], writes=[bxg])
        for n in range(2):
            rstd_from(ssq[n], rstd_x, brx, n, 1.0 / D)

        def gemm_x(wbuf, wv, mcols, epilogue):
            for (c0, M, tag) in mcols:
                for n in range(2):
                    a = acc.next()
                    P.op("pe", [I("matmul", a.t[0:M, :], wv[:, kc, c0:c0 + M], xg[:, kc, n * 512:(n + 1) * 512],
                                  start=(kc == 0), stop=(kc == 31)) for kc in range(32)],
                         reads=[wbuf, bxg], writes=[a.buf])
                    epilogue(a, M, tag, n)

        def ep_lat(kind):
            nm = 8 if kind == "q" else 4

            def ep(a, M, m, n):
                cs = slice(n * 512, (n + 1) * 512)
                tt = t32.next()
                P.op("dve", I("tensor_tensor", tt.t[:], a.t[:], rstd_x[:, cs], ALU.mult), reads=[a.buf, brx], writes=[tt.buf])
                s2 = sq2.next()
                P.op("act", I("activation", s2.t[:], tt.t[:], AF.Square), reads=[tt.buf], writes=[s2.buf])
                P.op("pe", I("matmul", ssq2[n].t[:], ones[:], s2.t[:], start=(m == 0), stop=(m == nm - 1)),
                     reads=[s2.buf, bconst], writes=[ssq2[n].buf])
                if kind == "q":
                    P.op("pool", I("tensor_scalar", cqg[:, m, cs], tt.t[:], qg[:, m:m + 1], None, ALU.mult),
                         reads=[tt.buf, bconst], writes=[bcqg])
                else:
                    P.op("pool", I("tensor_scalar", ckvn[:, m, cs], tt.t[:], kvg[:, m:m + 1], None, ALU.mult),
                         reads=[tt.buf, bconst], writes=[bckvn])
            return ep

        def rope_store(aA, aB, rs, brs, n, dst_ap):
            tA = t32.next()
            tB = t32.next()
            cs = slice(n * 512, (n + 1) * 512)
            P.op("dve", I("tensor_tensor", tA.t[0:64, :], aA.t[0:64, :], rs[0:64, cs], ALU.mult), reads=[aA.buf, brs], writes=[tA.buf])
            P.op("dve", I("tensor_tensor", tB.t[0:64, :], aB.t[0:64, :], rs[0:64, cs], ALU.mult), reads=[aB.buf, brs], writes=[tB.buf])
            P.op("pool", I("tensor_tensor", tA.t[0:64, :], tA.t[0:64, :], cosT[:, cs], ALU.mult), reads=[tA.buf, btab], writes=[tA.buf])
            P.op("pool", I("tensor_tensor", tB.t[0:64, :], tB.t[0:64, :], sinT[:, cs], ALU.mult), reads=[tB.buf, btab], writes=[tB.buf])
            o = ost.next()
            P.op("dve", I("tensor_tensor", o.t[0:64, :], tA.t[0:64, :], tB.t[0:64, :], ALU.add), reads=[tA.buf, tB.buf], writes=[o.buf])
            store(dst_ap, o, o.t[0:64, :])

        def ep_z(a, M, m, n):
            cs = slice(n * 512, (n + 1) * 512)
            tt = t32.next()
            P.op("dve", I("tensor_tensor", tt.t[:], a.t[:], rstd_x[:, cs], ALU.mult), reads=[a.buf, brx], writes=[tt.buf])
            o = ost.next()
            P.op("act", I("activation", o.t[:], tt.t[:], AF.Silu), reads=[tt.buf], writes=[o.buf])
            store(SZT[m * 128:(m + 1) * 128, t0 + n * 512:t0 + (n + 1) * 512], o, o.t[:])

        jobs = []
        for wt in range(4):
            jobs.append((lambda wt=wt, p=p: ws.load(w_lat, 0, 32, [(wt * 256, 256, 0)], 256, key=("cq", wt), first=(p == 0)),
                         lambda L, wt=wt: gemm_x(L[0], L[1], [(0, 128, wt * 2), (128, 128, wt * 2 + 1)], ep_lat("q"))))

        def after_q():
            for n in range(2):
                rstd_from(ssq2[n], rstd_q, brq, n, 1.0 / 1024)
        for wt in range(2):
            def comp(L, wt=wt):
                if wt == 0:
                    after_q()
                gemm_x(L[0], L[1], [(0, 128, wt * 2), (128, 128, wt * 2 + 1)], ep_lat("kv"))
            jobs.append((lambda wt=wt, p=p: ws.load(w_lat, 0, 32, [(1024 + wt * 256, 256, 0)], 256, key=("ckv", wt), first=(p == 0)), comp))

        def comp_krot(L):
            wbuf, wv = L
            for n in range(2):
                rstd_from(ssq2[n], rstd_kv, brkv, n, 1.0 / 512)
            for m in range(4):
                for n in range(2):
                    cs = slice(n * 512, (n + 1) * 512)
                    P.op("dve", I("tensor_tensor", ckvn[:, m, cs], ckvn[:, m, cs], rstd_kv[:, cs], ALU.mult),
                         reads=[bckvn, brkv], writes=[bckvn])
            for n in range(2):
                aA, aB = acc.next(), acc.next()
                calls = []
                for (a, c0) in ((aA, 0), (aB, 64)):
                    for kc in range(32):
                        calls.append(I("matmul", a.t[0:64, :], wv[:, kc, c0:c0 + 64], xg[:, kc, n * 512:(n + 1) * 512],
                                       start=(kc == 0), stop=(kc == 31)))
                P.op("pe", calls, reads=[wbuf, bxg], writes=[aA.buf, aB.buf])
                rope_store(aA, aB, rstd_x, brx, n, KrT[:, t0 + n * 512:t0 + (n + 1) * 512])
        jobs.append((lambda p=p: ws.load(w_lat, 0, 32, [(1536, 128, 0)], 128, key=("kr", 0), first=(p == 0)), comp_krot))

        for wt in range(4):
            jobs.append((lambda wt=wt, p=p: ws.load(w_z, 0, 32, [(wt * 256, 256, 0)], 256, key=("z", wt), first=(p == 0)),
                         lambda L, wt=wt: gemm_x(L[0], L[1], [(0, 128, wt * 2), (128, 128, wt * 2 + 1)], ep_z)))

        def comp_q(L, h):
            wbuf, wv = L
            for n in range(2):
                cs = slice(n * 512, (n + 1) * 512)
                a0, aA, aB = acc.next(), acc.next(), acc.next()
                calls = []
                for (a, c0, M) in ((a0, 0, 128), (aA, 128, 64), (aB, 192, 64)):
                    for kc in range(8):
                        calls.append(I("matmul", a.t[0:M, :], wv[:, kc, c0:c0 + M], cqg[:, kc, cs], start=(kc == 0), stop=(kc == 7)))
                P.op("pe", calls, reads=[wbuf, bcqg], writes=[a0.buf, aA.buf, aB.buf])
                o = ost.next()
                P.op("dve", I("tensor_tensor", o.t[:], a0.t[:], rstd_q[:, cs], ALU.mult), reads=[a0.buf, brq], writes=[o.buf])
                store(QnT[h, :, t0 + n * 512:t0 + (n + 1) * 512], o, o.t[:])
                rope_store(aA, aB, rstd_q, brq, n, QrT[h, :, t0 + n * 512:t0 + (n + 1) * 512])
        for h in range(HPC):
            jobs.append((lambda h=h, p=p: ws.load(w_uq, 0, 8, [(h * 256, 256, 0)], 256, key=("uq", h), first=(p == 0)), lambda L, h=h: comp_q(L, h)))

        def comp_kv(L, hg):
            wbuf, wv = L
            for hh in range(4):
                h = hg * 4 + hh
                for n in range(2):
                    cs = slice(n * 512, (n + 1) * 512)
                    a = acc.next()
                    P.op("pe", [I("matmul", a.t[:], wv[:, kc, hh * 256:hh * 256 + 128], ckvn[:, kc, cs], start=(kc == 0), stop=(kc == 3))
                                for kc in range(4)], reads=[wbuf, bckvn], writes=[a.buf])
                    o = ost.next()
                    P.op("act", I("activation", o.t[:], a.t[:], AF.Copy), reads=[a.buf], writes=[o.buf])
                    store(KnT[h, :, t0 + n * 512:t0 + (n + 1) * 512], o, o.t[:])
            wv4 = wv.rearrange("p k (h c) -> p k h c", c=256)
            for tb in range(NT // 128):
                a = acc.next()
                av = a.t[:].rearrange("p (h c) -> p h c", c=128)
                P.op("pe", [I("matmul", av, ckvn[:, kc, tb * 128:(tb + 1) * 128], wv4[:, kc, :, 128:256], start=(kc == 0), stop=(kc == 3))
                            for kc in range(4)], reads=[wbuf, bckvn], writes=[a.buf])
                o = ost.next()
                P.op("dve", I("tensor_copy", o.t[:], a.t[:]), reads=[a.buf], writes=[o.buf])
                blk = (t0 + tb * 128) // 128
                store(Vh[hg * 4:(hg + 1) * 4, :, blk, :].rearrange("h p d -> p h d"), o, o.t[:].rearrange("p (h d) -> p h d", d=128))
        for hg in range(HPC // 4):
            jobs.append((lambda hg=hg, p=p: ws.load(w_ukv, 0, 4, [(hg * 1024, 1024, 0)], 1024, key=("ukv", hg), first=(p == 0)), lambda L, hg=hg: comp_kv(L, hg)))
        pipeline(jobs)
    P.end_stage()


def stage_b(nc, t):
    QnT, QrT, KnT, KrT, Vh, SZT, trid, OGT = t["QnT"], t["QrT"], t["KnT"], t["KrT"], t["Vh"], t["SZT"], t["tri"], t["OGT"]
    P = Prog(nc)
    scale = 1.0 / math.sqrt(192.0)
    ones = P.sbuf("ones", [128, 128], BF16)
    tri = P.sbuf("tri_sb", [128, 128], BF16)
    bconst = Buf("const")
    P.op("dve", I("memset", ones[:], 1.0), writes=[bconst])
    P.dma(tri[:], trid, writes=[bconst])
    kr = P.slot("kr", [64, S], BF16)
    P.dma(kr.t[:], KrT, writes=[kr.buf])
    kn = P.ring("kn", 2, [128, S], BF16)
    vv = P.ring("vv", 2, [128, S // 128, 128], BF16)
    qn = P.ring("qn", 3, [128, 512], BF16)
    qr = P.ring("qr", 3, [64, 512], BF16)
    sz = P.ring("sz", 3, [128, 512], BF16)
    pT = P.ring("pT", 3, [128, 512], BF16)
    rs = P.ring("rs", 2, [128, 512], F32)
    o32 = P.ring("o32", 2, [128, 512], F32)
    og = P.ring("og", 2, [128, 512], BF16)
    sT = P.pring("sT", 3)
    oT = P.pring("oT", 2)
    sm = P.pring("sm", 2)
    out_toks = []

    def load_head(h):
        k = kn.next()
        v = vv.next()
        P.dma(k.t[:], KnT[h], writes=[k.buf])
        P.dma(v.t[:], Vh[h], writes=[v.buf])
        return k, v

    def load_q(h, t):
        a, b_, c = qn.next(), qr.next(), sz.next()
        cs = slice(t * 512, (t + 1) * 512)
        P.dma(a.t[:], QnT[h, :, cs], writes=[a.buf])
        P.dma(b_.t[:], QrT[h, :, cs], writes=[b_.buf])
        P.dma(c.t[:], SZT[h * 128:(h + 1) * 128, cs], writes=[c.buf])
        return a, b_, c

    NQT = S // 512
    tiles = [(h, t) for h in range(HPC) for t in range(NQT)]
    blocks = []
    for ti, (h, t) in enumerate(tiles):
        nb = 4 * t + 4
        for kb in range(nb):
            blocks.append((ti, kb, nb))
    state = {}

    def tile_ctx(ti):
        if ti in state:
            return state[ti]
        h, t = tiles[ti]
        if t == 0:
            hk = load_head(h)
        else:
            hk = state[ti - 1]["hk"]
        q = load_q(h, t)
        ctx = {"hk": hk, "q": q, "oT": oT.next(), "sm": sm.next()}
        state[ti] = ctx
        return ctx

    def emit_S(bi):
        ti, kb, nb = blocks[bi]
        h, t = tiles[ti]
        ctx = tile_ctx(ti)
        k, v = ctx["hk"]
        a, b_, c = ctx["q"]
        j = kb - 4 * t
        c0 = 128 * j if j > 0 else 0
        s = sT.next()
        ks = slice(kb * 128, (kb + 1) * 128)
        P.op("pe", [I("matmul", s.t[:, c0:512], k.t[:, ks], a.t[:, c0:512], start=True, stop=False),
                    I("matmul", s.t[:, c0:512], kr.t[:, ks], b_.t[:, c0:512], start=False, stop=True)],
             reads=[k.buf, kr.buf, a.buf, b_.buf], writes=[s.buf])
        return s, c0, j

    pend = {}
    nB = len(blocks)
    AHEAD = 2
    for bi in range(min(AHEAD, nB)):
        pend[bi] = emit_S(bi)
    for bi in range(nB):
        ti, kb, nb = blocks[bi]
        h, t = tiles[ti]
        ctx = state[ti]
        k, v = ctx["hk"]
        s, c0, j = pend.pop(bi)
        p = pT.next()
        P.op("act", I("activation", p.t[:, c0:512], s.t[:, c0:512], AF.Exp, scale=scale), reads=[s.buf], writes=[p.buf])
        if j >= 0:
            P.op("pool", I("tensor_tensor", p.t[:, c0:c0 + 128], p.t[:, c0:c0 + 128], tri[:], ALU.mult),
                 reads=[p.buf, bconst], writes=[p.buf])
        o_, m_ = ctx["oT"], ctx["sm"]
        P.op("pe", [I("matmul", o_.t[:, c0:512], v.t[:, kb, :], p.t[:, c0:512], start=(kb == 0), stop=(kb == nb - 1)),
                    I("matmul", m_.t[:, c0:512], ones[:], p.t[:, c0:512], start=(kb == 0), stop=(kb == nb - 1))],
             reads=[v.buf, p.buf, bconst], writes=[o_.buf, m_.buf])
        if bi + AHEAD < nB:
            pend[bi + AHEAD] = emit_S(bi + AHEAD)
        if kb == nb - 1:
            a, b_, c = ctx["q"]
            r_ = rs.next()
            P.op("dve", I("reciprocal", r_.t[:], m_.t[:]), reads=[m_.buf], writes=[r_.buf])
            x_ = o32.next()
            P.op("dve", I("tensor_tensor", x_.t[:], o_.t[:], r_.t[:], ALU.mult), reads=[o_.buf, r_.buf], writes=[x_.buf])
            g_ = og.next()
            P.op("dve", I("tensor_tensor", g_.t[:], x_.t[:], c.t[:], ALU.mult), reads=[x_.buf, c.buf], writes=[g_.buf])
            out_toks.append(P.dma(OGT[h * 128:(h + 1) * 128, t * 512:(t + 1) * 512], g_.t[:], reads=[g_.buf],
                                  owner=g_.buf, queue="pool"))
            if ti + 1 < len(tiles):
                tile_ctx(ti + 1)

    P.end_stage()


def stage_e(nc, t):
    qT, kT, gT, v2, szT, gnd, m2d, idd, smd, OGT = (t["q1T"], t["k1T"], t["g1T"], t["v2"], t["sz1T"], t["gn"], t["mask2"],
                                                    t["ident"], t["scanmask"], t["OG1T"])
    P = Prog(nc)
    bconst = Buf("const")
    ones = P.sbuf("ones", [128, 128], BF16)
    epsc = P.sbuf("epsc", [128, 1], F32)
    gn = P.sbuf("gns", [128, 1], F32)
    mask2 = P.sbuf("mask2s", [128, 128], BF16)
    ident = P.sbuf("idents", [128, 128], BF16)
    scanm = P.sbuf("scanms", [128, 512], F32)
    P.op("dve", I("memset", ones[:], 1.0), writes=[bconst])
    P.op("dve", I("memset", epsc[:], EPS), writes=[bconst])
    P.dma(gn[:], gnd, writes=[bconst])
    P.dma(mask2[:], m2d, writes=[bconst])
    P.dma(ident[:], idd, writes=[bconst])
    P.dma(scanm[:], smd, writes=[bconst])

    H = HPC
    S32 = [P.slot(f"S32_{h}", [128, 128], F32) for h in range(H)]
    Sbf = [P.slot(f"Sbf_{h}", [128, 128], BF16) for h in range(H)]
    for h in range(H):
        P.op("dve", I("memset", S32[h].t[:], 0.0), writes=[S32[h].buf])
        P.op("pool", I("memset", Sbf[h].t[:], 0.0), writes=[Sbf[h].buf])
    Qh = [P.slot(f"Qh_{h}", [128, 512], BF16) for h in range(H)]
    Kht = [P.slot(f"Kht_{h}", [128, 512], BF16) for h in range(H)]
    vt = [P.slot(f"vt_{h}", [128, 4, 128], BF16) for h in range(H)]
    o32 = [P.slot(f"o32_{h}", [128, 512], F32) for h in range(H)]
    dec = [P.slot(f"dec_{h}", [128, 8], F32) for h in range(H)]
    szs = [P.slot(f"szs_{h}", [128, 512], BF16) for h in range(H)]
    qin = P.ring("qin", 2, [128, 512], BF16)
    kin = P.ring("kin", 2, [128, 512], BF16)
    gin = P.ring("gin", 2, [128, 512], F32)
    bt = P.ring("bt", 2, [128, 512], F32)
    d1 = P.ring("d1_", 2, [128, 512], F32)
    d4 = P.ring("d4_", 2, [128, 512], F32)
    E = P.ring("E_", 4, [128, 512], F32)
    Qt = P.ring("Qt", 2, [128, 512], BF16)
    Kt = P.ring("Kt", 2, [128, 512], BF16)
    KhT = P.ring("KhT", 2, [128, 512], BF16)
    attm = P.ring("attm", 2, [128, 512], BF16)
    sqb = P.ring("sqb", 2, [128, 512], BF16)
    sd = P.ring("sd", 2, [128, 512], F32)
    y32 = P.ring("y32", 2, [128, 512], F32)
    ogs = P.ring("ogs", 2, [128, 512], BF16)
    trp = Slot(P.psum("trp", [128, 1024], BF16), "trp")
    attp = Slot(P.psum("attp", [128, 512]), "attp")
    oIp = Slot(P.psum("oIp", [128, 512]), "oIp")
    oXp = P.pring("oXp", 2)
    dSp = [Slot(P.psum(f"dSp{i}", [128, 512]), f"dSp{i}") for i in range(2)]
    ssqp = Slot(P.psum("ssqp", [128, 512]), "ssqp")
    out_toks = []

    NSB = S // 512
    for sb in range(NSB):
        cs = slice(sb * 512, (sb + 1) * 512)
        for h in range(H):
            qi, ki, gi = qin.next(), kin.next(), gin.next()
            P.dma(qi.t[:], qT[h, :, cs], writes=[qi.buf])
            P.dma(ki.t[:], kT[h, :, cs], writes=[ki.buf])
            P.dma(gi.t[:], gT[h, :, cs], writes=[gi.buf])
            P.dma(vt[h].t[:], v2[h, :, sb * 4:(sb + 1) * 4, :], writes=[vt[h].buf])
            P.dma(szs[h].t[:], szT[h * 128:(h + 1) * 128, cs], writes=[szs[h].buf])
            b_ = bt.next()
            P.op("dve", I("tensor_tensor_scan", b_.t[:], scanm[:], gi.t[:], 0.0, ALU.mult, ALU.add),
                 reads=[gi.buf, bconst], writes=[b_.buf])
            b3 = b_.t[:].rearrange("p (n c) -> p n c", c=64)
            x1, x4 = d1.next(), d4.next()
            P.op("dve", I("tensor_tensor", x1.t[:].rearrange("p (n c) -> p n c", c=64), b3,
                          b3[:, :, 32:33].broadcast_to([128, 8, 64]), ALU.subtract), reads=[b_.buf], writes=[x1.buf])
            P.op("pool", I("tensor_tensor", x4.t[:].rearrange("p (n c) -> p n c", c=64), b3,
                           b3[:, :, 63:64].broadcast_to([128, 8, 64]), ALU.subtract), reads=[b_.buf], writes=[x4.buf])
            e1, e2, e3, e4 = E.next(), E.next(), E.next(), E.next()
            P.op("act", I("activation", e1.t[:], x1.t[:], AF.Exp), reads=[x1.buf], writes=[e1.buf])
            P.op("act", I("activation", e2.t[:], x1.t[:], AF.Exp, scale=-1.0), reads=[x1.buf], writes=[e2.buf])
            P.op("act", I("activation", e3.t[:], b_.t[:], AF.Exp), reads=[b_.buf], writes=[e3.buf])
            P.op("act", I("activation", e4.t[:], x4.t[:], AF.Exp, scale=-1.0), reads=[x4.buf], writes=[e4.buf])
            P.op("act", I("activation", dec[h].t[:], b3[:, :, 63], AF.Exp), reads=[b_.buf], writes=[dec[h].buf])
            qt_, kt_, kh_ = Qt.next(), Kt.next(), KhT.next()
            P.op("dve", I("tensor_tensor", qt_.t[:], qi.t[:], e1.t[:], ALU.mult), reads=[qi.buf, e1.buf], writes=[qt_.buf])
            P.op("pool", I("tensor_tensor", kt_.t[:], ki.t[:], e2.t[:], ALU.mult), reads=[ki.buf, e2.buf], writes=[kt_.buf])
            P.op("dve", I("tensor_tensor", Qh[h].t[:], qi.t[:], e3.t[:], ALU.mult), reads=[qi.buf, e3.buf], writes=[Qh[h].buf])
            P.op("pool", I("tensor_tensor", kh_.t[:], ki.t[:], e4.t[:], ALU.mult), reads=[ki.buf, e4.buf], writes=[kh_.buf])
            P.op("pe", [I("transpose", trp.t[:, pr * 128:(pr + 1) * 128], kh_.t[:, pr * 128:(pr + 1) * 128], ident[:]) for pr in range(4)],
                 reads=[kh_.buf, bconst], writes=[trp.buf])
            P.op("act", I("activation", Kht[h].t[:], trp.t[:, 0:512], AF.Copy), reads=[trp.buf], writes=[Kht[h].buf])
            P.op("pe", [I("matmul", attp.t[:, pr * 128:(pr + 1) * 128], kt_.t[:, pr * 128:(pr + 1) * 128], qt_.t[:, pr * 128:(pr + 1) * 128],
                          start=True, stop=True) for pr in range(4)], reads=[kt_.buf, qt_.buf], writes=[attp.buf])
            am = attm.next()
            P.op("dve", I("tensor_tensor", am.t[:].rearrange("p (n c) -> p n c", c=128), attp.t[:].rearrange("p (n c) -> p n c", c=128),
                          mask2[:].rearrange("p (o c) -> p o c", o=1).broadcast_to([128, 4, 128]), ALU.mult),
                 reads=[attp.buf, bconst], writes=[am.buf])
            P.op("pe", [I("matmul", oIp.t[:, pr * 128:(pr + 1) * 128], vt[h].t[:, pr, :], am.t[:, pr * 128:(pr + 1) * 128],
                          start=True, stop=True) for pr in range(4)], reads=[vt[h].buf, am.buf], writes=[oIp.buf])
            P.op("act", I("activation", o32[h].t[:], oIp.t[:], AF.Copy), reads=[oIp.buf], writes=[o32[h].buf])
        for pr in range(4):
            for ch in range(2):
                c0 = pr * 128 + ch * 64
                ps = slice(ch * 64, ch * 64 + 64)
                ox = oXp.next()
                for h in range(H):
                    P.op("pe", I("matmul", ox.t[:, h * 64:(h + 1) * 64], Sbf[h].t[:], Qh[h].t[:, c0:c0 + 64], start=True, stop=True),
                         reads=[Sbf[h].buf, Qh[h].buf], writes=[ox.buf])
                for h in range(H):
                    dsp = dSp[h // 4]
                    P.op("pe", I("matmul", dsp.t[:, (h % 4) * 128:(h % 4 + 1) * 128], Kht[h].t[ps, pr * 128:(pr + 1) * 128],
                                 vt[h].t[ps, pr, :], start=True, stop=True),
                         reads=[Kht[h].buf, vt[h].buf], writes=[dsp.buf])
                for h in range(H):
                    P.op("dve", I("tensor_tensor", o32[h].t[:, c0:c0 + 64], ox.t[:, h * 64:(h + 1) * 64], o32[h].t[:, c0:c0 + 64], ALU.add),
                         reads=[ox.buf, o32[h].buf], writes=[o32[h].buf])
                for h in range(H):
                    dsp = dSp[h // 4]
                    P.op("dve", I("scalar_tensor_tensor", S32[h].t[:], S32[h].t[:], dec[h].t[:, pr * 2 + ch:pr * 2 + ch + 1],
                                  dsp.t[:, (h % 4) * 128:(h % 4 + 1) * 128], ALU.mult, ALU.add),
                         reads=[S32[h].buf, dec[h].buf, dsp.buf], writes=[S32[h].buf])
                    P.op("pool", I("tensor_copy", Sbf[h].t[:], S32[h].t[:]), reads=[S32[h].buf], writes=[Sbf[h].buf])
        for h in range(H):
            sq = sqb.next()
            P.op("act", I("activation", sq.t[:], o32[h].t[:], AF.Square), reads=[o32[h].buf], writes=[sq.buf])
            P.op("pe", I("matmul", ssqp.t[:], ones[:], sq.t[:], start=True, stop=True), reads=[sq.buf, bconst], writes=[ssqp.buf])
            s_ = sd.next()
            P.op("act", I("activation", s_.t[:], ssqp.t[:], AF.Sqrt, bias=epsc[:, 0:1], scale=1.0 / 128), reads=[ssqp.buf, bconst], writes=[s_.buf])
            P.op("dve", I("reciprocal", s_.t[:], s_.t[:]), reads=[s_.buf], writes=[s_.buf])
            y = y32.next()
            P.op("dve", I("scalar_tensor_tensor", y.t[:], o32[h].t[:], gn[:, 0:1], s_.t[:], ALU.mult, ALU.mult),
                 reads=[o32[h].buf, s_.buf, bconst], writes=[y.buf])
            g_ = ogs.next()
            P.op("pool", I("tensor_tensor", g_.t[:], y.t[:], szs[h].t[:], ALU.mult), reads=[y.buf, szs[h].buf], writes=[g_.buf])
            out_toks.append(P.dma(OGT[h * 128:(h + 1) * 128, cs], g_.t[:], reads=[g_.buf], owner=g_.buf, queue="pool"))
    P.end_stage()


def stage_gather(nc, src_ap, dst_ap, rows):
    P = Prog(nc)
    R = src_ap.shape[0]
    for j in range(R // rows):
        P.op("pool", I("collective_compute", "AllGather", ALU.bypass, replica_groups=GROUPS,
                       ins=[src_ap[j * rows:(j + 1) * rows, :]], outs=[dst_ap[j]]))
    P.end_stage()


def stage_wo(nc, t, final):
    P = Prog(nc)
    NT = NTP
    gOG, xT, w_o, outT = t["gOG"], t["xT"], t["w_o"], t["outT"]
    rcache = {}

    def rank_of(e):
        if "r" not in rcache:
            rcache["r"] = e.partition_id() % 4
        return rcache["r"]
    ws = WStream(P, nstage=2, nw=2)
    bconst = Buf("const")
    ones = P.sbuf("ones", [128, 128], BF16)
    epsc = P.sbuf("epsc", [128, 1], F32)
    P.op("dve", I("memset", ones[:], 1.0), writes=[bconst])
    P.op("dve", I("memset", epsc[:], EPS), writes=[bconst])
    if final:
        fg = P.sbuf("fgs", [128, 32], F32)
        P.dma(fg[:], t["fg"], writes=[bconst])
        x2T = t["x2T"]
    else:
        g1 = P.sbuf("g1s", [128, 32], F32)
        P.dma(g1[:], t["g1"], writes=[bconst])
        xg1T, rstdT = t["xg1T"], t["rstdT"]
        ostw = P.ring("ostw", 3, [128, 512], BF16)
    og = P.slot("og", [128, 32, NT], BF16)
    xs = P.ring("xs", 3, [128, 512], F32)
    t32 = P.ring("t32_", 3, [128, 512], F32)
    sq2 = P.ring("sq2_", 2, [128, 512], BF16)
    acc = P.pring("acc", 4)
    ssq = [Slot(P.psum(f"ssq{i}", [128, 512]), f"ssq{i}") for i in range(2)]
    rstd = P.sbuf("rstd", [128, NT], F32)
    brstd = Buf("rstd")
    bscr = Buf("x2scratch")

    ogown = t["ogown"]
    bown = Buf("ogown")
    g4 = gOG.rearrange("j (g i) s -> j g i s", i=64)
    for q in range(4):
        P.dma(ogown[q * 1024:(q + 1) * 1024, :].rearrange("(j i) s -> j i s", i=64),
              Lazy(lambda e, q=q: g4[:, q, :, bass.ds(rank_of(e) * TOK, TOK)]), writes=[bown])
    for p in range(TOK // NT):
        t0 = p * NT
        for kc in range(32):
            P.dma(og.t[:, kc, :], ogown[kc * 128:(kc + 1) * 128, t0:t0 + NT], reads=[bown], writes=[og.buf])

        def comp(L, wt):
            wbuf, wv = L
            for mi in range(2):
                m = wt * 2 + mi
                for n in range(2):
                    a = acc.next()
                    P.op("pe", [I("matmul", a.t[:], wv[:, kc, mi * 128:(mi + 1) * 128], og.t[:, kc, n * 512:(n + 1) * 512],
                                  start=(kc == 0), stop=(kc == 31)) for kc in range(32)], reads=[wbuf, og.buf], writes=[a.buf])
                    x_ = xs.next()
                    dsl = (slice(m * 128, (m + 1) * 128), slice(t0 + n * 512, t0 + (n + 1) * 512))
                    P.dma(x_.t[:], xT[dsl[0], dsl[1]], writes=[x_.buf])
                    tt = t32.next()
                    P.op("dve", I("tensor_tensor", tt.t[:], a.t[:], x_.t[:], ALU.add), reads=[a.buf, x_.buf], writes=[tt.buf])
                    if not final:
                        P.dma(outT[dsl[0], dsl[1]], tt.t[:], reads=[tt.buf], owner=tt.buf, queue="pool")
                        s2 = sq2.next()
                        P.op("act", I("activation", s2.t[:], tt.t[:], AF.Square), reads=[tt.buf], writes=[s2.buf])
                        P.op("pe", I("matmul", ssq[n].t[:], ones[:], s2.t[:], start=(m == 0), stop=(m == 31)),
                             reads=[s2.buf, bconst], writes=[ssq[n].buf])
                        o = ostw.next()
                        P.op("pool", I("tensor_scalar", o.t[:], tt.t[:], g1[:, m:m + 1], None, ALU.mult),
                             reads=[tt.buf, bconst], writes=[o.buf])
                        P.dma(xg1T[dsl[0], dsl[1]], o.t[:], reads=[o.buf], owner=o.buf, queue="pool")
                    else:
                        P.dma(x2T[dsl[0], dsl[1]], tt.t[:], reads=[tt.buf], writes=[bscr], owner=bscr, queue="pool")
                        s2 = sq2.next()
                        P.op("act", I("activation", s2.t[:], tt.t[:], AF.Square), reads=[tt.buf], writes=[s2.buf])
                        P.op("pe", I("matmul", ssq[n].t[:], ones[:], s2.t[:], start=(m == 0), stop=(m == 31)),
                             reads=[s2.buf, bconst], writes=[ssq[n].buf])
        jobs = [(lambda wt=wt, p=p: ws.load(w_o, 0, 32, [(wt * 256, 256, 0)], 256, key=("wo", wt), first=(p == 0)), lambda L, wt=wt: comp(L, wt)) for wt in range(16)]
        pipeline(jobs)
        if not final:
            for n in range(2):
                tt = t32.next()
                P.op("act", I("activation", tt.t[:], ssq[n].t[:], AF.Sqrt, bias=epsc[:, 0:1], scale=1.0 / D),
                     reads=[ssq[n].buf, bconst], writes=[tt.buf])
                P.op("dve", I("reciprocal", tt.t[:], tt.t[:]), reads=[tt.buf], writes=[tt.buf])
                P.dma(rstdT[:, t0 + n * 512:t0 + (n + 1) * 512], tt.t[:], reads=[tt.buf], owner=tt.buf, queue="pool")
        if final:
            for n in range(2):
                tt = t32.next()
                P.op("act", I("activation", tt.t[:], ssq[n].t[:], AF.Sqrt, bias=epsc[:, 0:1], scale=1.0 / D),
                     reads=[ssq[n].buf, bconst], writes=[tt.buf])
                P.op("dve", I("reciprocal", rstd[:, n * 512:(n + 1) * 512], tt.t[:]), reads=[tt.buf], writes=[brstd])
            for m in range(32):
                for n in range(2):
                    cs = slice(n * 512, (n + 1) * 512)
                    dsl = (slice(m * 128, (m + 1) * 128), slice(t0 + n * 512, t0 + (n + 1) * 512))
                    x_ = xs.next()
                    P.dma(x_.t[:], x2T[dsl[0], dsl[1]], reads=[bscr], writes=[x_.buf])
                    tt = t32.next()
                    P.op("dve", I("scalar_tensor_tensor", tt.t[:], x_.t[:], fg[:, m:m + 1], rstd[:, cs], ALU.mult, ALU.mult),
                         reads=[x_.buf, brstd, bconst], writes=[tt.buf])
                    P.dma(outT[dsl[0], dsl[1]], tt.t[:], reads=[tt.buf], owner=tt.buf, queue="pool")
    P.end_stage()


def stage_d(nc, t):
    P = Prog(nc)
    NT = NTP
    gXg, gRs, w1 = t["gXg"], t["gRs"], t["w1"]
    q1T, k1T, g1T, v2, sz1T = t["q1T"], t["k1T"], t["g1T"], t["v2"], t["sz1T"]
    ws = WStream(P, nstage=2, nw=2)
    bconst = Buf("const")
    ones = P.sbuf("ones", [128, 128], BF16)
    epsc = P.sbuf("epsc", [128, 1], F32)
    g1 = P.sbuf("g1s", [128, 32], F32)
    lbs = P.sbuf("lbss", [128, 2, HPC], F32)
    lb = P.sbuf("lb", [128, HPC], F32)
    oml = P.sbuf("oml", [128, HPC], F32)
    dlb = P.sbuf("dlb", [128, HPC], F32)
    ident = P.sbuf("identd", [128, 128], BF16)
    P.op("dve", I("memset", ones[:], 1.0), writes=[bconst])
    P.op("dve", I("memset", epsc[:], EPS), writes=[bconst])
    P.dma(g1[:], t["g1"], writes=[bconst])
    P.dma(lbs[:], t["lbs"], writes=[bconst])
    P.dma(ident[:], t["ident"], writes=[bconst])
    P.op("dve", I("tensor_tensor", dlb[:], lbs[:, 1, :], lbs[:, 0, :], ALU.subtract), reads=[bconst], writes=[bconst])
    P.op("act", I("activation", dlb[:], dlb[:], AF.Exp, scale=-1.0), reads=[bconst], writes=[bconst])
    P.op("dve", I("tensor_scalar", lb[:], dlb[:], 1.0, None, ALU.add), reads=[bconst], writes=[bconst])
    P.op("dve", I("reciprocal", lb[:], lb[:]), reads=[bconst], writes=[bconst])
    P.op("dve", I("tensor_tensor", oml[:], dlb[:], lb[:], ALU.mult), reads=[bconst], writes=[bconst])

    xg = P.sbuf("xg", [128, 32, NT], BF16)
    bxg = Buf("xg")
    sqb = P.ring("sqb", 2, [128, NT], BF16)
    t32 = P.ring("t32_", 3, [128, 512], F32)
    e32 = P.ring("e32_", 2, [128, 512], F32)
    s32 = P.ring("s32_", 2, [128, 512], F32)
    f32r = P.ring("f32_", 2, [128, 512], F32)
    gout = P.ring("gout", 2, [128, 512], F32)
    ost = P.ring("ost", 4, [128, 512], BF16)
    vtok = P.ring("vtok", 2, [128, 512], BF16)
    rstd_x = P.sbuf("rstd_x", [128, NT], F32)
    brx = Buf("rstd_x")
    acc = P.pring("acc", 4)
    ssq = [Slot(P.psum(f"ssq{i}", [128, 512]), f"ssq{i}") for i in range(2)]
    trp = Slot(P.psum("trpd", [128, 1024], BF16), "trpd")

    def store(dst_ap, src_slot, src_ap):
        P.dma(dst_ap, src_ap, reads=[src_slot.buf], owner=src_slot.buf, queue="pool")

    for p in range(S // NT):
        t0 = p * NT
        rk, c0 = p // 2, (p % 2) * NT
        for kc in range(32):
            P.dma(xg[:, kc, :], gXg[kc // 2, rk * 256 + (kc % 2) * 128:rk * 256 + (kc % 2) * 128 + 128, c0:c0 + NT], writes=[bxg])
        P.dma(rstd_x[:, :], gRs[0, rk * 128:(rk + 1) * 128, c0:c0 + NT], writes=[brx])

        def comp(L, wt):
            wbuf, wv = L
            kind = wt // 4
            for mi in range(2):
                mm = (wt % 4) * 2 + mi
                for n in range(2):
                    cs = slice(n * 512, (n + 1) * 512)
                    dsl = (slice(mm * 128, (mm + 1) * 128), slice(t0 + n * 512, t0 + (n + 1) * 512))
                    a = acc.next()
                    P.op("pe", [I("matmul", a.t[:], wv[:, kc, mi * 128:(mi + 1) * 128], xg[:, kc, cs], start=(kc == 0), stop=(kc == 31))
                                for kc in range(32)], reads=[wbuf, bxg], writes=[a.buf])
                    tt = t32.next()
                    P.op("dve", I("tensor_tensor", tt.t[:], a.t[:], rstd_x[:, cs], ALU.mult), reads=[a.buf, brx], writes=[tt.buf])
                    if kind == 0 or kind == 3:
                        o = ost.next()
                        P.op("act", I("activation", o.t[:], tt.t[:], AF.Silu), reads=[tt.buf], writes=[o.buf])
                        store((q1T if kind == 0 else sz1T)[dsl[0], dsl[1]], o, o.t[:])
                    elif kind == 2:
                        o = ost.next()
                        P.op("act", I("activation", o.t[:], tt.t[:], AF.Copy), reads=[tt.buf], writes=[o.buf])
                        P.op("pe", [I("transpose", trp.t[:, j * 128:(j + 1) * 128], o.t[:, j * 128:(j + 1) * 128], ident[:]) for j in range(4)],
                             reads=[o.buf, bconst], writes=[trp.buf])
                        vt_ = vtok.next()
                        P.op("dve", I("tensor_copy", vt_.t[:], trp.t[:, 0:512]), reads=[trp.buf], writes=[vt_.buf])
                        blk0 = (t0 + n * 512) // 128
                        store(v2[mm, :, blk0:blk0 + 4, :], vt_, vt_.t[:].rearrange("p (b d) -> p b d", d=128))
                    else:
                        e = e32.next()
                        P.op("act", I("activation", e.t[:], tt.t[:], AF.Exp, scale=-1.0), reads=[tt.buf], writes=[e.buf])
                        s_ = s32.next()
                        P.op("pool", I("tensor_scalar", s_.t[:], e.t[:], 1.0, None, ALU.add), reads=[e.buf], writes=[s_.buf])
                        P.op("dve", I("reciprocal", s_.t[:], s_.t[:]), reads=[s_.buf], writes=[s_.buf])
                        o = ost.next()
                        P.op("dve", I("scalar_tensor_tensor", o.t[:], e.t[:], oml[:, mm:mm + 1], s_.t[:], ALU.mult, ALU.mult),
                             reads=[e.buf, s_.buf, bconst], writes=[o.buf])
                        store(k1T[dsl[0], dsl[1]], o, o.t[:])
                        f_ = f32r.next()
                        P.op("pool", I("tensor_scalar", f_.t[:], s_.t[:], oml[:, mm:mm + 1], lb[:, mm:mm + 1], ALU.mult, ALU.add),
                             reads=[s_.buf, bconst], writes=[f_.buf])
                        g_ = gout.next()
                        P.op("act", I("activation", g_.t[:], f_.t[:], AF.Ln), reads=[f_.buf], writes=[g_.buf])
                        store(g1T[dsl[0], dsl[1]], g_, g_.t[:])
        jobs = [(lambda wt=wt, p=p: ws.load(w1, 0, 32, [(wt * 256, 256, 0)], 256, key=("w1", wt), first=(p == 0)), lambda L, wt=wt: comp(L, wt)) for wt in range(16)]
        pipeline(jobs)
    P.end_stage()


def build_fused():
    nc = new_nc()

    def ext(name, shape, dt):
        return nc.dram_tensor(name, list(shape), dt, kind="ExternalInput").ap()

    def internal(name, shape, dt):
        return nc.dram_tensor(name, list(shape), dt).ap()

    t = {}
    t["xTall"] = ext("xTall", [D, S], F32)
    t["pos"] = ext("pos", [S], I32)
    t["w_lat"] = ext("w_lat", [D, 1664], F32)
    t["w_z"] = ext("w_z", [D, 1024], F32)
    t["w_uq"] = ext("w_uq", [1024, HPC * 256], F32)
    t["w_ukv"] = ext("w_ukv", [512, HPC * 256], F32)
    t["g0"] = ext("g0", [128, 32], F32)
    t["qg"] = ext("qg", [128, 8], F32)
    t["kvg"] = ext("kvg", [128, 4], F32)
    t["ropec"] = ext("ropec", [64, 2], F32)
    t["tri"] = ext("tri", [128, 128], BF16)
    t["xTown"] = ext("xTown", [D, TOK], F32)
    t["w_o0"] = ext("w_o0", [D, D], F32)
    t["w1"] = ext("w1", [D, 4 * HPC * 128], F32)
    t["g1"] = ext("g1", [128, 32], F32)
    t["lbs"] = ext("lbs", [128, 2, HPC], F32)
    t["gn"] = ext("gn", [128, 1], F32)
    t["mask2"] = ext("mask2", [128, 128], BF16)
    t["ident"] = ext("ident", [128, 128], BF16)
    t["scanmask"] = ext("scanmask", [128, 512], F32)
    t["w_o1"] = ext("w_o1", [D, D], F32)
    t["fg"] = ext("fg", [128, 32], F32)
    outT = nc.dram_tensor("outT", [D, TOK], F32, kind="ExternalOutput").ap()

    t["QnT"] = internal("QnT_i", [HPC, 128, S], BF16)
    t["QrT"] = internal("QrT_i", [HPC, 64, S], BF16)
    t["KnT"] = internal("KnT_i", [HPC, 128, S], BF16)
    t["KrT"] = internal("KrT_i", [64, S], BF16)
    t["Vh"] = internal("Vh_i", [HPC, 128, S // 128, 128], BF16)
    t["SZT"] = internal("SZT_i", [HPC * 128, S], BF16)
    t["OGT"] = internal("OGT_i", [HPC * 128, S], BF16)
    gOG = internal("gOG_i", [16, 256, S], BF16)
    x1T = internal("x1T_i", [D, TOK], F32)
    xg1T = internal("xg1T_i", [D, TOK], BF16)
    rstdT = internal("rstdT_i", [128, TOK], F32)
    gXg = internal("gXg_i", [16, 1024, TOK], BF16)
    gRs = internal("gRs_i", [1, 512, TOK], F32)
    q1T = internal("q1T_i", [HPC * 128, S], BF16)
    k1T = internal("k1T_i", [HPC * 128, S], BF16)
    g1T = internal("g1T_i", [HPC * 128, S], F32)
    v2 = internal("v2_i", [HPC, 128, S // 128, 128], BF16)
    sz1T = internal("sz1T_i", [HPC * 128, S], BF16)
    OG1T = internal("OG1T_i", [HPC * 128, S], BF16)
    gOG1 = internal("gOG1_i", [16, 256, S], BF16)
    x2T = internal("x2T_i", [D, TOK], F32)

    stage_a(nc, t)
    stage_b(nc, t)
    stage_gather(nc, t["OGT"], gOG, 64)
    ogown0 = internal("ogown0_i", [D, TOK], BF16)
    ogown1 = internal("ogown1_i", [D, TOK], BF16)
    stage_wo(nc, {"gOG": gOG, "xT": t["xTown"], "w_o": t["w_o0"], "outT": x1T, "ogown": ogown0, "g1": t["g1"],
                  "xg1T": xg1T, "rstdT": rstdT}, final=False)
    stage_gather(nc, xg1T, gXg, 256)
    stage_gather(nc, rstdT, gRs, 128)
    stage_d(nc, {"gXg": gXg, "gRs": gRs, "w1": t["w1"], "g1": t["g1"], "lbs": t["lbs"], "ident": t["ident"],
                 "q1T": q1T, "k1T": k1T, "g1T": g1T, "v2": v2, "sz1T": sz1T})
    h3 = lambda ap: ap.rearrange("(h k) s -> h k s", k=128)
    stage_e(nc, {"q1T": h3(q1T), "k1T": h3(k1T), "g1T": h3(g1T), "v2": v2, "sz1T": sz1T, "gn": t["gn"], "mask2": t["mask2"],
                 "ident": t["ident"], "scanmask": t["scanmask"], "OG1T": OG1T})
    stage_gather(nc, OG1T, gOG1, 64)
    stage_wo(nc, {"gOG": gOG1, "xT": x1T, "w_o": t["w_o1"], "outT": outT, "fg": t["fg"], "x2T": x2T, "ogown": ogown1}, final=True)
    return nc


def kernel(x, positions, l0_norm, l0_w_in, l0_q_norm, l0_w_uq, l0_kv_norm, l0_w_ukv, l0_w_o,
           l1_norm, l1_w_in, l1_g_norm, l1_w_o, lower_bounds, final_norm):
    f32 = lambda a: np.ascontiguousarray(np.asarray(a, dtype=np.float32))
    x = np.asarray(x, dtype=np.float32)
    positions = np.asarray(positions, dtype=np.int32)
    l0_w_in, l0_w_uq, l0_w_ukv, l1_w_in = (np.asarray(a, dtype=np.float32) for a in (l0_w_in, l0_w_uq, l0_w_ukv, l1_w_in))
    lower_bounds = np.asarray(lower_bounds, dtype=np.float32)
    w_lat = np.ascontiguousarray(np.concatenate(
        [l0_w_in[:, :1600], l0_w_in[:, 1568:1600], l0_w_in[:, 1536:1568]], axis=1))
    m2, ident, sm = l4_consts()
    shared = {
        "w_lat": w_lat, "g0": chunked_gain(f32(l0_norm)), "qg": chunked_gain(f32(l0_q_norm)), "kvg": chunked_gain(f32(l0_kv_norm)),
        "ropec": rope_consts(), "tri": tri_const(), "w_o0": f32(l0_w_o), "g1": chunked_gain(f32(l1_norm)),
        "gn": f32(l1_g_norm).reshape(128, 1), "mask2": m2, "ident": ident, "scanmask": sm, "w_o1": f32(l1_w_o),
        "fg": chunked_gain(f32(final_norm)),
    }
    per_g = []
    for g in range(4):
        hs = range(g * HPC, (g + 1) * HPC)
        uq = np.concatenate([np.concatenate([l0_w_uq[:, h * 192:h * 192 + 192], l0_w_uq[:, h * 192 + 160:h * 192 + 192],
                                             l0_w_uq[:, h * 192 + 128:h * 192 + 160]], axis=1) for h in hs], axis=1)
        c0, c1 = g * HPC * 128, (g + 1) * HPC * 128
        per_g.append({
            "w_z": np.ascontiguousarray(l0_w_in[:, 1600 + c0:1600 + c1]),
            "w_uq": np.ascontiguousarray(uq),
            "w_ukv": np.ascontiguousarray(l0_w_ukv[:, g * HPC * 256:(g + 1) * HPC * 256]),
            "w1": np.ascontiguousarray(np.concatenate([l1_w_in[:, s_ * D + c0:s_ * D + c1] for s_ in range(4)], axis=1)),
            "lbs": np.ascontiguousarray(lower_bounds[:, c0:c1].reshape(2, HPC, 128).transpose(2, 0, 1)),
        })
    maps = []
    for b in range(B):
        xTall = np.ascontiguousarray(x[b].T)
        posb = np.ascontiguousarray(positions[b])
        for r in range(4):
            m = dict(shared)
            m.update(per_g[r])
            m["xTall"] = xTall
            m["pos"] = posb
            m["xTown"] = np.ascontiguousarray(xTall[:, r * TOK:(r + 1) * TOK])
            maps.append(m)
    nc = build_fused()
    res = run_bass_kernel_spmd(nc, maps, core_ids=list(range(NCORES)))
    out = np.empty((B, S, D), np.float32)
    for c in range(NCORES):
        b, r = divmod(c, 4)
        out[b, r * TOK:(r + 1) * TOK] = np.asarray(res.results[c]["outT"]).T
    return out
```

```python
import math
import numpy as np
from contextlib import ExitStack
import concourse.bass as bass
import concourse.mybir as mybir
from concourse.bass_utils import run_bass_kernel_spmd

F32 = mybir.dt.float32
BF16 = mybir.dt.bfloat16
I32 = mybir.dt.int32
AF = mybir.ActivationFunctionType
ALU = mybir.AluOpType

NCORES = 8
D = 4096
S = 8192
B = 2
TOK = 2048
NH = 32
EPS = 1e-6
EPOCH = 12000
TWO_PI = 2.0 * math.pi
CW1 = 6.28125
CW2 = TWO_PI - 6.28125
PI_LO = 3.1415925


def I(method, *args, **kw):
    return (method, args, kw)


class Lazy:
    def __init__(self, fn):
        self.fn = fn


class Buf:
    __slots__ = ("name", "w", "r", "sem", "cnt")

    def __init__(self, name=""):
        self.name = name
        self.w = None
        self.r = []
        self.sem = None
        self.cnt = 0


class Ring:
    def __init__(self, items):
        self.items = list(items)
        self.i = 0

    def next(self):
        it = self.items[self.i % len(self.items)]
        self.i += 1
        return it


class Slot:
    def __init__(self, t, name):
        self.t = t
        self.buf = Buf(name)


class Prog:
    ENGS = ("pe", "act", "dve", "pool", "sp")
    SEMUID = 0

    def __init__(self, nc, same_engine_sync=True):
        self.nc = nc
        self.es = ExitStack()
        self.sems = []
        self.dma_bufs = []
        self.lists = {e: [] for e in self.ENGS}
        self.cur_sem = {}
        self.cur_cnt = {}
        self.seen = {e: {} for e in self.ENGS}
        self.nsem = 0
        self.same_engine_sync = same_engine_sync
        self.ninst = 0
        for e in self.ENGS:
            self._new_epoch(e)

    def sem(self, name):
        self.nsem += 1
        Prog.SEMUID += 1
        h = self.nc.alloc_semaphore(name=f"{name}_{Prog.SEMUID}")
        self.sems.append(h)
        return h

    def _new_epoch(self, e):
        self.cur_sem[e] = self.sem(f"s_{e}")
        self.cur_cnt[e] = 0

    def sbuf(self, name, shape, dtype):
        Prog.SEMUID += 1
        return self.es.enter_context(self.nc.sbuf_tensor(f"{name}_u{Prog.SEMUID}", list(shape), dtype))

    def psum(self, name, shape, dtype=F32):
        Prog.SEMUID += 1
        return self.es.enter_context(self.nc.psum_tensor(f"{name}_u{Prog.SEMUID}", list(shape), dtype))

    def slot(self, name, shape, dtype):
        return Slot(self.sbuf(name, shape, dtype), name)

    def ring(self, name, n, shape, dtype):
        return Ring([self.slot(f"{name}{i}", shape, dtype) for i in range(n)])

    def pring(self, name, n):
        return Ring([Slot(self.psum(f"{name}{i}", [128, 512]), f"{name}{i}") for i in range(n)])

    def _deps(self, eng, reads, writes, own_sem=None):
        toks = []
        for b in reads:
            if b.w is not None:
                toks.append(b.w)
        for b in writes:
            if b.w is not None:
                toks.append(b.w)
            toks.extend(b.r)
        need = {}
        for (s, v, src) in toks:
            if src == eng and (eng in ("pe", "sp") or not self.same_engine_sync):
                continue
            if own_sem is not None and s is own_sem:
                continue
            k = id(s)
            if k not in need or need[k][1] < v:
                need[k] = (s, v)
        waits = []
        seen = self.seen[eng]
        for k, (s, v) in need.items():
            if seen.get(k, 0) >= v:
                continue
            seen[k] = v
            waits.append((s, v))
        return waits

    def _mark(self, tok, reads, writes):
        for b in reads:
            b.r.append(tok)
        for b in writes:
            b.w = tok
            b.r = []

    def op(self, eng, calls, reads=(), writes=()):
        if isinstance(calls, tuple):
            calls = [calls]
        waits = self._deps(eng, reads, writes)
        if self.cur_cnt[eng] >= EPOCH:
            self._new_epoch(eng)
        s = self.cur_sem[eng]
        self.cur_cnt[eng] += 1
        tok = (s, self.cur_cnt[eng], eng)
        self.lists[eng].append((waits, calls, s, 1))
        self._mark(tok, reads, writes)
        self.ninst += len(calls)
        return tok

    def dma(self, out, in_, reads=(), writes=(), queue="sp", owner=None):
        own = owner if owner is not None else (writes[0] if writes else reads[0])
        if own.sem is None:
            own.sem = self.sem("d_" + own.name)
            self.dma_bufs.append(own)
        waits = self._deps(queue, reads, writes, own_sem=own.sem)
        own.cnt += 16
        tok = (own.sem, own.cnt, "dma")
        self.lists[queue].append((waits, [I("dma_start", out=out, in_=in_)], own.sem, 16))
        self._mark(tok, reads, writes)
        self.ninst += 1
        return tok

    def finish(self, tokens):
        need = {}
        for (s, v, _) in tokens:
            k = id(s)
            if k not in need or need[k][1] < v:
                need[k] = (s, v)
        self.lists["sp"].append((list(need.values()), [], None, 0))

    def end_stage(self, last=False):
        need = []
        for e in self.ENGS:
            if e != "sp" and self.cur_cnt[e] > 0:
                need.append((self.cur_sem[e], self.cur_cnt[e]))
        for b in self.dma_bufs:
            need.append((b.sem, b.cnt))
        self.lists["sp"].append((need, [], None, 0))
        self.emit(close=False)
        nc = self.nc
        nc.all_engine_barrier()
        self.es.close()
        nc.clear_and_free_semaphores(self.sems)
        nc.all_engine_barrier()

    def emit(self, close=True):
        nc = self.nc
        lists = self.lists

        def replay(e, lst):
            for (waits, calls, s, inc) in lst:
                for (ws, wv) in waits:
                    e.wait_ge(ws, wv)
                inst = None
                for (m, a, k) in calls:
                    if any(isinstance(v, Lazy) for v in k.values()):
                        k = {kk: (v.fn(e) if isinstance(v, Lazy) else v) for kk, v in k.items()}
                    inst = getattr(e, m)(*a, **k)
                if inst is not None:
                    inst.then_inc(s, inc)

        with nc.Block() as block:
            @block.tensor
            def _(e):
                replay(e, lists["pe"])

            @block.scalar
            def _(e):
                replay(e, lists["act"])

            @block.vector
            def _(e):
                replay(e, lists["dve"])

            @block.gpsimd
            def _(e):
                replay(e, lists["pool"])

            @block.sync
            def _(e):
                replay(e, lists["sp"])
        if close:
            self.es.close()


class WStream:
    def __init__(self, P, nstage=2, nw=2, stage_elems=2048, w_elems=8192, cast_engs=("pool",)):
        self.P = P
        self.stage_elems = stage_elems
        self.stages = P.ring("wst", nstage, [128, stage_elems], F32)
        self.ws = P.ring("wbf", nw, [128, w_elems], BF16)
        self.cast = Ring(list(cast_engs))
        self.cache = {}

    def load(self, W, k0, KC, segs, C, key=None, first=True):
        P = self.P
        slot = self.ws.next()
        wv = slot.t[:, 0:KC * C].rearrange("p (k c) -> p k c", c=C)
        if key is not None and not first:
            cap, cbuf = self.cache[key]
            P.dma(slot.t[:, 0:KC * C], cap, reads=[cbuf], writes=[slot.buf])
            return slot.buf, wv
        kpp = max(1, self.stage_elems // C)
        for kc0 in range(0, KC, kpp):
            kn = min(kpp, KC - kc0)
            st = self.stages.next()
            stv = st.t[:, 0:kn * C].rearrange("p (k c) -> p k c", c=C)
            for (sc0, n, dc0) in segs:
                src = W[k0 + kc0 * 128:k0 + (kc0 + kn) * 128, sc0:sc0 + n].rearrange("(k p) c -> p k c", p=128)
                P.dma(stv[:, :, dc0:dc0 + n], src, writes=[st.buf])
            dst = wv[:, kc0:kc0 + kn, :]
            eng = self.cast.next()
            if eng == "act":
                P.op("act", I("activation", dst, stv, AF.Copy), reads=[st.buf], writes=[slot.buf])
            else:
                P.op(eng, I("tensor_copy", dst, stv), reads=[st.buf], writes=[slot.buf])
        if key is not None:
            Prog.SEMUID += 1
            cap = P.nc.dram_tensor(f"wcache_{Prog.SEMUID}", [128, KC * C], BF16).ap()
            cbuf = Buf(f"wc{Prog.SEMUID}")
            self.cache[key] = (cap, cbuf)
            P.dma(cap, slot.t[:, 0:KC * C], reads=[slot.buf], writes=[cbuf], owner=cbuf, queue="pool")
        return slot.buf, wv


def pipeline(jobs):
    if not jobs:
        return
    cur = jobs[0][0]()
    for i in range(len(jobs)):
        nxt = jobs[i + 1][0]() if i + 1 < len(jobs) else None
        jobs[i][1](cur)
        cur = nxt


def new_nc():
    return bass.Bass("TRN2", target_bir_lowering=False)


def build_l1():
    nc = new_nc()
    xT = nc.dram_tensor("xT", [D, TOK], F32, kind="ExternalInput").ap()
    pos = nc.dram_tensor("pos", [TOK], I32, kind="ExternalInput").ap()
    w_in = nc.dram_tensor("w_in", [D, 5696], F32, kind="ExternalInput").ap()
    w_uq = nc.dram_tensor("w_uq", [1024, 6144], F32, kind="ExternalInput").ap()
    w_ukv = nc.dram_tensor("w_ukv", [512, 8192], F32, kind="ExternalInput").ap()
    g0d = nc.dram_tensor("g0", [128, 32], F32, kind="ExternalInput").ap()
    qgd = nc.dram_tensor("qg", [128, 8], F32, kind="ExternalInput").ap()
    kvgd = nc.dram_tensor("kvg", [128, 4], F32, kind="ExternalInput").ap()
    rcd = nc.dram_tensor("ropec", [64, 2], F32, kind="ExternalInput").ap()
    QnT = nc.dram_tensor("QnT", [NH, 128, TOK], BF16, kind="ExternalOutput").ap()
    QrT = nc.dram_tensor("QrT", [NH, 64, TOK], BF16, kind="ExternalOutput").ap()
    KnT = nc.dram_tensor("KnT", [NH, 128, TOK], BF16, kind="ExternalOutput").ap()
    KrT = nc.dram_tensor("KrT", [64, TOK], BF16, kind="ExternalOutput").ap()
    Vo = nc.dram_tensor("V", [TOK, NH * 128], BF16, kind="ExternalOutput").ap()
    SZT = nc.dram_tensor("SZT", [D, TOK], BF16, kind="ExternalOutput").ap()

    P = Prog(nc)
    NT = 1024
    ws = WStream(P, nstage=2, nw=2)
    ones = P.sbuf("ones", [128, 128], BF16)
    g0 = P.sbuf("g0s", [128, 32], F32)
    qg = P.sbuf("qgs", [128, 8], F32)
    kvg = P.sbuf("kvgs", [128, 4], F32)
    rc = P.sbuf("rcs", [64, 2], F32)
    epsc = P.sbuf("epsc", [128, 1], F32)
    bconst = Buf("const")
    P.op("dve", I("memset", ones[:], 1.0), writes=[bconst])
    P.op("dve", I("memset", epsc[:], EPS), writes=[bconst])
    P.dma(g0[:], g0d, writes=[bconst])
    P.dma(qg[:], qgd, writes=[bconst])
    P.dma(kvg[:], kvgd, writes=[bconst])
    P.dma(rc[:], rcd, writes=[bconst])

    xg = P.sbuf("xg", [128, 32, NT], BF16)
    bxg = Buf("xg")
    sqb = P.ring("sqb", 2, [128, NT], BF16)
    t32 = P.ring("t32_", 4, [128, 512], F32)
    sq2 = P.ring("sq2_", 2, [128, 512], BF16)
    ost = P.ring("ost", 4, [128, 512], BF16)
    rstd_x = P.sbuf("rstd_x", [128, NT], F32)
    rstd_q = P.sbuf("rstd_q", [128, NT], F32)
    rstd_kv = P.sbuf("rstd_kv", [128, NT], F32)
    brx, brq, brkv = Buf("rstd_x"), Buf("rstd_q"), Buf("rstd_kv")
    cqg = P.sbuf("cqg", [128, 8, NT], BF16)
    bcqg = Buf("cqg")
    ckvn = P.sbuf("ckvn", [128, 4, NT], BF16)
    bckvn = Buf("ckvn")
    cosT = P.sbuf("cosT", [64, NT], F32)
    sinT = P.sbuf("sinT", [64, NT], F32)
    btab = Buf("ropetab")
    ra = P.sbuf("ra", [64, NT], F32)
    rb = P.sbuf("rb", [64, NT], F32)
    rcc = P.sbuf("rcc", [64, NT], F32)
    ri = P.sbuf("ri", [64, NT], I32)
    bra, brb, brcc, bri = Buf("ra"), Buf("rb"), Buf("rcc"), Buf("ri")

    acc = P.pring("acc", 4)
    ssq = [Slot(P.psum(f"ssq{i}", [128, 512]), f"ssq{i}") for i in range(2)]
    ssq2 = [Slot(P.psum(f"ssqb{i}", [128, 512]), f"ssqb{i}") for i in range(2)]
    out_toks = []

    def rstd_from(ssq_slot, dst, bdst, n, inv_n):
        t = t32.next()
        P.op("act", I("activation", t.t[:], ssq_slot.t[:], AF.Sqrt, bias=epsc[:, 0:1], scale=inv_n),
             reads=[ssq_slot.buf, bconst], writes=[t.buf])
        P.op("dve", I("reciprocal", dst[:, n * 512:(n + 1) * 512], t.t[:]), reads=[t.buf], writes=[bdst])

    def store(dst_ap, src_slot, src_ap):
        out_toks.append(P.dma(dst_ap, src_ap, reads=[src_slot.buf], owner=src_slot.buf))

    for p in range(TOK // NT):
        t0 = p * NT
        P.dma(ri[:], pos[t0:t0 + NT].partition_broadcast(64), writes=[bri])
        P.op("dve", I("tensor_copy", ra[:], ri[:]), reads=[bri], writes=[bra])
        P.op("dve", I("tensor_scalar", ra[:], ra[:], rc[:, 0:1], None, ALU.mult), reads=[bra, bconst], writes=[bra])
        P.op("dve", I("tensor_scalar", rb[:], ra[:], 1.0 / TWO_PI, None, ALU.mult), reads=[bra], writes=[brb])
        P.op("dve", I("tensor_copy", ri[:], rb[:]), reads=[brb], writes=[bri])
        P.op("dve", I("tensor_copy", rb[:], ri[:]), reads=[bri], writes=[brb])
        P.op("dve", I("scalar_tensor_tensor", ra[:], rb[:], -CW1, ra[:], ALU.mult, ALU.add), reads=[bra, brb], writes=[bra])
        P.op("dve", I("scalar_tensor_tensor", ra[:], rb[:], -CW2, ra[:], ALU.mult, ALU.add), reads=[bra, brb], writes=[bra])
        P.op("dve", I("tensor_single_scalar", rb[:], ra[:], 0.0, ALU.is_lt), reads=[bra], writes=[brb])
        P.op("dve", I("scalar_tensor_tensor", ra[:], rb[:], TWO_PI, ra[:], ALU.mult, ALU.add), reads=[bra, brb], writes=[bra])
        P.op("dve", I("tensor_scalar", rb[:], ra[:], -1.0, math.pi, ALU.mult, ALU.add), reads=[bra], writes=[brb])
        P.op("dve", I("tensor_scalar", rb[:], rb[:], PI_LO, -PI_LO, ALU.min, ALU.max), reads=[brb], writes=[brb])
        P.op("act", I("activation", sinT[:], rb[:], AF.Sin), reads=[brb], writes=[btab])
        P.op("dve", I("tensor_scalar", rcc[:], ra[:], math.pi / 2, None, ALU.add), reads=[bra], writes=[brcc])
        P.op("dve", I("tensor_single_scalar", rb[:], rcc[:], TWO_PI, ALU.is_ge), reads=[brcc, btab], writes=[brb])
        P.op("dve", I("scalar_tensor_tensor", rcc[:], rb[:], -TWO_PI, rcc[:], ALU.mult, ALU.add), reads=[brcc, brb], writes=[brcc])
        P.op("dve", I("tensor_scalar", rcc[:], rcc[:], -1.0, math.pi, ALU.mult, ALU.add), reads=[brcc], writes=[brcc])
        P.op("dve", I("tensor_scalar", rcc[:], rcc[:], PI_LO, -PI_LO, ALU.min, ALU.max), reads=[brcc], writes=[brcc])
        P.op("act", I("activation", cosT[:], rcc[:], AF.Sin), reads=[brcc], writes=[btab])
        P.op("dve", I("tensor_scalar", sinT[:], sinT[:], rc[:, 1:2], None, ALU.mult), reads=[btab, bconst], writes=[btab])

        for kc in range(32):
            st = ws.stages.next()
            stv = st.t[:, 0:NT]
            P.dma(stv, xT[kc * 128:(kc + 1) * 128, t0:t0 + NT], writes=[st.buf])
            sq = sqb.next()
            P.op("act", I("activation", sq.t[:], stv, AF.Square), reads=[st.buf], writes=[sq.buf])
            P.op("pe", [I("matmul", ssq[n].t[:], ones[:], sq.t[:, n * 512:(n + 1) * 512], start=(kc == 0), stop=(kc == 31))
                        for n in range(2)], reads=[sq.buf, bconst], writes=[ssq[0].buf, ssq[1].buf])
            P.op("dve", I("tensor_scalar", xg[:, kc, :], stv, g0[:, kc:kc + 1], None, ALU.mult),
                 reads=[st.buf, bconst], writes=[bxg])
        for n in range(2):
            rstd_from(ssq[n], rstd_x, brx, n, 1.0 / D)

        def gemm_x(wbuf, wv, mcols, epilogue):
            for (c0, M, tag) in mcols:
                for n in range(2):
                    a = acc.next()
                    P.op("pe", [I("matmul", a.t[0:M, :], wv[:, kc, c0:c0 + M], xg[:, kc, n * 512:(n + 1) * 512],
                                  start=(kc == 0), stop=(kc == 31)) for kc in range(32)],
                         reads=[wbuf, bxg], writes=[a.buf])
                    epilogue(a, M, tag, n)

        def ep_lat(kind):
            nm = 8 if kind == "q" else 4

            def ep(a, M, m, n):
                cs = slice(n * 512, (n + 1) * 512)
                t = t32.next()
                P.op("dve", I("tensor_tensor", t.t[:], a.t[:], rstd_x[:, cs], ALU.mult), reads=[a.buf, brx], writes=[t.buf])
                s2 = sq2.next()
                P.op("act", I("activation", s2.t[:], t.t[:], AF.Square), reads=[t.buf], writes=[s2.buf])
                P.op("pe", I("matmul", ssq2[n].t[:], ones[:], s2.t[:], start=(m == 0), stop=(m == nm - 1)),
                     reads=[s2.buf, bconst], writes=[ssq2[n].buf])
                if kind == "q":
                    P.op("pool", I("tensor_scalar", cqg[:, m, cs], t.t[:], qg[:, m:m + 1], None, ALU.mult),
                         reads=[t.buf, bconst], writes=[bcqg])
                else:
                    P.op("pool", I("tensor_scalar", ckvn[:, m, cs], t.t[:], kvg[:, m:m + 1], None, ALU.mult),
                         reads=[t.buf, bconst], writes=[bckvn])
            return ep

        def rope_store(aA, aB, rs, brs, n, dst_ap):
            tA = t32.next()
            tB = t32.next()
            cs = slice(n * 512, (n + 1) * 512)
            P.op("dve", I("tensor_tensor", tA.t[0:64, :], aA.t[0:64, :], rs[0:64, cs], ALU.mult), reads=[aA.buf, brs], writes=[tA.buf])
            P.op("dve", I("tensor_tensor", tB.t[0:64, :], aB.t[0:64, :], rs[0:64, cs], ALU.mult), reads=[aB.buf, brs], writes=[tB.buf])
            P.op("pool", I("tensor_tensor", tA.t[0:64, :], tA.t[0:64, :], cosT[:, cs], ALU.mult), reads=[tA.buf, btab], writes=[tA.buf])
            P.op("pool", I("tensor_tensor", tB.t[0:64, :], tB.t[0:64, :], sinT[:, cs], ALU.mult), reads=[tB.buf, btab], writes=[tB.buf])
            o = ost.next()
            P.op("dve", I("tensor_tensor", o.t[0:64, :], tA.t[0:64, :], tB.t[0:64, :], ALU.add), reads=[tA.buf, tB.buf], writes=[o.buf])
            store(dst_ap, o, o.t[0:64, :])

        def ep_z(a, M, m, n):
            cs = slice(n * 512, (n + 1) * 512)
            t = t32.next()
            P.op("dve", I("tensor_tensor", t.t[:], a.t[:], rstd_x[:, cs], ALU.mult), reads=[a.buf, brx], writes=[t.buf])
            o = ost.next()
            P.op("act", I("activation", o.t[:], t.t[:], AF.Silu), reads=[t.buf], writes=[o.buf])
            store(SZT[m * 128:(m + 1) * 128, t0 + n * 512:t0 + (n + 1) * 512], o, o.t[:])

        jobs = []
        for wt in range(4):
            jobs.append((lambda wt=wt: ws.load(w_in, 0, 32, [(wt * 256, 256, 0)], 256),
                         lambda L, wt=wt: gemm_x(L[0], L[1], [(0, 128, wt * 2), (128, 128, wt * 2 + 1)], ep_lat("q"))))

        def after_q(L):
            for n in range(2):
                rstd_from(ssq2[n], rstd_q, brq, n, 1.0 / 1024)
        for wt in range(2):
            def comp(L, wt=wt):
                if wt == 0:
                    after_q(L)
                gemm_x(L[0], L[1], [(0, 128, wt * 2), (128, 128, wt * 2 + 1)], ep_lat("kv"))
            jobs.append((lambda wt=wt: ws.load(w_in, 0, 32, [(1024 + wt * 256, 256, 0)], 256), comp))

        def comp_krot(L):
            wbuf, wv = L
            for n in range(2):
                rstd_from(ssq2[n], rstd_kv, brkv, n, 1.0 / 512)
            for m in range(4):
                for n in range(2):
                    cs = slice(n * 512, (n + 1) * 512)
                    P.op("dve", I("tensor_tensor", ckvn[:, m, cs], ckvn[:, m, cs], rstd_kv[:, cs], ALU.mult),
                         reads=[bckvn, brkv], writes=[bckvn])
            for n in range(2):
                aA, aB = acc.next(), acc.next()
                calls = []
                for (a, c0) in ((aA, 0), (aB, 64)):
                    for kc in range(32):
                        calls.append(I("matmul", a.t[0:64, :], wv[:, kc, c0:c0 + 64], xg[:, kc, n * 512:(n + 1) * 512],
                                       start=(kc == 0), stop=(kc == 31)))
                P.op("pe", calls, reads=[wbuf, bxg], writes=[aA.buf, aB.buf])
                rope_store(aA, aB, rstd_x, brx, n, KrT[:, t0 + n * 512:t0 + (n + 1) * 512])
        jobs.append((lambda: ws.load(w_in, 0, 32, [(1536, 64, 0), (1568, 32, 64), (1536, 32, 96)], 128), comp_krot))

        for wt in range(16):
            jobs.append((lambda wt=wt: ws.load(w_in, 0, 32, [(1600 + wt * 256, 256, 0)], 256),
                         lambda L, wt=wt: gemm_x(L[0], L[1], [(0, 128, wt * 2), (128, 128, wt * 2 + 1)], ep_z)))

        def comp_q(L, h):
            wbuf, wv = L
            for n in range(2):
                cs = slice(n * 512, (n + 1) * 512)
                a0, aA, aB = acc.next(), acc.next(), acc.next()
                calls = []
                for (a, c0, M) in ((a0, 0, 128), (aA, 128, 64), (aB, 192, 64)):
                    for kc in range(8):
                        calls.append(I("matmul", a.t[0:M, :], wv[:, kc, c0:c0 + M], cqg[:, kc, cs], start=(kc == 0), stop=(kc == 7)))
                P.op("pe", calls, reads=[wbuf, bcqg], writes=[a0.buf, aA.buf, aB.buf])
                o = ost.next()
                P.op("dve", I("tensor_tensor", o.t[:], a0.t[:], rstd_q[:, cs], ALU.mult), reads=[a0.buf, brq], writes=[o.buf])
                store(QnT[h, :, t0 + n * 512:t0 + (n + 1) * 512], o, o.t[:])
                rope_store(aA, aB, rstd_q, brq, n, QrT[h, :, t0 + n * 512:t0 + (n + 1) * 512])
        for h in range(NH):
            c = h * 192
            jobs.append((lambda c=c: ws.load(w_uq, 0, 8, [(c, 192, 0), (c + 160, 32, 192), (c + 128, 32, 224)], 256),
                         lambda L, h=h: comp_q(L, h)))

        def comp_kv(L, hg):
            wbuf, wv = L
            for hh in range(4):
                h = hg * 4 + hh
                for n in range(2):
                    cs = slice(n * 512, (n + 1) * 512)
                    a = acc.next()
                    P.op("pe", [I("matmul", a.t[:], wv[:, kc, hh * 256:hh * 256 + 128], ckvn[:, kc, cs], start=(kc == 0), stop=(kc == 3))
                                for kc in range(4)], reads=[wbuf, bckvn], writes=[a.buf])
                    o = ost.next()
                    P.op("act", I("activation", o.t[:], a.t[:], AF.Copy), reads=[a.buf], writes=[o.buf])
                    store(KnT[h, :, t0 + n * 512:t0 + (n + 1) * 512], o, o.t[:])
            wv4 = wv.rearrange("p k (h c) -> p k h c", c=256)
            for tb in range(NT // 128):
                a = acc.next()
                av = a.t[:].rearrange("p (h c) -> p h c", c=128)
                P.op("pe", [I("matmul", av, ckvn[:, kc, tb * 128:(tb + 1) * 128], wv4[:, kc, :, 128:256], start=(kc == 0), stop=(kc == 3))
                            for kc in range(4)], reads=[wbuf, bckvn], writes=[a.buf])
                o = ost.next()
                P.op("dve", I("tensor_copy", o.t[:], a.t[:]), reads=[a.buf], writes=[o.buf])
                store(Vo[t0 + tb * 128:t0 + (tb + 1) * 128, hg * 512:(hg + 1) * 512], o, o.t[:])
        for hg in range(NH // 4):
            jobs.append((lambda hg=hg: ws.load(w_ukv, 0, 4, [(hg * 1024, 1024, 0)], 1024),
                         lambda L, hg=hg: comp_kv(L, hg)))
        pipeline(jobs)

    P.finish(out_toks)
    P.emit()
    return nc


def rope_consts():
    inv = 1.0 / (10000.0 ** (np.arange(0, 64, 2, dtype=np.float32) / 64.0))
    inv = inv.astype(np.float32)
    rc = np.zeros((64, 2), np.float32)
    rc[:32, 0] = inv
    rc[32:, 0] = inv
    rc[:32, 1] = -1.0
    rc[32:, 1] = 1.0
    return rc


def chunked_gain(g):
    k = g.shape[0] // 128
    return np.ascontiguousarray(g.reshape(k, 128).T.astype(np.float32))


def l1_inputs(x, positions, l0_norm, l0_w_in, l0_q_norm, l0_w_uq, l0_kv_norm, l0_w_ukv):
    xf = x.reshape(B * S, D)
    pf = positions.reshape(B * S)
    maps = []
    for c in range(NCORES):
        maps.append({
            "xT": np.ascontiguousarray(xf[c * TOK:(c + 1) * TOK].T),
            "pos": np.ascontiguousarray(pf[c * TOK:(c + 1) * TOK]),
            "w_in": l0_w_in, "w_uq": l0_w_uq, "w_ukv": l0_w_ukv,
            "g0": chunked_gain(l0_norm), "qg": chunked_gain(l0_q_norm), "kvg": chunked_gain(l0_kv_norm),
            "ropec": rope_consts(),
        })
    return maps


HPC = 8


def build_l2():
    nc = new_nc()
    QnT = nc.dram_tensor("QnT", [HPC, 128, S], BF16, kind="ExternalInput").ap()
    QrT = nc.dram_tensor("QrT", [HPC, 64, S], BF16, kind="ExternalInput").ap()
    KnT = nc.dram_tensor("KnT", [HPC, 128, S], BF16, kind="ExternalInput").ap()
    KrT = nc.dram_tensor("KrT", [64, S], BF16, kind="ExternalInput").ap()
    Vh = nc.dram_tensor("Vh", [HPC, 128, S // 128, 128], BF16, kind="ExternalInput").ap()
    SZT = nc.dram_tensor("SZT", [HPC * 128, S], BF16, kind="ExternalInput").ap()
    trid = nc.dram_tensor("tri", [128, 128], BF16, kind="ExternalInput").ap()
    OGT = nc.dram_tensor("OGT", [HPC * 128, S], BF16, kind="ExternalOutput").ap()

    P = Prog(nc)
    scale = 1.0 / math.sqrt(192.0)
    ones = P.sbuf("ones", [128, 128], BF16)
    tri = P.sbuf("tri_sb", [128, 128], BF16)
    bconst = Buf("const")
    P.op("dve", I("memset", ones[:], 1.0), writes=[bconst])
    P.dma(tri[:], trid, writes=[bconst])
    kr = P.slot("kr", [64, S], BF16)
    P.dma(kr.t[:], KrT, writes=[kr.buf])
    kn = P.ring("kn", 2, [128, S], BF16)
    vv = P.ring("vv", 2, [128, S // 128, 128], BF16)
    qn = P.ring("qn", 3, [128, 512], BF16)
    qr = P.ring("qr", 3, [64, 512], BF16)
    sz = P.ring("sz", 3, [128, 512], BF16)
    pT = P.ring("pT", 3, [128, 512], BF16)
    rs = P.ring("rs", 2, [128, 512], F32)
    o32 = P.ring("o32", 2, [128, 512], F32)
    og = P.ring("og", 2, [128, 512], BF16)
    sT = P.pring("sT", 3)
    oT = P.pring("oT", 2)
    sm = P.pring("sm", 2)
    out_toks = []

    def load_head(h):
        k = kn.next()
        v = vv.next()
        P.dma(k.t[:], KnT[h], writes=[k.buf])
        P.dma(v.t[:], Vh[h], writes=[v.buf])
        return k, v

    def load_q(h, t):
        a, b_, c = qn.next(), qr.next(), sz.next()
        cs = slice(t * 512, (t + 1) * 512)
        P.dma(a.t[:], QnT[h, :, cs], writes=[a.buf])
        P.dma(b_.t[:], QrT[h, :, cs], writes=[b_.buf])
        P.dma(c.t[:], SZT[h * 128:(h + 1) * 128, cs], writes=[c.buf])
        return a, b_, c

    NQT = S // 512
    tiles = [(h, t) for h in range(HPC) for t in range(NQT)]
    blocks = []
    for ti, (h, t) in enumerate(tiles):
        nb = 4 * t + 4
        for kb in range(nb):
            blocks.append((ti, kb, nb))
    state = {}

    def tile_ctx(ti):
        if ti in state:
            return state[ti]
        h, t = tiles[ti]
        if t == 0:
            hk = load_head(h)
        else:
            hk = state[ti - 1]["hk"]
        q = load_q(h, t)
        ctx = {"hk": hk, "q": q, "oT": oT.next(), "sm": sm.next()}
        state[ti] = ctx
        return ctx

    def emit_S(bi):
        ti, kb, nb = blocks[bi]
        h, t = tiles[ti]
        ctx = tile_ctx(ti)
        k, v = ctx["hk"]
        a, b_, c = ctx["q"]
        j = kb - 4 * t
        c0 = 128 * j if j > 0 else 0
        s = sT.next()
        ks = slice(kb * 128, (kb + 1) * 128)
        P.op("pe", [I("matmul", s.t[:, c0:512], k.t[:, ks], a.t[:, c0:512], start=True, stop=False),
                    I("matmul", s.t[:, c0:512], kr.t[:, ks], b_.t[:, c0:512], start=False, stop=True)],
             reads=[k.buf, kr.buf, a.buf, b_.buf], writes=[s.buf])
        return s, c0, j

    pend = {}
    nB = len(blocks)
    AHEAD = 2
    for bi in range(min(AHEAD, nB)):
        pend[bi] = emit_S(bi)
    for bi in range(nB):
        ti, kb, nb = blocks[bi]
        h, t = tiles[ti]
        ctx = state[ti]
        k, v = ctx["hk"]
        s, c0, j = pend.pop(bi)
        p = pT.next()
        P.op("act", I("activation", p.t[:, c0:512], s.t[:, c0:512], AF.Exp, scale=scale), reads=[s.buf], writes=[p.buf])
        if j >= 0:
            P.op("pool", I("tensor_tensor", p.t[:, c0:c0 + 128], p.t[:, c0:c0 + 128], tri[:], ALU.mult),
                 reads=[p.buf, bconst], writes=[p.buf])
        o_, m_ = ctx["oT"], ctx["sm"]
        P.op("pe", [I("matmul", o_.t[:, c0:512], v.t[:, kb, :], p.t[:, c0:512], start=(kb == 0), stop=(kb == nb - 1)),
                    I("matmul", m_.t[:, c0:512], ones[:], p.t[:, c0:512], start=(kb == 0), stop=(kb == nb - 1))],
             reads=[v.buf, p.buf, bconst], writes=[o_.buf, m_.buf])
        if bi + AHEAD < nB:
            pend[bi + AHEAD] = emit_S(bi + AHEAD)
        if kb == nb - 1:
            a, b_, c = ctx["q"]
            r_ = rs.next()
            P.op("dve", I("reciprocal", r_.t[:], m_.t[:]), reads=[m_.buf], writes=[r_.buf])
            x_ = o32.next()
            P.op("dve", I("tensor_tensor", x_.t[:], o_.t[:], r_.t[:], ALU.mult), reads=[o_.buf, r_.buf], writes=[x_.buf])
            g_ = og.next()
            P.op("dve", I("tensor_tensor", g_.t[:], x_.t[:], c.t[:], ALU.mult), reads=[x_.buf, c.buf], writes=[g_.buf])
            out_toks.append(P.dma(OGT[h * 128:(h + 1) * 128, t * 512:(t + 1) * 512], g_.t[:], reads=[g_.buf],
                                  owner=g_.buf, queue="pool"))
            if ti + 1 < len(tiles):
                tile_ctx(ti + 1)

    P.finish(out_toks)
    P.emit()
    return nc


def tri_const():
    import ml_dtypes
    k = np.arange(128)[:, None]
    q = np.arange(128)[None, :]
    return (k <= q).astype(np.float32).astype(ml_dtypes.bfloat16)


def build_wo(final):
    nc = new_nc()
    ogT = nc.dram_tensor("ogT", [D, TOK], BF16, kind="ExternalInput").ap()
    xT = nc.dram_tensor("xT", [D, TOK], F32, kind="ExternalInput").ap()
    w_o = nc.dram_tensor("w_o", [D, D], F32, kind="ExternalInput").ap()
    if final:
        fgd = nc.dram_tensor("fg", [128, 32], F32, kind="ExternalInput").ap()
        x2T = nc.dram_tensor("x2T_scratch", [D, TOK], F32).ap()
    outT = nc.dram_tensor("outT", [D, TOK], F32, kind="ExternalOutput").ap()

    P = Prog(nc)
    NT = 1024
    ws = WStream(P, nstage=2, nw=2)
    bconst = Buf("const")
    ones = P.sbuf("ones", [128, 128], BF16)
    epsc = P.sbuf("epsc", [128, 1], F32)
    P.op("dve", I("memset", ones[:], 1.0), writes=[bconst])
    P.op("dve", I("memset", epsc[:], EPS), writes=[bconst])
    if final:
        fg = P.sbuf("fgs", [128, 32], F32)
        P.dma(fg[:], fgd, writes=[bconst])
    og = P.slot("og", [128, 32, NT], BF16)
    xs = P.ring("xs", 3, [128, 512], F32)
    t32 = P.ring("t32_", 3, [128, 512], F32)
    sq2 = P.ring("sq2_", 2, [128, 512], BF16)
    acc = P.pring("acc", 4)
    ssq = [Slot(P.psum(f"ssq{i}", [128, 512]), f"ssq{i}") for i in range(2)]
    rstd = P.sbuf("rstd", [128, NT], F32)
    brstd = Buf("rstd")
    bscr = Buf("x2scratch")
    out_toks = []

    for p in range(TOK // NT):
        t0 = p * NT
        for kc in range(32):
            P.dma(og.t[:, kc, :], ogT[kc * 128:(kc + 1) * 128, t0:t0 + NT], writes=[og.buf])

        def comp(L, wt):
            wbuf, wv = L
            for mi in range(2):
                m = wt * 2 + mi
                for n in range(2):
                    cs = slice(n * 512, (n + 1) * 512)
                    a = acc.next()
                    P.op("pe", [I("matmul", a.t[:], wv[:, kc, mi * 128:(mi + 1) * 128], og.t[:, kc, cs], start=(kc == 0), stop=(kc == 31))
                                for kc in range(32)], reads=[wbuf, og.buf], writes=[a.buf])
                    x_ = xs.next()
                    P.dma(x_.t[:], xT[m * 128:(m + 1) * 128, t0 + n * 512:t0 + (n + 1) * 512], writes=[x_.buf])
                    t = t32.next()
                    P.op("dve", I("tensor_tensor", t.t[:], a.t[:], x_.t[:], ALU.add), reads=[a.buf, x_.buf], writes=[t.buf])
                    if not final:
                        out_toks.append(P.dma(outT[m * 128:(m + 1) * 128, t0 + n * 512:t0 + (n + 1) * 512], t.t[:],
                                              reads=[t.buf], owner=t.buf, queue="pool"))
                    else:
                        P.dma(x2T[m * 128:(m + 1) * 128, t0 + n * 512:t0 + (n + 1) * 512], t.t[:],
                              reads=[t.buf], writes=[bscr], owner=bscr, queue="pool")
                        s2 = sq2.next()
                        P.op("act", I("activation", s2.t[:], t.t[:], AF.Square), reads=[t.buf], writes=[s2.buf])
                        P.op("pe", I("matmul", ssq[n].t[:], ones[:], s2.t[:], start=(m == 0), stop=(m == 31)),
                             reads=[s2.buf, bconst], writes=[ssq[n].buf])
        jobs = [(lambda wt=wt: ws.load(w_o, 0, 32, [(wt * 256, 256, 0)], 256), lambda L, wt=wt: comp(L, wt)) for wt in range(16)]
        pipeline(jobs)
        if final:
            for n in range(2):
                t = t32.next()
                P.op("act", I("activation", t.t[:], ssq[n].t[:], AF.Sqrt, bias=epsc[:, 0:1], scale=1.0 / D),
                     reads=[ssq[n].buf, bconst], writes=[t.buf])
                P.op("dve", I("reciprocal", rstd[:, n * 512:(n + 1) * 512], t.t[:]), reads=[t.buf], writes=[brstd])
            for m in range(32):
                for n in range(2):
                    cs = slice(n * 512, (n + 1) * 512)
                    x_ = xs.next()
                    P.dma(x_.t[:], x2T[m * 128:(m + 1) * 128, t0 + n * 512:t0 + (n + 1) * 512], reads=[bscr], writes=[x_.buf])
                    t = t32.next()
                    P.op("dve", I("scalar_tensor_tensor", t.t[:], x_.t[:], fg[:, m:m + 1], rstd[:, cs], ALU.mult, ALU.mult),
                         reads=[x_.buf, brstd, bconst], writes=[t.buf])
                    out_toks.append(P.dma(outT[m * 128:(m + 1) * 128, t0 + n * 512:t0 + (n + 1) * 512], t.t[:],
                                          reads=[t.buf], owner=t.buf, queue="pool"))
    P.finish(out_toks)
    P.emit()
    return nc


def build_l3b():
    nc = new_nc()
    xT = nc.dram_tensor("xT", [D, TOK], F32, kind="ExternalInput").ap()
    w_in = nc.dram_tensor("w_in", [D, 4 * D], F32, kind="ExternalInput").ap()
    g1d = nc.dram_tensor("g1", [128, 32], F32, kind="ExternalInput").ap()
    lbd = nc.dram_tensor("lbs", [128, 2, 32], F32, kind="ExternalInput").ap()
    q1T = nc.dram_tensor("q1T", [D, TOK], BF16, kind="ExternalOutput").ap()
    k1T = nc.dram_tensor("k1T", [D, TOK], BF16, kind="ExternalOutput").ap()
    g1T = nc.dram_tensor("g1T", [D, TOK], F32, kind="ExternalOutput").ap()
    v1T = nc.dram_tensor("v1T", [D, TOK], BF16, kind="ExternalOutput").ap()
    sz1T = nc.dram_tensor("sz1T", [D, TOK], BF16, kind="ExternalOutput").ap()

    P = Prog(nc)
    NT = 1024
    ws = WStream(P, nstage=2, nw=2)
    bconst = Buf("const")
    ones = P.sbuf("ones", [128, 128], BF16)
    epsc = P.sbuf("epsc", [128, 1], F32)
    g1 = P.sbuf("g1s", [128, 32], F32)
    lbs = P.sbuf("lbss", [128, 2, 32], F32)
    lb = P.sbuf("lb", [128, 32], F32)
    oml = P.sbuf("oml", [128, 32], F32)
    dlb = P.sbuf("dlb", [128, 32], F32)
    P.op("dve", I("memset", ones[:], 1.0), writes=[bconst])
    P.op("dve", I("memset", epsc[:], EPS), writes=[bconst])
    P.dma(g1[:], g1d, writes=[bconst])
    P.dma(lbs[:], lbd, writes=[bconst])
    P.op("dve", I("tensor_tensor", dlb[:], lbs[:, 1, :], lbs[:, 0, :], ALU.subtract), reads=[bconst], writes=[bconst])
    P.op("act", I("activation", dlb[:], dlb[:], AF.Exp, scale=-1.0), reads=[bconst], writes=[bconst])
    P.op("dve", I("tensor_scalar", lb[:], dlb[:], 1.0, None, ALU.add), reads=[bconst], writes=[bconst])
    P.op("dve", I("reciprocal", lb[:], lb[:]), reads=[bconst], writes=[bconst])
    P.op("dve", I("tensor_tensor", oml[:], dlb[:], lb[:], ALU.mult), reads=[bconst], writes=[bconst])

    xg = P.sbuf("xg", [128, 32, NT], BF16)
    bxg = Buf("xg")
    sqb = P.ring("sqb", 2, [128, NT], BF16)
    t32 = P.ring("t32_", 3, [128, 512], F32)
    e32 = P.ring("e32_", 2, [128, 512], F32)
    s32 = P.ring("s32_", 2, [128, 512], F32)
    f32 = P.ring("f32_", 2, [128, 512], F32)
    gout = P.ring("gout", 2, [128, 512], F32)
    ost = P.ring("ost", 4, [128, 512], BF16)
    rstd_x = P.sbuf("rstd_x", [128, NT], F32)
    brx = Buf("rstd_x")
    acc = P.pring("acc", 4)
    ssq = [Slot(P.psum(f"ssq{i}", [128, 512]), f"ssq{i}") for i in range(2)]
    out_toks = []

    def store(dst_ap, src_slot, src_ap):
        out_toks.append(P.dma(dst_ap, src_ap, reads=[src_slot.buf], owner=src_slot.buf, queue="pool"))

    for p in range(TOK // NT):
        t0 = p * NT
        for kc in range(32):
            st = ws.stages.next()
            stv = st.t[:, 0:NT]
            P.dma(stv, xT[kc * 128:(kc + 1) * 128, t0:t0 + NT], writes=[st.buf])
            sq = sqb.next()
            P.op("act", I("activation", sq.t[:], stv, AF.Square), reads=[st.buf], writes=[sq.buf])
            P.op("pe", [I("matmul", ssq[n].t[:], ones[:], sq.t[:, n * 512:(n + 1) * 512], start=(kc == 0), stop=(kc == 31))
                        for n in range(2)], reads=[sq.buf, bconst], writes=[ssq[0].buf, ssq[1].buf])
            P.op("dve", I("tensor_scalar", xg[:, kc, :], stv, g1[:, kc:kc + 1], None, ALU.mult),
                 reads=[st.buf, bconst], writes=[bxg])
        for n in range(2):
            t = t32.next()
            P.op("act", I("activation", t.t[:], ssq[n].t[:], AF.Sqrt, bias=epsc[:, 0:1], scale=1.0 / D),
                 reads=[ssq[n].buf, bconst], writes=[t.buf])
            P.op("dve", I("reciprocal", rstd_x[:, n * 512:(n + 1) * 512], t.t[:]), reads=[t.buf], writes=[brx])

        def comp(L, wt):
            wbuf, wv = L
            kind = wt // 16
            for mi in range(2):
                mm = (wt % 16) * 2 + mi
                for n in range(2):
                    cs = slice(n * 512, (n + 1) * 512)
                    dsl = (slice(mm * 128, (mm + 1) * 128), slice(t0 + n * 512, t0 + (n + 1) * 512))
                    a = acc.next()
                    P.op("pe", [I("matmul", a.t[:], wv[:, kc, mi * 128:(mi + 1) * 128], xg[:, kc, cs], start=(kc == 0), stop=(kc == 31))
                                for kc in range(32)], reads=[wbuf, bxg], writes=[a.buf])
                    t = t32.next()
                    P.op("dve", I("tensor_tensor", t.t[:], a.t[:], rstd_x[:, cs], ALU.mult), reads=[a.buf, brx], writes=[t.buf])
                    if kind == 0 or kind == 3:
                        o = ost.next()
                        P.op("act", I("activation", o.t[:], t.t[:], AF.Silu), reads=[t.buf], writes=[o.buf])
                        store((q1T if kind == 0 else sz1T)[dsl[0], dsl[1]], o, o.t[:])
                    elif kind == 2:
                        o = ost.next()
                        P.op("act", I("activation", o.t[:], t.t[:], AF.Copy), reads=[t.buf], writes=[o.buf])
                        store(v1T[dsl[0], dsl[1]], o, o.t[:])
                    else:
                        e = e32.next()
                        P.op("act", I("activation", e.t[:], t.t[:], AF.Exp, scale=-1.0), reads=[t.buf], writes=[e.buf])
                        s_ = s32.next()
                        P.op("pool", I("tensor_scalar", s_.t[:], e.t[:], 1.0, None, ALU.add), reads=[e.buf], writes=[s_.buf])
                        P.op("dve", I("reciprocal", s_.t[:], s_.t[:]), reads=[s_.buf], writes=[s_.buf])
                        o = ost.next()
                        P.op("dve", I("scalar_tensor_tensor", o.t[:], e.t[:], oml[:, mm:mm + 1], s_.t[:], ALU.mult, ALU.mult),
                             reads=[e.buf, s_.buf, bconst], writes=[o.buf])
                        store(k1T[dsl[0], dsl[1]], o, o.t[:])
                        f_ = f32.next()
                        P.op("pool", I("tensor_scalar", f_.t[:], s_.t[:], oml[:, mm:mm + 1], lb[:, mm:mm + 1], ALU.mult, ALU.add),
                             reads=[s_.buf, bconst], writes=[f_.buf])
                        g_ = gout.next()
                        P.op("act", I("activation", g_.t[:], f_.t[:], AF.Ln), reads=[f_.buf], writes=[g_.buf])
                        store(g1T[dsl[0], dsl[1]], g_, g_.t[:])
        jobs = [(lambda wt=wt: ws.load(w_in, 0, 32, [(wt * 256, 256, 0)], 256), lambda L, wt=wt: comp(L, wt)) for wt in range(64)]
        pipeline(jobs)
    P.finish(out_toks)
    P.emit()
    return nc


def build_l4():
    nc = new_nc()
    qT = nc.dram_tensor("qT", [HPC, 128, S], BF16, kind="ExternalInput").ap()
    kT = nc.dram_tensor("kT", [HPC, 128, S], BF16, kind="ExternalInput").ap()
    gT = nc.dram_tensor("gT", [HPC, 128, S], F32, kind="ExternalInput").ap()
    v2 = nc.dram_tensor("v2", [HPC, 128, S // 128, 128], BF16, kind="ExternalInput").ap()
    szT = nc.dram_tensor("szT", [HPC * 128, S], BF16, kind="ExternalInput").ap()
    gnd = nc.dram_tensor("gn", [128, 1], F32, kind="ExternalInput").ap()
    m2d = nc.dram_tensor("mask2", [128, 128], BF16, kind="ExternalInput").ap()
    idd = nc.dram_tensor("ident", [128, 128], BF16, kind="ExternalInput").ap()
    smd = nc.dram_tensor("scanmask", [128, 512], F32, kind="ExternalInput").ap()
    OGT = nc.dram_tensor("OG1T", [HPC * 128, S], BF16, kind="ExternalOutput").ap()

    P = Prog(nc)
    bconst = Buf("const")
    ones = P.sbuf("ones", [128, 128], BF16)
    epsc = P.sbuf("epsc", [128, 1], F32)
    gn = P.sbuf("gns", [128, 1], F32)
    mask2 = P.sbuf("mask2s", [128, 128], BF16)
    ident = P.sbuf("idents", [128, 128], BF16)
    scanm = P.sbuf("scanms", [128, 512], F32)
    P.op("dve", I("memset", ones[:], 1.0), writes=[bconst])
    P.op("dve", I("memset", epsc[:], EPS), writes=[bconst])
    P.dma(gn[:], gnd, writes=[bconst])
    P.dma(mask2[:], m2d, writes=[bconst])
    P.dma(ident[:], idd, writes=[bconst])
    P.dma(scanm[:], smd, writes=[bconst])

    H = HPC
    S32 = [P.slot(f"S32_{h}", [128, 128], F32) for h in range(H)]
    Sbf = [P.slot(f"Sbf_{h}", [128, 128], BF16) for h in range(H)]
    for h in range(H):
        P.op("dve", I("memset", S32[h].t[:], 0.0), writes=[S32[h].buf])
        P.op("pool", I("memset", Sbf[h].t[:], 0.0), writes=[Sbf[h].buf])
    Qh = [P.slot(f"Qh_{h}", [128, 512], BF16) for h in range(H)]
    Kht = [P.slot(f"Kht_{h}", [128, 512], BF16) for h in range(H)]
    vt = [P.slot(f"vt_{h}", [128, 4, 128], BF16) for h in range(H)]
    o32 = [P.slot(f"o32_{h}", [128, 512], F32) for h in range(H)]
    dec = [P.slot(f"dec_{h}", [128, 8], F32) for h in range(H)]
    szs = [P.slot(f"szs_{h}", [128, 512], BF16) for h in range(H)]
    qin = P.ring("qin", 2, [128, 512], BF16)
    kin = P.ring("kin", 2, [128, 512], BF16)
    gin = P.ring("gin", 2, [128, 512], F32)
    bt = P.ring("bt", 2, [128, 512], F32)
    d1 = P.ring("d1_", 2, [128, 512], F32)
    d4 = P.ring("d4_", 2, [128, 512], F32)
    E = P.ring("E_", 4, [128, 512], F32)
    Qt = P.ring("Qt", 2, [128, 512], BF16)
    Kt = P.ring("Kt", 2, [128, 512], BF16)
    KhT = P.ring("KhT", 2, [128, 512], BF16)
    attm = P.ring("attm", 2, [128, 512], BF16)
    sqb = P.ring("sqb", 2, [128, 512], BF16)
    sd = P.ring("sd", 2, [128, 512], F32)
    y32 = P.ring("y32", 2, [128, 512], F32)
    ogs = P.ring("ogs", 2, [128, 512], BF16)
    trp = Slot(P.psum("trp", [128, 1024], BF16), "trp")
    attp = Slot(P.psum("attp", [128, 512]), "attp")
    oIp = Slot(P.psum("oIp", [128, 512]), "oIp")
    oXp = P.pring("oXp", 2)
    dSp = [Slot(P.psum(f"dSp{i}", [128, 512]), f"dSp{i}") for i in range(2)]
    ssqp = Slot(P.psum("ssqp", [128, 512]), "ssqp")
    out_toks = []

    NSB = S // 512
    for sb in range(NSB):
        cs = slice(sb * 512, (sb + 1) * 512)
        for h in range(H):
            qi, ki, gi = qin.next(), kin.next(), gin.next()
            P.dma(qi.t[:], qT[h, :, cs], writes=[qi.buf])
            P.dma(ki.t[:], kT[h, :, cs], writes=[ki.buf])
            P.dma(gi.t[:], gT[h, :, cs], writes=[gi.buf])
            P.dma(vt[h].t[:], v2[h, :, sb * 4:(sb + 1) * 4, :], writes=[vt[h].buf])
            P.dma(szs[h].t[:], szT[h * 128:(h + 1) * 128, cs], writes=[szs[h].buf])
            b_ = bt.next()
            P.op("dve", I("tensor_tensor_scan", b_.t[:], scanm[:], gi.t[:], 0.0, ALU.mult, ALU.add),
                 reads=[gi.buf, bconst], writes=[b_.buf])
            b3 = b_.t[:].rearrange("p (n c) -> p n c", c=64)
            x1, x4 = d1.next(), d4.next()
            P.op("dve", I("tensor_tensor", x1.t[:].rearrange("p (n c) -> p n c", c=64), b3,
                          b3[:, :, 32:33].broadcast_to([128, 8, 64]), ALU.subtract), reads=[b_.buf], writes=[x1.buf])
            P.op("pool", I("tensor_tensor", x4.t[:].rearrange("p (n c) -> p n c", c=64), b3,
                           b3[:, :, 63:64].broadcast_to([128, 8, 64]), ALU.subtract), reads=[b_.buf], writes=[x4.buf])
            e1, e2, e3, e4 = E.next(), E.next(), E.next(), E.next()
            P.op("act", I("activation", e1.t[:], x1.t[:], AF.Exp), reads=[x1.buf], writes=[e1.buf])
            P.op("act", I("activation", e2.t[:], x1.t[:], AF.Exp, scale=-1.0), reads=[x1.buf], writes=[e2.buf])
            P.op("act", I("activation", e3.t[:], b_.t[:], AF.Exp), reads=[b_.buf], writes=[e3.buf])
            P.op("act", I("activation", e4.t[:], x4.t[:], AF.Exp, scale=-1.0), reads=[x4.buf], writes=[e4.buf])
            P.op("act", I("activation", dec[h].t[:], b3[:, :, 63], AF.Exp), reads=[b_.buf], writes=[dec[h].buf])
            qt_, kt_, kh_ = Qt.next(), Kt.next(), KhT.next()
            P.op("dve", I("tensor_tensor", qt_.t[:], qi.t[:], e1.t[:], ALU.mult), reads=[qi.buf, e1.buf], writes=[qt_.buf])
            P.op("pool", I("tensor_tensor", kt_.t[:], ki.t[:], e2.t[:], ALU.mult), reads=[ki.buf, e2.buf], writes=[kt_.buf])
            P.op("dve", I("tensor_tensor", Qh[h].t[:], qi.t[:], e3.t[:], ALU.mult), reads=[qi.buf, e3.buf], writes=[Qh[h].buf])
            P.op("pool", I("tensor_tensor", kh_.t[:], ki.t[:], e4.t[:], ALU.mult), reads=[ki.buf, e4.buf], writes=[kh_.buf])
            P.op("pe", [I("transpose", trp.t[:, pr * 128:(pr + 1) * 128], kh_.t[:, pr * 128:(pr + 1) * 128], ident[:]) for pr in range(4)],
                 reads=[kh_.buf, bconst], writes=[trp.buf])
            P.op("act", I("activation", Kht[h].t[:], trp.t[:, 0:512], AF.Copy), reads=[trp.buf], writes=[Kht[h].buf])
            P.op("pe", [I("matmul", attp.t[:, pr * 128:(pr + 1) * 128], kt_.t[:, pr * 128:(pr + 1) * 128], qt_.t[:, pr * 128:(pr + 1) * 128],
                          start=True, stop=True) for pr in range(4)], reads=[kt_.buf, qt_.buf], writes=[attp.buf])
            am = attm.next()
            P.op("dve", I("tensor_tensor", am.t[:].rearrange("p (n c) -> p n c", c=128), attp.t[:].rearrange("p (n c) -> p n c", c=128),
                          mask2[:].rearrange("p (o c) -> p o c", o=1).broadcast_to([128, 4, 128]), ALU.mult),
                 reads=[attp.buf, bconst], writes=[am.buf])
            P.op("pe", [I("matmul", oIp.t[:, pr * 128:(pr + 1) * 128], vt[h].t[:, pr, :], am.t[:, pr * 128:(pr + 1) * 128],
                          start=True, stop=True) for pr in range(4)], reads=[vt[h].buf, am.buf], writes=[oIp.buf])
            P.op("act", I("activation", o32[h].t[:], oIp.t[:], AF.Copy), reads=[oIp.buf], writes=[o32[h].buf])
        for pr in range(4):
            for ch in range(2):
                c0 = pr * 128 + ch * 64
                ps = slice(ch * 64, ch * 64 + 64)
                ox = oXp.next()
                for h in range(H):
                    P.op("pe", I("matmul", ox.t[:, h * 64:(h + 1) * 64], Sbf[h].t[:], Qh[h].t[:, c0:c0 + 64], start=True, stop=True),
                         reads=[Sbf[h].buf, Qh[h].buf], writes=[ox.buf])
                for h in range(H):
                    dsp = dSp[h // 4]
                    P.op("pe", I("matmul", dsp.t[:, (h % 4) * 128:(h % 4 + 1) * 128], Kht[h].t[ps, pr * 128:(pr + 1) * 128],
                                 vt[h].t[ps, pr, :], start=True, stop=True),
                         reads=[Kht[h].buf, vt[h].buf], writes=[dsp.buf])
                for h in range(H):
                    P.op("dve", I("tensor_tensor", o32[h].t[:, c0:c0 + 64], ox.t[:, h * 64:(h + 1) * 64], o32[h].t[:, c0:c0 + 64], ALU.add),
                         reads=[ox.buf, o32[h].buf], writes=[o32[h].buf])
                for h in range(H):
                    dsp = dSp[h // 4]
                    P.op("dve", I("scalar_tensor_tensor", S32[h].t[:], S32[h].t[:], dec[h].t[:, pr * 2 + ch:pr * 2 + ch + 1],
                                  dsp.t[:, (h % 4) * 128:(h % 4 + 1) * 128], ALU.mult, ALU.add),
                         reads=[S32[h].buf, dec[h].buf, dsp.buf], writes=[S32[h].buf])
                    P.op("pool", I("tensor_copy", Sbf[h].t[:], S32[h].t[:]), reads=[S32[h].buf], writes=[Sbf[h].buf])
        for h in range(H):
            sq = sqb.next()
            P.op("act", I("activation", sq.t[:], o32[h].t[:], AF.Square), reads=[o32[h].buf], writes=[sq.buf])
            P.op("pe", I("matmul", ssqp.t[:], ones[:], sq.t[:], start=True, stop=True), reads=[sq.buf, bconst], writes=[ssqp.buf])
            s_ = sd.next()
            P.op("act", I("activation", s_.t[:], ssqp.t[:], AF.Sqrt, bias=epsc[:, 0:1], scale=1.0 / 128), reads=[ssqp.buf, bconst], writes=[s_.buf])
            P.op("dve", I("reciprocal", s_.t[:], s_.t[:]), reads=[s_.buf], writes=[s_.buf])
            y = y32.next()
            P.op("dve", I("scalar_tensor_tensor", y.t[:], o32[h].t[:], gn[:, 0:1], s_.t[:], ALU.mult, ALU.mult),
                 reads=[o32[h].buf, s_.buf, bconst], writes=[y.buf])
            g_ = ogs.next()
            P.op("pool", I("tensor_tensor", g_.t[:], y.t[:], szs[h].t[:], ALU.mult), reads=[y.buf, szs[h].buf], writes=[g_.buf])
            out_toks.append(P.dma(OGT[h * 128:(h + 1) * 128, cs], g_.t[:], reads=[g_.buf], owner=g_.buf, queue="pool"))
    P.finish(out_toks)
    P.emit()
    return nc


def l4_consts():
    import ml_dtypes
    s = np.arange(128)[:, None]
    c = np.arange(128)[None, :]
    m2 = ((s // 64 == c // 64) & (s <= c)).astype(np.float32).astype(ml_dtypes.bfloat16)
    ident = np.eye(128, dtype=np.float32).astype(ml_dtypes.bfloat16)
    sm = np.ones((128, 512), np.float32)
    sm[:, ::64] = 0.0
    return m2, ident, sm


GROUPS = [[0, 1, 2, 3], [4, 5, 6, 7]]
NTP = 1024


def stage_a(nc, t):
    P = Prog(nc)
    NT = NTP
    xT, pos, w_lat, w_z, w_uq, w_ukv = t["xTall"], t["pos"], t["w_lat"], t["w_z"], t["w_uq"], t["w_ukv"]
    QnT, QrT, KnT, KrT, Vh, SZT = t["QnT"], t["QrT"], t["KnT"], t["KrT"], t["Vh"], t["SZT"]
    ws = WStream(P, nstage=4, nw=2)
    ones = P.sbuf("ones", [128, 128], BF16)
    g0 = P.sbuf("g0s", [128, 32], F32)
    qg = P.sbuf("qgs", [128, 8], F32)
    kvg = P.sbuf("kvgs", [128, 4], F32)
    rc = P.sbuf("rcs", [64, 2], F32)
    epsc = P.sbuf("epsc", [128, 1], F32)
    bconst = Buf("const")
    P.op("dve", I("memset", ones[:], 1.0), writes=[bconst])
    P.op("dve", I("memset", epsc[:], EPS), writes=[bconst])
    P.dma(g0[:], t["g0"], writes=[bconst])
    P.dma(qg[:], t["qg"], writes=[bconst])
    P.dma(kvg[:], t["kvg"], writes=[bconst])
    P.dma(rc[:], t["ropec"], writes=[bconst])

    xg = P.sbuf("xg", [128, 32, NT], BF16)
    bxg = Buf("xg")
    sqb = P.ring("sqb", 2, [128, NT], BF16)
    t32 = P.ring("t32_", 4, [128, 512], F32)
    sq2 = P.ring("sq2_", 2, [128, 512], BF16)
    ost = P.ring("ost", 4, [128, 512], BF16)
    rstd_x = P.sbuf("rstd_x", [128, NT], F32)
    rstd_q = P.sbuf("rstd_q", [128, NT], F32)
    rstd_kv = P.sbuf("rstd_kv", [128, NT], F32)
    brx, brq, brkv = Buf("rstd_x"), Buf("rstd_q"), Buf("rstd_kv")
    cqg = P.sbuf("cqg", [128, 8, NT], BF16)
    bcqg = Buf("cqg")
    ckvn = P.sbuf("ckvn", [128, 4, NT], BF16)
    bckvn = Buf("ckvn")
    cosT = P.sbuf("cosT", [64, NT], F32)
    sinT = P.sbuf("sinT", [64, NT], F32)
    btab = Buf("ropetab")
    ra = P.sbuf("ra", [64, NT], F32)
    rb = P.sbuf("rb", [64, NT], F32)
    rcc = P.sbuf("rcc", [64, NT], F32)
    ri = P.sbuf("ri", [64, NT], I32)
    bra, brb, brcc, bri = Buf("ra"), Buf("rb"), Buf("rcc"), Buf("ri")
    acc = P.pring("acc", 4)
    ssq = [Slot(P.psum(f"ssq{i}", [128, 512]), f"ssq{i}") for i in range(2)]
    ssq2 = [Slot(P.psum(f"ssqb{i}", [128, 512]), f"ssqb{i}") for i in range(2)]

    def rstd_from(ssq_slot, dst, bdst, n, inv_n):
        tt = t32.next()
        P.op("act", I("activation", tt.t[:], ssq_slot.t[:], AF.Sqrt, bias=epsc[:, 0:1], scale=inv_n),
             reads=[ssq_slot.buf, bconst], writes=[tt.buf])
        P.op("dve", I("reciprocal", dst[:, n * 512:(n + 1) * 512], tt.t[:]), reads=[tt.buf], writes=[bdst])

    def store(dst_ap, src_slot, src_ap):
        P.dma(dst_ap, src_ap, reads=[src_slot.buf], owner=src_slot.buf, queue="act")

    for p in range(S // NT):
        t0 = p * NT
        P.dma(ri[:], pos[t0:t0 + NT].partition_broadcast(64), writes=[bri])
        P.op("dve", I("tensor_copy", ra[:], ri[:]), reads=[bri], writes=[bra])
        P.op("dve", I("tensor_scalar", ra[:], ra[:], rc[:, 0:1], None, ALU.mult), reads=[bra, bconst], writes=[bra])
        P.op("dve", I("tensor_scalar", rb[:], ra[:], 1.0 / TWO_PI, None, ALU.mult), reads=[bra], writes=[brb])
        P.op("dve", I("tensor_copy", ri[:], rb[:]), reads=[brb], writes=[bri])
        P.op("dve", I("tensor_copy", rb[:], ri[:]), reads=[bri], writes=[brb])
        P.op("dve", I("scalar_tensor_tensor", ra[:], rb[:], -CW1, ra[:], ALU.mult, ALU.add), reads=[bra, brb], writes=[bra])
        P.op("dve", I("scalar_tensor_tensor", ra[:], rb[:], -CW2, ra[:], ALU.mult, ALU.add), reads=[bra, brb], writes=[bra])
        P.op("dve", I("tensor_single_scalar", rb[:], ra[:], 0.0, ALU.is_lt), reads=[bra], writes=[brb])
        P.op("dve", I("scalar_tensor_tensor", ra[:], rb[:], TWO_PI, ra[:], ALU.mult, ALU.add), reads=[bra, brb], writes=[bra])
        P.op("dve", I("tensor_scalar", rb[:], ra[:], -1.0, math.pi, ALU.mult, ALU.add), reads=[bra], writes=[brb])
        P.op("dve", I("tensor_scalar", rb[:], rb[:], PI_LO, -PI_LO, ALU.min, ALU.max), reads=[brb], writes=[brb])
        P.op("act", I("activation", sinT[:], rb[:], AF.Sin), reads=[brb], writes=[btab])
        P.op("dve", I("tensor_scalar", rcc[:], ra[:], math.pi / 2, None, ALU.add), reads=[bra], writes=[brcc])
        P.op("dve", I("tensor_single_scalar", rb[:], rcc[:], TWO_PI, ALU.is_ge), reads=[brcc, btab], writes=[brb])
        P.op("dve", I("scalar_tensor_tensor", rcc[:], rb[:], -TWO_PI, rcc[:], ALU.mult, ALU.add), reads=[brcc, brb], writes=[brcc])
        P.op("dve", I("tensor_scalar", rcc[:], rcc[:], -1.0, math.pi, ALU.mult, ALU.add), reads=[brcc], writes=[brcc])
        P.op("dve", I("tensor_scalar", rcc[:], rcc[:], PI_LO, -PI_LO, ALU.min, ALU.max), reads=[brcc], writes=[brcc])
        P.op("act", I("activation", cosT[:], rcc[:], AF.Sin), reads=[brcc], writes=[btab])
        P.op("dve", I("tensor_scalar", sinT[:], sinT[:], rc[:, 1:2], None, ALU.mult), reads=[btab, bconst], writes=[btab])

        for kc in range(32):
            st = ws.stages.next()
            stv = st.t[:, 0:NT]
            P.dma(stv, xT[kc * 128:(kc + 1) * 128, t0:t0 + NT], writes=[st.buf])
            sq = sqb.next()
            P.op("act", I("activation", sq.t[:], stv, AF.Square), reads=[st.buf], writes=[sq.buf])
            P.op("pe", [I("matmul", ssq[n].t[:], ones[:], sq.t[:, n * 512:(n + 1) * 512], start=(kc == 0), stop=(kc == 31))
                        for n in range(2)], reads=[sq.buf, bconst], writes=[ssq[0].buf, ssq[1].buf])
            P.op("dve", I("tensor_scalar", xg[:, kc, :], stv, g0[:, kc:kc + 1], None, ALU.mult),
                 reads=[st.buf, bconst], writes=[bxg])
        for n in range(2):
            rstd_from(ssq[n], rstd_x, brx, n, 1.0 / D)

        def gemm_x(wbuf, wv, mcols, epilogue):
            for (c0, M, tag) in mcols:
                for n in range(2):
                    a = acc.next()
                    P.op("pe", [I("matmul", a.t[0:M, :], wv[:, kc, c0:c0 + M], xg[:, kc, n * 512:(n + 1) * 512],
                                  start=(kc == 0), stop=(kc == 31)) for kc in range(32)],
                         reads=[wbuf, bxg], writes=[a.buf])
                    epilogue(a, M, tag, n)

        def ep_lat(kind):
            nm = 8 if kind == "q" else 4

            def ep(a, M, m, n):
                cs = slice(n * 512, (n + 1) * 512)
                tt = t32.next()
                P.op("dve", I("tensor_tensor", tt.t[:], a.t[:], rstd_x[:, cs], ALU.mult), reads=[a.buf, brx], writes=[tt.buf])
                s2 = sq2.next()
                P.op("act", I("activation", s2.t[:], tt.t[:], AF.Square), reads=[tt.buf], writes=[s2.buf])
                P.op("pe", I("matmul", ssq2[n].t[:], ones[:], s2.t[:], start=(m == 0), stop=(m == nm - 1)),
                     reads=[s2.buf, bconst], writes=[ssq2[n].buf])
                if kind == "q":
                    P.op("pool", I("tensor_scalar", cqg[:, m, cs], tt.t[:], qg[:, m:m + 1], None, ALU.mult),
                         reads=[tt.buf, bconst], writes=[bcqg])
                else:
                    P.op("pool", I("tensor_scalar", ckvn[:, m, cs], tt.t[:], kvg[:, m:m + 1], None, ALU.mult),
                         reads=[tt.buf, bconst], writes=[bckvn])
            return ep

        def rope_store(aA, aB, rs, brs, n, dst_ap):
            tA = t32.next()
            tB = t32.next()
            cs = slice(n * 512, (n + 1) * 512)
            P.op("dve", I("tensor_tensor", tA.t[0:64, :], aA.t[0:64, :], rs[0:64, cs], ALU.mult), reads=[aA.buf, brs], writes=[tA.buf])
            P.op("dve", I("tensor_tensor", tB.t[0:64, :], aB.t[0:64, :], rs[0:64, cs], ALU.mult), reads=[aB.buf, brs], writes=[tB.buf])
            P.op("pool", I("tensor_tensor", tA.t[0:64, :], tA.t[0:64, :], cosT[:, cs], ALU.mult), reads=[tA.buf, btab], writes=[tA.buf])
            P.op("pool", I("tensor_tensor", tB.t[0:64, :], tB.t[0:64, :], sinT[:, cs], ALU.mult), reads=[tB.buf, btab], writes=[tB.buf])
            o = ost.next()
            P.op("dve", I("tensor_tensor", o.t[0:64, :], tA.t[0:64, :], tB.t[0:64, :], ALU.add), reads=[tA.buf, tB.buf], writes=[o.buf])
            store(dst_ap, o, o.t[0:64, :])

        def ep_z(a, M, m, n):
            cs = slice(n * 512, (n + 1) * 512)
            tt = t32.next()
            P.op("dve", I("tensor_tensor", tt.t[:], a.t[:], rstd_x[:, cs], ALU.mult), reads=[a.buf, brx], writes=[tt.buf])
            o = ost.next()
            P.op("act", I("activation", o.t[:], tt.t[:], AF.Silu), reads=[tt.buf], writes=[o.buf])
            store(SZT[m * 128:(m + 1) * 128, t0 + n * 512:t0 + (n + 1) * 512], o, o.t[:])

        jobs = []
        for wt in range(4):
            jobs.append((lambda wt=wt, p=p: ws.load(w_lat, 0, 32, [(wt * 256, 256, 0)], 256, key=("cq", wt), first=(p == 0)),
                         lambda L, wt=wt: gemm_x(L[0], L[1], [(0, 128, wt * 2), (128, 128, wt * 2 + 1)], ep_lat("q"))))

        def after_q():
            for n in range(2):
                rstd_from(ssq2[n], rstd_q, brq, n, 1.0 / 1024)
        for wt in range(2):
            def comp(L, wt=wt):
                if wt == 0:
                    after_q()
                gemm_x(L[0], L[1], [(0, 128, wt * 2), (128, 128, wt * 2 + 1)], ep_lat("kv"))
            jobs.append((lambda wt=wt, p=p: ws.load(w_lat, 0, 32, [(1024 + wt * 256, 256, 0)], 256, key=("ckv", wt), first=(p == 0)), comp))

        def comp_krot(L):
            wbuf, wv = L
            for n in range(2):
                rstd_from(ssq2[n], rstd_kv, brkv, n, 1.0 / 512)
            for m in range(4):
                for n in range(2):
                    cs = slice(n * 512, (n + 1) * 512)
                    P.op("dve", I("tensor_tensor", ckvn[:, m, cs], ckvn[:, m, cs], rstd_kv[:, cs], ALU.mult),
                         reads=[bckvn, brkv], writes=[bckvn])
            for n in range(2):
                aA, aB = acc.next(), acc.next()
                calls = []
                for (a, c0) in ((aA, 0), (aB, 64)):
                    for kc in range(32):
                        calls.append(I("matmul", a.t[0:64, :], wv[:, kc, c0:c0 + 64], xg[:, kc, n * 512:(n + 1) * 512],
                                       start=(kc == 0), stop=(kc == 31)))
                P.op("pe", calls, reads=[wbuf, bxg], writes=[aA.buf, aB.buf])
                rope_store(aA, aB, rstd_x, brx, n, KrT[:, t0 + n * 512:t0 + (n + 1) * 512])
        jobs.append((lambda p=p: ws.load(w_lat, 0, 32, [(1536, 128, 0)], 128, key=("kr", 0), first=(p == 0)), comp_krot))

        for wt in range(4):
            jobs.append((lambda wt=wt, p=p: ws.load(w_z, 0, 32, [(wt * 256, 256, 0)], 256, key=("z", wt), first=(p == 0)),
                         lambda L, wt=wt: gemm_x(L[0], L[1], [(0, 128, wt * 2), (128, 128, wt * 2 + 1)], ep_z)))

        def comp_q(L, h):
            wbuf, wv = L
            for n in range(2):
                cs = slice(n * 512, (n + 1) * 512)
                a0, aA, aB = acc.next(), acc.next(), acc.next()
                calls = []
                for (a, c0, M) in ((a0, 0, 128), (aA, 128, 64), (aB, 192, 64)):
                    for kc in range(8):
                        calls.append(I("matmul", a.t[0:M, :], wv[:, kc, c0:c0 + M], cqg[:, kc, cs], start=(kc == 0), stop=(kc == 7)))
                P.op("pe", calls, reads=[wbuf, bcqg], writes=[a0.buf, aA.buf, aB.buf])
                o = ost.next()
                P.op("dve", I("tensor_tensor", o.t[:], a0.t[:], rstd_q[:, cs], ALU.mult), reads=[a0.buf, brq], writes=[o.buf])
                store(QnT[h, :, t0 + n * 512:t0 + (n + 1) * 512], o, o.t[:])
                rope_store(aA, aB, rstd_q, brq, n, QrT[h, :, t0 + n * 512:t0 + (n + 1) * 512])
        for h in range(HPC):
            jobs.append((lambda h=h, p=p: ws.load(w_uq, 0, 8, [(h * 256, 256, 0)], 256, key=("uq", h), first=(p == 0)), lambda L, h=h: comp_q(L, h)))

        def comp_kv(L, hg):
            wbuf, wv = L
            for hh in range(4):
                h = hg * 4 + hh
                for n in range(2):
                    cs = slice(n * 512, (n + 1) * 512)
                    a = acc.next()
                    P.op("pe", [I("matmul", a.t[:], wv[:, kc, hh * 256:hh * 256 + 128], ckvn[:, kc, cs], start=(kc == 0), stop=(kc == 3))
                                for kc in range(4)], reads=[wbuf, bckvn], writes=[a.buf])
                    o = ost.next()
                    P.op("act", I("activation", o.t[:], a.t[:], AF.Copy), reads=[a.buf], writes=[o.buf])
                    store(KnT[h, :, t0 + n * 512:t0 + (n + 1) * 512], o, o.t[:])
            wv4 = wv.rearrange("p k (h c) -> p k h c", c=256)
            for tb in range(NT // 128):
                a = acc.next()
                av = a.t[:].rearrange("p (h c) -> p h c", c=128)
                P.op("pe", [I("matmul", av, ckvn[:, kc, tb * 128:(tb + 1) * 128], wv4[:, kc, :, 128:256], start=(kc == 0), stop=(kc == 3))
                            for kc in range(4)], reads=[wbuf, bckvn], writes=[a.buf])
                o = ost.next()
                P.op("dve", I("tensor_copy", o.t[:], a.t[:]), reads=[a.buf], writes=[o.buf])
                blk = (t0 + tb * 128) // 128
                store(Vh[hg * 4:(hg + 1) * 4, :, blk, :].rearrange("h p d -> p h d"), o, o.t[:].rearrange("p (h d) -> p h d", d=128))
        for hg in range(HPC // 4):
            jobs.append((lambda hg=hg, p=p: ws.load(w_ukv, 0, 4, [(hg * 1024, 1024, 0)], 1024, key=("ukv", hg), first=(p == 0)), lambda L, hg=hg: comp_kv(L, hg)))
        pipeline(jobs)
    P.end_stage()


def stage_b(nc, t):
    QnT, QrT, KnT, KrT, Vh, SZT, trid, OGT = t["QnT"], t["QrT"], t["KnT"], t["KrT"], t["Vh"], t["SZT"], t["tri"], t["OGT"]
    P = Prog(nc)
    scale = 1.0 / math.sqrt(192.0)
    ones = P.sbuf("ones", [128, 128], BF16)
    tri = P.sbuf("tri_sb", [128, 128], BF16)
    bconst = Buf("const")
    P.op("dve", I("memset", ones[:], 1.0), writes=[bconst])
    P.dma(tri[:], trid, writes=[bconst])
    kr = P.slot("kr", [64, S], BF16)
    P.dma(kr.t[:], KrT, writes=[kr.buf])
    kn = P.ring("kn", 2, [128, S], BF16)
    vv = P.ring("vv", 2, [128, S // 128, 128], BF16)
    qn = P.ring("qn", 3, [128, 512], BF16)
    qr = P.ring("qr", 3, [64, 512], BF16)
    sz = P.ring("sz", 3, [128, 512], BF16)
    pT = P.ring("pT", 3, [128, 512], BF16)
    rs = P.ring("rs", 2, [128, 512], F32)
    o32 = P.ring("o32", 2, [128, 512], F32)
    og = P.ring("og", 2, [128, 512], BF16)
    sT = P.pring("sT", 3)
    oT = P.pring("oT", 2)
    sm = P.pring("sm", 2)
    out_toks = []

    def load_head(h):
        k = kn.next()
        v = vv.next()
        P.dma(k.t[:], KnT[h], writes=[k.buf])
        P.dma(v.t[:], Vh[h], writes=[v.buf])
        return k, v

    def load_q(h, t):
        a, b_, c = qn.next(), qr.next(), sz.next()
        cs = slice(t * 512, (t + 1) * 512)
        P.dma(a.t[:], QnT[h, :, cs], writes=[a.buf])
        P.dma(b_.t[:], QrT[h, :, cs], writes=[b_.buf])
        P.dma(c.t[:], SZT[h * 128:(h + 1) * 128, cs], writes=[c.buf])
        return a, b_, c

    NQT = S // 512
    tiles = [(h, t) for h in range(HPC) for t in range(NQT)]
    blocks = []
    for ti, (h, t) in enumerate(tiles):
        nb = 4 * t + 4
        for kb in range(nb):
            blocks.append((ti, kb, nb))
    state = {}

    def tile_ctx(ti):
        if ti in state:
            return state[ti]
        h, t = tiles[ti]
        if t == 0:
            hk = load_head(h)
        else:
            hk = state[ti - 1]["hk"]
        q = load_q(h, t)
        ctx = {"hk": hk, "q": q, "oT": oT.next(), "sm": sm.next()}
        state[ti] = ctx
        return ctx

    def emit_S(bi):
        ti, kb, nb = blocks[bi]
        h, t = tiles[ti]
        ctx = tile_ctx(ti)
        k, v = ctx["hk"]
        a, b_, c = ctx["q"]
        j = kb - 4 * t
        c0 = 128 * j if j > 0 else 0
        s = sT.next()
        ks = slice(kb * 128, (kb + 1) * 128)
        P.op("pe", [I("matmul", s.t[:, c0:512], k.t[:, ks], a.t[:, c0:512], start=True, stop=False),
                    I("matmul", s.t[:, c0:512], kr.t[:, ks], b_.t[:, c0:512], start=False, stop=True)],
             reads=[k.buf, kr.buf, a.buf, b_.buf], writes=[s.buf])
        return s, c0, j

    pend = {}
    nB = len(blocks)
    AHEAD = 2
    for bi in range(min(AHEAD, nB)):
        pend[bi] = emit_S(bi)
    for bi in range(nB):
        ti, kb, nb = blocks[bi]
        h, t = tiles[ti]
        ctx = state[ti]
        k, v = ctx["hk"]
        s, c0, j = pend.pop(bi)
        p = pT.next()
        P.op("act", I("activation", p.t[:, c0:512], s.t[:, c0:512], AF.Exp, scale=scale), reads=[s.buf], writes=[p.buf])
        if j >= 0:
            P.op("pool", I("tensor_tensor", p.t[:, c0:c0 + 128], p.t[:, c0:c0 + 128], tri[:], ALU.mult),
                 reads=[p.buf, bconst], writes=[p.buf])
        o_, m_ = ctx["oT"], ctx["sm"]
        P.op("pe", [I("matmul", o_.t[:, c0:512], v.t[:, kb, :], p.t[:, c0:512], start=(kb == 0), stop=(kb == nb - 1)),
                    I("matmul", m_.t[:, c0:512], ones[:], p.t[:, c0:512], start=(kb == 0), stop=(kb == nb - 1))],
             reads=[v.buf, p.buf, bconst], writes=[o_.buf, m_.buf])
        if bi + AHEAD < nB:
            pend[bi + AHEAD] = emit_S(bi + AHEAD)
        if kb == nb - 1:
            a, b_, c = ctx["q"]
            r_ = rs.next()
            P.op("dve", I("reciprocal", r_.t[:], m_.t[:]), reads=[m_.buf], writes=[r_.buf])
            x_ = o32.next()
            P.op("dve", I("tensor_tensor", x_.t[:], o_.t[:], r_.t[:], ALU.mult), reads=[o_.buf, r_.buf], writes=[x_.buf])
            g_ = og.next()
            P.op("dve", I("tensor_tensor", g_.t[:], x_.t[:], c.t[:], ALU.mult), reads=[x_.buf, c.buf], writes=[g_.buf])
            out_toks.append(P.dma(OGT[h * 128:(h + 1) * 128, t * 512:(t + 1) * 512], g_.t[:], reads=[g_.buf],
                                  owner=g_.buf, queue="pool"))
            if ti + 1 < len(tiles):
                tile_ctx(ti + 1)

    P.end_stage()


def stage_e(nc, t):
    qT, kT, gT, v2, szT, gnd, m2d, idd, smd, OGT = (t["q1T"], t["k1T"], t["g1T"], t["v2"], t["sz1T"], t["gn"], t["mask2"],
                                                    t["ident"], t["scanmask"], t["OG1T"])
    P = Prog(nc)
    bconst = Buf("const")
    ones = P.sbuf("ones", [128, 128], BF16)
    epsc = P.sbuf("epsc", [128, 1], F32)
    gn = P.sbuf("gns", [128, 1], F32)
    mask2 = P.sbuf("mask2s", [128, 128], BF16)
    ident = P.sbuf("idents", [128, 128], BF16)
    scanm = P.sbuf("scanms", [128, 512], F32)
    P.op("dve", I("memset", ones[:], 1.0), writes=[bconst])
    P.op("dve", I("memset", epsc[:], EPS), writes=[bconst])
    P.dma(gn[:], gnd, writes=[bconst])
    P.dma(mask2[:], m2d, writes=[bconst])
    P.dma(ident[:], idd, writes=[bconst])
    P.dma(scanm[:], smd, writes=[bconst])

    H = HPC
    S32 = [P.slot(f"S32_{h}", [128, 128], F32) for h in range(H)]
    Sbf = [P.slot(f"Sbf_{h}", [128, 128], BF16) for h in range(H)]
    for h in range(H):
        P.op("dve", I("memset", S32[h].t[:], 0.0), writes=[S32[h].buf])
        P.op("pool", I("memset", Sbf[h].t[:], 0.0), writes=[Sbf[h].buf])
    Qh = [P.slot(f"Qh_{h}", [128, 512], BF16) for h in range(H)]
    Kht = [P.slot(f"Kht_{h}", [128, 512], BF16) for h in range(H)]
    vt = [P.slot(f"vt_{h}", [128, 4, 128], BF16) for h in range(H)]
    o32 = [P.slot(f"o32_{h}", [128, 512], F32) for h in range(H)]
    dec = [P.slot(f"dec_{h}", [128, 8], F32) for h in range(H)]
    szs = [P.slot(f"szs_{h}", [128, 512], BF16) for h in range(H)]
    qin = P.ring("qin", 2, [128, 512], BF16)
    kin = P.ring("kin", 2, [128, 512], BF16)
    gin = P.ring("gin", 2, [128, 512], F32)
    bt = P.ring("bt", 2, [128, 512], F32)
    d1 = P.ring("d1_", 2, [128, 512], F32)
    d4 = P.ring("d4_", 2, [128, 512], F32)
    E = P.ring("E_", 4, [128, 512], F32)
    Qt = P.ring("Qt", 2, [128, 512], BF16)
    Kt = P.ring("Kt", 2, [128, 512], BF16)
    KhT = P.ring("KhT", 2, [128, 512], BF16)
    attm = P.ring("attm", 2, [128, 512], BF16)
    sqb = P.ring("sqb", 2, [128, 512], BF16)
    sd = P.ring("sd", 2, [128, 512], F32)
    y32 = P.ring("y32", 2, [128, 512], F32)
    ogs = P.ring("ogs", 2, [128, 512], BF16)
    trp = Slot(P.psum("trp", [128, 1024], BF16), "trp")
    attp = Slot(P.psum("attp", [128, 512]), "attp")
    oIp = Slot(P.psum("oIp", [128, 512]), "oIp")
    oXp = P.pring("oXp", 2)
    dSp = [Slot(P.psum(f"dSp{i}", [128, 512]), f"dSp{i}") for i in range(2)]
    ssqp = Slot(P.psum("ssqp", [128, 512]), "ssqp")
    out_toks = []

    NSB = S // 512
    for sb in range(NSB):
        cs = slice(sb * 512, (sb + 1) * 512)
        for h in range(H):
            qi, ki, gi = qin.next(), kin.next(), gin.next()
            P.dma(qi.t[:], qT[h, :, cs], writes=[qi.buf])
            P.dma(ki.t[:], kT[h, :, cs], writes=[ki.buf])
            P.dma(gi.t[:], gT[h, :, cs], writes=[gi.buf])
            P.dma(vt[h].t[:], v2[h, :, sb * 4:(sb + 1) * 4, :], writes=[vt[h].buf])
            P.dma(szs[h].t[:], szT[h * 128:(h + 1) * 128, cs], writes=[szs[h].buf])
            b_ = bt.next()
            P.op("dve", I("tensor_tensor_scan", b_.t[:], scanm[:], gi.t[:], 0.0, ALU.mult, ALU.add),
                 reads=[gi.buf, bconst], writes=[b_.buf])
            b3 = b_.t[:].rearrange("p (n c) -> p n c", c=64)
            x1, x4 = d1.next(), d4.next()
            P.op("dve", I("tensor_tensor", x1.t[:].rearrange("p (n c) -> p n c", c=64), b3,
                          b3[:, :, 32:33].broadcast_to([128, 8, 64]), ALU.subtract), reads=[b_.buf], writes=[x1.buf])
            P.op("pool", I("tensor_tensor", x4.t[:].rearrange("p (n c) -> p n c", c=64), b3,
                           b3[:, :, 63:64].broadcast_to([128, 8, 64]), ALU.subtract), reads=[b_.buf], writes=[x4.buf])
            e1, e2, e3, e4 = E.next(), E.next(), E.next(), E.next()
            P.op("act", I("activation", e1.t[:], x1.t[:], AF.Exp), reads=[x1.buf], writes=[e1.buf])
            P.op("act", I("activation", e2.t[:], x1.t[:], AF.Exp, scale=-1.0), reads=[x1.buf], writes=[e2.buf])
            P.op("act", I("activation", e3.t[:], b_.t[:], AF.Exp), reads=[b_.buf], writes=[e3.buf])
            P.op("act", I("activation", e4.t[:], x4.t[:], AF.Exp, scale=-1.0), reads=[x4.buf], writes=[e4.buf])
            P.op("act", I("activation", dec[h].t[:], b3[:, :, 63], AF.Exp), reads=[b_.buf], writes=[dec[h].buf])
            qt_, kt_, kh_ = Qt.next(), Kt.next(), KhT.next()
            P.op("dve", I("tensor_tensor", qt_.t[:], qi.t[:], e1.t[:], ALU.mult), reads=[qi.buf, e1.buf], writes=[qt_.buf])
            P.op("pool", I("tensor_tensor", kt_.t[:], ki.t[:], e2.t[:], ALU.mult), reads=[ki.buf, e2.buf], writes=[kt_.buf])
            P.op("dve", I("tensor_tensor", Qh[h].t[:], qi.t[:], e3.t[:], ALU.mult), reads=[qi.buf, e3.buf], writes=[Qh[h].buf])
            P.op("pool", I("tensor_tensor", kh_.t[:], ki.t[:], e4.t[:], ALU.mult), reads=[ki.buf, e4.buf], writes=[kh_.buf])
            P.op("pe", [I("transpose", trp.t[:, pr * 128:(pr + 1) * 128], kh_.t[:, pr * 128:(pr + 1) * 128], ident[:]) for pr in range(4)],
                 reads=[kh_.buf, bconst], writes=[trp.buf])
            P.op("act", I("activation", Kht[h].t[:], trp.t[:, 0:512], AF.Copy), reads=[trp.buf], writes=[Kht[h].buf])
            P.op("pe", [I("matmul", attp.t[:, pr * 128:(pr + 1) * 128], kt_.t[:, pr * 128:(pr + 1) * 128], qt_.t[:, pr * 128:(pr + 1) * 128],
                          start=True, stop=True) for pr in range(4)], reads=[kt_.buf, qt_.buf], writes=[attp.buf])
            am = attm.next()
            P.op("dve", I("tensor_tensor", am.t[:].rearrange("p (n c) -> p n c", c=128), attp.t[:].rearrange("p (n c) -> p n c", c=128),
                          mask2[:].rearrange("p (o c) -> p o c", o=1).broadcast_to([128, 4, 128]), ALU.mult),
                 reads=[attp.buf, bconst], writes=[am.buf])
            P.op("pe", [I("matmul", oIp.t[:, pr * 128:(pr + 1) * 128], vt[h].t[:, pr, :], am.t[:, pr * 128:(pr + 1) * 128],
                          start=True, stop=True) for pr in range(4)], reads=[vt[h].buf, am.buf], writes=[oIp.buf])
            P.op("act", I("activation", o32[h].t[:], oIp.t[:], AF.Copy), reads=[oIp.buf], writes=[o32[h].buf])
        for pr in range(4):
            for ch in range(2):
                c0 = pr * 128 + ch * 64
                ps = slice(ch * 64, ch * 64 + 64)
                ox = oXp.next()
                for h in range(H):
                    P.op("pe", I("matmul", ox.t[:, h * 64:(h + 1) * 64], Sbf[h].t[:], Qh[h].t[:, c0:c0 + 64], start=True, stop=True),
                         reads=[Sbf[h].buf, Qh[h].buf], writes=[ox.buf])
                for h in range(H):
                    dsp = dSp[h // 4]
                    P.op("pe", I("matmul", dsp.t[:, (h % 4) * 128:(h % 4 + 1) * 128], Kht[h].t[ps, pr * 128:(pr + 1) * 128],
                                 vt[h].t[ps, pr, :], start=True, stop=True),
                         reads=[Kht[h].buf, vt[h].buf], writes=[dsp.buf])
                for h in range(H):
                    P.op("dve", I("tensor_tensor", o32[h].t[:, c0:c0 + 64], ox.t[:, h * 64:(h + 1) * 64], o32[h].t[:, c0:c0 + 64], ALU.add),
                         reads=[ox.buf, o32[h].buf], writes=[o32[h].buf])
                for h in range(H):
                    dsp = dSp[h // 4]
                    P.op("dve", I("scalar_tensor_tensor", S32[h].t[:], S32[h].t[:], dec[h].t[:, pr * 2 + ch:pr * 2 + ch + 1],
                                  dsp.t[:, (h % 4) * 128:(h % 4 + 1) * 128], ALU.mult, ALU.add),
                         reads=[S32[h].buf, dec[h].buf, dsp.buf], writes=[S32[h].buf])
                    P.op("pool", I("tensor_copy", Sbf[h].t[:], S32[h].t[:]), reads=[S32[h].buf], writes=[Sbf[h].buf])
        for h in range(H):
            sq = sqb.next()
            P.op("act", I("activation", sq.t[:], o32[h].t[:], AF.Square), reads=[o32[h].buf], writes=[sq.buf])
            P.op("pe", I("matmul", ssqp.t[:], ones[:], sq.t[:], start=True, stop=True), reads=[sq.buf, bconst], writes=[ssqp.buf])
            s_ = sd.next()
            P.op("act", I("activation", s_.t[:], ssqp.t[:], AF.Sqrt, bias=epsc[:, 0:1], scale=1.0 / 128), reads=[ssqp.buf, bconst], writes=[s_.buf])
            P.op("dve", I("reciprocal", s_.t[:], s_.t[:]), reads=[s_.buf], writes=[s_.buf])
            y = y32.next()
            P.op("dve", I("scalar_tensor_tensor", y.t[:], o32[h].t[:], gn[:, 0:1], s_.t[:], ALU.mult, ALU.mult),
                 reads=[o32[h].buf, s_.buf, bconst], writes=[y.buf])
            g_ = ogs.next()
            P.op("pool", I("tensor_tensor", g_.t[:], y.t[:], szs[h].t[:], ALU.mult), reads=[y.buf, szs[h].buf], writes=[g_.buf])
            out_toks.append(P.dma(OGT[h * 128:(h + 1) * 128, cs], g_.t[:], reads=[g_.buf], owner=g_.buf, queue="pool"))
    P.end_stage()


def stage_gather(nc, src_ap, dst_ap, rows):
    P = Prog(nc)
    R = src_ap.shape[0]
    for j in range(R // rows):
        P.op("pool", I("collective_compute", "AllGather", ALU.bypass, replica_groups=GROUPS,
                       ins=[src_ap[j * rows:(j + 1) * rows, :]], outs=[dst_ap[j]]))
    P.end_stage()


def stage_wo(nc, t, final):
    P = Prog(nc)
    NT = NTP
    gOG, xT, w_o, outT = t["gOG"], t["xT"], t["w_o"], t["outT"]
    rcache = {}

    def rank_of(e):
        if "r" not in rcache:
            rcache["r"] = e.partition_id() % 4
        return rcache["r"]
    ws = WStream(P, nstage=2, nw=2)
    bconst = Buf("const")
    ones = P.sbuf("ones", [128, 128], BF16)
    epsc = P.sbuf("epsc", [128, 1], F32)
    P.op("dve", I("memset", ones[:], 1.0), writes=[bconst])
    P.op("dve", I("memset", epsc[:], EPS), writes=[bconst])
    if final:
        fg = P.sbuf("fgs", [128, 32], F32)
        P.dma(fg[:], t["fg"], writes=[bconst])
        x2T = t["x2T"]
    else:
        g1 = P.sbuf("g1s", [128, 32], F32)
        P.dma(g1[:], t["g1"], writes=[bconst])
        xg1T, rstdT = t["xg1T"], t["rstdT"]
        ostw = P.ring("ostw", 3, [128, 512], BF16)
    og = P.slot("og", [128, 32, NT], BF16)
    xs = P.ring("xs", 3, [128, 512], F32)
    t32 = P.ring("t32_", 3, [128, 512], F32)
    sq2 = P.ring("sq2_", 2, [128, 512], BF16)
    acc = P.pring("acc", 4)
    ssq = [Slot(P.psum(f"ssq{i}", [128, 512]), f"ssq{i}") for i in range(2)]
    rstd = P.sbuf("rstd", [128, NT], F32)
    brstd = Buf("rstd")
    bscr = Buf("x2scratch")

    ogown = t["ogown"]
    bown = Buf("ogown")
    g4 = gOG.rearrange("j (g i) s -> j g i s", i=64)
    for q in range(4):
        P.dma(ogown[q * 1024:(q + 1) * 1024, :].rearrange("(j i) s -> j i s", i=64),
              Lazy(lambda e, q=q: g4[:, q, :, bass.ds(rank_of(e) * TOK, TOK)]), writes=[bown])
    for p in range(TOK // NT):
        t0 = p * NT
        for kc in range(32):
            P.dma(og.t[:, kc, :], ogown[kc * 128:(kc + 1) * 128, t0:t0 + NT], reads=[bown], writes=[og.buf])

        def comp(L, wt):
            wbuf, wv = L
            for mi in range(2):
                m = wt * 2 + mi
                for n in range(2):
                    a = acc.next()
                    P.op("pe", [I("matmul", a.t[:], wv[:, kc, mi * 128:(mi + 1) * 128], og.t[:, kc, n * 512:(n + 1) * 512],
                                  start=(kc == 0), stop=(kc == 31)) for kc in range(32)], reads=[wbuf, og.buf], writes=[a.buf])
                    x_ = xs.next()
                    dsl = (slice(m * 128, (m + 1) * 128), slice(t0 + n * 512, t0 + (n + 1) * 512))
                    P.dma(x_.t[:], xT[dsl[0], dsl[1]], writes=[x_.buf])
                    tt = t32.next()
                    P.op("dve", I("tensor_tensor", tt.t[:], a.t[:], x_.t[:], ALU.add), reads=[a.buf, x_.buf], writes=[tt.buf])
                    if not final:
                        P.dma(outT[dsl[0], dsl[1]], tt.t[:], reads=[tt.buf], owner=tt.buf, queue="pool")
                        s2 = sq2.next()
                        P.op("act", I("activation", s2.t[:], tt.t[:], AF.Square), reads=[tt.buf], writes=[s2.buf])
                        P.op("pe", I("matmul", ssq[n].t[:], ones[:], s2.t[:], start=(m == 0), stop=(m == 31)),
                             reads=[s2.buf, bconst], writes=[ssq[n].buf])
                        o = ostw.next()
                        P.op("pool", I("tensor_scalar", o.t[:], tt.t[:], g1[:, m:m + 1], None, ALU.mult),
                             reads=[tt.buf, bconst], writes=[o.buf])
                        P.dma(xg1T[dsl[0], dsl[1]], o.t[:], reads=[o.buf], owner=o.buf, queue="pool")
                    else:
                        P.dma(x2T[dsl[0], dsl[1]], tt.t[:], reads=[tt.buf], writes=[bscr], owner=bscr, queue="pool")
                        s2 = sq2.next()
                        P.op("act", I("activation", s2.t[:], tt.t[:], AF.Square), reads=[tt.buf], writes=[s2.buf])
                        P.op("pe", I("matmul", ssq[n].t[:], ones[:], s2.t[:], start=(m == 0), stop=(m == 31)),
                             reads=[s2.buf, bconst], writes=[ssq[n].buf])
        jobs = [(lambda wt=wt, p=p: ws.load(w_o, 0, 32, [(wt * 256, 256, 0)], 256, key=("wo", wt), first=(p == 0)), lambda L, wt=wt: comp(L, wt)) for wt in range(16)]
        pipeline(jobs)
        if not final:
            for n in range(2):
                tt = t32.next()
                P.op("act", I("activation", tt.t[:], ssq[n].t[:], AF.Sqrt, bias=epsc[:, 0:1], scale=1.0 / D),
                     reads=[ssq[n].buf, bconst], writes=[tt.buf])
                P.op("dve", I("reciprocal", tt.t[:], tt.t[:]), reads=[tt.buf], writes=[tt.buf])
                P.dma(rstdT[:, t0 + n * 512:t0 + (n + 1) * 512], tt.t[:], reads=[tt.buf], owner=tt.buf, queue="pool")
        if final:
            for n in range(2):
                tt = t32.next()
                P.op("act", I("activation", tt.t[:], ssq[n].t[:], AF.Sqrt, bias=epsc[:, 0:1], scale=1.0 / D),
                     reads=[ssq[n].buf, bconst], writes=[tt.buf])
                P.op("dve", I("reciprocal", rstd[:, n * 512:(n + 1) * 512], tt.t[:]), reads=[tt.buf], writes=[brstd])
            for m in range(32):
                for n in range(2):
                    cs = slice(n * 512, (n + 1) * 512)
                    dsl = (slice(m * 128, (m + 1) * 128), slice(t0 + n * 512, t0 + (n + 1) * 512))
                    x_ = xs.next()
                    P.dma(x_.t[:], x2T[dsl[0], dsl[1]], reads=[bscr], writes=[x_.buf])
                    tt = t32.next()
                    P.op("dve", I("scalar_tensor_tensor", tt.t[:], x_.t[:], fg[:, m:m + 1], rstd[:, cs], ALU.mult, ALU.mult),
                         reads=[x_.buf, brstd, bconst], writes=[tt.buf])
                    P.dma(outT[dsl[0], dsl[1]], tt.t[:], reads=[tt.buf], owner=tt.buf, queue="pool")
    P.end_stage()


def stage_d(nc, t):
    P = Prog(nc)
    NT = NTP
    gXg, gRs, w1 = t["gXg"], t["gRs"], t["w1"]
    q1T, k1T, g1T, v2, sz1T = t["q1T"], t["k1T"], t["g1T"], t["v2"], t["sz1T"]
    ws = WStream(P, nstage=2, nw=2)
    bconst = Buf("const")
    ones = P.sbuf("ones", [128, 128], BF16)
    epsc = P.sbuf("epsc", [128, 1], F32)
    g1 = P.sbuf("g1s", [128, 32], F32)
    lbs = P.sbuf("lbss", [128, 2, HPC], F32)
    lb = P.sbuf("lb", [128, HPC], F32)
    oml = P.sbuf("oml", [128, HPC], F32)
    dlb = P.sbuf("dlb", [128, HPC], F32)
    ident = P.sbuf("identd", [128, 128], BF16)
    P.op("dve", I("memset", ones[:], 1.0), writes=[bconst])
    P.op("dve", I("memset", epsc[:], EPS), writes=[bconst])
    P.dma(g1[:], t["g1"], writes=[bconst])
    P.dma(lbs[:], t["lbs"], writes=[bconst])
    P.dma(ident[:], t["ident"], writes=[bconst])
    P.op("dve", I("tensor_tensor", dlb[:], lbs[:, 1, :], lbs[:, 0, :], ALU.subtract), reads=[bconst], writes=[bconst])
    P.op("act", I("activation", dlb[:], dlb[:], AF.Exp, scale=-1.0), reads=[bconst], writes=[bconst])
    P.op("dve", I("tensor_scalar", lb[:], dlb[:], 1.0, None, ALU.add), reads=[bconst], writes=[bconst])
    P.op("dve", I("reciprocal", lb[:], lb[:]), reads=[bconst], writes=[bconst])
    P.op("dve", I("tensor_tensor", oml[:], dlb[:], lb[:], ALU.mult), reads=[bconst], writes=[bconst])

    xg = P.sbuf("xg", [128, 32, NT], BF16)
    bxg = Buf("xg")
    sqb = P.ring("sqb", 2, [128, NT], BF16)
    t32 = P.ring("t32_", 3, [128, 512], F32)
    e32 = P.ring("e32_", 2, [128, 512], F32)
    s32 = P.ring("s32_", 2, [128, 512], F32)
    f32r = P.ring("f32_", 2, [128, 512], F32)
    gout = P.ring("gout", 2, [128, 512], F32)
    ost = P.ring("ost", 4, [128, 512], BF16)
    vtok = P.ring("vtok", 2, [128, 512], BF16)
    rstd_x = P.sbuf("rstd_x", [128, NT], F32)
    brx = Buf("rstd_x")
    acc = P.pring("acc", 4)
    ssq = [Slot(P.psum(f"ssq{i}", [128, 512]), f"ssq{i}") for i in range(2)]
    trp = Slot(P.psum("trpd", [128, 1024], BF16), "trpd")

    def store(dst_ap, src_slot, src_ap):
        P.dma(dst_ap, src_ap, reads=[src_slot.buf], owner=src_slot.buf, queue="act")

    for p in range(S // NT):
        t0 = p * NT
        rk, c0 = p // 2, (p % 2) * NT
        for kc in range(32):
            P.dma(xg[:, kc, :], gXg[kc // 2, rk * 256 + (kc % 2) * 128:rk * 256 + (kc % 2) * 128 + 128, c0:c0 + NT], writes=[bxg])
        P.dma(rstd_x[:, :], gRs[0, rk * 128:(rk + 1) * 128, c0:c0 + NT], writes=[brx])

        def comp(L, wt):
            wbuf, wv = L
            kind = wt // 4
            for mi in range(2):
                mm = (wt % 4) * 2 + mi
                for n in range(2):
                    cs = slice(n * 512, (n + 1) * 512)
                    dsl = (slice(mm * 128, (mm + 1) * 128), slice(t0 + n * 512, t0 + (n + 1) * 512))
                    a = acc.next()
                    P.op("pe", [I("matmul", a.t[:], wv[:, kc, mi * 128:(mi + 1) * 128], xg[:, kc, cs], start=(kc == 0), stop=(kc == 31))
                                for kc in range(32)], reads=[wbuf, bxg], writes=[a.buf])
                    tt = t32.next()
                    P.op("dve", I("tensor_tensor", tt.t[:], a.t[:], rstd_x[:, cs], ALU.mult), reads=[a.buf, brx], writes=[tt.buf])
                    if kind == 0 or kind == 3:
                        o = ost.next()
                        P.op("act", I("activation", o.t[:], tt.t[:], AF.Silu), reads=[tt.buf], writes=[o.buf])
                        store((q1T if kind == 0 else sz1T)[dsl[0], dsl[1]], o, o.t[:])
                    elif kind == 2:
                        o = ost.next()
                        P.op("act", I("activation", o.t[:], tt.t[:], AF.Copy), reads=[tt.buf], writes=[o.buf])
                        P.op("pe", [I("transpose", trp.t[:, j * 128:(j + 1) * 128], o.t[:, j * 128:(j + 1) * 128], ident[:]) for j in range(4)],
                             reads=[o.buf, bconst], writes=[trp.buf])
                        vt_ = vtok.next()
                        P.op("dve", I("tensor_copy", vt_.t[:], trp.t[:, 0:512]), reads=[trp.buf], writes=[vt_.buf])
                        blk0 = (t0 + n * 512) // 128
                        store(v2[mm, :, blk0:blk0 + 4, :], vt_, vt_.t[:].rearrange("p (b d) -> p b d", d=128))
                    else:
                        e = e32.next()
                        P.op("act", I("activation", e.t[:], tt.t[:], AF.Exp, scale=-1.0), reads=[tt.buf], writes=[e.buf])
                        s_ = s32.next()
                        P.op("pool", I("tensor_scalar", s_.t[:], e.t[:], 1.0, None, ALU.add), reads=[e.buf], writes=[s_.buf])
                        P.op("dve", I("reciprocal", s_.t[:], s_.t[:]), reads=[s_.buf], writes=[s_.buf])
                        o = ost.next()
                        P.op("dve", I("scalar_tensor_tensor", o.t[:], e.t[:], oml[:, mm:mm + 1], s_.t[:], ALU.mult, ALU.mult),
                             reads=[e.buf, s_.buf, bconst], writes=[o.buf])
                        store(k1T[dsl[0], dsl[1]], o, o.t[:])
                        f_ = f32r.next()
                        P.op("pool", I("tensor_scalar", f_.t[:], s_.t[:], oml[:, mm:mm + 1], lb[:, mm:mm + 1], ALU.mult, ALU.add),
                             reads=[s_.buf, bconst], writes=[f_.buf])
                        g_ = gout.next()
                        P.op("act", I("activation", g_.t[:], f_.t[:], AF.Ln), reads=[f_.buf], writes=[g_.buf])
                        store(g1T[dsl[0], dsl[1]], g_, g_.t[:])
        jobs = [(lambda wt=wt, p=p: ws.load(w1, 0, 32, [(wt * 256, 256, 0)], 256, key=("w1", wt), first=(p == 0)), lambda L, wt=wt: comp(L, wt)) for wt in range(16)]
        pipeline(jobs)
    P.end_stage()


def build_fused():
    nc = new_nc()

    def ext(name, shape, dt):
        return nc.dram_tensor(name, list(shape), dt, kind="ExternalInput").ap()

    def internal(name, shape, dt):
        return nc.dram_tensor(name, list(shape), dt).ap()

    t = {}
    t["xTall"] = ext("xTall", [D, S], F32)
    t["pos"] = ext("pos", [S], I32)
    t["w_lat"] = ext("w_lat", [D, 1664], F32)
    t["w_z"] = ext("w_z", [D, 1024], F32)
    t["w_uq"] = ext("w_uq", [1024, HPC * 256], F32)
    t["w_ukv"] = ext("w_ukv", [512, HPC * 256], F32)
    t["g0"] = ext("g0", [128, 32], F32)
    t["qg"] = ext("qg", [128, 8], F32)
    t["kvg"] = ext("kvg", [128, 4], F32)
    t["ropec"] = ext("ropec", [64, 2], F32)
    t["tri"] = ext("tri", [128, 128], BF16)
    t["xTown"] = ext("xTown", [D, TOK], F32)
    t["w_o0"] = ext("w_o0", [D, D], F32)
    t["w1"] = ext("w1", [D, 4 * HPC * 128], F32)
    t["g1"] = ext("g1", [128, 32], F32)
    t["lbs"] = ext("lbs", [128, 2, HPC], F32)
    t["gn"] = ext("gn", [128, 1], F32)
    t["mask2"] = ext("mask2", [128, 128], BF16)
    t["ident"] = ext("ident", [128, 128], BF16)
    t["scanmask"] = ext("scanmask", [128, 512], F32)
    t["w_o1"] = ext("w_o1", [D, D], F32)
    t["fg"] = ext("fg", [128, 32], F32)
    outT = nc.dram_tensor("outT", [D, TOK], F32, kind="ExternalOutput").ap()

    t["QnT"] = internal("QnT_i", [HPC, 128, S], BF16)
    t["QrT"] = internal("QrT_i", [HPC, 64, S], BF16)
    t["KnT"] = internal("KnT_i", [HPC, 128, S], BF16)
    t["KrT"] = internal("KrT_i", [64, S], BF16)
    t["Vh"] = internal("Vh_i", [HPC, 128, S // 128, 128], BF16)
    t["SZT"] = internal("SZT_i", [HPC * 128, S], BF16)
    t["OGT"] = internal("OGT_i", [HPC * 128, S], BF16)
    gOG = internal("gOG_i", [16, 256, S], BF16)
    x1T = internal("x1T_i", [D, TOK], F32)
    xg1T = internal("xg1T_i", [D, TOK], BF16)
    rstdT = internal("rstdT_i", [128, TOK], F32)
    gXg = internal("gXg_i", [16, 1024, TOK], BF16)
    gRs = internal("gRs_i", [1, 512, TOK], F32)
    q1T = internal("q1T_i", [HPC * 128, S], BF16)
    k1T = internal("k1T_i", [HPC * 128, S], BF16)
    g1T = internal("g1T_i", [HPC * 128, S], F32)
    v2 = internal("v2_i", [HPC, 128, S // 128, 128], BF16)
    sz1T = internal("sz1T_i", [HPC * 128, S], BF16)
    OG1T = internal("OG1T_i", [HPC * 128, S], BF16)
    gOG1 = internal("gOG1_i", [16, 256, S], BF16)
    x2T = internal("x2T_i", [D, TOK], F32)

    stage_a(nc, t)
    stage_b(nc, t)
    stage_gather(nc, t["OGT"], gOG, 64)
    ogown0 = internal("ogown0_i", [D, TOK], BF16)
    ogown1 = internal("ogown1_i", [D, TOK], BF16)
    stage_wo(nc, {"gOG": gOG, "xT": t["xTown"], "w_o": t["w_o0"], "outT": x1T, "ogown": ogown0, "g1": t["g1"],
                  "xg1T": xg1T, "rstdT": rstdT}, final=False)
    stage_gather(nc, xg1T, gXg, 256)
    stage_gather(nc, rstdT, gRs, 128)
    stage_d(nc, {"gXg": gXg, "gRs": gRs, "w1": t["w1"], "g1": t["g1"], "lbs": t["lbs"], "ident": t["ident"],
                 "q1T": q1T, "k1T": k1T, "g1T": g1T, "v2": v2, "sz1T": sz1T})
    h3 = lambda ap: ap.rearrange("(h k) s -> h k s", k=128)
    stage_e(nc, {"q1T": h3(q1T), "k1T": h3(k1T), "g1T": h3(g1T), "v2": v2, "sz1T": sz1T, "gn": t["gn"], "mask2": t["mask2"],
                 "ident": t["ident"], "scanmask": t["scanmask"], "OG1T": OG1T})
    stage_gather(nc, OG1T, gOG1, 64)
    stage_wo(nc, {"gOG": gOG1, "xT": x1T, "w_o": t["w_o1"], "outT": outT, "fg": t["fg"], "x2T": x2T, "ogown": ogown1}, final=True)
    return nc


def kernel(x, positions, l0_norm, l0_w_in, l0_q_norm, l0_w_uq, l0_kv_norm, l0_w_ukv, l0_w_o,
           l1_norm, l1_w_in, l1_g_norm, l1_w_o, lower_bounds, final_norm):
    f32 = lambda a: np.ascontiguousarray(np.asarray(a, dtype=np.float32))
    x = np.asarray(x, dtype=np.float32)
    positions = np.asarray(positions, dtype=np.int32)
    l0_w_in, l0_w_uq, l0_w_ukv, l1_w_in = (np.asarray(a, dtype=np.float32) for a in (l0_w_in, l0_w_uq, l0_w_ukv, l1_w_in))
    lower_bounds = np.asarray(lower_bounds, dtype=np.float32)
    w_lat = np.ascontiguousarray(np.concatenate(
        [l0_w_in[:, :1600], l0_w_in[:, 1568:1600], l0_w_in[:, 1536:1568]], axis=1))
    m2, ident, sm = l4_consts()
    shared = {
        "w_lat": w_lat, "g0": chunked_gain(f32(l0_norm)), "qg": chunked_gain(f32(l0_q_norm)), "kvg": chunked_gain(f32(l0_kv_norm)),
        "ropec": rope_consts(), "tri": tri_const(), "w_o0": f32(l0_w_o), "g1": chunked_gain(f32(l1_norm)),
        "gn": f32(l1_g_norm).reshape(128, 1), "mask2": m2, "ident": ident, "scanmask": sm, "w_o1": f32(l1_w_o),
        "fg": chunked_gain(f32(final_norm)),
    }
    per_g = []
    for g in range(4):
        hs = range(g * HPC, (g + 1) * HPC)
        uq = np.concatenate([np.concatenate([l0_w_uq[:, h * 192:h * 192 + 192], l0_w_uq[:, h * 192 + 160:h * 192 + 192],
                                             l0_w_uq[:, h * 192 + 128:h * 192 + 160]], axis=1) for h in hs], axis=1)
        c0, c1 = g * HPC * 128, (g + 1) * HPC * 128
        per_g.append({
            "w_z": np.ascontiguousarray(l0_w_in[:, 1600 + c0:1600 + c1]),
            "w_uq": np.ascontiguousarray(uq),
            "w_ukv": np.ascontiguousarray(l0_w_ukv[:, g * HPC * 256:(g + 1) * HPC * 256]),
            "w1": np.ascontiguousarray(np.concatenate([l1_w_in[:, s_ * D + c0:s_ * D + c1] for s_ in range(4)], axis=1)),
            "lbs": np.ascontiguousarray(lower_bounds[:, c0:c1].reshape(2, HPC, 128).transpose(2, 0, 1)),
        })
    maps = []
    for b in range(B):
        xTall = np.ascontiguousarray(x[b].T)
        posb = np.ascontiguousarray(positions[b])
        for r in range(4):
            m = dict(shared)
            m.update(per_g[r])
            m["xTall"] = xTall
            m["pos"] = posb
            m["xTown"] = np.ascontiguousarray(xTall[:, r * TOK:(r + 1) * TOK])
            maps.append(m)
    nc = build_fused()
    res = run_bass_kernel_spmd(nc, maps, core_ids=list(range(NCORES)))
    out = np.empty((B, S, D), np.float32)
    for c in range(NCORES):
        b, r = divmod(c, 4)
        out[b, r * TOK:(r + 1) * TOK] = np.asarray(res.results[c]["outT"]).T
    return out
```

```python
import math
import numpy as np
from contextlib import ExitStack
import concourse.bass as bass
import concourse.mybir as mybir
from concourse.bass_utils import run_bass_kernel_spmd

F32 = mybir.dt.float32
BF16 = mybir.dt.bfloat16
I32 = mybir.dt.int32
AF = mybir.ActivationFunctionType
ALU = mybir.AluOpType

NCORES = 8
D = 4096
S = 8192
B = 2
TOK = 2048
NH = 32
EPS = 1e-6
EPOCH = 12000
TWO_PI = 2.0 * math.pi
CW1 = 6.28125
CW2 = TWO_PI - 6.28125
PI_LO = 3.1415925


def I(method, *args, **kw):
    return (method, args, kw)


class Lazy:
    def __init__(self, fn):
        self.fn = fn


class Buf:
    __slots__ = ("name", "w", "r", "sem", "cnt")

    def __init__(self, name=""):
        self.name = name
        self.w = None
        self.r = []
        self.sem = None
        self.cnt = 0


class Ring:
    def __init__(self, items):
        self.items = list(items)
        self.i = 0

    def next(self):
        it = self.items[self.i % len(self.items)]
        self.i += 1
        return it


class Slot:
    def __init__(self, t, name):
        self.t = t
        self.buf = Buf(name)


class Prog:
    ENGS = ("pe", "act", "dve", "pool", "sp")
    SEMUID = 0

    def __init__(self, nc, same_engine_sync=True):
        self.nc = nc
        self.es = ExitStack()
        self.sems = []
        self.dma_bufs = []
        self.lists = {e: [] for e in self.ENGS}
        self.cur_sem = {}
        self.cur_cnt = {}
        self.seen = {e: {} for e in self.ENGS}
        self.nsem = 0
        self.same_engine_sync = same_engine_sync
        self.ninst = 0
        for e in self.ENGS:
            self._new_epoch(e)

    def sem(self, name):
        self.nsem += 1
        Prog.SEMUID += 1
        h = self.nc.alloc_semaphore(name=f"{name}_{Prog.SEMUID}")
        self.sems.append(h)
        return h

    def _new_epoch(self, e):
        self.cur_sem[e] = self.sem(f"s_{e}")
        self.cur_cnt[e] = 0

    def sbuf(self, name, shape, dtype):
        Prog.SEMUID += 1
        return self.es.enter_context(self.nc.sbuf_tensor(f"{name}_u{Prog.SEMUID}", list(shape), dtype))

    def psum(self, name, shape, dtype=F32):
        Prog.SEMUID += 1
        return self.es.enter_context(self.nc.psum_tensor(f"{name}_u{Prog.SEMUID}", list(shape), dtype))

    def slot(self, name, shape, dtype):
        return Slot(self.sbuf(name, shape, dtype), name)

    def ring(self, name, n, shape, dtype):
        return Ring([self.slot(f"{name}{i}", shape, dtype) for i in range(n)])

    def pring(self, name, n):
        return Ring([Slot(self.psum(f"{name}{i}", [128, 512]), f"{name}{i}") for i in range(n)])

    def _deps(self, eng, reads, writes, own_sem=None):
        toks = []
        for b in reads:
            if b.w is not None:
                toks.append(b.w)
        for b in writes:
            if b.w is not None:
                toks.append(b.w)
            toks.extend(b.r)
        need = {}
        for (s, v, src) in toks:
            if src == eng and (eng in ("pe", "sp") or not self.same_engine_sync):
                continue
            if own_sem is not None and s is own_sem:
                continue
            k = id(s)
            if k not in need or need[k][1] < v:
                need[k] = (s, v)
        waits = []
        seen = self.seen[eng]
        for k, (s, v) in need.items():
            if seen.get(k, 0) >= v:
                continue
            seen[k] = v
            waits.append((s, v))
        return waits

    def _mark(self, tok, reads, writes):
        for b in reads:
            b.r.append(tok)
        for b in writes:
            b.w = tok
            b.r = []

    def op(self, eng, calls, reads=(), writes=()):
        if isinstance(calls, tuple):
            calls = [calls]
        waits = self._deps(eng, reads, writes)
        if self.cur_cnt[eng] >= EPOCH:
            self._new_epoch(eng)
        s = self.cur_sem[eng]
        self.cur_cnt[eng] += 1
        tok = (s, self.cur_cnt[eng], eng)
        self.lists[eng].append((waits, calls, s, 1))
        self._mark(tok, reads, writes)
        self.ninst += len(calls)
        return tok

    def dma(self, out, in_, reads=(), writes=(), queue="sp", owner=None):
        own = owner if owner is not None else (writes[0] if writes else reads[0])
        if own.sem is None:
            own.sem = self.sem("d_" + own.name)
            self.dma_bufs.append(own)
        waits = self._deps(queue, reads, writes, own_sem=own.sem)
        own.cnt += 16
        tok = (own.sem, own.cnt, "dma")
        self.lists[queue].append((waits, [I("dma_start", out=out, in_=in_)], own.sem, 16))
        self._mark(tok, reads, writes)
        self.ninst += 1
        return tok

    def finish(self, tokens):
        need = {}
        for (s, v, _) in tokens:
            k = id(s)
            if k not in need or need[k][1] < v:
                need[k] = (s, v)
        self.lists["sp"].append((list(need.values()), [], None, 0))

    def end_stage(self, last=False):
        need = []
        for e in self.ENGS:
            if e != "sp" and self.cur_cnt[e] > 0:
                need.append((self.cur_sem[e], self.cur_cnt[e]))
        for b in self.dma_bufs:
            need.append((b.sem, b.cnt))
        self.lists["sp"].append((need, [], None, 0))
        self.emit(close=False)
        nc = self.nc
        nc.all_engine_barrier()
        self.es.close()
        nc.clear_and_free_semaphores(self.sems)
        nc.all_engine_barrier()

    def emit(self, close=True):
        nc = self.nc
        lists = self.lists

        def replay(e, lst):
            for (waits, calls, s, inc) in lst:
                for (ws, wv) in waits:
                    e.wait_ge(ws, wv)
                inst = None
                for (m, a, k) in calls:
                    if any(isinstance(v, Lazy) for v in k.values()):
                        k = {kk: (v.fn(e) if isinstance(v, Lazy) else v) for kk, v in k.items()}
                    inst = getattr(e, m)(*a, **k)
                if inst is not None:
                    inst.then_inc(s, inc)

        with nc.Block() as block:
            @block.tensor
            def _(e):
                replay(e, lists["pe"])

            @block.scalar
            def _(e):
                replay(e, lists["act"])

            @block.vector
            def _(e):
                replay(e, lists["dve"])

            @block.gpsimd
            def _(e):
                replay(e, lists["pool"])

            @block.sync
            def _(e):
                replay(e, lists["sp"])
        if close:
            self.es.close()


class WStream:
    def __init__(self, P, nstage=2, nw=2, stage_elems=2048, w_elems=8192, cast_engs=("pool",)):
        self.P = P
        self.stage_elems = stage_elems
        self.stages = P.ring("wst", nstage, [128, stage_elems], F32)
        self.ws = P.ring("wbf", nw, [128, w_elems], BF16)
        self.cast = Ring(list(cast_engs))
        self.cache = {}

    def load(self, W, k0, KC, segs, C, key=None, first=True):
        P = self.P
        slot = self.ws.next()
        wv = slot.t[:, 0:KC * C].rearrange("p (k c) -> p k c", c=C)
        if key is not None and not first:
            cap, cbuf = self.cache[key]
            P.dma(slot.t[:, 0:KC * C], cap, reads=[cbuf], writes=[slot.buf])
            return slot.buf, wv
        kpp = max(1, self.stage_elems // C)
        for kc0 in range(0, KC, kpp):
            kn = min(kpp, KC - kc0)
            st = self.stages.next()
            stv = st.t[:, 0:kn * C].rearrange("p (k c) -> p k c", c=C)
            for (sc0, n, dc0) in segs:
                src = W[k0 + kc0 * 128:k0 + (kc0 + kn) * 128, sc0:sc0 + n].rearrange("(k p) c -> p k c", p=128)
                P.dma(stv[:, :, dc0:dc0 + n], src, writes=[st.buf])
            dst = wv[:, kc0:kc0 + kn, :]
            eng = self.cast.next()
            if eng == "act":
                P.op("act", I("activation", dst, stv, AF.Copy), reads=[st.buf], writes=[slot.buf])
            else:
                P.op(eng, I("tensor_copy", dst, stv), reads=[st.buf], writes=[slot.buf])
        if key is not None:
            Prog.SEMUID += 1
            cap = P.nc.dram_tensor(f"wcache_{Prog.SEMUID}", [128, KC * C], BF16).ap()
            cbuf = Buf(f"wc{Prog.SEMUID}")
            self.cache[key] = (cap, cbuf)
            P.dma(cap, slot.t[:, 0:KC * C], reads=[slot.buf], writes=[cbuf], owner=cbuf, queue="pool")
        return slot.buf, wv


def pipeline(jobs):
    if not jobs:
        return
    cur = jobs[0][0]()
    for i in range(len(jobs)):
        nxt = jobs[i + 1][0]() if i + 1 < len(jobs) else None
        jobs[i][1](cur)
        cur = nxt


def new_nc():
    return bass.Bass("TRN2", target_bir_lowering=False)


def build_l1():
    nc = new_nc()
    xT = nc.dram_tensor("xT", [D, TOK], F32, kind="ExternalInput").ap()
    pos = nc.dram_tensor("pos", [TOK], I32, kind="ExternalInput").ap()
    w_in = nc.dram_tensor("w_in", [D, 5696], F32, kind="ExternalInput").ap()
    w_uq = nc.dram_tensor("w_uq", [1024, 6144], F32, kind="ExternalInput").ap()
    w_ukv = nc.dram_tensor("w_ukv", [512, 8192], F32, kind="ExternalInput").ap()
    g0d = nc.dram_tensor("g0", [128, 32], F32, kind="ExternalInput").ap()
    qgd = nc.dram_tensor("qg", [128, 8], F32, kind="ExternalInput").ap()
    kvgd = nc.dram_tensor("kvg", [128, 4], F32, kind="ExternalInput").ap()
    rcd = nc.dram_tensor("ropec", [64, 2], F32, kind="ExternalInput").ap()
    QnT = nc.dram_tensor("QnT", [NH, 128, TOK], BF16, kind="ExternalOutput").ap()
    QrT = nc.dram_tensor("QrT", [NH, 64, TOK], BF16, kind="ExternalOutput").ap()
    KnT = nc.dram_tensor("KnT", [NH, 128, TOK], BF16, kind="ExternalOutput").ap()
    KrT = nc.dram_tensor("KrT", [64, TOK], BF16, kind="ExternalOutput").ap()
    Vo = nc.dram_tensor("V", [TOK, NH * 128], BF16, kind="ExternalOutput").ap()
    SZT = nc.dram_tensor("SZT", [D, TOK], BF16, kind="ExternalOutput").ap()

    P = Prog(nc)
    NT = 1024
    ws = WStream(P, nstage=2, nw=2)
    ones = P.sbuf("ones", [128, 128], BF16)
    g0 = P.sbuf("g0s", [128, 32], F32)
    qg = P.sbuf("qgs", [128, 8], F32)
    kvg = P.sbuf("kvgs", [128, 4], F32)
    rc = P.sbuf("rcs", [64, 2], F32)
    epsc = P.sbuf("epsc", [128, 1], F32)
    bconst = Buf("const")
    P.op("dve", I("memset", ones[:], 1.0), writes=[bconst])
    P.op("dve", I("memset", epsc[:], EPS), writes=[bconst])
    P.dma(g0[:], g0d, writes=[bconst])
    P.dma(qg[:], qgd, writes=[bconst])
    P.dma(kvg[:], kvgd, writes=[bconst])
    P.dma(rc[:], rcd, writes=[bconst])

    xg = P.sbuf("xg", [128, 32, NT], BF16)
    bxg = Buf("xg")
    sqb = P.ring("sqb", 2, [128, NT], BF16)
    t32 = P.ring("t32_", 4, [128, 512], F32)
    sq2 = P.ring("sq2_", 2, [128, 512], BF16)
    ost = P.ring("ost", 4, [128, 512], BF16)
    rstd_x = P.sbuf("rstd_x", [128, NT], F32)
    rstd_q = P.sbuf("rstd_q", [128, NT], F32)
    rstd_kv = P.sbuf("rstd_kv", [128, NT], F32)
    brx, brq, brkv = Buf("rstd_x"), Buf("rstd_q"), Buf("rstd_kv")
    cqg = P.sbuf("cqg", [128, 8, NT], BF16)
    bcqg = Buf("cqg")
    ckvn = P.sbuf("ckvn", [128, 4, NT], BF16)
    bckvn = Buf("ckvn")
    cosT = P.sbuf("cosT", [64, NT], F32)
    sinT = P.sbuf("sinT", [64, NT], F32)
    btab = Buf("ropetab")
    ra = P.sbuf("ra", [64, NT], F32)
    rb = P.sbuf("rb", [64, NT], F32)
    rcc = P.sbuf("rcc", [64, NT], F32)
    ri = P.sbuf("ri", [64, NT], I32)
    bra, brb, brcc, bri = Buf("ra"), Buf("rb"), Buf("rcc"), Buf("ri")

    acc = P.pring("acc", 4)
    ssq = [Slot(P.psum(f"ssq{i}", [128, 512]), f"ssq{i}") for i in range(2)]
    ssq2 = [Slot(P.psum(f"ssqb{i}", [128, 512]), f"ssqb{i}") for i in range(2)]
    out_toks = []

    def rstd_from(ssq_slot, dst, bdst, n, inv_n):
        t = t32.next()
        P.op("act", I("activation", t.t[:], ssq_slot.t[:], AF.Sqrt, bias=epsc[:, 0:1], scale=inv_n),
             reads=[ssq_slot.buf, bconst], writes=[t.buf])
        P.op("dve", I("reciprocal", dst[:, n * 512:(n + 1) * 512], t.t[:]), reads=[t.buf], writes=[bdst])

    def store(dst_ap, src_slot, src_ap):
        out_toks.append(P.dma(dst_ap, src_ap, reads=[src_slot.buf], owner=src_slot.buf))

    for p in range(TOK // NT):
        t0 = p * NT
        P.dma(ri[:], pos[t0:t0 + NT].partition_broadcast(64), writes=[bri])
        P.op("dve", I("tensor_copy", ra[:], ri[:]), reads=[bri], writes=[bra])
        P.op("dve", I("tensor_scalar", ra[:], ra[:], rc[:, 0:1], None, ALU.mult), reads=[bra, bconst], writes=[bra])
        P.op("dve", I("tensor_scalar", rb[:], ra[:], 1.0 / TWO_PI, None, ALU.mult), reads=[bra], writes=[brb])
        P.op("dve", I("tensor_copy", ri[:], rb[:]), reads=[brb], writes=[bri])
        P.op("dve", I("tensor_copy", rb[:], ri[:]), reads=[bri], writes=[brb])
        P.op("dve", I("scalar_tensor_tensor", ra[:], rb[:], -CW1, ra[:], ALU.mult, ALU.add), reads=[bra, brb], writes=[bra])
        P.op("dve", I("scalar_tensor_tensor", ra[:], rb[:], -CW2, ra[:], ALU.mult, ALU.add), reads=[bra, brb], writes=[bra])
        P.op("dve", I("tensor_single_scalar", rb[:], ra[:], 0.0, ALU.is_lt), reads=[bra], writes=[brb])
        P.op("dve", I("scalar_tensor_tensor", ra[:], rb[:], TWO_PI, ra[:], ALU.mult, ALU.add), reads=[bra, brb], writes=[bra])
        P.op("dve", I("tensor_scalar", rb[:], ra[:], -1.0, math.pi, ALU.mult, ALU.add), reads=[bra], writes=[brb])
        P.op("dve", I("tensor_scalar", rb[:], rb[:], PI_LO, -PI_LO, ALU.min, ALU.max), reads=[brb], writes=[brb])
        P.op("act", I("activation", sinT[:], rb[:], AF.Sin), reads=[brb], writes=[btab])
        P.op("dve", I("tensor_scalar", rcc[:], ra[:], math.pi / 2, None, ALU.add), reads=[bra], writes=[brcc])
        P.op("dve", I("tensor_single_scalar", rb[:], rcc[:], TWO_PI, ALU.is_ge), reads=[brcc, btab], writes=[brb])
        P.op("dve", I("scalar_tensor_tensor", rcc[:], rb[:], -TWO_PI, rcc[:], ALU.mult, ALU.add), reads=[brcc, brb], writes=[brcc])
        P.op("dve", I("tensor_scalar", rcc[:], rcc[:], -1.0, math.pi, ALU.mult, ALU.add), reads=[brcc], writes=[brcc])
        P.op("dve", I("tensor_scalar", rcc[:], rcc[:], PI_LO, -PI_LO, ALU.min, ALU.max), reads=[brcc], writes=[brcc])
        P.op("act", I("activation", cosT[:], rcc[:], AF.Sin), reads=[brcc], writes=[btab])
        P.op("dve", I("tensor_scalar", sinT[:], sinT[:], rc[:, 1:2], None, ALU.mult), reads=[btab, bconst], writes=[btab])

        for kc in range(32):
            st = ws.stages.next()
            stv = st.t[:, 0:NT]
            P.dma(stv, xT[kc * 128:(kc + 1) * 128, t0:t0 + NT], writes=[st.buf])
            sq = sqb.next()
            P.op("act", I("activation", sq.t[:], stv, AF.Square), reads=[st.buf], writes=[sq.buf])
            P.op("pe", [I("matmul", ssq[n].t[:], ones[:], sq.t[:, n * 512:(n + 1) * 512], start=(kc == 0), stop=(kc == 31))
                        for n in range(2)], reads=[sq.buf, bconst], writes=[ssq[0].buf, ssq[1].buf])
            P.op("dve", I("tensor_scalar", xg[:, kc, :], stv, g0[:, kc:kc + 1], None, ALU.mult),
                 reads=[st.buf, bconst], writes=[bxg])
        for n in range(2):
            rstd_from(ssq[n], rstd_x, brx, n, 1.0 / D)

        def gemm_x(wbuf, wv, mcols, epilogue):
            for (c0, M, tag) in mcols:
                for n in range(2):
                    a = acc.next()
                    P.op("pe", [I("matmul", a.t[0:M, :], wv[:, kc, c0:c0 + M], xg[:, kc, n * 512:(n + 1) * 512],
                                  start=(kc == 0), stop=(kc == 31)) for kc in range(32)],
                         reads=[wbuf, bxg], writes=[a.buf])
                    epilogue(a, M, tag, n)

        def ep_lat(kind):
            nm = 8 if kind == "q" else 4

            def ep(a, M, m, n):
                cs = slice(n * 512, (n + 1) * 512)
                t = t32.next()
                P.op("dve", I("tensor_tensor", t.t[:], a.t[:], rstd_x[:, cs], ALU.mult), reads=[a.buf, brx], writes=[t.buf])
                s2 = sq2.next()
                P.op("act", I("activation", s2.t[:], t.t[:], AF.Square), reads=[t.buf], writes=[s2.buf])
                P.op("pe", I("matmul", ssq2[n].t[:], ones[:], s2.t[:], start=(m == 0), stop=(m == nm - 1)),
                     reads=[s2.buf, bconst], writes=[ssq2[n].buf])
                if kind == "q":
                    P.op("pool", I("tensor_scalar", cqg[:, m, cs], t.t[:], qg[:, m:m + 1], None, ALU.mult),
                         reads=[t.buf, bconst], writes=[bcqg])
                else:
                    P.op("pool", I("tensor_scalar", ckvn[:, m, cs], t.t[:], kvg[:, m:m + 1], None, ALU.mult),
                         reads=[t.buf, bconst], writes=[bckvn])
            return ep

        def rope_store(aA, aB, rs, brs, n, dst_ap):
            tA = t32.next()
            tB = t32.next()
            cs = slice(n * 512, (n + 1) * 512)
            P.op("dve", I("tensor_tensor", tA.t[0:64, :], aA.t[0:64, :], rs[0:64, cs], ALU.mult), reads=[aA.buf, brs], writes=[tA.buf])
            P.op("dve", I("tensor_tensor", tB.t[0:64, :], aB.t[0:64, :], rs[0:64, cs], ALU.mult), reads=[aB.buf, brs], writes=[tB.buf])
            P.op("pool", I("tensor_tensor", tA.t[0:64, :], tA.t[0:64, :], cosT[:, cs], ALU.mult), reads=[tA.buf, btab], writes=[tA.buf])
            P.op("pool", I("tensor_tensor", tB.t[0:64, :], tB.t[0:64, :], sinT[:, cs], ALU.mult), reads=[tB.buf, btab], writes=[tB.buf])
            o = ost.next()
            P.op("dve", I("tensor_tensor", o.t[0:64, :], tA.t[0:64, :], tB.t[0:64, :], ALU.add), reads=[tA.buf, tB.buf], writes=[o.buf])
            store(dst_ap, o, o.t[0:64, :])

        def ep_z(a, M, m, n):
            cs = slice(n * 512, (n + 1) * 512)
            t = t32.next()
            P.op("dve", I("tensor_tensor", t.t[:], a.t[:], rstd_x[:, cs], ALU.mult), reads=[a.buf, brx], writes=[t.buf])
            o = ost.next()
            P.op("act", I("activation", o.t[:], t.t[:], AF.Silu), reads=[t.buf], writes=[o.buf])
            store(SZT[m * 128:(m + 1) * 128, t0 + n * 512:t0 + (n + 1) * 512], o, o.t[:])

        jobs = []
        for wt in range(4):
            jobs.append((lambda wt=wt: ws.load(w_in, 0, 32, [(wt * 256, 256, 0)], 256),
                         lambda L, wt=wt: gemm_x(L[0], L[1], [(0, 128, wt * 2), (128, 128, wt * 2 + 1)], ep_lat("q"))))

        def after_q(L):
            for n in range(2):
                rstd_from(ssq2[n], rstd_q, brq, n, 1.0 / 1024)
        for wt in range(2):
            def comp(L, wt=wt):
                if wt == 0:
                    after_q(L)
                gemm_x(L[0], L[1], [(0, 128, wt * 2), (128, 128, wt * 2 + 1)], ep_lat("kv"))
            jobs.append((lambda wt=wt: ws.load(w_in, 0, 32, [(1024 + wt * 256, 256, 0)], 256), comp))

        def comp_krot(L):
            wbuf, wv = L
            for n in range(2):
                rstd_from(ssq2[n], rstd_kv, brkv, n, 1.0 / 512)
            for m in range(4):
                for n in range(2):
                    cs = slice(n * 512, (n + 1) * 512)
                    P.op("dve", I("tensor_tensor", ckvn[:, m, cs], ckvn[:, m, cs], rstd_kv[:, cs], ALU.mult),
                         reads=[bckvn, brkv], writes=[bckvn])
            for n in range(2):
                aA, aB = acc.next(), acc.next()
                calls = []
                for (a, c0) in ((aA, 0), (aB, 64)):
                    for kc in range(32):
                        calls.append(I("matmul", a.t[0:64, :], wv[:, kc, c0:c0 + 64], xg[:, kc, n * 512:(n + 1) * 512],
                                       start=(kc == 0), stop=(kc == 31)))
                P.op("pe", calls, reads=[wbuf, bxg], writes=[aA.buf, aB.buf])
                rope_store(aA, aB, rstd_x, brx, n, KrT[:, t0 + n * 512:t0 + (n + 1) * 512])
        jobs.append((lambda: ws.load(w_in, 0, 32, [(1536, 64, 0), (1568, 32, 64), (1536, 32, 96)], 128), comp_krot))

        for wt in range(16):
            jobs.append((lambda wt=wt: ws.load(w_in, 0, 32, [(1600 + wt * 256, 256, 0)], 256),
                         lambda L, wt=wt: gemm_x(L[0], L[1], [(0, 128, wt * 2), (128, 128, wt * 2 + 1)], ep_z)))

        def comp_q(L, h):
            wbuf, wv = L
            for n in range(2):
                cs = slice(n * 512, (n + 1) * 512)
                a0, aA, aB = acc.next(), acc.next(), acc.next()
                calls = []
                for (a, c0, M) in ((a0, 0, 128), (aA, 128, 64), (aB, 192, 64)):
                    for kc in range(8):
                        calls.append(I("matmul", a.t[0:M, :], wv[:, kc, c0:c0 + M], cqg[:, kc, cs], start=(kc == 0), stop=(kc == 7)))
                P.op("pe", calls, reads=[wbuf, bcqg], writes=[a0.buf, aA.buf, aB.buf])
                o = ost.next()
                P.op("dve", I("tensor_tensor", o.t[:], a0.t[:], rstd_q[:, cs], ALU.mult), reads=[a0.buf, brq], writes=[o.buf])
                store(QnT[h, :, t0 + n * 512:t0 + (n + 1) * 512], o, o.t[:])
                rope_store(aA, aB, rstd_q, brq, n, QrT[h, :, t0 + n * 512:t0 + (n + 1) * 512])
        for h in range(NH):
            c = h * 192
            jobs.append((lambda c=c: ws.load(w_uq, 0, 8, [(c, 192, 0), (c + 160, 32, 192), (c + 128, 32, 224)], 256),
                         lambda L, h=h: comp_q(L, h)))

        def comp_kv(L, hg):
            wbuf, wv = L
            for hh in range(4):
                h = hg * 4 + hh
                for n in range(2):
                    cs = slice(n * 512, (n + 1) * 512)
                    a = acc.next()
                    P.op("pe", [I("matmul", a.t[:], wv[:, kc, hh * 256:hh * 256 + 128], ckvn[:, kc, cs], start=(kc == 0), stop=(kc == 3))
                                for kc in range(4)], reads=[wbuf, bckvn], writes=[a.buf])
                    o = ost.next()
                    P.op("act", I("activation", o.t[:], a.t[:], AF.Copy), reads=[a.buf], writes=[o.buf])
                    store(KnT[h, :, t0 + n * 512:t0 + (n + 1) * 512], o, o.t[:])
            wv4 = wv.rearrange("p k (h c) -> p k h c", c=256)
            for tb in range(NT // 128):
                a = acc.next()
                av = a.t[:].rearrange("p (h c) -> p h c", c=128)
                P.op("pe", [I("matmul", av, ckvn[:, kc, tb * 128:(tb + 1) * 128], wv4[:, kc, :, 128:256], start=(kc == 0), stop=(kc == 3))
                            for kc in range(4)], reads=[wbuf, bckvn], writes=[a.buf])
                o = ost.next()
                P.op("dve", I("tensor_copy", o.t[:], a.t[:]), reads=[a.buf], writes=[o.buf])
                store(Vo[t0 + tb * 128:t0 + (tb + 1) * 128, hg * 512:(hg + 1) * 512], o, o.t[:])
        for hg in range(NH // 4):
            jobs.append((lambda hg=hg: ws.load(w_ukv, 0, 4, [(hg * 1024, 1024, 0)], 1024),
                         lambda L, hg=hg: comp_kv(L, hg)))
        pipeline(jobs)

    P.finish(out_toks)
    P.emit()
    return nc


def rope_consts():
    inv = 1.0 / (10000.0 ** (np.arange(0, 64, 2, dtype=np.float32) / 64.0))
    inv = inv.astype(np.float32)
    rc = np.zeros((64, 2), np.float32)
    rc[:32, 0] = inv
    rc[32:, 0] = inv
    rc[:32, 1] = -1.0
    rc[32:, 1] = 1.0
    return rc


def chunked_gain(g):
    k = g.shape[0] // 128
    return np.ascontiguousarray(g.reshape(k, 128).T.astype(np.float32))


def l1_inputs(x, positions, l0_norm, l0_w_in, l0_q_norm, l0_w_uq, l0_kv_norm, l0_w_ukv):
    xf = x.reshape(B * S, D)
    pf = positions.reshape(B * S)
    maps = []
    for c in range(NCORES):
        maps.append({
            "xT": np.ascontiguousarray(xf[c * TOK:(c + 1) * TOK].T),
            "pos": np.ascontiguousarray(pf[c * TOK:(c + 1) * TOK]),
            "w_in": l0_w_in, "w_uq": l0_w_uq, "w_ukv": l0_w_ukv,
            "g0": chunked_gain(l0_norm), "qg": chunked_gain(l0_q_norm), "kvg": chunked_gain(l0_kv_norm),
            "ropec": rope_consts(),
        })
    return maps


HPC = 8


def build_l2():
    nc = new_nc()
    QnT = nc.dram_tensor("QnT", [HPC, 128, S], BF16, kind="ExternalInput").ap()
    QrT = nc.dram_tensor("QrT", [HPC, 64, S], BF16, kind="ExternalInput").ap()
    KnT = nc.dram_tensor("KnT", [HPC, 128, S], BF16, kind="ExternalInput").ap()
    KrT = nc.dram_tensor("KrT", [64, S], BF16, kind="ExternalInput").ap()
    Vh = nc.dram_tensor("Vh", [HPC, 128, S // 128, 128], BF16, kind="ExternalInput").ap()
    SZT = nc.dram_tensor("SZT", [HPC * 128, S], BF16, kind="ExternalInput").ap()
    trid = nc.dram_tensor("tri", [128, 128], BF16, kind="ExternalInput").ap()
    OGT = nc.dram_tensor("OGT", [HPC * 128, S], BF16, kind="ExternalOutput").ap()

    P = Prog(nc)
    scale = 1.0 / math.sqrt(192.0)
    ones = P.sbuf("ones", [128, 128], BF16)
    tri = P.sbuf("tri_sb", [128, 128], BF16)
    bconst = Buf("const")
    P.op("dve", I("memset", ones[:], 1.0), writes=[bconst])
    P.dma(tri[:], trid, writes=[bconst])
    kr = P.slot("kr", [64, S], BF16)
    P.dma(kr.t[:], KrT, writes=[kr.buf])
    kn = P.ring("kn", 2, [128, S], BF16)
    vv = P.ring("vv", 2, [128, S // 128, 128], BF16)
    qn = P.ring("qn", 3, [128, 512], BF16)
    qr = P.ring("qr", 3, [64, 512], BF16)
    sz = P.ring("sz", 3, [128, 512], BF16)
    pT = P.ring("pT", 3, [128, 512], BF16)
    rs = P.ring("rs", 2, [128, 512], F32)
    o32 = P.ring("o32", 2, [128, 512], F32)
    og = P.ring("og", 2, [128, 512], BF16)
    sT = P.pring("sT", 3)
    oT = P.pring("oT", 2)
    sm = P.pring("sm", 2)
    out_toks = []

    def load_head(h):
        k = kn.next()
        v = vv.next()
        P.dma(k.t[:], KnT[h], writes=[k.buf])
        P.dma(v.t[:], Vh[h], writes=[v.buf])
        return k, v

    def load_q(h, t):
        a, b_, c = qn.next(), qr.next(), sz.next()
        cs = slice(t * 512, (t + 1) * 512)
        P.dma(a.t[:], QnT[h, :, cs], writes=[a.buf])
        P.dma(b_.t[:], QrT[h, :, cs], writes=[b_.buf])
        P.dma(c.t[:], SZT[h * 128:(h + 1) * 128, cs], writes=[c.buf])
        return a, b_, c

    NQT = S // 512
    tiles = [(h, t) for h in range(HPC) for t in range(NQT)]
    blocks = []
    for ti, (h, t) in enumerate(tiles):
        nb = 4 * t + 4
        for kb in range(nb):
            blocks.append((ti, kb, nb))
    state = {}

    def tile_ctx(ti):
        if ti in state:
            return state[ti]
        h, t = tiles[ti]
        if t == 0:
            hk = load_head(h)
        else:
            hk = state[ti - 1]["hk"]
        q = load_q(h, t)
        ctx = {"hk": hk, "q": q, "oT": oT.next(), "sm": sm.next()}
        state[ti] = ctx
        return ctx

    def emit_S(bi):
        ti, kb, nb = blocks[bi]
        h, t = tiles[ti]
        ctx = tile_ctx(ti)
        k, v = ctx["hk"]
        a, b_, c = ctx["q"]
        j = kb - 4 * t
        c0 = 128 * j if j > 0 else 0
        s = sT.next()
        ks = slice(kb * 128, (kb + 1) * 128)
        P.op("pe", [I("matmul", s.t[:, c0:512], k.t[:, ks], a.t[:, c0:512], start=True, stop=False),
                    I("matmul", s.t[:, c0:512], kr.t[:, ks], b_.t[:, c0:512], start=False, stop=True)],
             reads=[k.buf, kr.buf, a.buf, b_.buf], writes=[s.buf])
        return s, c0, j

    pend = {}
    nB = len(blocks)
    AHEAD = 2
    for bi in range(min(AHEAD, nB)):
        pend[bi] = emit_S(bi)
    for bi in range(nB):
        ti, kb, nb = blocks[bi]
        h, t = tiles[ti]
        ctx = state[ti]
        k, v = ctx["hk"]
        s, c0, j = pend.pop(bi)
        p = pT.next()
        P.op("act", I("activation", p.t[:, c0:512], s.t[:, c0:512], AF.Exp, scale=scale), reads=[s.buf], writes=[p.buf])
        if j >= 0:
            P.op("pool", I("tensor_tensor", p.t[:, c0:c0 + 128], p.t[:, c0:c0 + 128], tri[:], ALU.mult),
                 reads=[p.buf, bconst], writes=[p.buf])
        o_, m_ = ctx["oT"], ctx["sm"]
        P.op("pe", [I("matmul", o_.t[:, c0:512], v.t[:, kb, :], p.t[:, c0:512], start=(kb == 0), stop=(kb == nb - 1)),
                    I("matmul", m_.t[:, c0:512], ones[:], p.t[:, c0:512], start=(kb == 0), stop=(kb == nb - 1))],
             reads=[v.buf, p.buf, bconst], writes=[o_.buf, m_.buf])
        if bi + AHEAD < nB:
            pend[bi + AHEAD] = emit_S(bi + AHEAD)
        if kb == nb - 1:
            a, b_, c = ctx["q"]
            r_ = rs.next()
            P.op("dve", I("reciprocal", r_.t[:], m_.t[:]), reads=[m_.buf], writes=[r_.buf])
            x_ = o32.next()
            P.op("dve", I("tensor_tensor", x_.t[:], o_.t[:], r_.t[:], ALU.mult), reads=[o_.buf, r_.buf], writes=[x_.buf])
            g_ = og.next()
            P.op("dve", I("tensor_tensor", g_.t[:], x_.t[:], c.t[:], ALU.mult), reads=[x_.buf, c.buf], writes=[g_.buf])
            out_toks.append(P.dma(OGT[h * 128:(h + 1) * 128, t * 512:(t + 1) * 512], g_.t[:], reads=[g_.buf],
                                  owner=g_.buf, queue="pool"))
            if ti + 1 < len(tiles):
                tile_ctx(ti + 1)

    P.finish(out_toks)
    P.emit()
    return nc


def tri_const():
    import ml_dtypes
    k = np.arange(128)[:, None]
    q = np.arange(128)[None, :]
    return (k <= q).astype(np.float32).astype(ml_dtypes.bfloat16)


def build_wo(final):
    nc = new_nc()
    ogT = nc.dram_tensor("ogT", [D, TOK], BF16, kind="ExternalInput").ap()
    xT = nc.dram_tensor("xT", [D, TOK], F32, kind="ExternalInput").ap()
    w_o = nc.dram_tensor("w_o", [D, D], F32, kind="ExternalInput").ap()
    if final:
        fgd = nc.dram_tensor("fg", [128, 32], F32, kind="ExternalInput").ap()
        x2T = nc.dram_tensor("x2T_scratch", [D, TOK], F32).ap()
    outT = nc.dram_tensor("outT", [D, TOK], F32, kind="ExternalOutput").ap()

    P = Prog(nc)
    NT = 1024
    ws = WStream(P, nstage=2, nw=2)
    bconst = Buf("const")
    ones = P.sbuf("ones", [128, 128], BF16)
    epsc = P.sbuf("epsc", [128, 1], F32)
    P.op("dve", I("memset", ones[:], 1.0), writes=[bconst])
    P.op("dve", I("memset", epsc[:], EPS), writes=[bconst])
    if final:
        fg = P.sbuf("fgs", [128, 32], F32)
        P.dma(fg[:], fgd, writes=[bconst])
    og = P.slot("og", [128, 32, NT], BF16)
    xs = P.ring("xs", 3, [128, 512], F32)
    t32 = P.ring("t32_", 3, [128, 512], F32)
    sq2 = P.ring("sq2_", 2, [128, 512], BF16)
    acc = P.pring("acc", 4)
    ssq = [Slot(P.psum(f"ssq{i}", [128, 512]), f"ssq{i}") for i in range(2)]
    rstd = P.sbuf("rstd", [128, NT], F32)
    brstd = Buf("rstd")
    bscr = Buf("x2scratch")
    out_toks = []

    for p in range(TOK // NT):
        t0 = p * NT
        for kc in range(32):
            P.dma(og.t[:, kc, :], ogT[kc * 128:(kc + 1) * 128, t0:t0 + NT], writes=[og.buf])

        def comp(L, wt):
            wbuf, wv = L
            for mi in range(2):
                m = wt * 2 + mi
                for n in range(2):
                    cs = slice(n * 512, (n + 1) * 512)
                    a = acc.next()
                    P.op("pe", [I("matmul", a.t[:], wv[:, kc, mi * 128:(mi + 1) * 128], og.t[:, kc, cs], start=(kc == 0), stop=(kc == 31))
                                for kc in range(32)], reads=[wbuf, og.buf], writes=[a.buf])
                    x_ = xs.next()
                    P.dma(x_.t[:], xT[m * 128:(m + 1) * 128, t0 + n * 512:t0 + (n + 1) * 512], writes=[x_.buf])
                    t = t32.next()
                    P.op("dve", I("tensor_tensor", t.t[:], a.t[:], x_.t[:], ALU.add), reads=[a.buf, x_.buf], writes=[t.buf])
                    if not final:
                        out_toks.append(P.dma(outT[m * 128:(m + 1) * 128, t0 + n * 512:t0 + (n + 1) * 512], t.t[:],
                                              reads=[t.buf], owner=t.buf, queue="pool"))
                    else:
                        P.dma(x2T[m * 128:(m + 1) * 128, t0 + n * 512:t0 + (n + 1) * 512], t.t[:],
                              reads=[t.buf], writes=[bscr], owner=bscr, queue="pool")
                        s2 = sq2.next()
                        P.op("act", I("activation", s2.t[:], t.t[:], AF.Square), reads=[t.buf], writes=[s2.buf])
                        P.op("pe", I("matmul", ssq[n].t[:], ones[:], s2.t[:], start=(m == 0), stop=(m == 31)),
                             reads=[s2.buf, bconst], writes=[ssq[n].buf])
        jobs = [(lambda wt=wt: ws.load(w_o, 0, 32, [(wt * 256, 256, 0)], 256), lambda L, wt=wt: comp(L, wt)) for wt in range(16)]
        pipeline(jobs)
        if final:
            for n in range(2):
                t = t32.next()
                P.op("act", I("activation", t.t[:], ssq[n].t[:], AF.Sqrt, bias=epsc[:, 0:1], scale=1.0 / D),
                     reads=[ssq[n].buf, bconst], writes=[t.buf])
                P.op("dve", I("reciprocal", rstd[:, n * 512:(n + 1) * 512], t.t[:]), reads=[t.buf], writes=[brstd])
            for m in range(32):
                for n in range(2):
                    cs = slice(n * 512, (n + 1) * 512)
                    x_ = xs.next()
                    P.dma(x_.t[:], x2T[m * 128:(m + 1) * 128, t0 + n * 512:t0 + (n + 1) * 512], reads=[bscr], writes=[x_.buf])
                    t = t32.next()
                    P.op("dve", I("scalar_tensor_tensor", t.t[:], x_.t[:], fg[:, m:m + 1], rstd[:, cs], ALU.mult, ALU.mult),
                         reads=[x_.buf, brstd, bconst], writes=[t.buf])
                    out_toks.append(P.dma(outT[m * 128:(m + 1) * 128, t0 + n * 512:t0 + (n + 1) * 512], t.t[:],
                                          reads=[t.buf], owner=t.buf, queue="pool"))
    P.finish(out_toks)
    P.emit()
    return nc


def build_l3b():
    nc = new_nc()
    xT = nc.dram_tensor("xT", [D, TOK], F32, kind="ExternalInput").ap()
    w_in = nc.dram_tensor("w_in", [D, 4 * D], F32, kind="ExternalInput").ap()
    g1d = nc.dram_tensor("g1", [128, 32], F32, kind="ExternalInput").ap()
    lbd = nc.dram_tensor("lbs", [128, 2, 32], F32, kind="ExternalInput").ap()
    q1T = nc.dram_tensor("q1T", [D, TOK], BF16, kind="ExternalOutput").ap()
    k1T = nc.dram_tensor("k1T", [D, TOK], BF16, kind="ExternalOutput").ap()
    g1T = nc.dram_tensor("g1T", [D, TOK], F32, kind="ExternalOutput").ap()
    v1T = nc.dram_tensor("v1T", [D, TOK], BF16, kind="ExternalOutput").ap()
    sz1T = nc.dram_tensor("sz1T", [D, TOK], BF16, kind="ExternalOutput").ap()

    P = Prog(nc)
    NT = 1024
    ws = WStream(P, nstage=2, nw=2)
    bconst = Buf("const")
    ones = P.sbuf("ones", [128, 128], BF16)
    epsc = P.sbuf("epsc", [128, 1], F32)
    g1 = P.sbuf("g1s", [128, 32], F32)
    lbs = P.sbuf("lbss", [128, 2, 32], F32)
    lb = P.sbuf("lb", [128, 32], F32)
    oml = P.sbuf("oml", [128, 32], F32)
    dlb = P.sbuf("dlb", [128, 32], F32)
    P.op("dve", I("memset", ones[:], 1.0), writes=[bconst])
    P.op("dve", I("memset", epsc[:], EPS), writes=[bconst])
    P.dma(g1[:], g1d, writes=[bconst])
    P.dma(lbs[:], lbd, writes=[bconst])
    P.op("dve", I("tensor_tensor", dlb[:], lbs[:, 1, :], lbs[:, 0, :], ALU.subtract), reads=[bconst], writes=[bconst])
    P.op("act", I("activation", dlb[:], dlb[:], AF.Exp, scale=-1.0), reads=[bconst], writes=[bconst])
    P.op("dve", I("tensor_scalar", lb[:], dlb[:], 1.0, None, ALU.add), reads=[bconst], writes=[bconst])
    P.op("dve", I("reciprocal", lb[:], lb[:]), reads=[bconst], writes=[bconst])
    P.op("dve", I("tensor_tensor", oml[:], dlb[:], lb[:], ALU.mult), reads=[bconst], writes=[bconst])

    xg = P.sbuf("xg", [128, 32, NT], BF16)
    bxg = Buf("xg")
    sqb = P.ring("sqb", 2, [128, NT], BF16)
    t32 = P.ring("t32_", 3, [128, 512], F32)
    e32 = P.ring("e32_", 2, [128, 512], F32)
    s32 = P.ring("s32_", 2, [128, 512], F32)
    f32 = P.ring("f32_", 2, [128, 512], F32)
    gout = P.ring("gout", 2, [128, 512], F32)
    ost = P.ring("ost", 4, [128, 512], BF16)
    rstd_x = P.sbuf("rstd_x", [128, NT], F32)
    brx = Buf("rstd_x")
    acc = P.pring("acc", 4)
    ssq = [Slot(P.psum(f"ssq{i}", [128, 512]), f"ssq{i}") for i in range(2)]
    out_toks = []

    def store(dst_ap, src_slot, src_ap):
        out_toks.append(P.dma(dst_ap, src_ap, reads=[src_slot.buf], owner=src_slot.buf, queue="pool"))

    for p in range(TOK // NT):
        t0 = p * NT
        for kc in range(32):
            st = ws.stages.next()
            stv = st.t[:, 0:NT]
            P.dma(stv, xT[kc * 128:(kc + 1) * 128, t0:t0 + NT], writes=[st.buf])
            sq = sqb.next()
            P.op("act", I("activation", sq.t[:], stv, AF.Square), reads=[st.buf], writes=[sq.buf])
            P.op("pe", [I("matmul", ssq[n].t[:], ones[:], sq.t[:, n * 512:(n + 1) * 512], start=(kc == 0), stop=(kc == 31))
                        for n in range(2)], reads=[sq.buf, bconst], writes=[ssq[0].buf, ssq[1].buf])
            P.op("dve", I("tensor_scalar", xg[:, kc, :], stv, g1[:, kc:kc + 1], None, ALU.mult),
                 reads=[st.buf, bconst], writes=[bxg])
        for n in range(2):
            t = t32.next()
            P.op("act", I("activation", t.t[:], ssq[n].t[:], AF.Sqrt, bias=epsc[:, 0:1], scale=1.0 / D),
                 reads=[ssq[n].buf, bconst], writes=[t.buf])
            P.op("dve", I("reciprocal", rstd_x[:, n * 512:(n + 1) * 512], t.t[:]), reads=[t.buf], writes=[brx])

        def comp(L, wt):
            wbuf, wv = L
            kind = wt // 16
            for mi in range(2):
                mm = (wt % 16) * 2 + mi
                for n in range(2):
                    cs = slice(n * 512, (n + 1) * 512)
                    dsl = (slice(mm * 128, (mm + 1) * 128), slice(t0 + n * 512, t0 + (n + 1) * 512))
                    a = acc.next()
                    P.op("pe", [I("matmul", a.t[:], wv[:, kc, mi * 128:(mi + 1) * 128], xg[:, kc, cs], start=(kc == 0), stop=(kc == 31))
                                for kc in range(32)], reads=[wbuf, bxg], writes=[a.buf])
                    t = t32.next()
                    P.op("dve", I("tensor_tensor", t.t[:], a.t[:], rstd_x[:, cs], ALU.mult), reads=[a.buf, brx], writes=[t.buf])
                    if kind == 0 or kind == 3:
                        o = ost.next()
                        P.op("act", I("activation", o.t[:], t.t[:], AF.Silu), reads=[t.buf], writes=[o.buf])
                        store((q1T if kind == 0 else sz1T)[dsl[0], dsl[1]], o, o.t[:])
                    elif kind == 2:
                        o = ost.next()
                        P.op("act", I("activation", o.t[:], t.t[:], AF.Copy), reads=[t.buf], writes=[o.buf])
                        store(v1T[dsl[0], dsl[1]], o, o.t[:])
                    else:
                        e = e32.next()
                        P.op("act", I("activation", e.t[:], t.t[:], AF.Exp, scale=-1.0), reads=[t.buf], writes=[e.buf])
                        s_ = s32.next()
                        P.op("pool", I("tensor_scalar", s_.t[:], e.t[:], 1.0, None, ALU.add), reads=[e.buf], writes=[s_.buf])
                        P.op("dve", I("reciprocal", s_.t[:], s_.t[:]), reads=[s_.buf], writes=[s_.buf])
                        o = ost.next()
                        P.op("dve", I("scalar_tensor_tensor", o.t[:], e.t[:], oml[:, mm:mm + 1], s_.t[:], ALU.mult, ALU.mult),
                             reads=[e.buf, s_.buf, bconst], writes=[o.buf])
                        store(k1T[dsl[0], dsl[1]], o, o.t[:])
                        f_ = f32.next()
                        P.op("pool", I("tensor_scalar", f_.t[:], s_.t[:], oml[:, mm:mm + 1], lb[:, mm:mm + 1], ALU.mult, ALU.add),
                             reads=[s_.buf, bconst], writes=[f_.buf])
                        g_ = gout.next()
                        P.op("act", I("activation", g_.t[:], f_.t[:], AF.Ln), reads=[f_.buf], writes=[g_.buf])
                        store(g1T[dsl[0], dsl[1]], g_, g_.t[:])
        jobs = [(lambda wt=wt: ws.load(w_in, 0, 32, [(wt * 256, 256, 0)], 256), lambda L, wt=wt: comp(L, wt)) for wt in range(64)]
        pipeline(jobs)
    P.finish(out_toks)
    P.emit()
    return nc


def build_l4():
    nc = new_nc()
    qT = nc.dram_tensor("qT", [HPC, 128, S], BF16, kind="ExternalInput").ap()
    kT = nc.dram_tensor("kT", [HPC, 128, S], BF16, kind="ExternalInput").ap()
    gT = nc.dram_tensor("gT", [HPC, 128, S], F32, kind="ExternalInput").ap()
    v2 = nc.dram_tensor("v2", [HPC, 128, S // 128, 128], BF16, kind="ExternalInput").ap()
    szT = nc.dram_tensor("szT", [HPC * 128, S], BF16, kind="ExternalInput").ap()
    gnd = nc.dram_tensor("gn", [128, 1], F32, kind="ExternalInput").ap()
    m2d = nc.dram_tensor("mask2", [128, 128], BF16, kind="ExternalInput").ap()
    idd = nc.dram_tensor("ident", [128, 128], BF16, kind="ExternalInput").ap()
    smd = nc.dram_tensor("scanmask", [128, 512], F32, kind="ExternalInput").ap()
    OGT = nc.dram_tensor("OG1T", [HPC * 128, S], BF16, kind="ExternalOutput").ap()

    P = Prog(nc)
    bconst = Buf("const")
    ones = P.sbuf("ones", [128, 128], BF16)
    epsc = P.sbuf("epsc", [128, 1], F32)
    gn = P.sbuf("gns", [128, 1], F32)
    mask2 = P.sbuf("mask2s", [128, 128], BF16)
    ident = P.sbuf("idents", [128, 128], BF16)
    scanm = P.sbuf("scanms", [128, 512], F32)
    P.op("dve", I("memset", ones[:], 1.0), writes=[bconst])
    P.op("dve", I("memset", epsc[:], EPS), writes=[bconst])
    P.dma(gn[:], gnd, writes=[bconst])
    P.dma(mask2[:], m2d, writes=[bconst])
    P.dma(ident[:], idd, writes=[bconst])
    P.dma(scanm[:], smd, writes=[bconst])

    H = HPC
    S32 = [P.slot(f"S32_{h}", [128, 128], F32) for h in range(H)]
    Sbf = [P.slot(f"Sbf_{h}", [128, 128], BF16) for h in range(H)]
    for h in range(H):
        P.op("dve", I("memset", S32[h].t[:], 0.0), writes=[S32[h].buf])
        P.op("pool", I("memset", Sbf[h].t[:], 0.0), writes=[Sbf[h].buf])
    Qh = [P.slot(f"Qh_{h}", [128, 512], BF16) for h in range(H)]
    Kht = [P.slot(f"Kht_{h}", [128, 512], BF16) for h in range(H)]
    vt = [P.slot(f"vt_{h}", [128, 4, 128], BF16) for h in range(H)]
    o32 = [P.slot(f"o32_{h}", [128, 512], F32) for h in range(H)]
    dec = [P.slot(f"dec_{h}", [128, 8], F32) for h in range(H)]
    szs = [P.slot(f"szs_{h}", [128, 512], BF16) for h in range(H)]
    qin = P.ring("qin", 2, [128, 512], BF16)
    kin = P.ring("kin", 2, [128, 512], BF16)
    gin = P.ring("gin", 2, [128, 512], F32)
    bt = P.ring("bt", 2, [128, 512], F32)
    d1 = P.ring("d1_", 2, [128, 512], F32)
    d4 = P.ring("d4_", 2, [128, 512], F32)
    E = P.ring("E_", 4, [128, 512], F32)
    Qt = P.ring("Qt", 2, [128, 512], BF16)
    Kt = P.ring("Kt", 2, [128, 512], BF16)
    KhT = P.ring("KhT", 2, [128, 512], BF16)
    attm = P.ring("attm", 2, [128, 512], BF16)
    sqb = P.ring("sqb", 2, [128, 512], BF16)
    sd = P.ring("sd", 2, [128, 512], F32)
    y32 = P.ring("y32", 2, [128, 512], F32)
    ogs = P.ring("ogs", 2, [128, 512], BF16)
    trp = Slot(P.psum("trp", [128, 1024], BF16), "trp")
    attp = Slot(P.psum("attp", [128, 512]), "attp")
    oIp = Slot(P.psum("oIp", [128, 512]), "oIp")
    oXp = P.pring("oXp", 2)
    dSp = [Slot(P.psum(f"dSp{i}", [128, 512]), f"dSp{i}") for i in range(2)]
    ssqp = Slot(P.psum("ssqp", [128, 512]), "ssqp")
    out_toks = []

    NSB = S // 512
    for sb in range(NSB):
        cs = slice(sb * 512, (sb + 1) * 512)
        for h in range(H):
            qi, ki, gi = qin.next(), kin.next(), gin.next()
            P.dma(qi.t[:], qT[h, :, cs], writes=[qi.buf])
            P.dma(ki.t[:], kT[h, :, cs], writes=[ki.buf])
            P.dma(gi.t[:], gT[h, :, cs], writes=[gi.buf])
            P.dma(vt[h].t[:], v2[h, :, sb * 4:(sb + 1) * 4, :], writes=[vt[h].buf])
            P.dma(szs[h].t[:], szT[h * 128:(h + 1) * 128, cs], writes=[szs[h].buf])
            b_ = bt.next()
            P.op("dve", I("tensor_tensor_scan", b_.t[:], scanm[:], gi.t[:], 0.0, ALU.mult, ALU.add),
                 reads=[gi.buf, bconst], writes=[b_.buf])
            b3 = b_.t[:].rearrange("p (n c) -> p n c", c=64)
            x1, x4 = d1.next(), d4.next()
            P.op("dve", I("tensor_tensor", x1.t[:].rearrange("p (n c) -> p n c", c=64), b3,
                          b3[:, :, 32:33].broadcast_to([128, 8, 64]), ALU.subtract), reads=[b_.buf], writes=[x1.buf])
            P.op("pool", I("tensor_tensor", x4.t[:].rearrange("p (n c) -> p n c", c=64), b3,
                           b3[:, :, 63:64].broadcast_to([128, 8, 64]), ALU.subtract), reads=[b_.buf], writes=[x4.buf])
            e1, e2, e3, e4 = E.next(), E.next(), E.next(), E.next()
            P.op("act", I("activation", e1.t[:], x1.t[:], AF.Exp), reads=[x1.buf], writes=[e1.buf])
            P.op("act", I("activation", e2.t[:], x1.t[:], AF.Exp, scale=-1.0), reads=[x1.buf], writes=[e2.buf])
            P.op("act", I("activation", e3.t[:], b_.t[:], AF.Exp), reads=[b_.buf], writes=[e3.buf])
            P.op("act", I("activation", e4.t[:], x4.t[:], AF.Exp, scale=-1.0), reads=[x4.buf], writes=[e4.buf])
            P.op("act", I("activation", dec[h].t[:], b3[:, :, 63], AF.Exp), reads=[b_.buf], writes=[dec[h].buf])
            qt_, kt_, kh_ = Qt.next(), Kt.next(), KhT.next()
            P.op("dve", I("tensor_tensor", qt_.t[:], qi.t[:], e1.t[:], ALU.mult), reads=[qi.buf, e1.buf], writes=[qt_.buf])
            P.op("pool", I("tensor_tensor", kt_.t[:], ki.t[:], e2.t[:], ALU.mult), reads=[ki.buf, e2.buf], writes=[kt_.buf])
            P.op("dve", I("tensor_tensor", Qh[h].t[:], qi.t[:], e3.t[:], ALU.mult), reads=[qi.buf, e3.buf], writes=[Qh[h].buf])
            P.op("pool", I("tensor_tensor", kh_.t[:], ki.t[:], e4.t[:], ALU.mult), reads=[ki.buf, e4.buf], writes=[kh_.buf])
            P.op("pe", [I("transpose", trp.t[:, pr * 128:(pr + 1) * 128], kh_.t[:, pr * 128:(pr + 1) * 128], ident[:]) for pr in range(4)],
                 reads=[kh_.buf, bconst], writes=[trp.buf])
            P.op("act", I("activation", Kht[h].t[:], trp.t[:, 0:512], AF.Copy), reads=[trp.buf], writes=[Kht[h].buf])
            P.op("pe", [I("matmul", attp.t[:, pr * 128:(pr + 1) * 128], kt_.t[:, pr * 128:(pr + 1) * 128], qt_.t[:, pr * 128:(pr + 1) * 128],
                          start=True, stop=True) for pr in range(4)], reads=[kt_.buf, qt_.buf], writes=[attp.buf])
            am = attm.next()
            P.op("dve", I("tensor_tensor", am.t[:].rearrange("p (n c) -> p n c", c=128), attp.t[:].rearrange("p (n c) -> p n c", c=128),
                          mask2[:].rearrange("p (o c) -> p o c", o=1).broadcast_to([128, 4, 128]), ALU.mult),
                 reads=[attp.buf, bconst], writes=[am.buf])
            P.op("pe", [I("matmul", oIp.t[:, pr * 128:(pr + 1) * 128], vt[h].t[:, pr, :], am.t[:, pr * 128:(pr + 1) * 128],
                          start=True, stop=True) for pr in range(4)], reads=[vt[h].buf, am.buf], writes=[oIp.buf])
            P.op("act", I("activation", o32[h].t[:], oIp.t[:], AF.Copy), reads=[oIp.buf], writes=[o32[h].buf])
        for pr in range(4):
            for ch in range(2):
                c0 = pr * 128 + ch * 64
                ps = slice(ch * 64, ch * 64 + 64)
                ox = oXp.next()
                for h in range(H):
                    P.op("pe", I("matmul", ox.t[:, h * 64:(h + 1) * 64], Sbf[h].t[:], Qh[h].t[:, c0:c0 + 64], start=True, stop=True),
                         reads=[Sbf[h].buf, Qh[h].buf], writes=[ox.buf])
                for h in range(H):
                    dsp = dSp[h // 4]
                    P.op("pe", I("matmul", dsp.t[:, (h % 4) * 128:(h % 4 + 1) * 128], Kht[h].t[ps, pr * 128:(pr + 1) * 128],
                                 vt[h].t[ps, pr, :], start=True, stop=True),
                         reads=[Kht[h].buf, vt[h].buf], writes=[dsp.buf])
                for h in range(H):
                    P.op("dve", I("tensor_tensor", o32[h].t[:, c0:c0 + 64], ox.t[:, h * 64:(h + 1) * 64], o32[h].t[:, c0:c0 + 64], ALU.add),
                         reads=[ox.buf, o32[h].buf], writes=[o32[h].buf])
                for h in range(H):
                    dsp = dSp[h // 4]
                    P.op("dve", I("scalar_tensor_tensor", S32[h].t[:], S32[h].t[:], dec[h].t[:, pr * 2 + ch:pr * 2 + ch + 1],
                                  dsp.t[:, (h % 4) * 128:(h % 4 + 1) * 128], ALU.mult, ALU.add),
                         reads=[S32[h].buf, dec[h].buf, dsp.buf], writes=[S32[h].buf])
                    P.op("pool", I("tensor_copy", Sbf[h].t[:], S32[h].t[:]), reads=[S32[h].buf], writes=[Sbf[h].buf])
        for h in range(H):
            sq = sqb.next()
            P.op("act", I("activation", sq.t[:], o32[h].t[:], AF.Square), reads=[o32[h].buf], writes=[sq.buf])
            P.op("pe", I("matmul", ssqp.t[:], ones[:], sq.t[:], start=True, stop=True), reads=[sq.buf, bconst], writes=[ssqp.buf])
            s_ = sd.next()
            P.op("act", I("activation", s_.t[:], ssqp.t[:], AF.Sqrt, bias=epsc[:, 0:1], scale=1.0 / 128), reads=[ssqp.buf, bconst], writes=[s_.buf])
            P.op("dve", I("reciprocal", s_.t[:], s_.t[:]), reads=[s_.buf], writes=[s_.buf])
            y = y32.next()
            P.op("dve", I("scalar_tensor_tensor", y.t[:], o32[h].t[:], gn[:, 0:1], s_.t[:], ALU.mult, ALU.mult),
                 reads=[o32[h].buf, s_.buf, bconst], writes=[y.buf])
            g_ = ogs.next()
            P.op("pool", I("tensor_tensor", g_.t[:], y.t[:], szs[h].t[:], ALU.mult), reads=[y.buf, szs[h].buf], writes=[g_.buf])
            out_toks.append(P.dma(OGT[h * 128:(h + 1) * 128, cs], g_.t[:], reads=[g_.buf], owner=g_.buf, queue="pool"))
    P.finish(out_toks)
    P.emit()
    return nc


def l4_consts():
    import ml_dtypes
    s = np.arange(128)[:, None]
    c = np.arange(128)[None, :]
    m2 = ((s // 64 == c // 64) & (s <= c)).astype(np.float32).astype(ml_dtypes.bfloat16)
    ident = np.eye(128, dtype=np.float32).astype(ml_dtypes.bfloat16)
    sm = np.ones((128, 512), np.float32)
    sm[:, ::64] = 0.0
    return m2, ident, sm


GROUPS = [[0, 1, 2, 3], [4, 5, 6, 7]]
NTP = 1024


def stage_a(nc, t):
    P = Prog(nc)
    NT = NTP
    xT, pos, w_lat, w_z, w_uq, w_ukv = t["xTall"], t["pos"], t["w_lat"], t["w_z"], t["w_uq"], t["w_ukv"]
    QnT, QrT, KnT, KrT, Vh, SZT = t["QnT"], t["QrT"], t["KnT"], t["KrT"], t["Vh"], t["SZT"]
    ws = WStream(P, nstage=4, nw=2)
    ones = P.sbuf("ones", [128, 128], BF16)
    g0 = P.sbuf("g0s", [128, 32], F32)
    qg = P.sbuf("qgs", [128, 8], F32)
    kvg = P.sbuf("kvgs", [128, 4], F32)
    rc = P.sbuf("rcs", [64, 2], F32)
    epsc = P.sbuf("epsc", [128, 1], F32)
    bconst = Buf("const")
    P.op("dve", I("memset", ones[:], 1.0), writes=[bconst])
    P.op("dve", I("memset", epsc[:], EPS), writes=[bconst])
    P.dma(g0[:], t["g0"], writes=[bconst])
    P.dma(qg[:], t["qg"], writes=[bconst])
    P.dma(kvg[:], t["kvg"], writes=[bconst])
    P.dma(rc[:], t["ropec"], writes=[bconst])

    xg = P.sbuf("xg", [128, 32, NT], BF16)
    bxg = Buf("xg")
    sqb = P.ring("sqb", 2, [128, NT], BF16)
    t32 = P.ring("t32_", 4, [128, 512], F32)
    sq2 = P.ring("sq2_", 2, [128, 512], BF16)
    ost = P.ring("ost", 4, [128, 512], BF16)
    rstd_x = P.sbuf("rstd_x", [128, NT], F32)
    rstd_q = P.sbuf("rstd_q", [128, NT], F32)
    rstd_kv = P.sbuf("rstd_kv", [128, NT], F32)
    brx, brq, brkv = Buf("rstd_x"), Buf("rstd_q"), Buf("rstd_kv")
    cqg = P.sbuf("cqg", [128, 8, NT], BF16)
    bcqg = Buf("cqg")
    ckvn = P.sbuf("ckvn", [128, 4, NT], BF16)
    bckvn = Buf("ckvn")
    cosT = P.sbuf("cosT", [64, NT], F32)
    sinT = P.sbuf("sinT", [64, NT], F32)
    btab = Buf("ropetab")
    ra = P.sbuf("ra", [64, NT], F32)
    rb = P.sbuf("rb", [64, NT], F32)
    rcc = P.sbuf("rcc", [64, NT], F32)
    ri = P.sbuf("ri", [64, NT], I32)
    bra, brb, brcc, bri = Buf("ra"), Buf("rb"), Buf("rcc"), Buf("ri")
    acc = P.pring("acc", 4)
    ssq = [Slot(P.psum(f"ssq{i}", [128, 512]), f"ssq{i}") for i in range(2)]
    ssq2 = [Slot(P.psum(f"ssqb{i}", [128, 512]), f"ssqb{i}") for i in range(2)]

    def rstd_from(ssq_slot, dst, bdst, n, inv_n):
        tt = t32.next()
        P.op("act", I("activation", tt.t[:], ssq_slot.t[:], AF.Sqrt, bias=epsc[:, 0:1], scale=inv_n),
             reads=[ssq_slot.buf, bconst], writes=[tt.buf])
        P.op("dve", I("reciprocal", dst[:, n * 512:(n + 1) * 512], tt.t[:]), reads=[tt.buf], writes=[bdst])

    def store(dst_ap, src_slot, src_ap):
        P.dma(dst_ap, src_ap, reads=[src_slot.buf], owner=src_slot.buf, queue="act")

    for p in range(S // NT):
        t0 = p * NT
        P.dma(ri[:], pos[t0:t0 + NT].partition_broadcast(64), writes=[bri])
        P.op("dve", I("tensor_copy", ra[:], ri[:]), reads=[bri], writes=[bra])
        P.op("dve", I("tensor_scalar", ra[:], ra[:], rc[:, 0:1], None, ALU.mult), reads=[bra, bconst], writes=[bra])
        P.op("dve", I("tensor_scalar", rb[:], ra[:], 1.0 / TWO_PI, None, ALU.mult), reads=[bra], writes=[brb])
        P.op("dve", I("tensor_copy", ri[:], rb[:]), reads=[brb], writes=[bri])
        P.op("dve", I("tensor_copy", rb[:], ri[:]), reads=[bri], writes=[brb])
        P.op("dve", I("scalar_tensor_tensor", ra[:], rb[:], -CW1, ra[:], ALU.mult, ALU.add), reads=[bra, brb], writes=[bra])
        P.op("dve", I("scalar_tensor_tensor", ra[:], rb[:], -CW2, ra[:], ALU.mult, ALU.add), reads=[bra, brb], writes=[bra])
        P.op("dve", I("tensor_single_scalar", rb[:], ra[:], 0.0, ALU.is_lt), reads=[bra], writes=[brb])
        P.op("dve", I("scalar_tensor_tensor", ra[:], rb[:], TWO_PI, ra[:], ALU.mult, ALU.add), reads=[bra, brb], writes=[bra])
        P.op("dve", I("tensor_scalar", rb[:], ra[:], -1.0, math.pi, ALU.mult, ALU.add), reads=[bra], writes=[brb])
        P.op("dve", I("tensor_scalar", rb[:], rb[:], PI_LO, -PI_LO, ALU.min, ALU.max), reads=[brb], writes=[brb])
        P.op("act", I("activation", sinT[:], rb[:], AF.Sin), reads=[brb], writes=[btab])
        P.op("dve", I("tensor_scalar", rcc[:], ra[:], math.pi / 2, None, ALU.add), reads=[bra], writes=[brcc])
        P.op("dve", I("tensor_single_scalar", rb[:], rcc[:], TWO_PI, ALU.is_ge), reads=[brcc, btab], writes=[brb])
        P.op("dve", I("scalar_tensor_tensor", rcc[:], rb[:], -TWO_PI, rcc[:], ALU.mult, ALU.add), reads=[brcc, brb], writes=[brcc])
        P.op("dve", I("tensor_scalar", rcc[:], rcc[:], -1.0, math.pi, ALU.mult, ALU.add), reads=[brcc], writes=[brcc])
        P.op("dve", I("tensor_scalar", rcc[:], rcc[:], PI_LO, -PI_LO, ALU.min, ALU.max), reads=[brcc], writes=[brcc])
        P.op("act", I("activation", cosT[:], rcc[:], AF.Sin), reads=[brcc], writes=[btab])
        P.op("dve", I("tensor_scalar", sinT[:], sinT[:], rc[:, 1:2], None, ALU.mult), reads=[btab, bconst], writes=[btab])

        for kc in range(32):
            st = ws.stages.next()
            stv = st.t[:, 0:NT]
            P.dma(stv, xT[kc * 128:(kc + 1) * 128, t0:t0 + NT], writes=[st.buf])
            sq = sqb.next()
            P.op("act", I("activation", sq.t[:], stv, AF.Square), reads=[st.buf], writes=[sq.buf])
            P.op("pe", [I("matmul", ssq[n].t[:], ones[:], sq.t[:, n * 512:(n + 1) * 512], start=(kc == 0), stop=(kc == 31))
                        for n in range(2)], reads=[sq.buf, bconst], writes=[ssq[0].buf, ssq[1].buf])
            P.op("dve", I("tensor_scalar", xg[:, kc, :], stv, g0[:, kc:kc + 1], None, ALU.mult),
                 reads=[st.buf, bconst], writes=[bxg])
        for n in range(2):
            rstd_from(ssq[n], rstd_x, brx, n, 1.0 / D)

        def gemm_x(wbuf, wv, mcols, epilogue):
            for (c0, M, tag) in mcols:
                for n in range(2):
                    a = acc.next()
                    P.op("pe", [I("matmul", a.t[0:M, :], wv[:, kc, c0:c0 + M], xg[:, kc, n * 512:(n + 1) * 512],
                                  start=(kc == 0), stop=(kc == 31)) for kc in range(32)],
                         reads=[wbuf, bxg], writes=[a.buf])
                    epilogue(a, M, tag, n)

        def ep_lat(kind):
            nm = 8 if kind == "q" else 4

            def ep(a, M, m, n):
                cs = slice(n * 512, (n + 1) * 512)
                tt = t32.next()
                P.op("dve", I("tensor_tensor", tt.t[:], a.t[:], rstd_x[:, cs], ALU.mult), reads=[a.buf, brx], writes=[tt.buf])
                s2 = sq2.next()
                P.op("act", I("activation", s2.t[:], tt.t[:], AF.Square), reads=[tt.buf], writes=[s2.buf])
                P.op("pe", I("matmul", ssq2[n].t[:], ones[:], s2.t[:], start=(m == 0), stop=(m == nm - 1)),
                     reads=[s2.buf, bconst], writes=[ssq2[n].buf])
                if kind == "q":
                    P.op("pool", I("tensor_scalar", cqg[:, m, cs], tt.t[:], qg[:, m:m + 1], None, ALU.mult),
                         reads=[tt.buf, bconst], writes=[bcqg])
                else:
                    P.op("pool", I("tensor_scalar", ckvn[:, m, cs], tt.t[:], kvg[:, m:m + 1], None, ALU.mult),
                         reads=[tt.buf, bconst], writes=[bckvn])
            return ep

        def rope_store(aA, aB, rs, brs, n, dst_ap):
            tA = t32.next()
            tB = t32.next()
            cs = slice(n * 512, (n + 1) * 512)
            P.op("dve", I("tensor_tensor", tA.t[0:64, :], aA.t[0:64, :], rs[0:64, cs], ALU.mult), reads=[aA.buf, brs], writes=[tA.buf])
            P.op("dve", I("tensor_tensor", tB.t[0:64, :], aB.t[0:64, :], rs[0:64, cs], ALU.mult), reads=[aB.buf, brs], writes=[tB.buf])
            P.op("pool", I("tensor_tensor", tA.t[0:64, :], tA.t[0:64, :], cosT[:, cs], ALU.mult), reads=[tA.buf, btab], writes=[tA.buf])
            P.op("pool", I("tensor_tensor", tB.t[0:64, :], tB.t[0:64, :], sinT[:, cs], ALU.mult), reads=[tB.buf, btab], writes=[tB.buf])
            o = ost.next()
            P.op("dve", I("tensor_tensor", o.t[0:64, :], tA.t[0:64, :], tB.t[0:64, :], ALU.add), reads=[tA.buf, tB.buf], writes=[o.buf])
            store(dst_ap, o, o.t[0:64, :])

        def ep_z(a, M, m, n):
            cs = slice(n * 512, (n + 1) * 512)
            tt = t32.next()
            P.op("dve", I("tensor_tensor", tt.t[:], a.t[:], rstd_x[:, cs], ALU.mult), reads=[a.buf, brx], writes=[tt.buf])
            o = ost.next()
            P.op("act", I("activation", o.t[:], tt.t[:], AF.Silu), reads=[tt.buf], writes=[o.buf])
            store(SZT[m * 128:(m + 1) * 128, t0 + n * 512:t0 + (n + 1) * 512], o, o.t[:])

        jobs = []
        for wt in range(4):
            jobs.append((lambda wt=wt, p=p: ws.load(w_lat, 0, 32, [(wt * 256, 256, 0)], 256, key=("cq", wt), first=(p == 0)),
                         lambda L, wt=wt: gemm_x(L[0], L[1], [(0, 128, wt * 2), (128, 128, wt * 2 + 1)], ep_lat("q"))))

        def after_q():
            for n in range(2):
                rstd_from(ssq2[n], rstd_q, brq, n, 1.0 / 1024)
        for wt in range(2):
            def comp(L, wt=wt):
                if wt == 0:
                    after_q()
                gemm_x(L[0], L[1], [(0, 128, wt * 2), (128, 128, wt * 2 + 1)], ep_lat("kv"))
            jobs.append((lambda wt=wt, p=p: ws.load(w_lat, 0, 32, [(1024 + wt * 256, 256, 0)], 256, key=("ckv", wt), first=(p == 0)), comp))

        def comp_krot(L):
            wbuf, wv = L
            for n in range(2):
                rstd_from(ssq2[n], rstd_kv, brkv, n, 1.0 / 512)
            for m in range(4):
                for n in range(2):
                    cs = slice(n * 512, (n + 1) * 512)
                    P.op("dve", I("tensor_tensor", ckvn[:, m, cs], ckvn[:, m, cs], rstd_kv[:, cs], ALU.mult),
                         reads=[bckvn, brkv], writes=[bckvn])
            for n in range(2):
                aA, aB = acc.next(), acc.next()
                calls = []
                for (a, c0) in ((aA, 0), (aB, 64)):
                    for kc in range(32):
                        calls.append(I("matmul", a.t[0:64, :], wv[:, kc, c0:c0 + 64], xg[:, kc, n * 512:(n + 1) * 512],
                                       start=(kc == 0), stop=(kc == 31)))
                P.op("pe", calls, reads=[wbuf, bxg], writes=[aA.buf, aB.buf])
                rope_store(aA, aB, rstd_x, brx, n, KrT[:, t0 + n * 512:t0 + (n + 1) * 512])
        jobs.append((lambda p=p: ws.load(w_lat, 0, 32, [(1536, 128, 0)], 128, key=("kr", 0), first=(p == 0)), comp_krot))

        for wt in range(4):
            jobs.append((lambda wt=wt, p=p: ws.load(w_z, 0, 32, [(wt * 256, 256, 0)], 256, key=("z", wt), first=(p == 0)),
                         lambda L, wt=wt: gemm_x(L[0], L[1], [(0, 128, wt * 2), (128, 128, wt * 2 + 1)], ep_z)))

        def comp_q(L, h):
            wbuf, wv = L
            for n in range(2):
                cs = slice(n * 512, (n + 1) * 512)
                a0, aA, aB = acc.next(), acc.next(), acc.next()
                calls = []
                for (a, c0, M) in ((a0, 0, 128), (aA, 128, 64), (aB, 192, 64)):
                    for kc in range(8):
                        calls.append(I("matmul", a.t[0:M, :], wv[:, kc, c0:c0 + M], cqg[:, kc, cs], start=(kc == 0), stop=(kc == 7)))
                P.op("pe", calls, reads=[wbuf, bcqg], writes=[a0.buf, aA.buf, aB.buf])
                o = ost.next()
                P.op("dve", I("tensor_tensor", o.t[:], a0.t[:], rstd_q[:, cs], ALU.mult), reads=[a0.buf, brq], writes=[o.buf])
                store(QnT[h, :, t0 + n * 512:t0 + (n + 1) * 512], o, o.t[:])
                rope_store(aA, aB, rstd_q, brq, n, QrT[h, :, t0 + n * 512:t0 + (n + 1) * 512])
        for h in range(HPC):
            jobs.append((lambda h=h, p=p: ws.load(w_uq, 0, 8, [(h * 256, 256, 0)], 256, key=("uq", h), first=(p == 0)), lambda L, h=h: comp_q(L, h)))

        def comp_kv(L, hg):
            wbuf, wv = L
            for hh in range(4):
                h = hg * 4 + hh
                for n in range(2):
                    cs = slice(n * 512, (n + 1) * 512)
                    a = acc.next()
                    P.op("pe", [I("matmul", a.t[:], wv[:, kc, hh * 256:hh * 256 + 128], ckvn[:, kc, cs], start=(kc == 0), stop=(kc == 3))
                                for kc in range(4)], reads=[wbuf, bckvn], writes=[a.buf])
                    o = ost.next()
                    P.op("act", I("activation", o.t[:], a.t[:], AF.Copy), reads=[a.buf], writes=[o.buf])
                    store(KnT[h, :, t0 + n * 512:t0 + (n + 1) * 512], o, o.t[:])
            wv4 = wv.rearrange("p k (h c) -> p k h c", c=256)
            for tb in range(NT // 128):
                a = acc.next()
                av = a.t[:].rearrange("p (h c) -> p h c", c=128)
                P.op("pe", [I("matmul", av, ckvn[:, kc, tb * 128:(tb + 1) * 128], wv4[:, kc, :, 128:256], start=(kc == 0), stop=(kc == 3))
                            for kc in range(4)], reads=[wbuf, bckvn], writes=[a.buf])
                o = ost.next()
                P.op("dve", I("tensor_copy", o.t[:], a.t[:]), reads=[a.buf], writes=[o.buf])
                blk = (t0 + tb * 128) // 128
                store(Vh[hg * 4:(hg + 1) * 4, :, blk, :].rearrange("h p d -> p h d"), o, o.t[:].rearrange("p (h d) -> p h d", d=128))
        for hg in range(HPC // 4):
            jobs.append((lambda hg=hg, p=p: ws.load(w_ukv, 0, 4, [(hg * 1024, 1024, 0)], 1024, key=("ukv", hg), first=(p == 0)), lambda L, hg=hg: comp_kv(L, hg)))
        pipeline(jobs)
    P.end_stage()


def stage_b(nc, t):
    QnT, QrT, KnT, KrT, Vh, SZT, trid, OGT = t["QnT"], t["QrT"], t["KnT"], t["KrT"], t["Vh"], t["SZT"], t["tri"], t["OGT"]
    P = Prog(nc)
    scale = 1.0 / math.sqrt(192.0)
    ones = P.sbuf("ones", [128, 128], BF16)
    tri = P.sbuf("tri_sb", [128, 128], BF16)
    bconst = Buf("const")
    P.op("dve", I("memset", ones[:], 1.0), writes=[bconst])
    P.dma(tri[:], trid, writes=[bconst])
    kr = P.slot("kr", [64, S], BF16)
    P.dma(kr.t[:], KrT, writes=[kr.buf])
    kn = P.ring("kn", 2, [128, S], BF16)
    vv = P.ring("vv", 2, [128, S // 128, 128], BF16)
    qn = P.ring("qn", 3, [128, 512], BF16)
    qr = P.ring("qr", 3, [64, 512], BF16)
    sz = P.ring("sz", 3, [128, 512], BF16)
    pT = P.ring("pT", 3, [128, 512], BF16)
    rs = P.ring("rs", 2, [128, 512], F32)
    o32 = P.ring("o32", 2, [128, 512], F32)
    og = P.ring("og", 2, [128, 512], BF16)
    sT = P.pring("sT", 3)
    oT = P.pring("oT", 2)
    sm = P.pring("sm", 2)
    out_toks = []

    def load_head(h):
        k = kn.next()
        v = vv.next()
        P.dma(k.t[:], KnT[h], writes=[k.buf])
        P.dma(v.t[:], Vh[h], writes=[v.buf])
        return k, v

    def load_q(h, t):
        a, b_, c = qn.next(), qr.next(), sz.next()
        cs = slice(t * 512, (t + 1) * 512)
        P.dma(a.t[:], QnT[h, :, cs], writes=[a.buf])
        P.dma(b_.t[:], QrT[h, :, cs], writes=[b_.buf])
        P.dma(c.t[:], SZT[h * 128:(h + 1) * 128, cs], writes=[c.buf])
        return a, b_, c

    NQT = S // 512
    tiles = [(h, t) for h in range(HPC) for t in range(NQT)]
    blocks = []
    for ti, (h, t) in enumerate(tiles):
        nb = 4 * t + 4
        for kb in range(nb):
            blocks.append((ti, kb, nb))
    state = {}

    def tile_ctx(ti):
        if ti in state:
            return state[ti]
        h, t = tiles[ti]
        if t == 0:
            hk = load_head(h)
        else:
            hk = state[ti - 1]["hk"]
        q = load_q(h, t)
        ctx = {"hk": hk, "q": q, "oT": oT.next(), "sm": sm.next()}
        state[ti] = ctx
        return ctx

    def emit_S(bi):
        ti, kb, nb = blocks[bi]
        h, t = tiles[ti]
        ctx = tile_ctx(ti)
        k, v = ctx["hk"]
        a, b_, c = ctx["q"]
        j = kb - 4 * t
        c0 = 128 * j if j > 0 else 0
        s = sT.next()
        ks = slice(kb * 128, (kb + 1) * 128)
        P.op("pe", [I("matmul", s.t[:, c0:512], k.t[:, ks], a.t[:, c0:512], start=True, stop=False),
                    I("matmul", s.t[:, c0:512], kr.t[:, ks], b_.t[:, c0:512], start=False, stop=True)],
             reads=[k.buf, kr.buf, a.buf, b_.buf], writes=[s.buf])
        return s, c0, j

    pend = {}
    nB = len(blocks)
    AHEAD = 2
    for bi in range(min(AHEAD, nB)):
        pend[bi] = emit_S(bi)
    for bi in range(nB):
        ti, kb, nb = blocks[bi]
        h, t = tiles[ti]
        ctx = state[ti]
        k, v = ctx["hk"]
        s, c0, j = pend.pop(bi)
        p = pT.next()
        P.op("act", I("activation", p.t[:, c0:512], s.t[:, c0:512], AF.Exp, scale=scale), reads=[s.buf], writes=[p.buf])
        if j >= 0:
            P.op("pool", I("tensor_tensor", p.t[:, c0:c0 + 128], p.t[:, c0:c0 + 128], tri[:], ALU.mult),
                 reads=[p.buf, bconst], writes=[p.buf])
        o_, m_ = ctx["oT"], ctx["sm"]
        P.op("pe", [I("matmul", o_.t[:, c0:512], v.t[:, kb, :], p.t[:, c0:512], start=(kb == 0), stop=(kb == nb - 1)),
                    I("matmul", m_.t[:, c0:512], ones[:], p.t[:, c0:512], start=(kb == 0), stop=(kb == nb - 1))],
             reads=[v.buf, p.buf, bconst], writes=[o_.buf, m_.buf])
        if bi + AHEAD < nB:
            pend[bi + AHEAD] = emit_S(bi + AHEAD)
        if kb == nb - 1:
            a, b_, c = ctx["q"]
            r_ = rs.next()
            P.op("dve", I("reciprocal", r_.t[:], m_.t[:]), reads=[m_.buf], writes=[r_.buf])
            x_ = o32.next()
            P.op("dve", I("tensor_tensor", x_.t[:], o_.t[:], r_.t[:], ALU.mult), reads=[o_.buf, r_.buf], writes=[x_.buf])
            g_ = og.next()
            P.op("dve", I("tensor_tensor", g_.t[:], x_.t[:], c.t[:], ALU.mult), reads=[x_.buf, c.buf], writes=[g_.buf])
            out_toks.append(P.dma(OGT[h * 128:(h + 1) * 128, t * 512:(t + 1) * 512], g_.t[:], reads=[g_.buf],
                                  owner=g_.buf, queue="pool"))
            if ti + 1 < len(tiles):
                tile_ctx(ti + 1)

    P.end_stage()


def stage_e(nc, t):
    qT, kT, gT, v2, szT, gnd, m2d, idd, smd, OGT = (t["q1T"], t["k1T"], t["g1T"], t["v2"], t["sz1T"], t["gn"], t["mask2"],
                                                    t["ident"], t["scanmask"], t["OG1T"])
    P = Prog(nc)
    bconst = Buf("const")
    ones = P.sbuf("ones", [128, 128], BF16)
    epsc = P.sbuf("epsc", [128, 1], F32)
    gn = P.sbuf("gns", [128, 1], F32)
    mask2 = P.sbuf("mask2s", [128, 128], BF16)
    ident = P.sbuf("idents", [128, 128], BF16)
    scanm = P.sbuf("scanms", [128, 512], F32)
    P.op("dve", I("memset", ones[:], 1.0), writes=[bconst])
    P.op("dve", I("memset", epsc[:], EPS), writes=[bconst])
    P.dma(gn[:], gnd, writes=[bconst])
    P.dma(mask2[:], m2d, writes=[bconst])
    P.dma(ident[:], idd, writes=[bconst])
    P.dma(scanm[:], smd, writes=[bconst])

    H = HPC
    S32 = [P.slot(f"S32_{h}", [128, 128], F32) for h in range(H)]
    Sbf = [P.slot(f"Sbf_{h}", [128, 128], BF16) for h in range(H)]
    for h in range(H):
        P.op("dve", I("memset", S32[h].t[:], 0.0), writes=[S32[h].buf])
        P.op("pool", I("memset", Sbf[h].t[:], 0.0), writes=[Sbf[h].buf])
    Qh = [P.slot(f"Qh_{h}", [128, 512], BF16) for h in range(H)]
    Kht = [P.slot(f"Kht_{h}", [128, 512], BF16) for h in range(H)]
    vt = [P.slot(f"vt_{h}", [128, 4, 128], BF16) for h in range(H)]
    o32 = [P.slot(f"o32_{h}", [128, 512], F32) for h in range(H)]
    dec = [P.slot(f"dec_{h}", [128, 8], F32) for h in range(H)]
    szs = [P.slot(f"szs_{h}", [128, 512], BF16) for h in range(H)]
    qin = P.ring("qin", 2, [128, 512], BF16)
    kin = P.ring("kin", 2, [128, 512], BF16)
    gin = P.ring("gin", 2, [128, 512], F32)
    bt = P.ring("bt", 2, [128, 512], F32)
    d1 = P.ring("d1_", 2, [128, 512], F32)
    d4 = P.ring("d4_", 2, [128, 512], F32)
    E = P.ring("E_", 4, [128, 512], F32)
    Qt = P.ring("Qt", 2, [128, 512], BF16)
    Kt = P.ring("Kt", 2, [128, 512], BF16)
    KhT = P.ring("KhT", 2, [128, 512], BF16)
    attm = P.ring("attm", 2, [128, 512], BF16)
    sqb = P.ring("sqb", 2, [128, 512], BF16)
    sd = P.ring("sd", 2, [128, 512], F32)
    y32 = P.ring("y32", 2, [128, 512], F32)
    ogs = P.ring("ogs", 2, [128, 512], BF16)
    trp = Slot(P.psum("trp", [128, 1024], BF16), "trp")
    attp = Slot(P.psum("attp", [128, 512]), "attp")
    oIp = Slot(P.psum("oIp", [128, 512]), "oIp")
    oXp = P.pring("oXp", 2)
    dSp = [Slot(P.psum(f"dSp{i}", [128, 512]), f"dSp{i}") for i in range(2)]
    ssqp = Slot(P.psum("ssqp", [128, 512]), "ssqp")
    out_toks = []

    NSB = S // 512
    for sb in range(NSB):
        cs = slice(sb * 512, (sb + 1) * 512)
        for h in range(H):
            qi, ki, gi = qin.next(), kin.next(), gin.next()
            P.dma(qi.t[:], qT[h, :, cs], writes=[qi.buf])
            P.dma(ki.t[:], kT[h, :, cs], writes=[ki.buf])
            P.dma(gi.t[:], gT[h, :, cs], writes=[gi.buf])
            P.dma(vt[h].t[:], v2[h, :, sb * 4:(sb + 1) * 4, :], writes=[vt[h].buf])
            P.dma(szs[h].t[:], szT[h * 128:(h + 1) * 128, cs], writes=[szs[h].buf])
            b_ = bt.next()
            P.op("dve", I("tensor_tensor_scan", b_.t[:], scanm[:], gi.t[:], 0.0, ALU.mult, ALU.add),
                 reads=[gi.buf, bconst], writes=[b_.buf])
            b3 = b_.t[:].rearrange("p (n c) -> p n c", c=64)
            x1, x4 = d1.next(), d4.next()
            P.op("dve", I("tensor_tensor", x1.t[:].rearrange("p (n c) -> p n c", c=64), b3,
                          b3[:, :, 32:33].broadcast_to([128, 8, 64]), ALU.subtract), reads=[b_.buf], writes=[x1.buf])
            P.op("pool", I("tensor_tensor", x4.t[:].rearrange("p (n c) -> p n c", c=64), b3,
                           b3[:, :, 63:64].broadcast_to([128, 8, 64]), ALU.subtract), reads=[b_.buf], writes=[x4.buf])
            e1, e2, e3, e4 = E.next(), E.next(), E.next(), E.next()
            P.op("act", I("activation", e1.t[:], x1.t[:], AF.Exp), reads=[x1.buf], writes=[e1.buf])
            P.op("act", I("activation", e2.t[:], x1.t[:], AF.Exp, scale=-1.0), reads=[x1.buf], writes=[e2.buf])
            P.op("act", I("activation", e3.t[:], b_.t[:], AF.Exp), reads=[b_.buf], writes=[e3.buf])
            P.op("act", I("activation", e4.t[:], x4.t[:], AF.Exp, scale=-1.0), reads=[x4.buf], writes=[e4.buf])
            P.op("act", I("activation", dec[h].t[:], b3[:, :, 63], AF.Exp), reads=[b_.buf], writes=[dec[h].buf])
            qt_, kt_, kh_ = Qt.next(), Kt.next(), KhT.next()
            P.op("dve", I("tensor_tensor", qt_.t[:], qi.t[:], e1.t[:], ALU.mult), reads=[qi.buf, e1.buf], writes=[qt_.buf])
            P.op("pool", I("tensor_tensor", kt_.t[:], ki.t[:], e2.t[:], ALU.mult), reads=[ki.buf, e2.buf], writes=[kt_.buf])
            P.op("dve", I("tensor_tensor", Qh[h].t[:], qi.t[:], e3.t[:], ALU.mult), reads=[qi.buf, e3.buf], writes=[Qh[h].buf])
            P.op("pool", I("tensor_tensor", kh_.t[:], ki.t[:], e4.t[:], ALU.mult), reads=[ki.buf, e4.buf], writes=[kh_.buf])
            P.op("pe", [I("transpose", trp.t[:, pr * 128:(pr + 1) * 128], kh_.t[:, pr * 128:(pr + 1) * 128], ident[:]) for pr in range(4)],
                 reads=[kh_.buf, bconst], writes=[trp.buf])
            P.op("act", I("activation", Kht[h].t[:], trp.t[:, 0:512], AF.Copy), reads=[trp.buf], writes=[Kht[h].buf])
            P.op("pe", [I("matmul", attp.t[:, pr * 128:(pr + 1) * 128], kt_.t[:, pr * 128:(pr + 1) * 128], qt_.t[:, pr * 128:(pr + 1) * 128],
                          start=True, stop=True) for pr in range(4)], reads=[kt_.buf, qt_.buf], writes=[attp.buf])
            am = attm.next()
            P.op("dve", I("tensor_tensor", am.t[:].rearrange("p (n c) -> p n c", c=128), attp.t[:].rearrange("p (n c) -> p n c", c=128),
                          mask2[:].rearrange("p (o c) -> p o c", o=1).broadcast_to([128, 4, 128]), ALU.mult),
                 reads=[attp.buf, bconst], writes=[am.buf])
            P.op("pe", [I("matmul", oIp.t[:, pr * 128:(pr + 1) * 128], vt[h].t[:, pr, :], am.t[:, pr * 128:(pr + 1) * 128],
                          start=True, stop=True) for pr in range(4)], reads=[vt[h].buf, am.buf], writes=[oIp.buf])
            P.op("act", I("activation", o32[h].t[:], oIp.t[:], AF.Copy), reads=[oIp.buf], writes=[o32[h].buf])
        for pr in range(4):
            for ch in range(2):
                c0 = pr * 128 + ch * 64
                ps = slice(ch * 64, ch * 64 + 64)
                ox = oXp.next()
                for h in range(H):
                    P.op("pe", I("matmul", ox.t[:, h * 64:(h + 1) * 64], Sbf[h].t[:], Qh[h].t[:, c0:c0 + 64], start=True, stop=True),
                         reads=[Sbf[h].buf, Qh[h].buf], writes=[ox.buf])
                for h in range(H):
                    dsp = dSp[h // 4]
                    P.op("pe", I("matmul", dsp.t[:, (h % 4) * 128:(h % 4 + 1) * 128], Kht[h].t[ps, pr * 128:(pr + 1) * 128],
                                 vt[h].t[ps, pr, :], start=True, stop=True),
                         reads=[Kht[h].buf, vt[h].buf], writes=[dsp.buf])
                for h in range(H):
                    P.op("dve", I("tensor_tensor", o32[h].t[:, c0:c0 + 64], ox.t[:, h * 64:(h + 1) * 64], o32[h].t[:, c0:c0 + 64], ALU.add),
                         reads=[ox.buf, o32[h].buf], writes=[o32[h].buf])
                for h in range(H):
                    dsp = dSp[h // 4]
                    P.op("dve", I("scalar_tensor_tensor", S32[h].t[:], S32[h].t[:], dec[h].t[:, pr * 2 + ch:pr * 2 + ch + 1],
                                  dsp.t[:, (h % 4) * 128:(h % 4 + 1) * 128], ALU.mult, ALU.add),
                         reads=[S32[h].buf, dec[h].buf, dsp.buf], writes=[S32[h].buf])
                    P.op("pool", I("tensor_copy", Sbf[h].t[:], S32[h].t[:]), reads=[S32[h].buf], writes=[Sbf[h].buf])
        for h in range(H):
            sq = sqb.next()
            P.op("act", I("activation", sq.t[:], o32[h].t[:], AF.Square), reads=[o32[h].buf], writes=[sq.buf])
            P.op("pe", I("matmul", ssqp.t[:], ones[:], sq.t[:], start=True, stop=True), reads=[sq.buf, bconst], writes=[ssqp.buf])
            s_ = sd.next()
            P.op("act", I("activation", s_.t[:], ssqp.t[:], AF.Sqrt, bias=epsc[:, 0:1], scale=1.0 / 128), reads=[ssqp.buf, bconst], writes=[s_.buf])
            P.op("dve", I("reciprocal", s_.t[:], s_.t[:]), reads=[s_.buf], writes=[s_.buf])
            y = y32.next()
            P.op("dve", I("scalar_tensor_tensor", y.t[:], o32[h].t[:], gn[:, 0:1], s_.t[:], ALU.mult, ALU.mult),
                 reads=[o32[h].buf, s_.buf, bconst], writes=[y.buf])
            g_ = ogs.next()
            P.op("pool", I("tensor_tensor", g_.t[:], y.t[:], szs[h].t[:], ALU.mult), reads=[y.buf, szs[h].buf], writes=[g_.buf])
            out_toks.append(P.dma(OGT[h * 128:(h + 1) * 128, cs], g_.t[:], reads=[g_.buf], owner=g_.buf, queue="pool"))
    P.end_stage()


def stage_gather(nc, src_ap, dst_ap, rows):
    P = Prog(nc)
    R = src_ap.shape[0]
    for j in range(R // rows):
        P.op("pool", I("collective_compute", "AllGather", ALU.bypass, replica_groups=GROUPS,
                       ins=[src_ap[j * rows:(j + 1) * rows, :]], outs=[dst_ap[j]]))
    P.end_stage()


def stage_wo(nc, t, final):
    P = Prog(nc)
    NT = NTP
    gOG, xT, w_o, outT = t["gOG"], t["xT"], t["w_o"], t["outT"]
    rcache = {}

    def rank_of(e):
        if "r" not in rcache:
            rcache["r"] = e.partition_id() % 4
        return rcache["r"]
    ws = WStream(P, nstage=2, nw=2)
    bconst = Buf("const")
    ones = P.sbuf("ones", [128, 128], BF16)
    epsc = P.sbuf("epsc", [128, 1], F32)
    P.op("dve", I("memset", ones[:], 1.0), writes=[bconst])
    P.op("dve", I("memset", epsc[:], EPS), writes=[bconst])
    if final:
        fg = P.sbuf("fgs", [128, 32], F32)
        P.dma(fg[:], t["fg"], writes=[bconst])
        x2T = t["x2T"]
    else:
        g1 = P.sbuf("g1s", [128, 32], F32)
        P.dma(g1[:], t["g1"], writes=[bconst])
        xg1T, rstdT = t["xg1T"], t["rstdT"]
        ostw = P.ring("ostw", 3, [128, 512], BF16)
    ogs2 = [P.slot(f"og{i}", [128, 32, NT], BF16) for i in range(2)]
    xs = P.ring("xs", 3, [128, 512], F32)
    t32 = P.ring("t32_", 3, [128, 512], F32)
    sq2 = P.ring("sq2_", 2, [128, 512], BF16)
    acc = P.pring("acc", 4)
    ssq = [Slot(P.psum(f"ssq{i}", [128, 512]), f"ssq{i}") for i in range(2)]
    rstd = P.sbuf("rstd", [128, NT], F32)
    brstd = Buf("rstd")
    bscr = Buf("x2scratch")

    ogown = t["ogown"]
    bown = Buf("ogown")
    g4 = gOG.rearrange("j (g i) s -> j g i s", i=64)
    for q in range(4):
        P.dma(ogown[q * 1024:(q + 1) * 1024, :].rearrange("(j i) s -> j i s", i=64),
              Lazy(lambda e, q=q: g4[:, q, :, bass.ds(rank_of(e) * TOK, TOK)]), writes=[bown])
    def load_og(p):
        o_ = ogs2[p % 2]
        for kc in range(32):
            P.dma(o_.t[:, kc, :], ogown[kc * 128:(kc + 1) * 128, p * NT:(p + 1) * NT], reads=[bown], writes=[o_.buf])

    load_og(0)
    for p in range(TOK // NT):
        t0 = p * NT
        og = ogs2[p % 2]
        if p + 1 < TOK // NT:
            load_og(p + 1)

        def comp(L, wt):
            wbuf, wv = L
            for mi in range(2):
                m = wt * 2 + mi
                for n in range(2):
                    a = acc.next()
                    P.op("pe", [I("matmul", a.t[:], wv[:, kc, mi * 128:(mi + 1) * 128], og.t[:, kc, n * 512:(n + 1) * 512],
                                  start=(kc == 0), stop=(kc == 31)) for kc in range(32)], reads=[wbuf, og.buf], writes=[a.buf])
                    x_ = xs.next()
                    dsl = (slice(m * 128, (m + 1) * 128), slice(t0 + n * 512, t0 + (n + 1) * 512))
                    P.dma(x_.t[:], xT[dsl[0], dsl[1]], writes=[x_.buf])
                    tt = t32.next()
                    P.op("dve", I("tensor_tensor", tt.t[:], a.t[:], x_.t[:], ALU.add), reads=[a.buf, x_.buf], writes=[tt.buf])
                    if not final:
                        P.dma(outT[dsl[0], dsl[1]], tt.t[:], reads=[tt.buf], owner=tt.buf, queue="pool")
                        s2 = sq2.next()
                        P.op("act", I("activation", s2.t[:], tt.t[:], AF.Square), reads=[tt.buf], writes=[s2.buf])
                        P.op("pe", I("matmul", ssq[n].t[:], ones[:], s2.t[:], start=(m == 0), stop=(m == 31)),
                             reads=[s2.buf, bconst], writes=[ssq[n].buf])
                        o = ostw.next()
                        P.op("pool", I("tensor_scalar", o.t[:], tt.t[:], g1[:, m:m + 1], None, ALU.mult),
                             reads=[tt.buf, bconst], writes=[o.buf])
                        P.dma(xg1T[dsl[0], dsl[1]], o.t[:], reads=[o.buf], owner=o.buf, queue="pool")
                    else:
                        P.dma(x2T[dsl[0], dsl[1]], tt.t[:], reads=[tt.buf], writes=[bscr], owner=bscr, queue="pool")
                        s2 = sq2.next()
                        P.op("act", I("activation", s2.t[:], tt.t[:], AF.Square), reads=[tt.buf], writes=[s2.buf])
                        P.op("pe", I("matmul", ssq[n].t[:], ones[:], s2.t[:], start=(m == 0), stop=(m == 31)),
                             reads=[s2.buf, bconst], writes=[ssq[n].buf])
        jobs = [(lambda wt=wt, p=p: ws.load(w_o, 0, 32, [(wt * 256, 256, 0)], 256, key=("wo", wt), first=(p == 0)), lambda L, wt=wt: comp(L, wt)) for wt in range(16)]
        pipeline(jobs)
        if not final:
            for n in range(2):
                tt = t32.next()
                P.op("act", I("activation", tt.t[:], ssq[n].t[:], AF.Sqrt, bias=epsc[:, 0:1], scale=1.0 / D),
                     reads=[ssq[n].buf, bconst], writes=[tt.buf])
                P.op("dve", I("reciprocal", tt.t[:], tt.t[:]), reads=[tt.buf], writes=[tt.buf])
                P.dma(rstdT[:, t0 + n * 512:t0 + (n + 1) * 512], tt.t[:], reads=[tt.buf], owner=tt.buf, queue="pool")
        if final:
            for n in range(2):
                tt = t32.next()
                P.op("act", I("activation", tt.t[:], ssq[n].t[:], AF.Sqrt, bias=epsc[:, 0:1], scale=1.0 / D),
                     reads=[ssq[n].buf, bconst], writes=[tt.buf])
                P.op("dve", I("reciprocal", rstd[:, n * 512:(n + 1) * 512], tt.t[:]), reads=[tt.buf], writes=[brstd])
            for m in range(32):
                for n in range(2):
                    cs = slice(n * 512, (n + 1) * 512)
                    dsl = (slice(m * 128, (m + 1) * 128), slice(t0 + n * 512, t0 + (n + 1) * 512))
                    x_ = xs.next()
                    P.dma(x_.t[:], x2T[dsl[0], dsl[1]], reads=[bscr], writes=[x_.buf])
                    tt = t32.next()
                    P.op("dve", I("scalar_tensor_tensor", tt.t[:], x_.t[:], fg[:, m:m + 1], rstd[:, cs], ALU.mult, ALU.mult),
                         reads=[x_.buf, brstd, bconst], writes=[tt.buf])
                    P.dma(outT[dsl[0], dsl[1]], tt.t[:], reads=[tt.buf], owner=tt.buf, queue="pool")
    P.end_stage()


def stage_d(nc, t):
    P = Prog(nc)
    NT = NTP
    gXg, gRs, w1 = t["gXg"], t["gRs"], t["w1"]
    q1T, k1T, g1T, v2, sz1T = t["q1T"], t["k1T"], t["g1T"], t["v2"], t["sz1T"]
    ws = WStream(P, nstage=2, nw=2, stage_elems=1024)
    bconst = Buf("const")
    ones = P.sbuf("ones", [128, 128], BF16)
    epsc = P.sbuf("epsc", [128, 1], F32)
    g1 = P.sbuf("g1s", [128, 32], F32)
    lbs = P.sbuf("lbss", [128, 2, HPC], F32)
    lb = P.sbuf("lb", [128, HPC], F32)
    oml = P.sbuf("oml", [128, HPC], F32)
    dlb = P.sbuf("dlb", [128, HPC], F32)
    ident = P.sbuf("identd", [128, 128], BF16)
    P.op("dve", I("memset", ones[:], 1.0), writes=[bconst])
    P.op("dve", I("memset", epsc[:], EPS), writes=[bconst])
    P.dma(g1[:], t["g1"], writes=[bconst])
    P.dma(lbs[:], t["lbs"], writes=[bconst])
    P.dma(ident[:], t["ident"], writes=[bconst])
    P.op("dve", I("tensor_tensor", dlb[:], lbs[:, 1, :], lbs[:, 0, :], ALU.subtract), reads=[bconst], writes=[bconst])
    P.op("act", I("activation", dlb[:], dlb[:], AF.Exp, scale=-1.0), reads=[bconst], writes=[bconst])
    P.op("dve", I("tensor_scalar", lb[:], dlb[:], 1.0, None, ALU.add), reads=[bconst], writes=[bconst])
    P.op("dve", I("reciprocal", lb[:], lb[:]), reads=[bconst], writes=[bconst])
    P.op("dve", I("tensor_tensor", oml[:], dlb[:], lb[:], ALU.mult), reads=[bconst], writes=[bconst])

    xgs = [P.slot(f"xg{i}", [128, 32, NT], BF16) for i in range(2)]
    rsx = [P.slot(f"rstdx{i}", [128, NT], F32) for i in range(2)]
    t32 = P.ring("t32_", 3, [128, 512], F32)
    e32 = P.ring("e32_", 2, [128, 512], F32)
    s32 = P.ring("s32_", 2, [128, 512], F32)
    f32r = P.ring("f32_", 2, [128, 512], F32)
    gout = P.ring("gout", 2, [128, 512], F32)
    ost = P.ring("ost", 4, [128, 512], BF16)
    vtok = P.ring("vtok", 2, [128, 512], BF16)
    acc = P.pring("acc", 4)
    ssq = [Slot(P.psum(f"ssq{i}", [128, 512]), f"ssq{i}") for i in range(2)]
    trp = Slot(P.psum("trpd", [128, 1024], BF16), "trpd")

    def store(dst_ap, src_slot, src_ap):
        P.dma(dst_ap, src_ap, reads=[src_slot.buf], owner=src_slot.buf, queue="act")

    def load_pass(p):
        rk, c0 = p // 2, (p % 2) * NT
        xs_, rs_ = xgs[p % 2], rsx[p % 2]
        for kc in range(32):
            P.dma(xs_.t[:, kc, :], gXg[kc // 2, rk * 256 + (kc % 2) * 128:rk * 256 + (kc % 2) * 128 + 128, c0:c0 + NT],
                  writes=[xs_.buf])
        P.dma(rs_.t[:, :], gRs[0, rk * 128:(rk + 1) * 128, c0:c0 + NT], writes=[rs_.buf])

    load_pass(0)
    for p in range(S // NT):
        t0 = p * NT
        xg, bxg = xgs[p % 2].t, xgs[p % 2].buf
        rstd_x, brx = rsx[p % 2].t, rsx[p % 2].buf
        if p + 1 < S // NT:
            load_pass(p + 1)

        def comp(L, wt):
            wbuf, wv = L
            kind = wt // 4
            for mi in range(2):
                mm = (wt % 4) * 2 + mi
                for n in range(2):
                    cs = slice(n * 512, (n + 1) * 512)
                    dsl = (slice(mm * 128, (mm + 1) * 128), slice(t0 + n * 512, t0 + (n + 1) * 512))
                    a = acc.next()
                    P.op("pe", [I("matmul", a.t[:], wv[:, kc, mi * 128:(mi + 1) * 128], xg[:, kc, cs], start=(kc == 0), stop=(kc == 31))
                                for kc in range(32)], reads=[wbuf, bxg], writes=[a.buf])
                    tt = t32.next()
                    P.op("dve", I("tensor_tensor", tt.t[:], a.t[:], rstd_x[:, cs], ALU.mult), reads=[a.buf, brx], writes=[tt.buf])
                    if kind == 0 or kind == 3:
                        o = ost.next()
                        P.op("act", I("activation", o.t[:], tt.t[:], AF.Silu), reads=[tt.buf], writes=[o.buf])
                        store((q1T if kind == 0 else sz1T)[dsl[0], dsl[1]], o, o.t[:])
                    elif kind == 2:
                        o = ost.next()
                        P.op("act", I("activation", o.t[:], tt.t[:], AF.Copy), reads=[tt.buf], writes=[o.buf])
                        P.op("pe", [I("transpose", trp.t[:, j * 128:(j + 1) * 128], o.t[:, j * 128:(j + 1) * 128], ident[:]) for j in range(4)],
                             reads=[o.buf, bconst], writes=[trp.buf])
                        vt_ = vtok.next()
                        P.op("dve", I("tensor_copy", vt_.t[:], trp.t[:, 0:512]), reads=[trp.buf], writes=[vt_.buf])
                        blk0 = (t0 + n * 512) // 128
                        store(v2[mm, :, blk0:blk0 + 4, :], vt_, vt_.t[:].rearrange("p (b d) -> p b d", d=128))
                    else:
                        e = e32.next()
                        P.op("act", I("activation", e.t[:], tt.t[:], AF.Exp, scale=-1.0), reads=[tt.buf], writes=[e.buf])
                        s_ = s32.next()
                        P.op("pool", I("tensor_scalar", s_.t[:], e.t[:], 1.0, None, ALU.add), reads=[e.buf], writes=[s_.buf])
                        P.op("dve", I("reciprocal", s_.t[:], s_.t[:]), reads=[s_.buf], writes=[s_.buf])
                        o = ost.next()
                        P.op("dve", I("scalar_tensor_tensor", o.t[:], e.t[:], oml[:, mm:mm + 1], s_.t[:], ALU.mult, ALU.mult),
                             reads=[e.buf, s_.buf, bconst], writes=[o.buf])
                        store(k1T[dsl[0], dsl[1]], o, o.t[:])
                        f_ = f32r.next()
                        P.op("pool", I("tensor_scalar", f_.t[:], s_.t[:], oml[:, mm:mm + 1], lb[:, mm:mm + 1], ALU.mult, ALU.add),
                             reads=[s_.buf, bconst], writes=[f_.buf])
                        g_ = gout.next()
                        P.op("act", I("activation", g_.t[:], f_.t[:], AF.Ln), reads=[f_.buf], writes=[g_.buf])
                        store(g1T[dsl[0], dsl[1]], g_, g_.t[:])
        jobs = [(lambda wt=wt, p=p: ws.load(w1, 0, 32, [(wt * 256, 256, 0)], 256, key=("w1", wt), first=(p == 0)), lambda L, wt=wt: comp(L, wt)) for wt in range(16)]
        pipeline(jobs)
    P.end_stage()


def build_fused():
    nc = new_nc()

    def ext(name, shape, dt):
        return nc.dram_tensor(name, list(shape), dt, kind="ExternalInput").ap()

    def internal(name, shape, dt):
        return nc.dram_tensor(name, list(shape), dt).ap()

    t = {}
    t["xTall"] = ext("xTall", [D, S], F32)
    t["pos"] = ext("pos", [S], I32)
    t["w_lat"] = ext("w_lat", [D, 1664], F32)
    t["w_z"] = ext("w_z", [D, 1024], F32)
    t["w_uq"] = ext("w_uq", [1024, HPC * 256], F32)
    t["w_ukv"] = ext("w_ukv", [512, HPC * 256], F32)
    t["g0"] = ext("g0", [128, 32], F32)
    t["qg"] = ext("qg", [128, 8], F32)
    t["kvg"] = ext("kvg", [128, 4], F32)
    t["ropec"] = ext("ropec", [64, 2], F32)
    t["tri"] = ext("tri", [128, 128], BF16)
    t["xTown"] = ext("xTown", [D, TOK], F32)
    t["w_o0"] = ext("w_o0", [D, D], F32)
    t["w1"] = ext("w1", [D, 4 * HPC * 128], F32)
    t["g1"] = ext("g1", [128, 32], F32)
    t["lbs"] = ext("lbs", [128, 2, HPC], F32)
    t["gn"] = ext("gn", [128, 1], F32)
    t["mask2"] = ext("mask2", [128, 128], BF16)
    t["ident"] = ext("ident", [128, 128], BF16)
    t["scanmask"] = ext("scanmask", [128, 512], F32)
    t["w_o1"] = ext("w_o1", [D, D], F32)
    t["fg"] = ext("fg", [128, 32], F32)
    outT = nc.dram_tensor("outT", [D, TOK], F32, kind="ExternalOutput").ap()

    t["QnT"] = internal("QnT_i", [HPC, 128, S], BF16)
    t["QrT"] = internal("QrT_i", [HPC, 64, S], BF16)
    t["KnT"] = internal("KnT_i", [HPC, 128, S], BF16)
    t["KrT"] = internal("KrT_i", [64, S], BF16)
    t["Vh"] = internal("Vh_i", [HPC, 128, S // 128, 128], BF16)
    t["SZT"] = internal("SZT_i", [HPC * 128, S], BF16)
    t["OGT"] = internal("OGT_i", [HPC * 128, S], BF16)
    gOG = internal("gOG_i", [16, 256, S], BF16)
    x1T = internal("x1T_i", [D, TOK], F32)
    xg1T = internal("xg1T_i", [D, TOK], BF16)
    rstdT = internal("rstdT_i", [128, TOK], F32)
    gXg = internal("gXg_i", [16, 1024, TOK], BF16)
    gRs = internal("gRs_i", [1, 512, TOK], F32)
    q1T = internal("q1T_i", [HPC * 128, S], BF16)
    k1T = internal("k1T_i", [HPC * 128, S], BF16)
    g1T = internal("g1T_i", [HPC * 128, S], F32)
    v2 = internal("v2_i", [HPC, 128, S // 128, 128], BF16)
    sz1T = internal("sz1T_i", [HPC * 128, S], BF16)
    OG1T = internal("OG1T_i", [HPC * 128, S], BF16)
    gOG1 = internal("gOG1_i", [16, 256, S], BF16)
    x2T = internal("x2T_i", [D, TOK], F32)

    stage_a(nc, t)
    stage_b(nc, t)
    stage_gather(nc, t["OGT"], gOG, 64)
    ogown0 = internal("ogown0_i", [D, TOK], BF16)
    ogown1 = internal("ogown1_i", [D, TOK], BF16)
    stage_wo(nc, {"gOG": gOG, "xT": t["xTown"], "w_o": t["w_o0"], "outT": x1T, "ogown": ogown0, "g1": t["g1"],
                  "xg1T": xg1T, "rstdT": rstdT}, final=False)
    stage_gather(nc, xg1T, gXg, 256)
    stage_gather(nc, rstdT, gRs, 128)
    stage_d(nc, {"gXg": gXg, "gRs": gRs, "w1": t["w1"], "g1": t["g1"], "lbs": t["lbs"], "ident": t["ident"],
                 "q1T": q1T, "k1T": k1T, "g1T": g1T, "v2": v2, "sz1T": sz1T})
    h3 = lambda ap: ap.rearrange("(h k) s -> h k s", k=128)
    stage_e(nc, {"q1T": h3(q1T), "k1T": h3(k1T), "g1T": h3(g1T), "v2": v2, "sz1T": sz1T, "gn": t["gn"], "mask2": t["mask2"],
                 "ident": t["ident"], "scanmask": t["scanmask"], "OG1T": OG1T})
    stage_gather(nc, OG1T, gOG1, 64)
    stage_wo(nc, {"gOG": gOG1, "xT": x1T, "w_o": t["w_o1"], "outT": outT, "fg": t["fg"], "x2T": x2T, "ogown": ogown1}, final=True)
    return nc


def kernel(x, positions, l0_norm, l0_w_in, l0_q_norm, l0_w_uq, l0_kv_norm, l0_w_ukv, l0_w_o,
           l1_norm, l1_w_in, l1_g_norm, l1_w_o, lower_bounds, final_norm):
    f32 = lambda a: np.ascontiguousarray(np.asarray(a, dtype=np.float32))
    x = np.asarray(x, dtype=np.float32)
    positions = np.asarray(positions, dtype=np.int32)
    l0_w_in, l0_w_uq, l0_w_ukv, l1_w_in = (np.asarray(a, dtype=np.float32) for a in (l0_w_in, l0_w_uq, l0_w_ukv, l1_w_in))
    lower_bounds = np.asarray(lower_bounds, dtype=np.float32)
    w_lat = np.ascontiguousarray(np.concatenate(
        [l0_w_in[:, :1600], l0_w_in[:, 1568:1600], l0_w_in[:, 1536:1568]], axis=1))
    m2, ident, sm = l4_consts()
    shared = {
        "w_lat": w_lat, "g0": chunked_gain(f32(l0_norm)), "qg": chunked_gain(f32(l0_q_norm)), "kvg": chunked_gain(f32(l0_kv_norm)),
        "ropec": rope_consts(), "tri": tri_const(), "w_o0": f32(l0_w_o), "g1": chunked_gain(f32(l1_norm)),
        "gn": f32(l1_g_norm).reshape(128, 1), "mask2": m2, "ident": ident, "scanmask": sm, "w_o1": f32(l1_w_o),
        "fg": chunked_gain(f32(final_norm)),
    }
    per_g = []
    for g in range(4):
        hs = range(g * HPC, (g + 1) * HPC)
        uq = np.concatenate([np.concatenate([l0_w_uq[:, h * 192:h * 192 + 192], l0_w_uq[:, h * 192 + 160:h * 192 + 192],
                                             l0_w_uq[:, h * 192 + 128:h * 192 + 160]], axis=1) for h in hs], axis=1)
        c0, c1 = g * HPC * 128, (g + 1) * HPC * 128
        per_g.append({
            "w_z": np.ascontiguousarray(l0_w_in[:, 1600 + c0:1600 + c1]),
            "w_uq": np.ascontiguousarray(uq),
            "w_ukv": np.ascontiguousarray(l0_w_ukv[:, g * HPC * 256:(g + 1) * HPC * 256]),
            "w1": np.ascontiguousarray(np.concatenate([l1_w_in[:, s_ * D + c0:s_ * D + c1] for s_ in range(4)], axis=1)),
            "lbs": np.ascontiguousarray(lower_bounds[:, c0:c1].reshape(2, HPC, 128).transpose(2, 0, 1)),
        })
    maps = []
    for b in range(B):
        xTall = np.ascontiguousarray(x[b].T)
        posb = np.ascontiguousarray(positions[b])
        for r in range(4):
            m = dict(shared)
            m.update(per_g[r])
            m["xTall"] = xTall
            m["pos"] = posb
            m["xTown"] = np.ascontiguousarray(xTall[:, r * TOK:(r + 1) * TOK])
            maps.append(m)
    nc = build_fused()
    res = run_bass_kernel_spmd(nc, maps, core_ids=list(range(NCORES)))
    out = np.empty((B, S, D), np.float32)
    for c in range(NCORES):
        b, r = divmod(c, 4)
        out[b, r * TOK:(r + 1) * TOK] = np.asarray(res.results[c]["outT"]).T
    return out
```
